# Optimizing a Trainium2 kernel written in Bass

```python
import math
import jax, jax.numpy as jnp
from jax import lax
import numpy as np

D_MODEL = 1024
BATCH = 2
SEQ = 16384
DEPTH = 1
DEC_BATCH = 16
DEC_SEQ = 16
PAST_LEN = 2048

CHUNK = 64
HEAD_DIM = 64
FOX_HEADS = 8
FOX_WIDTH = FOX_HEADS * HEAD_DIM
RWKV_HEADS = 8
RWKV_WIDTH = RWKV_HEADS * HEAD_DIM
MIX_WIDTH = FOX_WIDTH + RWKV_WIDTH
DECAY_LORA = 32
AAA_LORA = 32
GATE_LORA = 96
FOX_COLS = 3 * FOX_WIDTH + FOX_HEADS
RWKV_COLS = 3 * RWKV_WIDTH + DECAY_LORA + AAA_LORA + GATE_LORA
IN_COLS = FOX_COLS + RWKV_COLS
FOX_Q_BLOCK = 128
FOX_BF_INIT = 3.0
N_KEYS = 128
N_EXPERTS = N_KEYS * N_KEYS
PEER_HEADS = 8
PEER_TOPK = 16
PEER_QDIM = 128
PEER_BLOCK = 256
NORM_EPS = 1e-6
LNX_EPS = 64e-5

kernel_name = 'hymba_fox_rwkv7_peer_stream_step'

F32 = jnp.float32


def rms_norm(x, g):
    xf = x.astype(F32)
    y = xf * lax.rsqrt(jnp.mean(xf * xf, axis=-1, keepdims=True) + NORM_EPS)
    return (y * g.astype(F32)).astype(x.dtype)


def fox_attend(q, k, v, logf, q_off):
    B, T, H, dh = q.shape
    L = k.shape[1]
    C = jnp.cumsum(logf.astype(F32), axis=1)
    C_bhk = jnp.transpose(C, (0, 2, 1))
    kf = k.astype(F32)
    vf = v.astype(F32)
    qb = min(FOX_Q_BLOCK, T)
    nb = T // qb
    qs = jnp.transpose(q.reshape(B, nb, qb, H, dh), (1, 0, 2, 3, 4))
    key_pos = jnp.arange(L)
    scale = 1.0 / math.sqrt(dh)

    def one_block(args):
        qblk, bi = args
        start = q_off + bi * qb
        q_pos = start + jnp.arange(qb)
        Cq = jnp.transpose(lax.dynamic_slice_in_dim(C, start, qb, axis=1), (0, 2, 1))
        s = jnp.einsum('bqhd,bkhd->bhqk', qblk.astype(F32), kf) * scale
        s = s + Cq[..., :, None] - C_bhk[:, :, None, :]
        s = jnp.where(key_pos[None, None, None, :] <= q_pos[None, None, :, None], s, -jnp.inf)
        p = jax.nn.softmax(s, axis=-1)
        return jnp.einsum('bhqk,bkhd->bqhd', p, vf)

    out = lax.map(one_block, (qs, jnp.arange(nb)))
    return jnp.transpose(out, (1, 0, 2, 3, 4)).reshape(B, T, H, dh).astype(q.dtype)


def rwkv_time_mix(p, shift0, S0, mu, w0, w2, a0, a2, g2, k_k, k_a, r_k, lnx_w, lnx_b):
    B, T, _ = p.shape
    W = RWKV_WIDTH
    prev = jnp.concatenate([shift0.astype(p.dtype), p[:, :-1]], axis=1)
    ps = p + mu * (prev - p)
    r, k, v = ps[..., :W], ps[..., W:2 * W], ps[..., 2 * W:3 * W]
    o = 3 * W
    wl = ps[..., o:o + DECAY_LORA]
    al = ps[..., o + DECAY_LORA:o + DECAY_LORA + AAA_LORA]
    gl = ps[..., o + DECAY_LORA + AAA_LORA:]
    w_log = -jax.nn.softplus(-(w0 + jnp.tanh(wl) @ w2)) - 0.5
    decay = jnp.exp(-jnp.exp(w_log.astype(F32)))
    a = jax.nn.sigmoid(a0 + al @ a2)
    g = jax.nn.sigmoid(gl) @ g2

    def heads(z):
        return z.reshape(B, T, RWKV_HEADS, HEAD_DIM).astype(F32)

    kk = heads(k * k_k)
    kk = kk / jnp.maximum(jnp.sqrt(jnp.sum(kk * kk, axis=-1, keepdims=True)), 1e-12)
    k_mod = heads(k * (1 + (a - 1) * k_a))
    r_h, v_h, a_h, w_h = heads(r), heads(v), heads(a), heads(decay)
    seq = tuple(jnp.moveaxis(z, 1, 0) for z in (r_h, w_h, k_mod, v_h, kk, a_h))

    def step(S, inp):
        r_t, w_t, k_t, v_t, kk_t, a_t = inp
        Skk = jnp.einsum('bhij,bhj->bhi', S, -kk_t)
        S = (S * w_t[:, :, None, :]
             + jnp.einsum('bhi,bhj->bhij', Skk, kk_t * a_t)
             + jnp.einsum('bhi,bhj->bhij', v_t, k_t))
        return S, jnp.einsum('bhij,bhj->bhi', S, r_t)

    S_T, y = lax.scan(step, S0.astype(F32), seq)
    y = jnp.moveaxis(y, 0, 1)
    mean = jnp.mean(y, axis=-1, keepdims=True)
    var = jnp.mean(jnp.square(y - mean), axis=-1, keepdims=True)
    yn = ((y - mean) * lax.rsqrt(var + LNX_EPS)).reshape(B, T, W) * lnx_w.astype(F32) + lnx_b.astype(F32)
    bonus = (jnp.sum(r_h * k_mod * r_k.astype(F32), axis=-1, keepdims=True) * v_h).reshape(B, T, W)
    out = ((yn + bonus) * g.astype(F32)).astype(p.dtype)
    return out, S_T.astype(S0.dtype), p[:, -1:]


def peer_ffn(x, w_q, sub_keys, expert_u, expert_v):
    T, D = x.shape
    nb = -(-T // PEER_BLOCK)
    xp = jnp.pad(x, ((0, nb * PEER_BLOCK - T), (0, 0))).reshape(nb, PEER_BLOCK, D)

    def block(xb):
        q = (xb @ w_q).reshape(PEER_BLOCK, PEER_HEADS, 2, PEER_QDIM // 2).astype(F32)
        s = jnp.einsum('thcd,hcnd->thcn', q, sub_keys.astype(F32))
        s1, i1 = lax.top_k(s[:, :, 0], PEER_TOPK)
        s2, i2 = lax.top_k(s[:, :, 1], PEER_TOPK)
        cand = (s1[..., :, None] + s2[..., None, :]).reshape(PEER_BLOCK, PEER_HEADS, PEER_TOPK * PEER_TOPK)
        cidx = (i1[..., :, None] * N_KEYS + i2[..., None, :]).reshape(PEER_BLOCK, PEER_HEADS, PEER_TOPK * PEER_TOPK)
        top, pos = lax.top_k(cand, PEER_TOPK)
        eidx = jnp.take_along_axis(cidx, pos, axis=-1)
        gate = jax.nn.softmax(top, axis=-1)
        u = expert_u[eidx]
        act = jax.nn.gelu(jnp.einsum('thkd,td->thk', u, xb).astype(F32), approximate=False)
        return jnp.einsum('thk,thkd->td', (gate * act).astype(xb.dtype), expert_v[eidx])

    y = lax.map(block, xp).reshape(nb * PEER_BLOCK, D)
    return y[:T]


def layer_step(x, k_past, v_past, lf_past, S0, shift0,
               norm_mix_g, w_in, fox_b_f, rwkv_mu, rwkv_w0, rwkv_w2, rwkv_a0, rwkv_a2,
               rwkv_g2, rwkv_k_k, rwkv_k_a, rwkv_r_k, rwkv_lnx_w, rwkv_lnx_b, w_out,
               norm_ffn_g, peer_w_q, peer_sub_keys, peer_u, peer_v):
    B, T, D = x.shape
    h = rms_norm(x, norm_mix_g)
    proj = h @ w_in
    FW = FOX_WIDTH
    q = proj[..., :FW].reshape(B, T, FOX_HEADS, HEAD_DIM)
    k = proj[..., FW:2 * FW].reshape(B, T, FOX_HEADS, HEAD_DIM)
    v = proj[..., 2 * FW:3 * FW].reshape(B, T, FOX_HEADS, HEAD_DIM)
    logf = jax.nn.log_sigmoid((proj[..., 3 * FW:FOX_COLS] + fox_b_f).astype(F32))
    fox_out = fox_attend(q,
                         jnp.concatenate([k_past.astype(k.dtype), k], axis=1),
                         jnp.concatenate([v_past.astype(v.dtype), v], axis=1),
                         jnp.concatenate([lf_past.astype(F32), logf], axis=1),
                         k_past.shape[1])
    rw_out, S_T, shift_T = rwkv_time_mix(proj[..., FOX_COLS:], shift0, S0, rwkv_mu, rwkv_w0,
                                         rwkv_w2, rwkv_a0, rwkv_a2, rwkv_g2, rwkv_k_k,
                                         rwkv_k_a, rwkv_r_k, rwkv_lnx_w, rwkv_lnx_b)
    mix = jnp.concatenate([fox_out.reshape(B, T, FW), rw_out], axis=-1) @ w_out
    x = x + mix
    ff = peer_ffn(rms_norm(x, norm_ffn_g).reshape(B * T, D), peer_w_q, peer_sub_keys, peer_u, peer_v)
    x = x + ff.reshape(B, T, D)
    return x, k, v, logf.astype(lf_past.dtype), S_T, shift_T


def setup_inputs(seed: int = 0) -> dict:
    key = jax.random.key(seed)
    ks = jax.random.split(key, 32)

    def nrm(k, shape, scale):
        return jax.random.normal(k, shape, F32) * scale

    L = DEPTH
    return {
        'x_prompt': nrm(ks[0], (BATCH, SEQ, D_MODEL), 1.0),
        'x_sample': nrm(ks[1], (DEC_BATCH, DEC_SEQ, D_MODEL), 1.0),
        'cache_fox_k': nrm(ks[2], (L, DEC_BATCH, PAST_LEN, FOX_HEADS, HEAD_DIM), 1.0),
        'cache_fox_v': nrm(ks[3], (L, DEC_BATCH, PAST_LEN, FOX_HEADS, HEAD_DIM), 1.0),
        'cache_fox_logf': jax.nn.log_sigmoid(FOX_BF_INIT + nrm(ks[4], (L, DEC_BATCH, PAST_LEN, FOX_HEADS), 1.0)),
        'state_rwkv': nrm(ks[5], (L, DEC_BATCH, RWKV_HEADS, HEAD_DIM, HEAD_DIM), 0.1),
        'state_shift': nrm(ks[6], (L, DEC_BATCH, 1, RWKV_COLS), 1.0),
        'norm_mix_g': 1.0 + nrm(ks[7], (L, D_MODEL), 0.01),
        'w_in': nrm(ks[8], (L, D_MODEL, IN_COLS), D_MODEL ** -0.5),
        'fox_b_f': FOX_BF_INIT + nrm(ks[9], (L, FOX_HEADS), 0.1),
        'rwkv_mu': jax.random.uniform(ks[10], (L, RWKV_COLS), F32),
        'rwkv_w0': -2.0 + nrm(ks[11], (L, RWKV_WIDTH), 0.5),
        'rwkv_w2': nrm(ks[12], (L, DECAY_LORA, RWKV_WIDTH), 0.1),
        'rwkv_a0': nrm(ks[13], (L, RWKV_WIDTH), 0.1),
        'rwkv_a2': nrm(ks[14], (L, AAA_LORA, RWKV_WIDTH), 0.1),
        'rwkv_g2': nrm(ks[15], (L, GATE_LORA, RWKV_WIDTH), GATE_LORA ** -0.5),
        'rwkv_k_k': 0.85 + nrm(ks[16], (L, RWKV_WIDTH), 0.02),
        'rwkv_k_a': 1.0 + nrm(ks[17], (L, RWKV_WIDTH), 0.02),
        'rwkv_r_k': nrm(ks[18], (L, RWKV_HEADS, HEAD_DIM), 0.1),
        'rwkv_lnx_w': 1.0 + nrm(ks[19], (L, RWKV_WIDTH), 0.01),
        'rwkv_lnx_b': nrm(ks[20], (L, RWKV_WIDTH), 0.01),
        'w_out': nrm(ks[21], (L, MIX_WIDTH, D_MODEL), MIX_WIDTH ** -0.5),
        'norm_ffn_g': 1.0 + nrm(ks[22], (L, D_MODEL), 0.01),
        'peer_w_q': nrm(ks[23], (L, D_MODEL, PEER_HEADS * PEER_QDIM), D_MODEL ** -0.5),
        'peer_sub_keys': nrm(ks[24], (L, PEER_HEADS, 2, N_KEYS, PEER_QDIM // 2), (PEER_QDIM // 2) ** -0.5),
        'peer_u': nrm(ks[25], (L, N_EXPERTS, D_MODEL), D_MODEL ** -0.5),
        'peer_v': nrm(ks[26], (L, N_EXPERTS, D_MODEL), 0.1),
        'norm_final_g': 1.0 + nrm(ks[27], (D_MODEL,), 0.01),
    }


def reference(x_prompt, x_sample, cache_fox_k, cache_fox_v, cache_fox_logf, state_rwkv, state_shift,
              norm_mix_g, w_in, fox_b_f, rwkv_mu, rwkv_w0, rwkv_w2, rwkv_a0, rwkv_a2, rwkv_g2,
              rwkv_k_k, rwkv_k_a, rwkv_r_k, rwkv_lnx_w, rwkv_lnx_b, w_out, norm_ffn_g,
              peer_w_q, peer_sub_keys, peer_u, peer_v, norm_final_g):
    yp, ys = x_prompt, x_sample
    Bp = x_prompt.shape[0]
    dt = x_prompt.dtype
    kp_l, vp_l, lfp_l, Sp_l, shp_l = [], [], [], [], []
    ks_l, vs_l, lfs_l, Ss_l, shs_l = [], [], [], [], []
    for l in range(DEPTH):
        lp = (norm_mix_g[l], w_in[l], fox_b_f[l], rwkv_mu[l], rwkv_w0[l], rwkv_w2[l], rwkv_a0[l],
              rwkv_a2[l], rwkv_g2[l], rwkv_k_k[l], rwkv_k_a[l], rwkv_r_k[l], rwkv_lnx_w[l],
              rwkv_lnx_b[l], w_out[l], norm_ffn_g[l], peer_w_q[l], peer_sub_keys[l], peer_u[l], peer_v[l])
        empty_kv = jnp.zeros((Bp, 0, FOX_HEADS, HEAD_DIM), dt)
        empty_lf = jnp.zeros((Bp, 0, FOX_HEADS), dt)
        S_zero = jnp.zeros((Bp, RWKV_HEADS, HEAD_DIM, HEAD_DIM), dt)
        sh_zero = jnp.zeros((Bp, 1, RWKV_COLS), dt)
        yp, kp, vp, lfp, Sp, shp = layer_step(yp, empty_kv, empty_kv, empty_lf, S_zero, sh_zero, *lp)
        ys, kss, vss, lfs, Ss, shs = layer_step(ys, cache_fox_k[l], cache_fox_v[l], cache_fox_logf[l],
                                                state_rwkv[l], state_shift[l], *lp)
        kp_l.append(kp); vp_l.append(vp); lfp_l.append(lfp); Sp_l.append(Sp); shp_l.append(shp)
        ks_l.append(kss); vs_l.append(vss); lfs_l.append(lfs); Ss_l.append(Ss); shs_l.append(shs)
    y_prompt = rms_norm(yp, norm_final_g)
    y_sample = rms_norm(ys, norm_final_g)
    return (y_prompt, y_sample,
            jnp.stack(kp_l), jnp.stack(vp_l), jnp.stack(lfp_l), jnp.stack(Sp_l), jnp.stack(shp_l),
            jnp.stack(ks_l), jnp.stack(vs_l), jnp.stack(lfs_l), jnp.stack(Ss_l), jnp.stack(shs_l))
```

```python
from contextlib import ExitStack
import math
import numpy as np
import concourse.bass as bass
import concourse.mybir as mybir
from concourse.bass_utils import run_bass_kernel_spmd

F32 = mybir.dt.float32
BF16 = mybir.dt.bfloat16
I32 = mybir.dt.int32
U32 = mybir.dt.uint32
ALU = mybir.AluOpType
AF = mybir.ActivationFunctionType
AX = mybir.AxisListType

D = 1024
IN_COLS = 3240
FOX_COLS = 1544
RW_COLS = 1696
NCORE = 8


class Prog:
    ENGS = ("pe", "act", "dve", "pool", "sp")

    def __init__(self, nc):
        self.nc = nc
        self.st = ExitStack()
        self.ops = {e: [] for e in self.ENGS}
        self.cnt = {}
        self.waited = {e: {} for e in self.ENGS}
        self.lastw = {}
        self.readers = {}
        self.ndma = {e: 0 for e in self.ENGS}
        self.NS = 8
        self.fence_t = {}
        self.uid = 0

    def sb(self, name, shape, dt):
        return self.st.enter_context(self.nc.sbuf_tensor("sb_" + name, list(shape), dt))

    def ps(self, name, shape, dt):
        return self.st.enter_context(self.nc.psum_tensor("ps_" + name, list(shape), dt))

    def _deps(self, eng, reads, writes):
        deps = []
        for b in reads:
            if b in self.lastw:
                deps.append(self.lastw[b])
        for b in writes:
            if b in self.lastw:
                deps.append(self.lastw[b])
            deps.extend(self.readers.get(b, ()))
        best = {}
        for (k, v) in deps:
            if eng == "pe" and k == "pe":
                continue
            if self.waited[eng].get(k, 0) >= v:
                continue
            best[k] = max(best.get(k, 0), v)
        for k, v in best.items():
            self.waited[eng][k] = v
        return list(best.items())

    def _record(self, tok, reads, writes):
        for b in reads:
            self.readers.setdefault(b, []).append(tok)
        for b in writes:
            self.lastw[b] = tok
            self.readers[b] = []

    def op(self, eng, fn, reads=(), writes=()):
        waits = self._deps(eng, reads, writes)
        self.cnt[eng] = self.cnt.get(eng, 0) + 1
        self.ops[eng].append((waits, fn, eng, 1))
        self._record((eng, self.cnt[eng]), reads, writes)

    def fence(self, eng, names):
        if eng not in self.fence_t:
            self.fence_t[eng] = self.sb("fence_" + eng, [128, 2], F32)
        t = self.fence_t[eng]
        if eng == "act":
            self.op("act", lambda e: e.copy(out=t[:, 1:2], in_=t[:, 0:1]), reads=(), writes=list(names))
        else:
            self.op("dve", lambda e: e.tensor_copy(out=t[:, 1:2], in_=t[:, 0:1]), reads=(), writes=list(names))

    def dma(self, q, fn, reads=(), writes=()):
        waits = self._deps(q, reads, writes)
        k = "d_%s_%d" % (q, self.ndma[q] % self.NS)
        self.ndma[q] += 1
        prev = self.cnt.get(k, 0)
        if prev and self.waited[q].get(k, 0) < prev:
            self.waited[q][k] = prev
            waits = [w for w in waits if w[0] != k] + [(k, prev)]
        self.cnt[k] = self.cnt.get(k, 0) + 16
        self.ops[q].append((waits, fn, k, 16))
        self._record((k, self.cnt[k]), reads, writes)

    def emit(self):
        nc = self.nc
        keys = sorted(self.cnt.keys())
        sems = {k: self.st.enter_context(nc.semaphore("s_" + k)) for k in keys}
        final = [(k, self.cnt[k]) for k in keys]
        ops = self.ops

        def run(name, e):
            for (waits, fn, k, inc) in ops[name]:
                for (wk, wv) in waits:
                    e.wait_ge(sems[wk], wv)
                fn(e).then_inc(sems[k], inc)
            if name == "sp":
                for (k, v) in final:
                    e.wait_ge(sems[k], v)

        with nc.Block() as block:
            @block.tensor
            def _(e):
                run("pe", e)

            @block.scalar
            def _(e):
                run("act", e)

            @block.vector
            def _(e):
                run("dve", e)

            @block.gpsimd
            def _(e):
                run("pool", e)

            @block.sync
            def _(e):
                run("sp", e)
        self.st.close()


def _din(nc, name, shape, dt=F32):
    return nc.dram_tensor(name, list(shape), dt, kind="ExternalInput").ap()


def _dout(nc, name, shape, dt=F32):
    return nc.dram_tensor(name, list(shape), dt, kind="ExternalOutput").ap()


def _load_cast(P, name, dram_ap, shape, stage, stage_name, q="sp", eng="act"):
    t = P.sb(name, shape, BF16)
    p, n = shape
    P.dma(q, lambda e: e.dma_start(out=stage[0:p, 0:n], in_=dram_ap), writes=[stage_name])
    if eng == "act":
        P.op("act", lambda e: e.copy(out=t[:, :], in_=stage[0:p, 0:n]), reads=[stage_name], writes=[name])
    else:
        P.op("dve", lambda e: e.tensor_copy(out=t[:, :], in_=stage[0:p, 0:n]), reads=[stage_name], writes=[name])
    return t


def build_phase1(NT):
    nc = bass.Bass("TRN2", target_bir_lowering=False)
    x = _din(nc, "x", [NT * 128, D])
    gbc = _din(nc, "gbc", [128, D])
    w = _din(nc, "w_in", [D, IN_COLS])
    bfb = _din(nc, "bfb", [128, 8])
    identd = _din(nc, "ident", [128, 128])
    proj = _dout(nc, "proj", [NT * 128, IN_COLS])
    P = Prog(nc)
    wst = P.sb("wst", [128, IN_COLS], F32)
    w_bf = P.sb("w_bf", [128, 8, IN_COLS], BF16)
    g_sb = P.sb("g_sb", [128, D], F32)
    bf_sb = P.sb("bf_sb", [128, 8], F32)
    id_f = P.sb("id_f", [128, 128], F32)
    id_b = P.sb("id_b", [128, 128], BF16)
    P.dma("sp", lambda e: e.dma_start(out=g_sb[:, :], in_=gbc), writes=["g"])
    P.dma("sp", lambda e: e.dma_start(out=bf_sb[:, :], in_=bfb), writes=["bf"])
    P.dma("sp", lambda e: e.dma_start(out=id_f[:, :], in_=identd), writes=["idf"])
    P.op("dve", lambda e: e.tensor_copy(out=id_b[:, :], in_=id_f[:, :]), reads=["idf"], writes=["idb"])
    for dc in range(8):
        P.dma("sp", lambda e, dc=dc: e.dma_start(out=wst[:, :], in_=w[dc * 128:(dc + 1) * 128, :]),
              writes=["wst"])
        P.op("act", lambda e, dc=dc: e.copy(out=w_bf[:, dc, :], in_=wst[:, :]), reads=["wst"], writes=["w%d" % dc])
    wnames = ["w%d" % dc for dc in range(8)]
    xt = [P.sb("xt%d" % i, [128, D], F32) for i in range(2)]
    junk = P.sb("junk", [128, D], BF16)
    ss = [P.sb("ss%d" % i, [128, 1], F32) for i in range(2)]
    rstd = [P.sb("rstd%d" % i, [128, 1], F32) for i in range(2)]
    h = [P.sb("h%d" % i, [128, D], BF16) for i in range(2)]
    hT = [P.sb("hT%d" % i, [128, D], BF16) for i in range(2)]
    pr = [P.sb("pr%d" % i, [128, IN_COLS], F32) for i in range(2)]
    lz = P.sb("lz", [128, 8], F32)
    psT = [P.ps("psT%d" % i, [128, D], BF16) for i in range(2)]
    psP = [P.ps("psP%d" % i, [128, 512], F32) for i in range(4)]
    groups = [(c0, min(c0 + 512, IN_COLS)) for c0 in range(0, IN_COLS, 512)]
    gi = 0
    for i in range(NT):
        b = i % 2
        X, H, HT, PR = xt[b], h[b], hT[b], pr[b]
        P.dma("sp", lambda e, X=X, i=i: e.dma_start(out=X[:, :], in_=x[i * 128:(i + 1) * 128, :]),
              writes=["xt%d" % b])
        P.op("act", lambda e, X=X, b=b: e.activation(out=junk[:, :], in_=X[:, :], func=AF.Square,
                                                      accum_out=ss[b][:, 0:1]),
             reads=["xt%d" % b], writes=["junk", "ss%d" % b])
        P.op("act", lambda e, b=b: e.activation(out=rstd[b][:, :], in_=ss[b][:, :], func=AF.Sqrt, bias=1e-6,
                                                scale=1.0 / D),
             reads=["ss%d" % b], writes=["rstd%d" % b])
        P.op("dve", lambda e, b=b: e.reciprocal(out=rstd[b][:, :], in_=rstd[b][:, :]),
             reads=["rstd%d" % b], writes=["rstd%d" % b])
        P.op("dve", lambda e, X=X, H=H, b=b: e.scalar_tensor_tensor(
            out=H[:, :], in0=X[:, :], scalar=rstd[b][:, 0:1], in1=g_sb[:, :], op0=ALU.mult, op1=ALU.mult),
            reads=["xt%d" % b, "rstd%d" % b, "g"], writes=["h%d" % b])
        for dc in range(8):
            P.op("pe", lambda e, H=H, b=b, dc=dc: e.transpose(
                out=psT[b][:, dc * 128:(dc + 1) * 128], in_=H[:, dc * 128:(dc + 1) * 128], identity=id_b[:, :]),
                reads=["h%d" % b, "idb"], writes=["psT%d" % b])
        P.op("act", lambda e, HT=HT, b=b: e.copy(out=HT[:, :], in_=psT[b][:, :]),
             reads=["psT%d" % b], writes=["hT%d" % b])
        for (c0, c1) in groups:
            pp = gi % 4
            gi += 1
            n = c1 - c0
            for dc in range(8):
                P.op("pe", lambda e, HT=HT, pp=pp, dc=dc, c0=c0, c1=c1, n=n: e.matmul(
                    psP[pp][:, 0:n], lhsT=HT[:, dc * 128:(dc + 1) * 128], rhs=w_bf[:, dc, c0:c1],
                    start=(dc == 0), stop=(dc == 7)),
                    reads=["hT%d" % b] + wnames, writes=["psP%d" % pp])
            if gi % 2 == 0:
                P.op("act", lambda e, PR=PR, pp=pp, c0=c0, c1=c1, n=n: e.copy(out=PR[:, c0:c1], in_=psP[pp][:, 0:n]),
                     reads=["psP%d" % pp], writes=["pr%d" % b])
            else:
                P.op("dve", lambda e, PR=PR, pp=pp, c0=c0, c1=c1, n=n: e.tensor_copy(out=PR[:, c0:c1],
                                                                                     in_=psP[pp][:, 0:n]),
                     reads=["psP%d" % pp], writes=["pr%d" % b])
        P.op("dve", lambda e, PR=PR: e.tensor_tensor(out=lz[:, :], in0=PR[:, 1536:1544], in1=bf_sb[:, :], op=ALU.add),
             reads=["pr%d" % b, "bf"], writes=["lz"])
        P.op("act", lambda e: e.activation(out=lz[:, :], in_=lz[:, :], func=AF.Exp, scale=-1.0),
             reads=["lz"], writes=["lz"])
        P.op("act", lambda e: e.activation(out=lz[:, :], in_=lz[:, :], func=AF.Ln, bias=1.0, scale=1.0),
             reads=["lz"], writes=["lz"])
        P.op("dve", lambda e, PR=PR: e.tensor_single_scalar(out=PR[:, 1536:1544], in_=lz[:, :], scalar=-1.0,
                                                            op=ALU.mult),
             reads=["lz"], writes=["pr%d" % b])
        P.dma("sp", lambda e, PR=PR, i=i: e.dma_start(out=proj[i * 128:(i + 1) * 128, :], in_=PR[:, :]),
              reads=["pr%d" % b], writes=["out%d" % i])
    P.emit()
    return nc


def _ident():
    return np.eye(128, dtype=np.float32)


def run_phase1(x_prompt, x_sample, norm_mix_g, w_in, fox_b_f):
    NT = 33
    xp = np.ascontiguousarray(x_prompt).reshape(-1, D)
    xs = np.ascontiguousarray(x_sample).reshape(-1, D)
    nc = build_phase1(NT)
    gbc = np.ascontiguousarray(np.broadcast_to(norm_mix_g.reshape(1, D), (128, D))).astype(np.float32)
    bfb = np.ascontiguousarray(np.broadcast_to(fox_b_f.reshape(1, 8), (128, 8))).astype(np.float32)
    w = np.ascontiguousarray(w_in.reshape(D, IN_COLS))
    in_maps = []
    for c in range(NCORE):
        xc = np.zeros((NT * 128, D), np.float32)
        xc[:4096] = xp[c * 4096:(c + 1) * 4096]
        xc[4096:4128] = xs[c * 32:(c + 1) * 32]
        in_maps.append({"x": xc, "gbc": gbc, "w_in": w, "bfb": bfb, "ident": _ident()})
    res = run_bass_kernel_spmd(nc, in_maps, core_ids=list(range(NCORE)))
    pp = np.concatenate([res.results[c]["proj"][:4096] for c in range(NCORE)], axis=0)
    psm = np.concatenate([res.results[c]["proj"][4096:4128] for c in range(NCORE)], axis=0)
    return pp, psm


TP = 16384
TS = 2176
NSEQ_S = 16


def _fox_consts(P, nc):
    c = {}
    tri_d = _din(nc, "tri", [128, 128])
    ones_d = _din(nc, "ones", [128, 128])
    id_d = _din(nc, "ident", [128, 128])
    mask_d = _din(nc, "mask", [128, 4 * 512])
    c["tri"] = P.sb("tri", [128, 128], F32)
    c["ones"] = P.sb("ones", [128, 128], F32)
    c["idf"] = P.sb("idf", [128, 128], F32)
    c["idb"] = P.sb("idb", [128, 128], BF16)
    c["maskf"] = P.sb("maskf", [128, 2048], F32)
    c["mask"] = P.sb("maskb", [128, 4, 512], BF16)
    P.dma("sp", lambda e: e.dma_start(out=c["tri"][:, :], in_=tri_d), writes=["tri"])
    P.dma("sp", lambda e: e.dma_start(out=c["ones"][:, :], in_=ones_d), writes=["ones"])
    P.dma("sp", lambda e: e.dma_start(out=c["idf"][:, :], in_=id_d), writes=["idf"])
    P.dma("sp", lambda e: e.dma_start(out=c["maskf"][:, :], in_=mask_d), writes=["maskf"])
    P.op("dve", lambda e: e.tensor_copy(out=c["idb"][:, :], in_=c["idf"][:, :]), reads=["idf"], writes=["idb"])
    P.op("dve", lambda e: e.tensor_copy(out=c["mask"][:, :, :], in_=c["maskf"][:, :].rearrange("p (a b) -> p a b", a=4)),
         reads=["maskf"], writes=["maskb"])
    return c


def _fox_seq(P, c, B, tag, NT, qT_src, nq_tot, kT_src, v_src, lf_src, groups, out_fn):
    T = NT * 128
    qT, kT, vv, stage = B["qT"], B["kT"], B["vv"], B["stage"]
    k = 0
    for (dst, src, n, nm) in ((qT, qT_src, nq_tot, "qT"), (kT, kT_src, T, "kT")):
        for c0 in range(0, n, 2048):
            w = min(2048, n - c0)
            s = k % 2
            k += 1
            P.dma("sp", lambda e, s=s, src=src, c0=c0, w=w: e.dma_start(out=stage[s][0:64, 0:w], in_=src[:, c0:c0 + w]),
                  writes=["stage%d" % s])
            eng = "act" if k % 2 else "dve"
            if eng == "act":
                P.op("act", lambda e, s=s, dst=dst, c0=c0, w=w: e.copy(out=dst[:, c0:c0 + w], in_=stage[s][0:64, 0:w]),
                     reads=["stage%d" % s], writes=[nm])
            else:
                P.op("dve", lambda e, s=s, dst=dst, c0=c0, w=w: e.tensor_copy(out=dst[:, c0:c0 + w], in_=stage[s][0:64, 0:w]),
                     reads=["stage%d" % s], writes=[nm])
    for j0 in range(0, NT, 32):
        nj = min(32, NT - j0)
        s = k % 2
        k += 1
        P.dma("sp", lambda e, s=s, j0=j0, nj=nj: e.dma_start(
            out=stage[s][:, 0:nj * 64].rearrange("p (j d) -> p j d", d=64), in_=v_src[:, j0:j0 + nj, :]),
            writes=["stage%d" % s])
        P.op("dve", lambda e, s=s, j0=j0, nj=nj: e.tensor_copy(
            out=vv[:, j0:j0 + nj, 0:64], in_=stage[s][:, 0:nj * 64].rearrange("p (j d) -> p j d", d=64)),
            reads=["stage%d" % s], writes=["vv"])
    L = B["L"]
    P.dma("sp", lambda e: e.dma_start(out=L[:, 0:NT], in_=lf_src), writes=["L"])
    cl_ps, tot_ps = B["cl_ps"], B["tot_ps"]
    P.op("pe", lambda e: e.matmul(cl_ps[:, 0:NT], lhsT=c["tri"][:, :], rhs=L[:, 0:NT], start=True, stop=True),
         reads=["L", "tri"], writes=["cl_ps"])
    P.op("pe", lambda e: e.matmul(tot_ps[:, 0:NT], lhsT=c["ones"][:, :], rhs=L[:, 0:NT], start=True, stop=True),
         reads=["L", "ones"], writes=["tot_ps"])
    sa, sbb = B["scanA"], B["scanB"]
    P.op("dve", lambda e: e.tensor_copy(out=sa[:, 0:NT], in_=tot_ps[:, 0:NT]), reads=["tot_ps"], writes=["scanA"])
    cur, nxt, cn, nn = sa, sbb, "scanA", "scanB"
    sh = 1
    while sh < NT:
        P.op("dve", lambda e, cur=cur, nxt=nxt, sh=sh: e.tensor_tensor(
            out=nxt[:, sh:NT], in0=cur[:, sh:NT], in1=cur[:, 0:NT - sh], op=ALU.add), reads=[cn], writes=[nn])
        P.op("dve", lambda e, cur=cur, nxt=nxt, sh=sh: e.tensor_copy(out=nxt[:, 0:sh], in_=cur[:, 0:sh]),
             reads=[cn], writes=[nn])
        cur, nxt, cn, nn = nxt, cur, nn, cn
        sh *= 2
    pex, negC = B["pex"], B["negC"]
    P.op("dve", lambda e, cur=cur: e.tensor_tensor(out=pex[:, 0:NT], in0=cur[:, 0:NT], in1=tot_ps[:, 0:NT],
                                                   op=ALU.subtract), reads=[cn, "tot_ps"], writes=["pex"])
    P.op("dve", lambda e: e.scalar_tensor_tensor(out=negC[:, 0:NT], in0=pex[:, 0:NT], scalar=-1.0, in1=cl_ps[:, 0:NT],
                                                 op0=ALU.mult, op1=ALU.subtract),
         reads=["pex", "cl_ps"], writes=["negC"])
    bias = B["bias"]
    for gi, (q0, nq, nk, d0, ct) in enumerate(groups):
        P.op("dve", lambda e, gi=gi, nk=nk, ct=ct: e.tensor_scalar(
            out=bias[:, gi, 0:nk], in0=negC[:, 0:nk], scalar1=pex[:, ct:ct + 1], scalar2=0.0,
            op0=ALU.add, op1=ALU.add), reads=["negC", "pex"], writes=["bias"])
    it = B["it"]
    for gi, (q0, nq, nk, d0, ct) in enumerate(groups):
        ob = B["gcount"] % 2
        B["gcount"] += 1
        OT = B["OT"][ob]
        for j in range(nk):
            sb_ = it % 2
            pb = it % 3
            it += 1
            sT = B["sT"][sb_]
            pT = B["pT"][pb]
            diag = j >= d0
            P.op("pe", lambda e, sT=sT, j=j, q0=q0, nq=nq, diag=diag: e.matmul(
                sT[:, 0:nq], lhsT=kT[:, j * 128:(j + 1) * 128], rhs=qT[:, q0:q0 + nq], start=True, stop=(not diag)),
                reads=["kT", "qT"], writes=["sT%d" % sb_])
            if diag:
                jl = j - d0
                P.op("pe", lambda e, sT=sT, jl=jl, nq=nq: e.matmul(
                    sT[:, 0:nq], lhsT=c["idb"][:, :], rhs=c["mask"][:, jl, 0:nq], start=False, stop=True),
                    reads=["idb", "maskb"], writes=["sT%d" % sb_])
            P.op("act", lambda e, sT=sT, pT=pT, gi=gi, j=j, nq=nq: e.activation(
                out=pT[:, 0:nq], in_=sT[:, 0:nq], func=AF.Exp, bias=bias[:, gi, j:j + 1], scale=0.125),
                reads=["sT%d" % sb_, "bias"], writes=["pT%d" % pb])
            P.op("pe", lambda e, OT=OT, pT=pT, j=j, nq=nq, nk=nk: e.matmul(
                OT[0:65, 0:nq], lhsT=vv[:, j, :], rhs=pT[:, 0:nq], start=(j == 0), stop=(j == nk - 1)),
                reads=["vv", "pT%d" % pb], writes=["OT%d" % ob])
        oT = B["oT"]
        P.op("act", lambda e, OT=OT, nq=nq: e.copy(out=oT[0:65, 0:nq], in_=OT[0:65, 0:nq]),
             reads=["OT%d" % ob], writes=["oT"])
        oq, rec, osb = B["oq"], B["rec"], B["osb"]
        nqi = (nq + 127) // 128
        for qi in range(nqi):
            w = min(128, nq - qi * 128)
            P.op("pe", lambda e, qi=qi, w=w: e.transpose(out=oq[0:w, qi, :], in_=oT[0:65, qi * 128:qi * 128 + w],
                                                         identity=c["idf"][0:65, 0:65]),
                 reads=["oT", "idf"], writes=["oq"])
        wmax = min(128, nq)
        for qi in range(nqi):
            P.op("dve", lambda e, qi=qi: e.reciprocal(out=rec[0:wmax, qi:qi + 1], in_=oq[0:wmax, qi, 64:65]),
                 reads=["oq"], writes=["rec"])
            P.op("dve", lambda e, qi=qi: e.tensor_scalar_mul(out=osb[0:wmax, qi, :], in0=oq[0:wmax, qi, 0:64],
                                                             scalar1=rec[0:wmax, qi:qi + 1]),
                 reads=["oq", "rec"], writes=["osb"])
        out_fn(P, osb, q0, nq, nqi, wmax)
    B["it"] = it


def _fox_bufs(P):
    B = {}
    B["qT"] = P.sb("qT", [64, TP], BF16)
    B["kT"] = P.sb("kT", [64, TP], BF16)
    B["vv"] = P.sb("vv", [128, 128, 65], BF16)
    B["stage"] = [P.sb("stage%d" % i, [128, 2048], F32) for i in range(2)]
    B["L"] = P.sb("L", [128, 128], F32)
    B["scanA"] = P.sb("scanA", [128, 128], F32)
    B["scanB"] = P.sb("scanB", [128, 128], F32)
    B["pex"] = P.sb("pex", [128, 128], F32)
    B["negC"] = P.sb("negC", [128, 128], F32)
    B["bias"] = P.sb("bias", [128, 32, 128], F32)
    B["pT"] = [P.sb("pT%d" % i, [128, 512], BF16) for i in range(3)]
    B["oT"] = P.sb("oT", [65, 512], F32)
    B["rec"] = P.sb("rec", [128, 4], F32)
    B["osb"] = P.sb("osb", [128, 4, 64], F32)
    B["cl_ps"] = P.ps("cl_ps", [128, 128], F32)
    B["tot_ps"] = P.ps("tot_ps", [128, 128], F32)
    B["sT"] = [P.ps("sT%d" % i, [128, 512], F32) for i in range(2)]
    B["OT"] = [P.ps("OT%d" % i, [128, 512], F32) for i in range(2)]
    B["oq"] = P.ps("oq", [128, 4, 65], F32)
    B["it"] = 0
    B["gcount"] = 0
    P.op("pool", lambda e: e.memset(B["vv"][:, :, 64:65], 1.0), writes=["vv"])
    return B


def build_phase2_fox(n_prompt=2, n_sample=NSEQ_S, ngroups=32):
    nc = bass.Bass("TRN2", target_bir_lowering=False)
    qTp = _din(nc, "qTp", [2, 64, TP])
    kTp = _din(nc, "kTp", [2, 64, TP])
    vp = _din(nc, "vp", [2, 128, 128, 64])
    lfp = _din(nc, "lfp", [2, 128, 128])
    qTs = _din(nc, "qTs", [NSEQ_S, 64, 16])
    kTs = _din(nc, "kTs", [NSEQ_S, 64, TS])
    vs = _din(nc, "vs", [NSEQ_S, 128, 17, 64])
    lfs = _din(nc, "lfs", [NSEQ_S, 128, 17])
    op_ = _dout(nc, "o_p", [2, TP, 64])
    os_ = _dout(nc, "o_s", [NSEQ_S, 16, 64])
    P = Prog(nc)
    c = _fox_consts(P, nc)
    B = _fox_bufs(P)
    for b in range(n_prompt):
        groups = [(512 * g, 512, 4 * g + 4, 4 * g, 4 * g + 2) for g in range(ngroups)]

        def out_fn(P, osb, q0, nq, nqi, wmax, b=b):
            P.dma("sp", lambda e: e.dma_start(
                out=op_[b, q0:q0 + nq, :].rearrange("(a p) d -> p a d", p=128), in_=osb[:, 0:nqi, :]),
                reads=["osb"], writes=["o_out"])
        _fox_seq(P, c, B, "p%d" % b, 128, qTp[b], TP, kTp[b], vp[b], lfp[b], groups, out_fn)
    for s in range(n_sample):
        groups = [(0, 16, 17, 16, 16)]

        def out_fn(P, osb, q0, nq, nqi, wmax, s=s):
            P.dma("sp", lambda e: e.dma_start(out=os_[s, :, :], in_=osb[0:16, 0, :]), reads=["osb"], writes=["o_out"])
        _fox_seq(P, c, B, "s%d" % s, 17, qTs[s], 16, kTs[s], vs[s], lfs[s], groups, out_fn)
    P.emit()
    return nc


def _fox_const_inputs():
    p = np.arange(128)
    tri = (p[:, None] <= p[None, :]).astype(np.float32)
    ones = np.ones((128, 128), np.float32)
    col = np.arange(512)
    mask = np.zeros((128, 4, 512), np.float32)
    for jl in range(4):
        mask[:, jl, :] = np.where(jl * 128 + p[:, None] > col[None, :], -30000.0, 0.0)
    return {"tri": tri, "ones": ones, "ident": _ident(), "mask": mask.reshape(128, 2048)}


def _tile_major(a, nt):
    return np.ascontiguousarray(np.swapaxes(a.reshape((nt, 128) + a.shape[1:]), 0, 1))


def fox_inputs(pp, psm, cache_k, cache_v, cache_lf):
    ppb = pp.reshape(2, TP, IN_COLS)
    pss = psm.reshape(NSEQ_S, 16, IN_COLS)
    maps = []
    for h in range(NCORE):
        m = dict(_fox_const_inputs())
        m["qTp"] = np.ascontiguousarray(np.swapaxes(ppb[:, :, h * 64:(h + 1) * 64], 1, 2))
        m["kTp"] = np.ascontiguousarray(np.swapaxes(ppb[:, :, 512 + h * 64:512 + (h + 1) * 64], 1, 2))
        m["vp"] = np.stack([_tile_major(ppb[b, :, 1024 + h * 64:1024 + (h + 1) * 64], 128) for b in range(2)])
        m["lfp"] = np.stack([_tile_major(ppb[b, :, 1536 + h], 128) for b in range(2)])
        kfull = np.zeros((NSEQ_S, TS, 64), np.float32)
        vfull = np.zeros((NSEQ_S, TS, 64), np.float32)
        lfull = np.zeros((NSEQ_S, TS), np.float32)
        kfull[:, :2048] = cache_k[:, :, h, :]
        vfull[:, :2048] = cache_v[:, :, h, :]
        lfull[:, :2048] = cache_lf[:, :, h]
        kfull[:, 2048:2064] = pss[:, :, 512 + h * 64:512 + (h + 1) * 64]
        vfull[:, 2048:2064] = pss[:, :, 1024 + h * 64:1024 + (h + 1) * 64]
        lfull[:, 2048:2064] = pss[:, :, 1536 + h]
        m["qTs"] = np.ascontiguousarray(np.swapaxes(pss[:, :, h * 64:(h + 1) * 64], 1, 2))
        m["kTs"] = np.ascontiguousarray(np.swapaxes(kfull, 1, 2))
        m["vs"] = np.stack([_tile_major(vfull[s], 17) for s in range(NSEQ_S)])
        m["lfs"] = np.stack([_tile_major(lfull[s], 17) for s in range(NSEQ_S)])
        maps.append(m)
    return maps


NPAR = 352 + 7 * 64
EXPM05 = math.exp(-0.5)


def _rw_setup(P, nc, B):
    R = {}
    par_d = _din(nc, "rw_par", [128, NPAR])
    w2_d = _din(nc, "rw_w2", [32, 64])
    a2_d = _din(nc, "rw_a2", [32, 64])
    g2_d = _din(nc, "rw_g2", [96, 64])
    sel_d = _din(nc, "rw_sel", [2, 128])
    R["par"] = P.sb("rw_par", [128, NPAR], F32)
    R["w2"] = P.sb("rw_w2", [32, 64], F32)
    R["a2"] = P.sb("rw_a2", [32, 64], F32)
    R["g2"] = P.sb("rw_g2", [96, 64], F32)
    R["sel"] = P.sb("rw_sel", [2, 128], F32)
    R["omk"] = P.sb("rw_omk", [128, 64], F32)
    for nm, d_ in (("par", par_d), ("w2", w2_d), ("a2", a2_d), ("g2", g2_d), ("sel", sel_d)):
        P.dma("sp", lambda e, nm=nm, d_=d_: e.dma_start(out=R[nm][:, :], in_=d_), writes=["rwc_" + nm])
    o = 352
    R["mu"] = R["par"][:, 0:352]
    names = ["w0", "a0", "kk", "ka", "rk", "lnw", "lnb"]
    for i, nm in enumerate(names):
        R[nm] = R["par"][:, o + i * 64:o + (i + 1) * 64]
    P.op("dve", lambda e: e.tensor_scalar(out=R["omk"][:, :], in0=R["ka"], scalar1=-1.0, scalar2=1.0,
                                          op0=ALU.mult, op1=ALU.add), reads=["rwc_par"], writes=["rwc_omk"])
    R["cur"] = [P.sb("rw_cur%d" % b, [128, 352], F32) for b in range(2)]
    R["prv"] = [P.sb("rw_prv%d" % b, [128, 352], F32) for b in range(2)]
    R["R"] = [[P.sb("rw_R%d_%d" % (b, t), [128, 320], F32) for t in range(2)] for b in range(2)]
    R["GB"] = [[P.sb("rw_GB%d_%d" % (b, t), [128, 128], F32) for t in range(2)] for b in range(2)]
    R["VV"] = [P.sb("rw_VV%d" % t, [128, 128], F32) for t in range(2)]
    R["vT"] = [P.sb("rw_vT%d" % t, [128, 128], F32) for t in range(2)]
    R["yT"] = [P.sb("rw_yT%d" % t, [128, 128], F32) for t in range(2)]
    R["twl"] = P.sb("rw_twl", [32, 128], F32)
    R["alT"] = P.sb("rw_alT", [32, 128], F32)
    R["sgl"] = P.sb("rw_sgl", [96, 128], F32)
    for nm in ("zt", "at", "kkt", "tmp", "t1", "junk", "cen", "ob"):
        R[nm] = P.sb("rw_" + nm, [128, 64], F32)
    for nm in ("ssq", "rks", "mean", "var", "sk"):
        R[nm] = P.sb("rw_" + nm, [128, 1], F32)
    R["S"] = P.sb("rw_S", [128, 64], F32)
    R["stmp"] = P.sb("rw_stmp", [128, 64], F32)
    R["rowbuf"] = [P.sb("rw_rowbuf%d" % i, [2, 16 * 320], F32) for i in range(2)]
    R["rowp"] = [B["sT"][0], B["sT"][1]]
    R["trp"] = B["OT"][0]
    R["lop"] = B["OT"][1]
    R["vtp"] = B["cl_ps"]
    R["ytp"] = B["tot_ps"]
    R["k"] = 0
    R["step"] = 0
    return R


def _rw_prep(P, c, R, n, ntok, cur_src, prev_src, rows_scr):
    tp = n % 2
    idf = c["idf"]
    t0 = n * ntok
    for b in range(2):
        cur, prv = R["cur"][b], R["prv"][b]
        cn, pn = "rw_cur%d" % b, "rw_prv%d" % b
        Rt, GB = R["R"][b][tp], R["GB"][b][tp]
        rn, gn = "rw_R%d_%d" % (b, tp), "rw_GB%d_%d" % (b, tp)
        P.dma("sp", lambda e, cur=cur, b=b: e.dma_start(out=cur[0:ntok, :], in_=cur_src(b, n)), writes=[cn])
        P.dma("sp", lambda e, prv=prv, b=b: e.dma_start(out=prv[0:ntok, :], in_=prev_src(b, n)), writes=[pn])
        P.op("dve", lambda e, cur=cur, prv=prv: e.tensor_tensor(out=prv[0:ntok, :], in0=prv[0:ntok, :], in1=cur[0:ntok, :],
                                                                op=ALU.subtract), reads=[cn, pn], writes=[pn])
        P.op("dve", lambda e, prv=prv: e.tensor_tensor(out=prv[0:ntok, :], in0=prv[0:ntok, :], in1=R["mu"][0:ntok, :],
                                                       op=ALU.mult), reads=[pn, "rwc_par"], writes=[pn])
        P.op("dve", lambda e, cur=cur, prv=prv: e.tensor_tensor(out=cur[0:ntok, :], in0=cur[0:ntok, :], in1=prv[0:ntok, :],
                                                                op=ALU.add), reads=[cn, pn], writes=[cn])
        trp, lop = R["trp"], R["lop"]
        for (o0, c0, c1, m) in ((0, 192, 224, 32), (128, 224, 256, 32), (256, 256, 352, 96)):
            P.op("pe", lambda e, cur=cur, o0=o0, c0=c0, c1=c1, m=m: e.transpose(
                out=trp[0:m, o0:o0 + ntok], in_=cur[0:ntok, c0:c1], identity=idf[0:ntok, 0:ntok]),
                reads=[cn, "idf"], writes=["OT0"])
        P.op("act", lambda e: e.activation(out=R["twl"][0:32, 0:ntok], in_=trp[0:32, 0:ntok], func=AF.Tanh),
             reads=["OT0"], writes=["rw_twl"])
        P.op("act", lambda e: e.copy(out=R["alT"][0:32, 0:ntok], in_=trp[0:32, 128:128 + ntok]),
             reads=["OT0"], writes=["rw_alT"])
        P.op("act", lambda e: e.activation(out=R["sgl"][0:96, 0:ntok], in_=trp[0:96, 256:256 + ntok], func=AF.Sigmoid),
             reads=["OT0"], writes=["rw_sgl"])
        P.op("pe", lambda e: e.matmul(lop[0:ntok, 0:64], lhsT=R["twl"][0:32, 0:ntok], rhs=R["w2"][:, :], start=True, stop=True),
             reads=["rw_twl", "rwc_w2"], writes=["OT1"])
        P.op("pe", lambda e: e.matmul(lop[0:ntok, 64:128], lhsT=R["alT"][0:32, 0:ntok], rhs=R["a2"][:, :], start=True, stop=True),
             reads=["rw_alT", "rwc_a2"], writes=["OT1"])
        P.op("pe", lambda e: e.matmul(lop[0:ntok, 128:192], lhsT=R["sgl"][0:96, 0:ntok], rhs=R["g2"][:, :], start=True, stop=True),
             reads=["rw_sgl", "rwc_g2"], writes=["OT1"])
        zt, at, kkt, tmp, t1, junk = R["zt"], R["at"], R["kkt"], R["tmp"], R["t1"], R["junk"]
        ssq, rks = R["ssq"], R["rks"]
        P.op("dve", lambda e: e.tensor_tensor(out=zt[0:ntok, :], in0=lop[0:ntok, 0:64], in1=R["w0"][0:ntok, :], op=ALU.add),
             reads=["OT1", "rwc_par"], writes=["rw_zt"])
        P.op("act", lambda e: e.activation(out=zt[0:ntok, :], in_=zt[0:ntok, :], func=AF.Sigmoid),
             reads=["rw_zt"], writes=["rw_zt"])
        P.op("act", lambda e, Rt=Rt: e.activation(out=Rt[0:ntok, 0:64], in_=zt[0:ntok, :], func=AF.Exp, scale=-EXPM05),
             reads=["rw_zt"], writes=[rn])
        P.op("dve", lambda e: e.tensor_tensor(out=at[0:ntok, :], in0=lop[0:ntok, 64:128], in1=R["a0"][0:ntok, :], op=ALU.add),
             reads=["OT1", "rwc_par"], writes=["rw_at"])
        P.op("act", lambda e: e.activation(out=at[0:ntok, :], in_=at[0:ntok, :], func=AF.Sigmoid),
             reads=["rw_at"], writes=["rw_at"])
        P.op("act", lambda e, GB=GB: e.copy(out=GB[0:ntok, 0:64], in_=lop[0:ntok, 128:192]), reads=["OT1"], writes=[gn])
        P.op("dve", lambda e, cur=cur: e.tensor_tensor(out=kkt[0:ntok, :], in0=cur[0:ntok, 64:128], in1=R["kk"][0:ntok, :],
                                                       op=ALU.mult), reads=[cn, "rwc_par"], writes=["rw_kkt"])
        P.op("dve", lambda e: e.scalar_tensor_tensor(out=junk[0:ntok, :], in0=kkt[0:ntok, :], scalar=1.0, in1=kkt[0:ntok, :],
                                                     op0=ALU.mult, op1=ALU.mult, accum_out=ssq[0:ntok, 0:1]),
             reads=["rw_kkt"], writes=["rw_junk", "rw_ssq"])
        P.fence("dve", ["rw_ssq"])
        P.op("act", lambda e: e.activation(out=ssq[0:ntok, :], in_=ssq[0:ntok, :], func=AF.Sqrt), reads=["rw_ssq"], writes=["rw_ssq"])
        P.op("dve", lambda e: e.tensor_scalar_max(out=ssq[0:ntok, :], in0=ssq[0:ntok, :], scalar1=1e-12),
             reads=["rw_ssq"], writes=["rw_ssq"])
        P.op("dve", lambda e: e.reciprocal(out=ssq[0:ntok, :], in_=ssq[0:ntok, :]), reads=["rw_ssq"], writes=["rw_ssq"])
        P.op("dve", lambda e: e.tensor_scalar_mul(out=kkt[0:ntok, :], in0=kkt[0:ntok, :], scalar1=ssq[0:ntok, 0:1]),
             reads=["rw_kkt", "rw_ssq"], writes=["rw_kkt"])
        P.op("dve", lambda e, Rt=Rt: e.tensor_single_scalar(out=Rt[0:ntok, 64:128], in_=kkt[0:ntok, :], scalar=-1.0, op=ALU.mult),
             reads=["rw_kkt"], writes=[rn])
        P.op("dve", lambda e, Rt=Rt: e.tensor_tensor(out=Rt[0:ntok, 128:192], in0=kkt[0:ntok, :], in1=at[0:ntok, :], op=ALU.mult),
             reads=["rw_kkt", "rw_at"], writes=[rn])
        P.op("dve", lambda e: e.tensor_tensor(out=tmp[0:ntok, :], in0=at[0:ntok, :], in1=R["ka"][0:ntok, :], op=ALU.mult),
             reads=["rw_at", "rwc_par"], writes=["rw_tmp"])
        P.op("dve", lambda e: e.tensor_tensor(out=tmp[0:ntok, :], in0=tmp[0:ntok, :], in1=R["omk"][0:ntok, :], op=ALU.add),
             reads=["rw_tmp", "rwc_omk"], writes=["rw_tmp"])
        P.op("dve", lambda e, cur=cur, Rt=Rt: e.tensor_tensor(out=Rt[0:ntok, 192:256], in0=cur[0:ntok, 64:128], in1=tmp[0:ntok, :],
                                                              op=ALU.mult), reads=[cn, "rw_tmp"], writes=[rn])
        P.op("act", lambda e, cur=cur, Rt=Rt: e.copy(out=Rt[0:ntok, 256:320], in_=cur[0:ntok, 0:64]), reads=[cn], writes=[rn])
        P.op("dve", lambda e, cur=cur, Rt=Rt: e.tensor_tensor(out=t1[0:ntok, :], in0=cur[0:ntok, 0:64], in1=Rt[0:ntok, 192:256],
                                                              op=ALU.mult), reads=[cn, rn], writes=["rw_t1"])
        P.op("dve", lambda e: e.scalar_tensor_tensor(out=junk[0:ntok, :], in0=t1[0:ntok, :], scalar=1.0, in1=R["rk"][0:ntok, :],
                                                     op0=ALU.mult, op1=ALU.mult, accum_out=rks[0:ntok, 0:1]),
             reads=["rw_t1", "rwc_par"], writes=["rw_junk", "rw_rks"])
        P.op("dve", lambda e, cur=cur, GB=GB: e.tensor_scalar_mul(out=GB[0:ntok, 64:128], in0=cur[0:ntok, 128:192],
                                                                  scalar1=rks[0:ntok, 0:1]),
             reads=[cn, "rw_rks"], writes=[gn])
        P.op("act", lambda e, cur=cur, b=b: e.copy(out=R["VV"][tp][0:ntok, b * 64:(b + 1) * 64], in_=cur[0:ntok, 128:192]),
             reads=[cn], writes=["rw_VV%d" % tp])
        P.dma("sp", lambda e, Rt=Rt, b=b: e.dma_start(out=rows_scr[b, t0:t0 + ntok, :], in_=Rt[0:ntok, :]),
              reads=[rn], writes=["rows%d_%d" % (b, tp)])
    P.op("pe", lambda e: e.transpose(out=R["vtp"][:, 0:ntok], in_=R["VV"][tp][0:ntok, :], identity=idf[0:ntok, 0:ntok]),
         reads=["rw_VV%d" % tp, "idf"], writes=["cl_ps"])
    P.op("act", lambda e: e.copy(out=R["vT"][tp][:, 0:ntok], in_=R["vtp"][:, 0:ntok]), reads=["cl_ps"], writes=["rw_vT%d" % tp])


def _rw_scan(P, c, R, n, ntok, rows_scr):
    tp = n % 2
    t0 = n * ntok
    S, stmp, sk = R["S"], R["stmp"], R["sk"]
    vT, yT = R["vT"][tp], R["yT"][tp]
    vn, yn = "rw_vT%d" % tp, "rw_yT%d" % tp
    for blk in range(0, ntok, 16):
        nb = min(16, ntok - blk)
        rb = R["k"] % 2
        R["k"] += 1
        rowbuf = R["rowbuf"][rb]
        P.dma("sp", lambda e, rowbuf=rowbuf, blk=blk, nb=nb: e.dma_start(
            out=rowbuf[0:2, 0:nb * 320].rearrange("b (s c) -> b s c", c=320), in_=rows_scr[:, t0 + blk:t0 + blk + nb, :]),
            reads=["rows0_%d" % tp, "rows1_%d" % tp], writes=["rw_rowbuf%d" % rb])
        for s in range(nb):
            pb = R["step"] % 2
            R["step"] += 1
            rowp = R["rowp"][pb]
            pn = "sT%d" % pb
            t = blk + s
            P.op("pe", lambda e, rowp=rowp, rowbuf=rowbuf, s=s: e.matmul(
                rowp[:, 0:320], lhsT=R["sel"][0:2, :], rhs=rowbuf[0:2, s * 320:(s + 1) * 320], start=True, stop=True),
                reads=["rw_rowbuf%d" % rb, "rwc_sel"], writes=[pn])
            P.op("dve", lambda e, rowp=rowp: e.scalar_tensor_tensor(
                out=stmp[:, :], in0=S[:, :], scalar=1.0, in1=rowp[:, 64:128], op0=ALU.mult, op1=ALU.mult,
                accum_out=sk[:, 0:1]), reads=["rw_S", pn], writes=["rw_stmp", "rw_sk"])
            P.op("dve", lambda e, rowp=rowp: e.tensor_tensor(out=S[:, :], in0=S[:, :], in1=rowp[:, 0:64], op=ALU.mult),
                 reads=["rw_S", pn], writes=["rw_S"])
            P.op("dve", lambda e, rowp=rowp: e.scalar_tensor_tensor(
                out=S[:, :], in0=rowp[:, 128:192], scalar=sk[:, 0:1], in1=S[:, :], op0=ALU.mult, op1=ALU.add),
                reads=["rw_S", "rw_sk", pn], writes=["rw_S"])
            P.op("dve", lambda e, rowp=rowp, t=t: e.scalar_tensor_tensor(
                out=S[:, :], in0=rowp[:, 192:256], scalar=vT[:, t:t + 1], in1=S[:, :], op0=ALU.mult, op1=ALU.add),
                reads=["rw_S", vn, pn], writes=["rw_S"])
            P.op("dve", lambda e, rowp=rowp, t=t: e.scalar_tensor_tensor(
                out=stmp[:, :], in0=S[:, :], scalar=1.0, in1=rowp[:, 256:320], op0=ALU.mult, op1=ALU.mult,
                accum_out=yT[:, t:t + 1]), reads=["rw_S", pn], writes=["rw_stmp", yn])


def _rw_post(P, c, R, n, ntok, out_dst):
    tp = n % 2
    idf = c["idf"]
    ytp = R["ytp"]
    cen, ob, junk, mean, var = R["cen"], R["ob"], R["junk"], R["mean"], R["var"]
    P.op("pe", lambda e: e.transpose(out=ytp[0:ntok, 0:128], in_=R["yT"][tp][:, 0:ntok], identity=idf[:, :]),
         reads=["rw_yT%d" % tp, "idf"], writes=["tot_ps"])
    for b in range(2):
        GB = R["GB"][b][tp]
        gn = "rw_GB%d_%d" % (b, tp)
        ysl = ytp[0:ntok, b * 64:(b + 1) * 64]
        P.op("dve", lambda e, ysl=ysl: e.tensor_reduce(out=mean[0:ntok, :], in_=ysl, axis=AX.X, op=ALU.add),
             reads=["tot_ps"], writes=["rw_mean"])
        P.op("dve", lambda e: e.tensor_single_scalar(out=mean[0:ntok, :], in_=mean[0:ntok, :], scalar=1.0 / 64, op=ALU.mult),
             reads=["rw_mean"], writes=["rw_mean"])
        P.op("dve", lambda e, ysl=ysl: e.tensor_scalar(out=cen[0:ntok, :], in0=ysl, scalar1=mean[0:ntok, 0:1], scalar2=0.0,
                                                       op0=ALU.subtract, op1=ALU.add),
             reads=["tot_ps", "rw_mean"], writes=["rw_cen"])
        P.op("dve", lambda e: e.scalar_tensor_tensor(out=junk[0:ntok, :], in0=cen[0:ntok, :], scalar=1.0, in1=cen[0:ntok, :],
                                                     op0=ALU.mult, op1=ALU.mult, accum_out=var[0:ntok, 0:1]),
             reads=["rw_cen"], writes=["rw_junk", "rw_var"])
        P.fence("dve", ["rw_var"])
        P.op("act", lambda e: e.activation(out=var[0:ntok, :], in_=var[0:ntok, :], func=AF.Sqrt, bias=64e-5, scale=1.0 / 64),
             reads=["rw_var"], writes=["rw_var"])
        P.op("dve", lambda e: e.reciprocal(out=var[0:ntok, :], in_=var[0:ntok, :]), reads=["rw_var"], writes=["rw_var"])
        P.op("dve", lambda e: e.scalar_tensor_tensor(out=cen[0:ntok, :], in0=cen[0:ntok, :], scalar=var[0:ntok, 0:1],
                                                     in1=R["lnw"][0:ntok, :], op0=ALU.mult, op1=ALU.mult),
             reads=["rw_cen", "rw_var", "rwc_par"], writes=["rw_cen"])
        P.op("dve", lambda e: e.tensor_tensor(out=cen[0:ntok, :], in0=cen[0:ntok, :], in1=R["lnb"][0:ntok, :], op=ALU.add),
             reads=["rw_cen", "rwc_par"], writes=["rw_cen"])
        P.op("dve", lambda e, GB=GB: e.tensor_tensor(out=cen[0:ntok, :], in0=cen[0:ntok, :], in1=GB[0:ntok, 64:128], op=ALU.add),
             reads=["rw_cen", gn], writes=["rw_cen"])
        P.op("dve", lambda e, GB=GB: e.tensor_tensor(out=ob[0:ntok, :], in0=cen[0:ntok, :], in1=GB[0:ntok, 0:64], op=ALU.mult),
             reads=["rw_cen", gn], writes=["rw_ob"])
        P.dma("sp", lambda e, b=b: e.dma_start(out=out_dst(b, n), in_=ob[0:ntok, :]), reads=["rw_ob"], writes=["rw_out"])


def _rw_pair(P, c, R, ntiles, ntok, cur_src, prev_src, rows_scr, S0_src, out_dst, ST_dst):
    S = R["S"]
    if S0_src is None:
        P.op("dve", lambda e: e.memset(S[:, :], 0.0), writes=["rw_S"])
    else:
        P.dma("sp", lambda e: e.dma_start(out=S[:, :], in_=S0_src), writes=["rw_S"])
    _rw_prep(P, c, R, 0, ntok, cur_src, prev_src, rows_scr)
    for n in range(ntiles):
        if n + 1 < ntiles:
            _rw_prep(P, c, R, n + 1, ntok, cur_src, prev_src, rows_scr)
        _rw_scan(P, c, R, n, ntok, rows_scr)
        _rw_post(P, c, R, n, ntok, out_dst)
    P.dma("sp", lambda e: e.dma_start(out=ST_dst, in_=S[:, :]), reads=["rw_S"], writes=["rw_STout"])


def build_phase2(n_prompt=2, n_sample=NSEQ_S, ngroups=32, rw_tiles=128, rw_pairs=8, do_fox=True):
    nc = bass.Bass("TRN2", target_bir_lowering=False)
    qTp = _din(nc, "qTp", [2, 64, TP])
    kTp = _din(nc, "kTp", [2, 64, TP])
    vp = _din(nc, "vp", [2, 128, 128, 64])
    lfp = _din(nc, "lfp", [2, 128, 128])
    qTs = _din(nc, "qTs", [NSEQ_S, 64, 16])
    kTs = _din(nc, "kTs", [NSEQ_S, 64, TS])
    vs = _din(nc, "vs", [NSEQ_S, 128, 17, 64])
    lfs = _din(nc, "lfs", [NSEQ_S, 128, 17])
    op_ = _dout(nc, "o_p", [2, TP, 64])
    os_ = _dout(nc, "o_s", [NSEQ_S, 16, 64])
    curp = _din(nc, "rw_curp", [2, TP, 352])
    prvp = _din(nc, "rw_prvp", [2, TP, 352])
    curs = _din(nc, "rw_curs", [NSEQ_S, 16, 352])
    prvs = _din(nc, "rw_prvs", [NSEQ_S, 16, 352])
    S0s = _din(nc, "rw_S0s", [8, 128, 64])
    rwp = _dout(nc, "rw_p", [2, TP, 64])
    rws = _dout(nc, "rw_s", [NSEQ_S, 16, 64])
    STp = _dout(nc, "ST_p", [128, 64])
    STs = _dout(nc, "ST_s", [8, 128, 64])
    rows_p = nc.dram_tensor("rows_p", [2, TP, 320], F32).ap()
    rows_s = nc.dram_tensor("rows_s", [8, 2, 16, 320], F32).ap()
    P = Prog(nc)
    c = _fox_consts(P, nc)
    B = _fox_bufs(P)
    if do_fox:
        for b in range(n_prompt):
            groups = [(512 * g, 512, 4 * g + 4, 4 * g, 4 * g + 2) for g in range(ngroups)]

            def out_fn(P, osb, q0, nq, nqi, wmax, b=b):
                P.dma("sp", lambda e: e.dma_start(
                    out=op_[b, q0:q0 + nq, :].rearrange("(a p) d -> p a d", p=128), in_=osb[:, 0:nqi, :]),
                    reads=["osb"], writes=["o_out"])
            _fox_seq(P, c, B, "p%d" % b, 128, qTp[b], TP, kTp[b], vp[b], lfp[b], groups, out_fn)
        for s in range(n_sample):
            groups = [(0, 16, 17, 16, 16)]

            def out_fn(P, osb, q0, nq, nqi, wmax, s=s):
                P.dma("sp", lambda e: e.dma_start(out=os_[s, :, :], in_=osb[0:16, 0, :]), reads=["osb"], writes=["o_out"])
            _fox_seq(P, c, B, "s%d" % s, 17, qTs[s], 16, kTs[s], vs[s], lfs[s], groups, out_fn)
    R = _rw_setup(P, nc, B)
    if rw_tiles > 0:
        _rw_pair(P, c, R, rw_tiles, 128,
                 lambda b, n: curp[b, n * 128:(n + 1) * 128, :], lambda b, n: prvp[b, n * 128:(n + 1) * 128, :],
                 rows_p, None, lambda b, n: rwp[b, n * 128:(n + 1) * 128, :], STp)
    for pr in range(rw_pairs):
        _rw_pair(P, c, R, 1, 16,
                 lambda b, n, pr=pr: curs[2 * pr + b, :, :], lambda b, n, pr=pr: prvs[2 * pr + b, :, :],
                 rows_s[pr], S0s[pr], lambda b, n, pr=pr: rws[2 * pr + b, :, :], STs[pr])
    P.emit()
    return nc


def rw_inputs(pp, psm, state_rwkv, state_shift, prm):
    ppb = pp.reshape(2, TP, IN_COLS)[:, :, FOX_COLS:]
    pss = psm.reshape(NSEQ_S, 16, IN_COLS)[:, :, FOX_COLS:]
    prev_p = np.zeros_like(ppb)
    prev_p[:, 1:] = ppb[:, :-1]
    prev_s = np.empty_like(pss)
    prev_s[:, 1:] = pss[:, :-1]
    prev_s[:, 0] = state_shift[:, 0, :]
    maps = []
    sel = np.zeros((2, 128), np.float32)
    sel[0, :64] = 1.0
    sel[1, 64:] = 1.0
    for h in range(NCORE):
        cols = np.concatenate([np.arange(h * 64, (h + 1) * 64), 512 + np.arange(h * 64, (h + 1) * 64),
                               1024 + np.arange(h * 64, (h + 1) * 64), np.arange(1536, 1696)])
        hs = slice(h * 64, (h + 1) * 64)
        par = np.concatenate([prm["rwkv_mu"][cols], prm["rwkv_w0"][hs], prm["rwkv_a0"][hs], prm["rwkv_k_k"][hs],
                              prm["rwkv_k_a"][hs], prm["rwkv_r_k"][h], prm["rwkv_lnx_w"][hs], prm["rwkv_lnx_b"][hs]])
        m = {
            "rw_curp": np.ascontiguousarray(ppb[:, :, cols]), "rw_prvp": np.ascontiguousarray(prev_p[:, :, cols]),
            "rw_curs": np.ascontiguousarray(pss[:, :, cols]), "rw_prvs": np.ascontiguousarray(prev_s[:, :, cols]),
            "rw_S0s": np.ascontiguousarray(state_rwkv[:, h].reshape(8, 128, 64)),
            "rw_par": np.ascontiguousarray(np.broadcast_to(par[None, :], (128, NPAR))).astype(np.float32),
            "rw_w2": np.ascontiguousarray(prm["rwkv_w2"][:, hs]), "rw_a2": np.ascontiguousarray(prm["rwkv_a2"][:, hs]),
            "rw_g2": np.ascontiguousarray(prm["rwkv_g2"][:, hs]), "rw_sel": sel,
        }
        maps.append(m)
    return maps


STAGE = 99


def build_phase3(NT, nslots=128):
    nc = bass.Bass("TRN2", target_bir_lowering=False)
    x = _din(nc, "x", [NT * 128, D])
    mixT = _din(nc, "mixT", [NT, 128, 8, 128])
    wout = _din(nc, "w_out", [D, D])
    wq = _din(nc, "w_q", [D, D])
    skT_d = _din(nc, "skT", [64, 16 * 128])
    g1 = _din(nc, "gffn", [128, D])
    g2 = _din(nc, "gfin", [128, D])
    identd = _din(nc, "ident", [128, 128])
    iota_d = _din(nc, "iota", [128, 256])
    pu = _din(nc, "peer_u", [16384, D])
    pv = _din(nc, "peer_v", [16384, D])
    y = _dout(nc, "y", [NT * 128, D])
    P = Prog(nc)
    wst = P.sb("wst", [128, D], F32)
    wo_bf = P.sb("wo_bf", [128, 8, D], BF16)
    wq_bf = P.sb("wq_bf", [128, 8, D], BF16)
    sk_bf = P.sb("sk_bf", [64, 16, 128], BF16)
    g1s = P.sb("g1s", [128, D], F32)
    g2s = P.sb("g2s", [128, D], F32)
    id_f = P.sb("id_f", [128, 128], F32)
    id_b = P.sb("id_b", [128, 128], BF16)
    P.dma("sp", lambda e: e.dma_start(out=g1s[:, :], in_=g1), writes=["g1"])
    P.dma("sp", lambda e: e.dma_start(out=g2s[:, :], in_=g2), writes=["g2"])
    P.dma("sp", lambda e: e.dma_start(out=id_f[:, :], in_=identd), writes=["idf"])
    P.op("dve", lambda e: e.tensor_copy(out=id_b[:, :], in_=id_f[:, :]), reads=["idf"], writes=["idb"])
    for (src, dst, nm) in ((wout, wo_bf, "wo"), (wq, wq_bf, "wq")):
        for dc in range(8):
            P.dma("sp", lambda e, src=src, dc=dc: e.dma_start(out=wst[:, :], in_=src[dc * 128:(dc + 1) * 128, :]), writes=["wst"])
            P.op("act", lambda e, dst=dst, dc=dc: e.copy(out=dst[:, dc, :], in_=wst[:, :]), reads=["wst"], writes=[nm])
    for hf in range(2):
        P.dma("sp", lambda e, hf=hf: e.dma_start(out=wst[0:64, :], in_=skT_d[:, hf * 1024:(hf + 1) * 1024]), writes=["wst"])
        P.op("act", lambda e, hf=hf: e.copy(out=sk_bf[:, hf * 8:(hf + 1) * 8, :],
                                            in_=wst[0:64, :].rearrange("p (h n) -> p h n", h=8)), reads=["wst"], writes=["sk"])

    xt = P.sb("xt", [128, D], F32)
    mt = P.sb("mt", [128, 8, 128], F32)
    mtb = P.sb("mtb", [128, 8, 128], BF16)
    x2 = P.sb("x2", [128, D], F32)
    xn = P.sb("xn", [128, D], F32)
    xnb = P.sb("xnb", [128, D], BF16)
    xnT = P.sb("xnT", [128, D], BF16)
    qT = P.sb("qT", [64, 16 * 128], BF16)
    junkb = P.sb("junkb", [128, D], BF16)
    junkD = P.sb("junkD", [128, D], F32)
    ss = P.sb("ss", [128, 1], F32)
    s_sb = P.sb("s_sb", [128, 16, 128], F32)
    s2 = P.sb("s2", [128, 128], F32)
    sv = P.sb("sv", [128, 16, 16], F32)
    si = P.sb("si", [128, 16, 16], U32)
    sif = P.sb("sif", [128, 16, 16], F32)
    cand = P.sb("cand", [128, 8, 256], F32)
    cidx = P.sb("cidx", [128, 8, 256], F32)
    c2 = P.sb("c2", [128, 256], F32)
    j256 = P.sb("j256", [128, 256], F32)
    tv = P.sb("tv", [128, 8, 16], F32)
    pi = P.sb("pi", [128, 8, 16], U32)
    pif = P.sb("pif", [128, 8, 16], F32)
    iota = P.sb("iota", [128, 256], F32)
    P.dma("sp", lambda e: e.dma_start(out=iota[:, :], in_=iota_d), writes=["iota"])
    negm = P.sb("negm", [128, 8], F32)
    gt = P.sb("gt", [128, 8, 16], F32)
    Z = P.sb("Z", [128, 8], F32)
    ef = P.sb("ef", [128, 128], F32)
    ei = P.sb("ei", [128, 128], I32)
    apre = P.sb("apre", [128, 128], F32)
    coef = P.sb("coef", [128, 128], F32)
    acc = P.sb("acc", [128, D], F32)
    yo = P.sb("yo", [128, D], F32)
    NG = 4
    gb = [P.sb("gbuf%d" % i, [128, D], F32) for i in range(NG)]
    psA = P.ps("psA", [128, D], F32)
    psT = P.ps("psT", [128, D], BF16)
    psS = P.ps("psS", [128, 16, 128], F32)
    gk = 0

    def rms(src, srcn, dst, dstn, gs, gn):
        P.op("act", lambda e: e.activation(out=junkb[:, :], in_=src[:, :], func=AF.Square, accum_out=ss[:, 0:1]),
             reads=[srcn], writes=["junkb", "ss"])
        P.op("act", lambda e: e.activation(out=ss[:, :], in_=ss[:, :], func=AF.Sqrt, bias=1e-6, scale=1.0 / D),
             reads=["ss"], writes=["ss"])
        P.op("dve", lambda e: e.reciprocal(out=ss[:, :], in_=ss[:, :]), reads=["ss"], writes=["ss"])
        P.op("dve", lambda e: e.scalar_tensor_tensor(out=dst[:, :], in0=src[:, :], scalar=ss[:, 0:1], in1=gs[:, :],
                                                     op0=ALU.mult, op1=ALU.mult), reads=[srcn, "ss", gn], writes=[dstn])

    for i in range(NT):
        P.dma("sp", lambda e, i=i: e.dma_start(out=xt[:, :], in_=x[i * 128:(i + 1) * 128, :]), writes=["xt"])
        P.dma("sp", lambda e, i=i: e.dma_start(out=mt[:, :, :], in_=mixT[i]), writes=["mt"])
        P.op("act", lambda e: e.copy(out=mtb[:, :, :], in_=mt[:, :, :]), reads=["mt"], writes=["mtb"])
        for g in range(2):
            for kc in range(8):
                P.op("pe", lambda e, g=g, kc=kc: e.matmul(psA[:, g * 512:(g + 1) * 512], lhsT=mtb[:, kc, :],
                                                          rhs=wo_bf[:, kc, g * 512:(g + 1) * 512], start=(kc == 0), stop=(kc == 7)),
                     reads=["mtb", "wo"], writes=["psA%d" % g])
        for g in range(2):
            P.op("dve", lambda e, g=g: e.tensor_tensor(out=x2[:, g * 512:(g + 1) * 512], in0=psA[:, g * 512:(g + 1) * 512],
                                                       in1=xt[:, g * 512:(g + 1) * 512], op=ALU.add),
                 reads=["psA%d" % g, "xt"], writes=["x2"])
        rms(x2, "x2", xn, "xn", g1s, "g1")
        P.op("act", lambda e: e.copy(out=xnb[:, :], in_=xn[:, :]), reads=["xn"], writes=["xnb"])
        for dc in range(8 if STAGE >= 2 else 0):
            P.op("pe", lambda e, dc=dc: e.transpose(out=psT[:, dc * 128:(dc + 1) * 128], in_=xnb[:, dc * 128:(dc + 1) * 128],
                                                    identity=id_b[:, :]), reads=["xnb", "idb"], writes=["psT"])
        P.op("act", lambda e: e.copy(out=xnT[:, :], in_=psT[:, :]), reads=["psT"], writes=["xnT"])
        for hc in range(16 if STAGE >= 3 else 0):
            bank = hc % 2
            for dc in range(8):
                P.op("pe", lambda e, hc=hc, dc=dc, bank=bank: e.matmul(
                    psA[0:64, bank * 512:bank * 512 + 128], lhsT=wq_bf[:, dc, hc * 64:(hc + 1) * 64],
                    rhs=xnT[:, dc * 128:(dc + 1) * 128], start=(dc == 0), stop=(dc == 7)),
                    reads=["xnT", "wq"], writes=["psA%d" % bank])
            if hc % 2 == 0:
                P.op("act", lambda e, hc=hc, bank=bank: e.copy(out=qT[0:64, hc * 128:(hc + 1) * 128],
                                                               in_=psA[0:64, bank * 512:bank * 512 + 128]),
                     reads=["psA%d" % bank], writes=["qT"])
            else:
                P.op("dve", lambda e, hc=hc, bank=bank: e.tensor_copy(out=qT[0:64, hc * 128:(hc + 1) * 128],
                                                                      in_=psA[0:64, bank * 512:bank * 512 + 128]),
                     reads=["psA%d" % bank], writes=["qT"])
        for hc in range(16 if STAGE >= 4 else 0):
            P.op("pe", lambda e, hc=hc: e.matmul(psS[:, hc, :], lhsT=qT[0:64, hc * 128:(hc + 1) * 128],
                                                 rhs=sk_bf[0:64, hc, :], start=True, stop=True),
                 reads=["qT", "sk"], writes=["psS"])
        for bk in range(4):
            if bk % 2 == 0:
                P.op("act", lambda e, bk=bk: e.copy(out=s_sb[:, 4 * bk:4 * bk + 4, :], in_=psS[:, 4 * bk:4 * bk + 4, :]),
                     reads=["psS"], writes=["s_sb"])
            else:
                P.op("dve", lambda e, bk=bk: e.tensor_copy(out=s_sb[:, 4 * bk:4 * bk + 4, :], in_=psS[:, 4 * bk:4 * bk + 4, :]),
                     reads=["psS"], writes=["s_sb"])
        for hc in range(16 if STAGE >= 5 else 0):
            P.op("dve", lambda e, hc=hc: e.max(out=sv[:, hc, 0:8], in_=s_sb[:, hc, :]), reads=["s_sb"], writes=["sv"])
            P.op("dve", lambda e, hc=hc: e.max_index(out=si[:, hc, 0:8], in_max=sv[:, hc, 0:8], in_values=s_sb[:, hc, :]),
                 reads=["s_sb", "sv"], writes=["si"])
            P.op("dve", lambda e, hc=hc: e.match_replace(out=s2[:, :], in_to_replace=sv[:, hc, 0:8], in_values=s_sb[:, hc, :],
                                                         imm_value=-1e30), reads=["s_sb", "sv"], writes=["s2"])
            P.op("dve", lambda e, hc=hc: e.max(out=sv[:, hc, 8:16], in_=s2[:, :]), reads=["s2"], writes=["sv"])
            P.op("dve", lambda e, hc=hc: e.max_index(out=si[:, hc, 8:16], in_max=sv[:, hc, 8:16], in_values=s2[:, :]),
                 reads=["s2", "sv"], writes=["si"])
        P.op("dve", lambda e: e.tensor_copy(out=sif[:, :, :], in_=si[:, :, :]), reads=["si"], writes=["sif"])
        for h in range(8 if STAGE >= 6 else 0):
            P.op("dve", lambda e, h=h: e.tensor_single_scalar(out=sif[:, 2 * h, :], in_=sif[:, 2 * h, :], scalar=128.0, op=ALU.mult),
                 reads=["sif"], writes=["sif"])
            P.op("dve", lambda e, h=h: e.tensor_tensor(
                out=cand[:, h, :].rearrange("p (a b) -> p a b", a=16),
                in0=sv[:, 2 * h, :].unsqueeze(2).to_broadcast([128, 16, 16]),
                in1=sv[:, 2 * h + 1, :].unsqueeze(1).to_broadcast([128, 16, 16]), op=ALU.add), reads=["sv"], writes=["cand"])
            P.op("dve", lambda e, h=h: e.tensor_tensor(
                out=cidx[:, h, :].rearrange("p (a b) -> p a b", a=16),
                in0=sif[:, 2 * h, :].unsqueeze(2).to_broadcast([128, 16, 16]),
                in1=sif[:, 2 * h + 1, :].unsqueeze(1).to_broadcast([128, 16, 16]), op=ALU.add), reads=["sif"], writes=["cidx"])
            P.op("dve", lambda e, h=h: e.max(out=tv[:, h, 0:8], in_=cand[:, h, :]), reads=["cand"], writes=["tv"])
            P.op("dve", lambda e, h=h: e.max_index(out=pi[:, h, 0:8], in_max=tv[:, h, 0:8], in_values=cand[:, h, :]),
                 reads=["cand", "tv"], writes=["pi"])
            P.op("dve", lambda e, h=h: e.match_replace(out=c2[:, :], in_to_replace=tv[:, h, 0:8], in_values=cand[:, h, :],
                                                       imm_value=-1e30), reads=["cand", "tv"], writes=["c2"])
            P.op("dve", lambda e, h=h: e.max(out=tv[:, h, 8:16], in_=c2[:, :]), reads=["c2"], writes=["tv"])
            P.op("dve", lambda e, h=h: e.max_index(out=pi[:, h, 8:16], in_max=tv[:, h, 8:16], in_values=c2[:, :]),
                 reads=["c2", "tv"], writes=["pi"])
            P.op("dve", lambda e, h=h: e.tensor_copy(out=pif[:, h, :], in_=pi[:, h, :]), reads=["pi"], writes=["pif"])
            for k in range(16):
                P.op("dve", lambda e, h=h, k=k: e.scalar_tensor_tensor(
                    out=j256[:, :], in0=iota[:, :], scalar=pif[:, h, k:k + 1], in1=cidx[:, h, :], op0=ALU.is_equal,
                    op1=ALU.mult, accum_out=ef[:, h * 16 + k:h * 16 + k + 1]), reads=["iota", "cidx", "pif"], writes=["j256", "ef"])
        P.op("dve", lambda e: e.tensor_copy(out=ei[:, :], in_=ef[:, :]), reads=["ef"], writes=["ei"])
        P.op("dve", lambda e: e.tensor_single_scalar(out=negm[:, :], in_=tv[:, :, 0], scalar=-1.0, op=ALU.mult),
             reads=["tv"], writes=["negm"])
        for h in range(8 if STAGE >= 7 else 0):
            P.op("act", lambda e, h=h: e.activation(out=gt[:, h, :], in_=tv[:, h, :], func=AF.Exp, bias=negm[:, h:h + 1],
                                                    scale=1.0, accum_out=Z[:, h:h + 1]), reads=["tv", "negm"], writes=["gt", "Z"])
        P.fence("act", ["Z", "gt"])
        P.op("dve", lambda e: e.reciprocal(out=Z[:, :], in_=Z[:, :]), reads=["Z"], writes=["Z"])
        for h in range(8):
            P.op("dve", lambda e, h=h: e.tensor_scalar_mul(out=gt[:, h, :], in0=gt[:, h, :], scalar1=Z[:, h:h + 1]),
                 reads=["gt", "Z"], writes=["gt"])
        for sl in range(nslots):
            r = gk % NG
            gk += 1
            P.dma("pool", lambda e, r=r, sl=sl: e.indirect_dma_start(
                out=gb[r][:, :], out_offset=None, in_=pu[:, :],
                in_offset=bass.IndirectOffsetOnAxis(ap=ei[:, sl:sl + 1], axis=0)), reads=["ei"], writes=["gbuf%d" % r])
            P.op("dve", lambda e, r=r, sl=sl: e.scalar_tensor_tensor(
                out=junkD[:, :], in0=gb[r][:, :], scalar=1.0, in1=xn[:, :], op0=ALU.mult, op1=ALU.mult,
                accum_out=apre[:, sl:sl + 1]), reads=["gbuf%d" % r, "xn"], writes=["junkD", "apre"])
        if nslots > 0:
            P.fence("dve", ["apre"])
            P.op("act", lambda e: e.activation(out=coef[:, 0:nslots], in_=apre[:, 0:nslots], func=AF.Gelu),
                 reads=["apre"], writes=["coef"])
            P.op("dve", lambda e: e.tensor_tensor(out=coef[:, 0:nslots], in0=coef[:, 0:nslots],
                                                  in1=gt[:, :, :].rearrange("p h k -> p (h k)")[:, 0:nslots], op=ALU.mult),
                 reads=["coef", "gt"], writes=["coef"])
        for sl in range(nslots):
            r = gk % NG
            gk += 1
            P.dma("pool", lambda e, r=r, sl=sl: e.indirect_dma_start(
                out=gb[r][:, :], out_offset=None, in_=pv[:, :],
                in_offset=bass.IndirectOffsetOnAxis(ap=ei[:, sl:sl + 1], axis=0)), reads=["ei"], writes=["gbuf%d" % r])
            if sl == 0:
                P.op("dve", lambda e, r=r: e.tensor_scalar_mul(out=acc[:, :], in0=gb[r][:, :], scalar1=coef[:, 0:1]),
                     reads=["gbuf%d" % r, "coef"], writes=["acc"])
            else:
                P.op("dve", lambda e, r=r, sl=sl: e.scalar_tensor_tensor(
                    out=acc[:, :], in0=gb[r][:, :], scalar=coef[:, sl:sl + 1], in1=acc[:, :], op0=ALU.mult, op1=ALU.add),
                    reads=["gbuf%d" % r, "coef", "acc"], writes=["acc"])
        if nslots > 0:
            P.op("dve", lambda e: e.tensor_tensor(out=x2[:, :], in0=x2[:, :], in1=acc[:, :], op=ALU.add),
                 reads=["x2", "acc"], writes=["x2"])
        rms(x2, "x2", yo, "yo", g2s, "g2")
        if nslots == 0:
            P.op("dve", lambda e: e.tensor_copy(out=yo[:, 0:128], in_=ef[:, :]), reads=["ef", "yo"], writes=["yo"])
            P.op("dve", lambda e: e.tensor_copy(out=yo[:, 128:256], in_=gt[:, :, :].rearrange("p h k -> p (h k)")),
                 reads=["gt", "yo"], writes=["yo"])
        P.dma("sp", lambda e, i=i: e.dma_start(out=y[i * 128:(i + 1) * 128, :], in_=yo[:, :]), reads=["yo"], writes=["yout"])
    P.emit()
    return nc


def run_phase3(x_prompt, x_sample, mix_p, mix_s, w_out, norm_ffn_g, peer_w_q, peer_sub_keys, peer_u, peer_v, norm_final_g,
               NT=33, nslots=128):
    xp = np.ascontiguousarray(x_prompt).reshape(-1, D)
    xs = np.ascontiguousarray(x_sample).reshape(-1, D)
    nc = build_phase3(NT, nslots)
    g1 = np.ascontiguousarray(np.broadcast_to(norm_ffn_g.reshape(1, D), (128, D))).astype(np.float32)
    g2 = np.ascontiguousarray(np.broadcast_to(norm_final_g.reshape(1, D), (128, D))).astype(np.float32)
    skT = np.ascontiguousarray(np.transpose(peer_sub_keys, (3, 0, 1, 2)).reshape(64, 16 * 128))
    iota_h = np.ascontiguousarray(np.broadcast_to(np.arange(256, dtype=np.float32)[None, :], (128, 256)))
    in_maps = []
    for c in range(NCORE):
        xc = np.zeros((33 * 128, D), np.float32)
        mc = np.zeros((33 * 128, D), np.float32)
        xc[:4096] = xp[c * 4096:(c + 1) * 4096]
        xc[4096:4128] = xs[c * 32:(c + 1) * 32]
        mc[:4096] = mix_p[c * 4096:(c + 1) * 4096]
        mc[4096:4128] = mix_s[c * 32:(c + 1) * 32]
        mT = np.ascontiguousarray(np.transpose(mc.reshape(33, 128, 8, 128), (0, 3, 2, 1)))
        in_maps.append({"x": xc[:NT * 128], "mixT": mT[:NT], "w_out": np.ascontiguousarray(w_out), "w_q": np.ascontiguousarray(peer_w_q),
                        "skT": skT, "iota": iota_h, "gffn": g1, "gfin": g2, "ident": _ident(), "peer_u": np.ascontiguousarray(peer_u),
                        "peer_v": np.ascontiguousarray(peer_v)})
    res = run_bass_kernel_spmd(nc, in_maps, core_ids=list(range(NCORE)))
    yp = np.concatenate([res.results[c]["y"][:4096] for c in range(NCORE)], axis=0) if NT == 33 else None
    ysm = np.concatenate([res.results[c]["y"][4096:4128] for c in range(NCORE)], axis=0) if NT == 33 else None
    return yp, ysm, res


def kernel(x_prompt, x_sample, cache_fox_k, cache_fox_v, cache_fox_logf, state_rwkv, state_shift,
           norm_mix_g, w_in, fox_b_f, rwkv_mu, rwkv_w0, rwkv_w2, rwkv_a0, rwkv_a2, rwkv_g2,
           rwkv_k_k, rwkv_k_a, rwkv_r_k, rwkv_lnx_w, rwkv_lnx_b, w_out, norm_ffn_g,
           peer_w_q, peer_sub_keys, peer_u, peer_v, norm_final_g):
    f = lambda a: np.asarray(a, dtype=np.float32)
    x_prompt, x_sample = f(x_prompt), f(x_sample)
    pp, psm = run_phase1(x_prompt, x_sample, f(norm_mix_g)[0], f(w_in)[0], f(fox_b_f)[0])
    prm = {"rwkv_mu": f(rwkv_mu)[0], "rwkv_w0": f(rwkv_w0)[0], "rwkv_w2": f(rwkv_w2)[0], "rwkv_a0": f(rwkv_a0)[0],
           "rwkv_a2": f(rwkv_a2)[0], "rwkv_g2": f(rwkv_g2)[0], "rwkv_k_k": f(rwkv_k_k)[0], "rwkv_k_a": f(rwkv_k_a)[0],
           "rwkv_r_k": f(rwkv_r_k)[0], "rwkv_lnx_w": f(rwkv_lnx_w)[0], "rwkv_lnx_b": f(rwkv_lnx_b)[0]}
    maps = fox_inputs(pp, psm, f(cache_fox_k)[0], f(cache_fox_v)[0], f(cache_fox_logf)[0])
    rmaps = rw_inputs(pp, psm, f(state_rwkv)[0], f(state_shift)[0], prm)
    for m, r in zip(maps, rmaps):
        m.update(r)
    nc2 = build_phase2()
    res2 = run_bass_kernel_spmd(nc2, maps, core_ids=list(range(NCORE)))
    del maps, rmaps
    R2 = res2.results
    mix_p = np.empty((2, TP, 1024), np.float32)
    mix_s = np.empty((NSEQ_S, 16, 1024), np.float32)
    S_p = np.empty((1, 2, 8, 64, 64), np.float32)
    S_s = np.empty((1, NSEQ_S, 8, 64, 64), np.float32)
    for h in range(NCORE):
        mix_p[:, :, h * 64:(h + 1) * 64] = R2[h]["o_p"]
        mix_p[:, :, 512 + h * 64:512 + (h + 1) * 64] = R2[h]["rw_p"]
        mix_s[:, :, h * 64:(h + 1) * 64] = R2[h]["o_s"]
        mix_s[:, :, 512 + h * 64:512 + (h + 1) * 64] = R2[h]["rw_s"]
        S_p[0, :, h] = R2[h]["ST_p"].reshape(2, 64, 64)
        S_s[0, :, h] = R2[h]["ST_s"].reshape(NSEQ_S, 64, 64)
    yp, ysm, _ = run_phase3(x_prompt, x_sample, mix_p.reshape(-1, 1024), mix_s.reshape(-1, 1024), f(w_out)[0],
                            f(norm_ffn_g)[0], f(peer_w_q)[0], f(peer_sub_keys)[0], f(peer_u)[0], f(peer_v)[0],
                            f(norm_final_g))
    ppb = pp.reshape(2, TP, IN_COLS)
    pss = psm.reshape(NSEQ_S, 16, IN_COLS)
    c = np.ascontiguousarray
    return (
        c(yp.reshape(2, TP, 1024)), c(ysm.reshape(NSEQ_S, 16, 1024)),
        c(ppb[:, :, 512:1024].reshape(1, 2, TP, 8, 64)), c(ppb[:, :, 1024:1536].reshape(1, 2, TP, 8, 64)),
        c(ppb[:, :, 1536:1544].reshape(1, 2, TP, 8)), S_p, c(ppb[:, -1:, FOX_COLS:].reshape(1, 2, 1, RW_COLS)),
        c(pss[:, :, 512:1024].reshape(1, NSEQ_S, 16, 8, 64)), c(pss[:, :, 1024:1536].reshape(1, NSEQ_S, 16, 8, 64)),
        c(pss[:, :, 1536:1544].reshape(1, NSEQ_S, 16, 8)), S_s, c(pss[:, -1:, FOX_COLS:].reshape(1, NSEQ_S, 1, RW_COLS)),
    )
```

```python
from contextlib import ExitStack
import math
import numpy as np
import concourse.bass as bass
import concourse.mybir as mybir
from concourse.bass_utils import run_bass_kernel_spmd

F32 = mybir.dt.float32
BF16 = mybir.dt.bfloat16
I32 = mybir.dt.int32
U32 = mybir.dt.uint32
ALU = mybir.AluOpType
AF = mybir.ActivationFunctionType
AX = mybir.AxisListType

D = 1024
IN_COLS = 3240
FOX_COLS = 1544
RW_COLS = 1696
NCORE = 8


class Prog:
    ENGS = ("pe", "act", "dve", "pool", "sp")

    def __init__(self, nc):
        self.nc = nc
        self.st = ExitStack()
        self.ops = {e: [] for e in self.ENGS}
        self.cnt = {}
        self.waited = {e: {} for e in self.ENGS}
        self.lastw = {}
        self.readers = {}
        self.ndma = {e: 0 for e in self.ENGS}
        self.NS = 8
        self.nosame = set()
        self.fence_t = {}
        self.uid = 0

    def sb(self, name, shape, dt):
        return self.st.enter_context(self.nc.sbuf_tensor("sb_" + name, list(shape), dt))

    def ps(self, name, shape, dt):
        return self.st.enter_context(self.nc.psum_tensor("ps_" + name, list(shape), dt))

    def _deps(self, eng, reads, writes):
        deps = []
        for b in reads:
            if b in self.lastw:
                deps.append(self.lastw[b])
        for b in writes:
            if b in self.lastw:
                deps.append(self.lastw[b])
            deps.extend(self.readers.get(b, ()))
        best = {}
        for (k, v) in deps:
            if eng == "pe" and k == "pe":
                continue
            if k == eng and eng in self.nosame:
                continue
            if self.waited[eng].get(k, 0) >= v:
                continue
            best[k] = max(best.get(k, 0), v)
        for k, v in best.items():
            self.waited[eng][k] = v
        return list(best.items())

    def _record(self, tok, reads, writes):
        for b in reads:
            self.readers.setdefault(b, []).append(tok)
        for b in writes:
            self.lastw[b] = tok
            self.readers[b] = []

    def op(self, eng, fn, reads=(), writes=()):
        waits = self._deps(eng, reads, writes)
        self.cnt[eng] = self.cnt.get(eng, 0) + 1
        self.ops[eng].append((waits, fn, eng, 1))
        self._record((eng, self.cnt[eng]), reads, writes)

    def fence(self, eng, names):
        if eng not in self.fence_t:
            self.fence_t[eng] = self.sb("fence_" + eng, [128, 2], F32)
        t = self.fence_t[eng]
        if eng == "act":
            self.op("act", lambda e: e.copy(out=t[:, 1:2], in_=t[:, 0:1]), reads=(), writes=list(names))
        else:
            self.op("dve", lambda e: e.tensor_copy(out=t[:, 1:2], in_=t[:, 0:1]), reads=(), writes=list(names))

    def dma(self, q, fn, reads=(), writes=()):
        waits = self._deps(q, reads, writes)
        k = "d_%s_%d" % (q, self.ndma[q] % self.NS)
        self.ndma[q] += 1
        prev = self.cnt.get(k, 0)
        if prev and self.waited[q].get(k, 0) < prev:
            self.waited[q][k] = prev
            waits = [w for w in waits if w[0] != k] + [(k, prev)]
        self.cnt[k] = self.cnt.get(k, 0) + 16
        self.ops[q].append((waits, fn, k, 16))
        self._record((k, self.cnt[k]), reads, writes)

    def emit(self):
        nc = self.nc
        keys = sorted(self.cnt.keys())
        sems = {k: self.st.enter_context(nc.semaphore("s_" + k)) for k in keys}
        final = [(k, self.cnt[k]) for k in keys]
        ops = self.ops

        def run(name, e):
            for (waits, fn, k, inc) in ops[name]:
                for (wk, wv) in waits:
                    e.wait_ge(sems[wk], wv)
                fn(e).then_inc(sems[k], inc)
            if name == "sp":
                for (k, v) in final:
                    e.wait_ge(sems[k], v)

        with nc.Block() as block:
            @block.tensor
            def _(e):
                run("pe", e)

            @block.scalar
            def _(e):
                run("act", e)

            @block.vector
            def _(e):
                run("dve", e)

            @block.gpsimd
            def _(e):
                run("pool", e)

            @block.sync
            def _(e):
                run("sp", e)
        self.st.close()


def _din(nc, name, shape, dt=F32):
    return nc.dram_tensor(name, list(shape), dt, kind="ExternalInput").ap()


def _dout(nc, name, shape, dt=F32):
    return nc.dram_tensor(name, list(shape), dt, kind="ExternalOutput").ap()


def _load_cast(P, name, dram_ap, shape, stage, stage_name, q="sp", eng="act"):
    t = P.sb(name, shape, BF16)
    p, n = shape
    P.dma(q, lambda e: e.dma_start(out=stage[0:p, 0:n], in_=dram_ap), writes=[stage_name])
    if eng == "act":
        P.op("act", lambda e: e.copy(out=t[:, :], in_=stage[0:p, 0:n]), reads=[stage_name], writes=[name])
    else:
        P.op("dve", lambda e: e.tensor_copy(out=t[:, :], in_=stage[0:p, 0:n]), reads=[stage_name], writes=[name])
    return t


def build_phase1(NT):
    nc = bass.Bass("TRN2", target_bir_lowering=False)
    x = _din(nc, "x", [NT * 128, D])
    gbc = _din(nc, "gbc", [128, D])
    w = _din(nc, "w_in", [D, IN_COLS])
    bfb = _din(nc, "bfb", [128, 8])
    identd = _din(nc, "ident", [128, 128])
    proj = _dout(nc, "proj", [NT * 128, IN_COLS])
    P = Prog(nc)
    wst = P.sb("wst", [128, IN_COLS], F32)
    w_bf = P.sb("w_bf", [128, 8, IN_COLS], BF16)
    g_sb = P.sb("g_sb", [128, D], F32)
    bf_sb = P.sb("bf_sb", [128, 8], F32)
    id_f = P.sb("id_f", [128, 128], F32)
    id_b = P.sb("id_b", [128, 128], BF16)
    P.dma("sp", lambda e: e.dma_start(out=g_sb[:, :], in_=gbc), writes=["g"])
    P.dma("sp", lambda e: e.dma_start(out=bf_sb[:, :], in_=bfb), writes=["bf"])
    P.dma("sp", lambda e: e.dma_start(out=id_f[:, :], in_=identd), writes=["idf"])
    P.op("dve", lambda e: e.tensor_copy(out=id_b[:, :], in_=id_f[:, :]), reads=["idf"], writes=["idb"])
    for dc in range(8):
        P.dma("sp", lambda e, dc=dc: e.dma_start(out=wst[:, :], in_=w[dc * 128:(dc + 1) * 128, :]),
              writes=["wst"])
        P.op("act", lambda e, dc=dc: e.copy(out=w_bf[:, dc, :], in_=wst[:, :]), reads=["wst"], writes=["w%d" % dc])
    wnames = ["w%d" % dc for dc in range(8)]
    xt = [P.sb("xt%d" % i, [128, D], F32) for i in range(2)]
    junk = P.sb("junk", [128, D], BF16)
    ss = [P.sb("ss%d" % i, [128, 1], F32) for i in range(2)]
    rstd = [P.sb("rstd%d" % i, [128, 1], F32) for i in range(2)]
    h = [P.sb("h%d" % i, [128, D], BF16) for i in range(2)]
    hT = [P.sb("hT%d" % i, [128, D], BF16) for i in range(2)]
    pr = [P.sb("pr%d" % i, [128, IN_COLS], F32) for i in range(2)]
    lz = P.sb("lz", [128, 8], F32)
    psT = [P.ps("psT%d" % i, [128, D], BF16) for i in range(2)]
    psP = [P.ps("psP%d" % i, [128, 512], F32) for i in range(4)]
    groups = [(c0, min(c0 + 512, IN_COLS)) for c0 in range(0, IN_COLS, 512)]
    gi = 0
    for i in range(NT):
        b = i % 2
        X, H, HT, PR = xt[b], h[b], hT[b], pr[b]
        P.dma("sp", lambda e, X=X, i=i: e.dma_start(out=X[:, :], in_=x[i * 128:(i + 1) * 128, :]),
              writes=["xt%d" % b])
        P.op("act", lambda e, X=X, b=b: e.activation(out=junk[:, :], in_=X[:, :], func=AF.Square,
                                                      accum_out=ss[b][:, 0:1]),
             reads=["xt%d" % b], writes=["junk", "ss%d" % b])
        P.op("act", lambda e, b=b: e.activation(out=rstd[b][:, :], in_=ss[b][:, :], func=AF.Sqrt, bias=1e-6,
                                                scale=1.0 / D),
             reads=["ss%d" % b], writes=["rstd%d" % b])
        P.op("dve", lambda e, b=b: e.reciprocal(out=rstd[b][:, :], in_=rstd[b][:, :]),
             reads=["rstd%d" % b], writes=["rstd%d" % b])
        P.op("dve", lambda e, X=X, H=H, b=b: e.scalar_tensor_tensor(
            out=H[:, :], in0=X[:, :], scalar=rstd[b][:, 0:1], in1=g_sb[:, :], op0=ALU.mult, op1=ALU.mult),
            reads=["xt%d" % b, "rstd%d" % b, "g"], writes=["h%d" % b])
        for dc in range(8):
            P.op("pe", lambda e, H=H, b=b, dc=dc: e.transpose(
                out=psT[b][:, dc * 128:(dc + 1) * 128], in_=H[:, dc * 128:(dc + 1) * 128], identity=id_b[:, :]),
                reads=["h%d" % b, "idb"], writes=["psT%d" % b])
        P.op("act", lambda e, HT=HT, b=b: e.copy(out=HT[:, :], in_=psT[b][:, :]),
             reads=["psT%d" % b], writes=["hT%d" % b])
        for (c0, c1) in groups:
            pp = gi % 4
            gi += 1
            n = c1 - c0
            for dc in range(8):
                P.op("pe", lambda e, HT=HT, pp=pp, dc=dc, c0=c0, c1=c1, n=n: e.matmul(
                    psP[pp][:, 0:n], lhsT=HT[:, dc * 128:(dc + 1) * 128], rhs=w_bf[:, dc, c0:c1],
                    start=(dc == 0), stop=(dc == 7)),
                    reads=["hT%d" % b] + wnames, writes=["psP%d" % pp])
            if gi % 2 == 0:
                P.op("act", lambda e, PR=PR, pp=pp, c0=c0, c1=c1, n=n: e.copy(out=PR[:, c0:c1], in_=psP[pp][:, 0:n]),
                     reads=["psP%d" % pp], writes=["pr%d" % b])
            else:
                P.op("dve", lambda e, PR=PR, pp=pp, c0=c0, c1=c1, n=n: e.tensor_copy(out=PR[:, c0:c1],
                                                                                     in_=psP[pp][:, 0:n]),
                     reads=["psP%d" % pp], writes=["pr%d" % b])
        P.op("dve", lambda e, PR=PR: e.tensor_tensor(out=lz[:, :], in0=PR[:, 1536:1544], in1=bf_sb[:, :], op=ALU.add),
             reads=["pr%d" % b, "bf"], writes=["lz"])
        P.op("act", lambda e: e.activation(out=lz[:, :], in_=lz[:, :], func=AF.Exp, scale=-1.0),
             reads=["lz"], writes=["lz"])
        P.op("act", lambda e: e.activation(out=lz[:, :], in_=lz[:, :], func=AF.Ln, bias=1.0, scale=1.0),
             reads=["lz"], writes=["lz"])
        P.op("dve", lambda e, PR=PR: e.tensor_single_scalar(out=PR[:, 1536:1544], in_=lz[:, :], scalar=-1.0,
                                                            op=ALU.mult),
             reads=["lz"], writes=["pr%d" % b])
        P.dma("sp", lambda e, PR=PR, i=i: e.dma_start(out=proj[i * 128:(i + 1) * 128, :], in_=PR[:, :]),
              reads=["pr%d" % b], writes=["out%d" % i])
    P.emit()
    return nc


def _ident():
    return np.eye(128, dtype=np.float32)


def run_phase1(x_prompt, x_sample, norm_mix_g, w_in, fox_b_f):
    NT = 33
    xp = np.ascontiguousarray(x_prompt).reshape(-1, D)
    xs = np.ascontiguousarray(x_sample).reshape(-1, D)
    nc = build_phase1(NT)
    gbc = np.ascontiguousarray(np.broadcast_to(norm_mix_g.reshape(1, D), (128, D))).astype(np.float32)
    bfb = np.ascontiguousarray(np.broadcast_to(fox_b_f.reshape(1, 8), (128, 8))).astype(np.float32)
    w = np.ascontiguousarray(w_in.reshape(D, IN_COLS))
    in_maps = []
    for c in range(NCORE):
        xc = np.zeros((NT * 128, D), np.float32)
        xc[:4096] = xp[c * 4096:(c + 1) * 4096]
        xc[4096:4128] = xs[c * 32:(c + 1) * 32]
        in_maps.append({"x": xc, "gbc": gbc, "w_in": w, "bfb": bfb, "ident": _ident()})
    res = run_bass_kernel_spmd(nc, in_maps, core_ids=list(range(NCORE)))
    pp = np.concatenate([res.results[c]["proj"][:4096] for c in range(NCORE)], axis=0)
    psm = np.concatenate([res.results[c]["proj"][4096:4128] for c in range(NCORE)], axis=0)
    return pp, psm


TP = 16384
TS = 2176
NSEQ_S = 16


def _fox_consts(P, nc):
    c = {}
    tri_d = _din(nc, "tri", [128, 128])
    ones_d = _din(nc, "ones", [128, 128])
    id_d = _din(nc, "ident", [128, 128])
    mask_d = _din(nc, "mask", [128, 4 * 512])
    c["tri"] = P.sb("tri", [128, 128], F32)
    c["ones"] = P.sb("ones", [128, 128], F32)
    c["idf"] = P.sb("idf", [128, 128], F32)
    c["idb"] = P.sb("idb", [128, 128], BF16)
    c["maskf"] = P.sb("maskf", [128, 2048], F32)
    c["mask"] = P.sb("maskb", [128, 4, 512], BF16)
    P.dma("sp", lambda e: e.dma_start(out=c["tri"][:, :], in_=tri_d), writes=["tri"])
    P.dma("sp", lambda e: e.dma_start(out=c["ones"][:, :], in_=ones_d), writes=["ones"])
    P.dma("sp", lambda e: e.dma_start(out=c["idf"][:, :], in_=id_d), writes=["idf"])
    P.dma("sp", lambda e: e.dma_start(out=c["maskf"][:, :], in_=mask_d), writes=["maskf"])
    P.op("dve", lambda e: e.tensor_copy(out=c["idb"][:, :], in_=c["idf"][:, :]), reads=["idf"], writes=["idb"])
    P.op("dve", lambda e: e.tensor_copy(out=c["mask"][:, :, :], in_=c["maskf"][:, :].rearrange("p (a b) -> p a b", a=4)),
         reads=["maskf"], writes=["maskb"])
    return c


def _fox_seq(P, c, B, tag, NT, qT_src, nq_tot, kT_src, v_src, lf_src, groups, out_fn):
    T = NT * 128
    qT, kT, vv, stage = B["qT"], B["kT"], B["vv"], B["stage"]
    k = 0
    for (dst, src, n, nm) in ((qT, qT_src, nq_tot, "qT"), (kT, kT_src, T, "kT")):
        for c0 in range(0, n, 2048):
            w = min(2048, n - c0)
            s = k % 2
            k += 1
            P.dma("sp", lambda e, s=s, src=src, c0=c0, w=w: e.dma_start(out=stage[s][0:64, 0:w], in_=src[:, c0:c0 + w]),
                  writes=["stage%d" % s])
            eng = "act" if k % 2 else "dve"
            if eng == "act":
                P.op("act", lambda e, s=s, dst=dst, c0=c0, w=w: e.copy(out=dst[:, c0:c0 + w], in_=stage[s][0:64, 0:w]),
                     reads=["stage%d" % s], writes=[nm])
            else:
                P.op("dve", lambda e, s=s, dst=dst, c0=c0, w=w: e.tensor_copy(out=dst[:, c0:c0 + w], in_=stage[s][0:64, 0:w]),
                     reads=["stage%d" % s], writes=[nm])
    for j0 in range(0, NT, 32):
        nj = min(32, NT - j0)
        s = k % 2
        k += 1
        P.dma("sp", lambda e, s=s, j0=j0, nj=nj: e.dma_start(
            out=stage[s][:, 0:nj * 64].rearrange("p (j d) -> p j d", d=64), in_=v_src[:, j0:j0 + nj, :]),
            writes=["stage%d" % s])
        P.op("dve", lambda e, s=s, j0=j0, nj=nj: e.tensor_copy(
            out=vv[:, j0:j0 + nj, 0:64], in_=stage[s][:, 0:nj * 64].rearrange("p (j d) -> p j d", d=64)),
            reads=["stage%d" % s], writes=["vv"])
    L = B["L"]
    P.dma("sp", lambda e: e.dma_start(out=L[:, 0:NT], in_=lf_src), writes=["L"])
    cl_ps, tot_ps = B["cl_ps"], B["tot_ps"]
    P.op("pe", lambda e: e.matmul(cl_ps[:, 0:NT], lhsT=c["tri"][:, :], rhs=L[:, 0:NT], start=True, stop=True),
         reads=["L", "tri"], writes=["cl_ps"])
    P.op("pe", lambda e: e.matmul(tot_ps[:, 0:NT], lhsT=c["ones"][:, :], rhs=L[:, 0:NT], start=True, stop=True),
         reads=["L", "ones"], writes=["tot_ps"])
    sa, sbb = B["scanA"], B["scanB"]
    P.op("dve", lambda e: e.tensor_copy(out=sa[:, 0:NT], in_=tot_ps[:, 0:NT]), reads=["tot_ps"], writes=["scanA"])
    cur, nxt, cn, nn = sa, sbb, "scanA", "scanB"
    sh = 1
    while sh < NT:
        P.op("dve", lambda e, cur=cur, nxt=nxt, sh=sh: e.tensor_tensor(
            out=nxt[:, sh:NT], in0=cur[:, sh:NT], in1=cur[:, 0:NT - sh], op=ALU.add), reads=[cn], writes=[nn])
        P.op("dve", lambda e, cur=cur, nxt=nxt, sh=sh: e.tensor_copy(out=nxt[:, 0:sh], in_=cur[:, 0:sh]),
             reads=[cn], writes=[nn])
        cur, nxt, cn, nn = nxt, cur, nn, cn
        sh *= 2
    pex, negC = B["pex"], B["negC"]
    P.op("dve", lambda e, cur=cur: e.tensor_tensor(out=pex[:, 0:NT], in0=cur[:, 0:NT], in1=tot_ps[:, 0:NT],
                                                   op=ALU.subtract), reads=[cn, "tot_ps"], writes=["pex"])
    P.op("dve", lambda e: e.scalar_tensor_tensor(out=negC[:, 0:NT], in0=pex[:, 0:NT], scalar=-1.0, in1=cl_ps[:, 0:NT],
                                                 op0=ALU.mult, op1=ALU.subtract),
         reads=["pex", "cl_ps"], writes=["negC"])
    bias = B["bias"]
    for gi, (q0, nq, nk, d0, ct) in enumerate(groups):
        P.op("dve", lambda e, gi=gi, nk=nk, ct=ct: e.tensor_scalar(
            out=bias[:, gi, 0:nk], in0=negC[:, 0:nk], scalar1=pex[:, ct:ct + 1], scalar2=0.0,
            op0=ALU.add, op1=ALU.add), reads=["negC", "pex"], writes=["bias"])
    it = B["it"]
    for gi, (q0, nq, nk, d0, ct) in enumerate(groups):
        ob = B["gcount"] % 2
        B["gcount"] += 1
        OT = B["OT"][ob]
        def emit_score(j, it_):
            sb_ = it_ % 2
            sT = B["sT"][sb_]
            diag = j >= d0
            P.op("pe", lambda e, sT=sT, j=j, q0=q0, nq=nq, diag=diag: e.matmul(
                sT[:, 0:nq], lhsT=kT[:, j * 128:(j + 1) * 128], rhs=qT[:, q0:q0 + nq], start=True, stop=(not diag)),
                reads=["kT", "qT"], writes=["sT%d" % sb_])
            if diag:
                jl = j - d0
                P.op("pe", lambda e, sT=sT, jl=jl, nq=nq: e.matmul(
                    sT[:, 0:nq], lhsT=c["idb"][:, :], rhs=c["mask"][:, jl, 0:nq], start=False, stop=True),
                    reads=["idb", "maskb"], writes=["sT%d" % sb_])

        emit_score(0, it)
        for j in range(nk):
            sb_ = it % 2
            pb = it % 3
            sT = B["sT"][sb_]
            pT = B["pT"][pb]
            if j + 1 < nk:
                emit_score(j + 1, it + 1)
            it += 1
            P.op("act", lambda e, sT=sT, pT=pT, gi=gi, j=j, nq=nq: e.activation(
                out=pT[:, 0:nq], in_=sT[:, 0:nq], func=AF.Exp, bias=bias[:, gi, j:j + 1], scale=0.125),
                reads=["sT%d" % sb_, "bias"], writes=["pT%d" % pb])
            P.op("pe", lambda e, OT=OT, pT=pT, j=j, nq=nq, nk=nk: e.matmul(
                OT[0:65, 0:nq], lhsT=vv[:, j, :], rhs=pT[:, 0:nq], start=(j == 0), stop=(j == nk - 1)),
                reads=["vv", "pT%d" % pb], writes=["OT%d" % ob])
        oT = B["oT"]
        P.op("act", lambda e, OT=OT, nq=nq: e.copy(out=oT[0:65, 0:nq], in_=OT[0:65, 0:nq]),
             reads=["OT%d" % ob], writes=["oT"])
        oq, rec, osb = B["oq"], B["rec"], B["osb"]
        nqi = (nq + 127) // 128
        for qi in range(nqi):
            w = min(128, nq - qi * 128)
            P.op("pe", lambda e, qi=qi, w=w: e.transpose(out=oq[0:w, qi, :], in_=oT[0:65, qi * 128:qi * 128 + w],
                                                         identity=c["idf"][0:65, 0:65]),
                 reads=["oT", "idf"], writes=["oq"])
        wmax = min(128, nq)
        for qi in range(nqi):
            P.op("dve", lambda e, qi=qi: e.reciprocal(out=rec[0:wmax, qi:qi + 1], in_=oq[0:wmax, qi, 64:65]),
                 reads=["oq"], writes=["rec"])
            P.op("dve", lambda e, qi=qi: e.tensor_scalar_mul(out=osb[0:wmax, qi, :], in0=oq[0:wmax, qi, 0:64],
                                                             scalar1=rec[0:wmax, qi:qi + 1]),
                 reads=["oq", "rec"], writes=["osb"])
        out_fn(P, osb, q0, nq, nqi, wmax)
    B["it"] = it


def _fox_bufs(P):
    B = {}
    B["qT"] = P.sb("qT", [64, TP], BF16)
    B["kT"] = P.sb("kT", [64, TP], BF16)
    B["vv"] = P.sb("vv", [128, 128, 65], BF16)
    B["stage"] = [P.sb("stage%d" % i, [128, 2048], F32) for i in range(2)]
    B["L"] = P.sb("L", [128, 128], F32)
    B["scanA"] = P.sb("scanA", [128, 128], F32)
    B["scanB"] = P.sb("scanB", [128, 128], F32)
    B["pex"] = P.sb("pex", [128, 128], F32)
    B["negC"] = P.sb("negC", [128, 128], F32)
    B["bias"] = P.sb("bias", [128, 32, 128], F32)
    B["pT"] = [P.sb("pT%d" % i, [128, 512], BF16) for i in range(3)]
    B["oT"] = P.sb("oT", [65, 512], F32)
    B["rec"] = P.sb("rec", [128, 4], F32)
    B["osb"] = P.sb("osb", [128, 4, 64], F32)
    B["cl_ps"] = P.ps("cl_ps", [128, 128], F32)
    B["tot_ps"] = P.ps("tot_ps", [128, 128], F32)
    B["sT"] = [P.ps("sT%d" % i, [128, 512], F32) for i in range(2)]
    B["OT"] = [P.ps("OT%d" % i, [128, 512], F32) for i in range(2)]
    B["oq"] = P.ps("oq", [128, 4, 65], F32)
    B["it"] = 0
    B["gcount"] = 0
    P.op("pool", lambda e: e.memset(B["vv"][:, :, 64:65], 1.0), writes=["vv"])
    return B


def build_phase2_fox(n_prompt=2, n_sample=NSEQ_S, ngroups=32):
    nc = bass.Bass("TRN2", target_bir_lowering=False)
    qTp = _din(nc, "qTp", [2, 64, TP])
    kTp = _din(nc, "kTp", [2, 64, TP])
    vp = _din(nc, "vp", [2, 128, 128, 64])
    lfp = _din(nc, "lfp", [2, 128, 128])
    qTs = _din(nc, "qTs", [NSEQ_S, 64, 16])
    kTs = _din(nc, "kTs", [NSEQ_S, 64, TS])
    vs = _din(nc, "vs", [NSEQ_S, 128, 17, 64])
    lfs = _din(nc, "lfs", [NSEQ_S, 128, 17])
    op_ = _dout(nc, "o_p", [2, TP, 64])
    os_ = _dout(nc, "o_s", [NSEQ_S, 16, 64])
    P = Prog(nc)
    c = _fox_consts(P, nc)
    B = _fox_bufs(P)
    for b in range(n_prompt):
        groups = [(512 * g, 512, 4 * g + 4, 4 * g, 4 * g + 2) for g in range(ngroups)]

        def out_fn(P, osb, q0, nq, nqi, wmax, b=b):
            P.dma("sp", lambda e: e.dma_start(
                out=op_[b, q0:q0 + nq, :].rearrange("(a p) d -> p a d", p=128), in_=osb[:, 0:nqi, :]),
                reads=["osb"], writes=["o_out"])
        _fox_seq(P, c, B, "p%d" % b, 128, qTp[b], TP, kTp[b], vp[b], lfp[b], groups, out_fn)
    for s in range(n_sample):
        groups = [(0, 16, 17, 16, 16)]

        def out_fn(P, osb, q0, nq, nqi, wmax, s=s):
            P.dma("sp", lambda e: e.dma_start(out=os_[s, :, :], in_=osb[0:16, 0, :]), reads=["osb"], writes=["o_out"])
        _fox_seq(P, c, B, "s%d" % s, 17, qTs[s], 16, kTs[s], vs[s], lfs[s], groups, out_fn)
    P.emit()
    return nc


def _fox_const_inputs():
    p = np.arange(128)
    tri = (p[:, None] <= p[None, :]).astype(np.float32)
    ones = np.ones((128, 128), np.float32)
    col = np.arange(512)
    mask = np.zeros((128, 4, 512), np.float32)
    for jl in range(4):
        mask[:, jl, :] = np.where(jl * 128 + p[:, None] > col[None, :], -30000.0, 0.0)
    return {"tri": tri, "ones": ones, "ident": _ident(), "mask": mask.reshape(128, 2048)}


def _tile_major(a, nt):
    return np.ascontiguousarray(np.swapaxes(a.reshape((nt, 128) + a.shape[1:]), 0, 1))


def fox_inputs(pp, psm, cache_k, cache_v, cache_lf):
    ppb = pp.reshape(2, TP, IN_COLS)
    pss = psm.reshape(NSEQ_S, 16, IN_COLS)
    maps = []
    for h in range(NCORE):
        m = dict(_fox_const_inputs())
        m["qTp"] = np.ascontiguousarray(np.swapaxes(ppb[:, :, h * 64:(h + 1) * 64], 1, 2))
        m["kTp"] = np.ascontiguousarray(np.swapaxes(ppb[:, :, 512 + h * 64:512 + (h + 1) * 64], 1, 2))
        m["vp"] = np.stack([_tile_major(ppb[b, :, 1024 + h * 64:1024 + (h + 1) * 64], 128) for b in range(2)])
        m["lfp"] = np.stack([_tile_major(ppb[b, :, 1536 + h], 128) for b in range(2)])
        kfull = np.zeros((NSEQ_S, TS, 64), np.float32)
        vfull = np.zeros((NSEQ_S, TS, 64), np.float32)
        lfull = np.zeros((NSEQ_S, TS), np.float32)
        kfull[:, :2048] = cache_k[:, :, h, :]
        vfull[:, :2048] = cache_v[:, :, h, :]
        lfull[:, :2048] = cache_lf[:, :, h]
        kfull[:, 2048:2064] = pss[:, :, 512 + h * 64:512 + (h + 1) * 64]
        vfull[:, 2048:2064] = pss[:, :, 1024 + h * 64:1024 + (h + 1) * 64]
        lfull[:, 2048:2064] = pss[:, :, 1536 + h]
        m["qTs"] = np.ascontiguousarray(np.swapaxes(pss[:, :, h * 64:(h + 1) * 64], 1, 2))
        m["kTs"] = np.ascontiguousarray(np.swapaxes(kfull, 1, 2))
        m["vs"] = np.stack([_tile_major(vfull[s], 17) for s in range(NSEQ_S)])
        m["lfs"] = np.stack([_tile_major(lfull[s], 17) for s in range(NSEQ_S)])
        maps.append(m)
    return maps


NPAR = 352 + 7 * 64
EXPM05 = math.exp(-0.5)


def _rw_setup(P, nc, B):
    R = {}
    par_d = _din(nc, "rw_par", [128, NPAR])
    w2_d = _din(nc, "rw_w2", [32, 64])
    a2_d = _din(nc, "rw_a2", [32, 64])
    g2_d = _din(nc, "rw_g2", [96, 64])
    sel_d = _din(nc, "rw_sel", [6, 128])
    R["par"] = P.sb("rw_par", [128, NPAR], F32)
    R["w2"] = P.sb("rw_w2", [32, 64], F32)
    R["a2"] = P.sb("rw_a2", [32, 64], F32)
    R["g2"] = P.sb("rw_g2", [96, 64], F32)
    R["sel"] = P.sb("rw_sel", [6, 128], F32)
    R["selb"] = P.sb("rw_selb", [6, 128], BF16)
    R["omk"] = P.sb("rw_omk", [128, 64], F32)
    for nm, d_ in (("par", par_d), ("w2", w2_d), ("a2", a2_d), ("g2", g2_d), ("sel", sel_d)):
        P.dma("sp", lambda e, nm=nm, d_=d_: e.dma_start(out=R[nm][:, :], in_=d_), writes=["rwc_" + nm])
    P.op("dve", lambda e: e.tensor_copy(out=R["selb"][:, :], in_=R["sel"][:, :]), reads=["rwc_sel"], writes=["rwc_selb"])
    R["R3"] = [[P.sb("rw_R3_%d_%d" % (b, t), [128, 3, 320], BF16) for t in range(2)] for b in range(2)]
    R["r1"] = P.sb("rw_r1", [128, 320], F32)
    R["r2"] = P.sb("rw_r2", [128, 320], F32)
    o = 352
    R["mu"] = R["par"][:, 0:352]
    names = ["w0", "a0", "kk", "ka", "rk", "lnw", "lnb"]
    for i, nm in enumerate(names):
        R[nm] = R["par"][:, o + i * 64:o + (i + 1) * 64]
    P.op("dve", lambda e: e.tensor_scalar(out=R["omk"][:, :], in0=R["ka"], scalar1=-1.0, scalar2=1.0,
                                          op0=ALU.mult, op1=ALU.add), reads=["rwc_par"], writes=["rwc_omk"])
    R["cur"] = [P.sb("rw_cur%d" % b, [128, 352], F32) for b in range(2)]
    R["prv"] = [P.sb("rw_prv%d" % b, [128, 352], F32) for b in range(2)]
    R["R"] = [[P.sb("rw_R%d_%d" % (b, t), [128, 320], F32) for t in range(2)] for b in range(2)]
    R["GB"] = [[P.sb("rw_GB%d_%d" % (b, t), [128, 128], F32) for t in range(2)] for b in range(2)]
    R["VV"] = [P.sb("rw_VV%d" % t, [128, 128], F32) for t in range(2)]
    R["vT"] = [P.sb("rw_vT%d" % t, [128, 128], F32) for t in range(2)]
    R["yT"] = [P.sb("rw_yT%d" % t, [128, 128], F32) for t in range(2)]
    R["twl"] = P.sb("rw_twl", [32, 128], F32)
    R["alT"] = P.sb("rw_alT", [32, 128], F32)
    R["sgl"] = P.sb("rw_sgl", [96, 128], F32)
    for nm in ("zt", "at", "kkt", "tmp", "t1", "junk", "cen", "ob"):
        R[nm] = P.sb("rw_" + nm, [128, 64], F32)
    for nm in ("ssq", "rks", "mean", "var", "sk"):
        R[nm] = P.sb("rw_" + nm, [128, 1], F32)
    R["S"] = P.sb("rw_S", [128, 64], F32)
    R["stmp"] = P.sb("rw_stmp", [128, 64], F32)
    R["rowbuf"] = [P.sb("rw_rowbuf%d" % i, [6, 16 * 320], BF16) for i in range(2)]
    R["rowp"] = [B["sT"][0], B["sT"][1]]
    R["trp"] = B["OT"][0]
    R["lop"] = B["OT"][1]
    R["vtp"] = B["cl_ps"]
    R["ytp"] = B["tot_ps"]
    R["k"] = 0
    R["step"] = 0
    return R


def _rw_prep(P, c, R, n, ntok, cur_src, prev_src, rows_scr):
    tp = n % 2
    idf = c["idf"]
    t0 = n * ntok
    for b in range(2):
        cur, prv = R["cur"][b], R["prv"][b]
        cn, pn = "rw_cur%d" % b, "rw_prv%d" % b
        Rt, GB = R["R"][b][tp], R["GB"][b][tp]
        rn, gn = "rw_R%d_%d" % (b, tp), "rw_GB%d_%d" % (b, tp)
        P.dma("sp", lambda e, cur=cur, b=b: e.dma_start(out=cur[0:ntok, :], in_=cur_src(b, n)), writes=[cn])
        P.dma("sp", lambda e, prv=prv, b=b: e.dma_start(out=prv[0:ntok, :], in_=prev_src(b, n)), writes=[pn])
        P.op("dve", lambda e, cur=cur, prv=prv: e.tensor_tensor(out=prv[0:ntok, :], in0=prv[0:ntok, :], in1=cur[0:ntok, :],
                                                                op=ALU.subtract), reads=[cn, pn], writes=[pn])
        P.op("dve", lambda e, prv=prv: e.tensor_tensor(out=prv[0:ntok, :], in0=prv[0:ntok, :], in1=R["mu"][0:ntok, :],
                                                       op=ALU.mult), reads=[pn, "rwc_par"], writes=[pn])
        P.op("dve", lambda e, cur=cur, prv=prv: e.tensor_tensor(out=cur[0:ntok, :], in0=cur[0:ntok, :], in1=prv[0:ntok, :],
                                                                op=ALU.add), reads=[cn, pn], writes=[cn])
        trp, lop = R["trp"], R["lop"]
        for (o0, c0, c1, m) in ((0, 192, 224, 32), (128, 224, 256, 32), (256, 256, 352, 96)):
            P.op("pe", lambda e, cur=cur, o0=o0, c0=c0, c1=c1, m=m: e.transpose(
                out=trp[0:m, o0:o0 + ntok], in_=cur[0:ntok, c0:c1], identity=idf[0:ntok, 0:ntok]),
                reads=[cn, "idf"], writes=["OT0"])
        P.op("act", lambda e: e.activation(out=R["twl"][0:32, 0:ntok], in_=trp[0:32, 0:ntok], func=AF.Tanh),
             reads=["OT0"], writes=["rw_twl"])
        P.op("act", lambda e: e.copy(out=R["alT"][0:32, 0:ntok], in_=trp[0:32, 128:128 + ntok]),
             reads=["OT0"], writes=["rw_alT"])
        P.op("act", lambda e: e.activation(out=R["sgl"][0:96, 0:ntok], in_=trp[0:96, 256:256 + ntok], func=AF.Sigmoid),
             reads=["OT0"], writes=["rw_sgl"])
        P.op("pe", lambda e: e.matmul(lop[0:ntok, 0:64], lhsT=R["twl"][0:32, 0:ntok], rhs=R["w2"][:, :], start=True, stop=True),
             reads=["rw_twl", "rwc_w2"], writes=["OT1"])
        P.op("pe", lambda e: e.matmul(lop[0:ntok, 64:128], lhsT=R["alT"][0:32, 0:ntok], rhs=R["a2"][:, :], start=True, stop=True),
             reads=["rw_alT", "rwc_a2"], writes=["OT1"])
        P.op("pe", lambda e: e.matmul(lop[0:ntok, 128:192], lhsT=R["sgl"][0:96, 0:ntok], rhs=R["g2"][:, :], start=True, stop=True),
             reads=["rw_sgl", "rwc_g2"], writes=["OT1"])
        zt, at, kkt, tmp, t1, junk = R["zt"], R["at"], R["kkt"], R["tmp"], R["t1"], R["junk"]
        ssq, rks = R["ssq"], R["rks"]
        P.op("dve", lambda e: e.tensor_tensor(out=zt[0:ntok, :], in0=lop[0:ntok, 0:64], in1=R["w0"][0:ntok, :], op=ALU.add),
             reads=["OT1", "rwc_par"], writes=["rw_zt"])
        P.op("act", lambda e: e.activation(out=zt[0:ntok, :], in_=zt[0:ntok, :], func=AF.Sigmoid),
             reads=["rw_zt"], writes=["rw_zt"])
        P.op("act", lambda e, Rt=Rt: e.activation(out=Rt[0:ntok, 0:64], in_=zt[0:ntok, :], func=AF.Exp, scale=-EXPM05),
             reads=["rw_zt"], writes=[rn])
        P.op("dve", lambda e: e.tensor_tensor(out=at[0:ntok, :], in0=lop[0:ntok, 64:128], in1=R["a0"][0:ntok, :], op=ALU.add),
             reads=["OT1", "rwc_par"], writes=["rw_at"])
        P.op("act", lambda e: e.activation(out=at[0:ntok, :], in_=at[0:ntok, :], func=AF.Sigmoid),
             reads=["rw_at"], writes=["rw_at"])
        P.op("act", lambda e, GB=GB: e.copy(out=GB[0:ntok, 0:64], in_=lop[0:ntok, 128:192]), reads=["OT1"], writes=[gn])
        P.op("dve", lambda e, cur=cur: e.tensor_tensor(out=kkt[0:ntok, :], in0=cur[0:ntok, 64:128], in1=R["kk"][0:ntok, :],
                                                       op=ALU.mult), reads=[cn, "rwc_par"], writes=["rw_kkt"])
        P.op("dve", lambda e: e.scalar_tensor_tensor(out=junk[0:ntok, :], in0=kkt[0:ntok, :], scalar=1.0, in1=kkt[0:ntok, :],
                                                     op0=ALU.mult, op1=ALU.mult, accum_out=ssq[0:ntok, 0:1]),
             reads=["rw_kkt"], writes=["rw_junk", "rw_ssq"])
        P.fence("dve", ["rw_ssq"])
        P.op("act", lambda e: e.activation(out=ssq[0:ntok, :], in_=ssq[0:ntok, :], func=AF.Sqrt), reads=["rw_ssq"], writes=["rw_ssq"])
        P.op("dve", lambda e: e.tensor_scalar_max(out=ssq[0:ntok, :], in0=ssq[0:ntok, :], scalar1=1e-12),
             reads=["rw_ssq"], writes=["rw_ssq"])
        P.op("dve", lambda e: e.reciprocal(out=ssq[0:ntok, :], in_=ssq[0:ntok, :]), reads=["rw_ssq"], writes=["rw_ssq"])
        P.op("dve", lambda e: e.tensor_scalar_mul(out=kkt[0:ntok, :], in0=kkt[0:ntok, :], scalar1=ssq[0:ntok, 0:1]),
             reads=["rw_kkt", "rw_ssq"], writes=["rw_kkt"])
        P.op("dve", lambda e, Rt=Rt: e.tensor_single_scalar(out=Rt[0:ntok, 64:128], in_=kkt[0:ntok, :], scalar=-1.0, op=ALU.mult),
             reads=["rw_kkt"], writes=[rn])
        P.op("dve", lambda e, Rt=Rt: e.tensor_tensor(out=Rt[0:ntok, 128:192], in0=kkt[0:ntok, :], in1=at[0:ntok, :], op=ALU.mult),
             reads=["rw_kkt", "rw_at"], writes=[rn])
        P.op("dve", lambda e: e.tensor_tensor(out=tmp[0:ntok, :], in0=at[0:ntok, :], in1=R["ka"][0:ntok, :], op=ALU.mult),
             reads=["rw_at", "rwc_par"], writes=["rw_tmp"])
        P.op("dve", lambda e: e.tensor_tensor(out=tmp[0:ntok, :], in0=tmp[0:ntok, :], in1=R["omk"][0:ntok, :], op=ALU.add),
             reads=["rw_tmp", "rwc_omk"], writes=["rw_tmp"])
        P.op("dve", lambda e, cur=cur, Rt=Rt: e.tensor_tensor(out=Rt[0:ntok, 192:256], in0=cur[0:ntok, 64:128], in1=tmp[0:ntok, :],
                                                              op=ALU.mult), reads=[cn, "rw_tmp"], writes=[rn])
        P.op("act", lambda e, cur=cur, Rt=Rt: e.copy(out=Rt[0:ntok, 256:320], in_=cur[0:ntok, 0:64]), reads=[cn], writes=[rn])
        P.op("dve", lambda e, cur=cur, Rt=Rt: e.tensor_tensor(out=t1[0:ntok, :], in0=cur[0:ntok, 0:64], in1=Rt[0:ntok, 192:256],
                                                              op=ALU.mult), reads=[cn, rn], writes=["rw_t1"])
        P.op("dve", lambda e: e.scalar_tensor_tensor(out=junk[0:ntok, :], in0=t1[0:ntok, :], scalar=1.0, in1=R["rk"][0:ntok, :],
                                                     op0=ALU.mult, op1=ALU.mult, accum_out=rks[0:ntok, 0:1]),
             reads=["rw_t1", "rwc_par"], writes=["rw_junk", "rw_rks"])
        P.op("dve", lambda e, cur=cur, GB=GB: e.tensor_scalar_mul(out=GB[0:ntok, 64:128], in0=cur[0:ntok, 128:192],
                                                                  scalar1=rks[0:ntok, 0:1]),
             reads=[cn, "rw_rks"], writes=[gn])
        P.op("act", lambda e, cur=cur, b=b: e.copy(out=R["VV"][tp][0:ntok, b * 64:(b + 1) * 64], in_=cur[0:ntok, 128:192]),
             reads=[cn], writes=["rw_VV%d" % tp])
        R3 = R["R3"][b][tp]
        r3n = "rw_R3_%d_%d" % (b, tp)
        r1, r2 = R["r1"], R["r2"]
        P.op("act", lambda e, Rt=Rt, R3=R3: e.copy(out=R3[0:ntok, 0, :], in_=Rt[0:ntok, :]), reads=[rn], writes=[r3n])
        P.op("dve", lambda e, Rt=Rt, R3=R3: e.tensor_tensor(out=r1[0:ntok, :], in0=Rt[0:ntok, :], in1=R3[0:ntok, 0, :],
                                                            op=ALU.subtract), reads=[rn, r3n], writes=["rw_r1"])
        P.op("act", lambda e, R3=R3: e.copy(out=R3[0:ntok, 1, :], in_=r1[0:ntok, :]), reads=["rw_r1"], writes=[r3n])
        P.op("dve", lambda e, R3=R3: e.tensor_tensor(out=r2[0:ntok, :], in0=r1[0:ntok, :], in1=R3[0:ntok, 1, :],
                                                     op=ALU.subtract), reads=["rw_r1", r3n], writes=["rw_r2"])
        P.op("act", lambda e, R3=R3: e.copy(out=R3[0:ntok, 2, :], in_=r2[0:ntok, :]), reads=["rw_r2"], writes=[r3n])
        P.dma("sp", lambda e, R3=R3, b=b: e.dma_start(out=rows_scr[b, :, t0:t0 + ntok, :].rearrange("p t c -> t p c"),
                                                      in_=R3[0:ntok, :, :]),
              reads=[r3n], writes=["rows%d_%d" % (b, tp)])
    P.op("pe", lambda e: e.transpose(out=R["vtp"][:, 0:ntok], in_=R["VV"][tp][0:ntok, :], identity=idf[0:ntok, 0:ntok]),
         reads=["rw_VV%d" % tp, "idf"], writes=["cl_ps"])
    P.op("act", lambda e: e.copy(out=R["vT"][tp][:, 0:ntok], in_=R["vtp"][:, 0:ntok]), reads=["cl_ps"], writes=["rw_vT%d" % tp])


def _rw_scan(P, c, R, n, ntok, rows_scr):
    P.nosame = {"dve"}
    _rw_scan_body(P, c, R, n, ntok, rows_scr)
    P.nosame = set()


def _rw_scan_body(P, c, R, n, ntok, rows_scr):
    tp = n % 2
    t0 = n * ntok
    S, stmp, sk = R["S"], R["stmp"], R["sk"]
    vT, yT = R["vT"][tp], R["yT"][tp]
    vn, yn = "rw_vT%d" % tp, "rw_yT%d" % tp
    for blk in range(0, ntok, 16):
        nb = min(16, ntok - blk)
        rb = R["k"] % 2
        R["k"] += 1
        rowbuf = R["rowbuf"][rb]
        P.dma("sp", lambda e, rowbuf=rowbuf, blk=blk, nb=nb: e.dma_start(
            out=rowbuf[0:6, 0:nb * 320].rearrange("q (s c) -> q s c", c=320),
            in_=rows_scr[:, :, t0 + blk:t0 + blk + nb, :].rearrange("b p s c -> (b p) s c")),
            reads=["rows0_%d" % tp, "rows1_%d" % tp], writes=["rw_rowbuf%d" % rb])
        for s in range(nb):
            pb = R["step"] % 2
            R["step"] += 1
            rowp = R["rowp"][pb]
            pn = "sT%d" % pb
            t = blk + s
            P.op("pe", lambda e, rowp=rowp, rowbuf=rowbuf, s=s: e.matmul(
                rowp[:, 0:320], lhsT=R["selb"][0:6, :], rhs=rowbuf[0:6, s * 320:(s + 1) * 320], start=True, stop=True),
                reads=["rw_rowbuf%d" % rb, "rwc_selb"], writes=[pn])
            P.op("dve", lambda e, rowp=rowp: e.scalar_tensor_tensor(
                out=stmp[:, :], in0=S[:, :], scalar=1.0, in1=rowp[:, 64:128], op0=ALU.mult, op1=ALU.mult,
                accum_out=sk[:, 0:1]), reads=["rw_S", pn], writes=["rw_stmp", "rw_sk"])
            P.op("dve", lambda e, rowp=rowp: e.tensor_tensor(out=S[:, :], in0=S[:, :], in1=rowp[:, 0:64], op=ALU.mult),
                 reads=["rw_S", pn], writes=["rw_S"])
            P.op("dve", lambda e, rowp=rowp: e.scalar_tensor_tensor(
                out=S[:, :], in0=rowp[:, 128:192], scalar=sk[:, 0:1], in1=S[:, :], op0=ALU.mult, op1=ALU.add),
                reads=["rw_S", "rw_sk", pn], writes=["rw_S"])
            P.op("dve", lambda e, rowp=rowp, t=t: e.scalar_tensor_tensor(
                out=S[:, :], in0=rowp[:, 192:256], scalar=vT[:, t:t + 1], in1=S[:, :], op0=ALU.mult, op1=ALU.add),
                reads=["rw_S", vn, pn], writes=["rw_S"])
            P.op("dve", lambda e, rowp=rowp, t=t: e.scalar_tensor_tensor(
                out=stmp[:, :], in0=S[:, :], scalar=1.0, in1=rowp[:, 256:320], op0=ALU.mult, op1=ALU.mult,
                accum_out=yT[:, t:t + 1]), reads=["rw_S", pn], writes=["rw_stmp", yn])


def _rw_post(P, c, R, n, ntok, out_dst):
    tp = n % 2
    idf = c["idf"]
    ytp = R["ytp"]
    cen, ob, junk, mean, var = R["cen"], R["ob"], R["junk"], R["mean"], R["var"]
    P.op("pe", lambda e: e.transpose(out=ytp[0:ntok, 0:128], in_=R["yT"][tp][:, 0:ntok], identity=idf[:, :]),
         reads=["rw_yT%d" % tp, "idf"], writes=["tot_ps"])
    for b in range(2):
        GB = R["GB"][b][tp]
        gn = "rw_GB%d_%d" % (b, tp)
        ysl = ytp[0:ntok, b * 64:(b + 1) * 64]
        P.op("dve", lambda e, ysl=ysl: e.tensor_reduce(out=mean[0:ntok, :], in_=ysl, axis=AX.X, op=ALU.add),
             reads=["tot_ps"], writes=["rw_mean"])
        P.op("dve", lambda e: e.tensor_single_scalar(out=mean[0:ntok, :], in_=mean[0:ntok, :], scalar=1.0 / 64, op=ALU.mult),
             reads=["rw_mean"], writes=["rw_mean"])
        P.op("dve", lambda e, ysl=ysl: e.tensor_scalar(out=cen[0:ntok, :], in0=ysl, scalar1=mean[0:ntok, 0:1], scalar2=0.0,
                                                       op0=ALU.subtract, op1=ALU.add),
             reads=["tot_ps", "rw_mean"], writes=["rw_cen"])
        P.op("dve", lambda e: e.scalar_tensor_tensor(out=junk[0:ntok, :], in0=cen[0:ntok, :], scalar=1.0, in1=cen[0:ntok, :],
                                                     op0=ALU.mult, op1=ALU.mult, accum_out=var[0:ntok, 0:1]),
             reads=["rw_cen"], writes=["rw_junk", "rw_var"])
        P.fence("dve", ["rw_var"])
        P.op("act", lambda e: e.activation(out=var[0:ntok, :], in_=var[0:ntok, :], func=AF.Sqrt, bias=64e-5, scale=1.0 / 64),
             reads=["rw_var"], writes=["rw_var"])
        P.op("dve", lambda e: e.reciprocal(out=var[0:ntok, :], in_=var[0:ntok, :]), reads=["rw_var"], writes=["rw_var"])
        P.op("dve", lambda e: e.scalar_tensor_tensor(out=cen[0:ntok, :], in0=cen[0:ntok, :], scalar=var[0:ntok, 0:1],
                                                     in1=R["lnw"][0:ntok, :], op0=ALU.mult, op1=ALU.mult),
             reads=["rw_cen", "rw_var", "rwc_par"], writes=["rw_cen"])
        P.op("dve", lambda e: e.tensor_tensor(out=cen[0:ntok, :], in0=cen[0:ntok, :], in1=R["lnb"][0:ntok, :], op=ALU.add),
             reads=["rw_cen", "rwc_par"], writes=["rw_cen"])
        P.op("dve", lambda e, GB=GB: e.tensor_tensor(out=cen[0:ntok, :], in0=cen[0:ntok, :], in1=GB[0:ntok, 64:128], op=ALU.add),
             reads=["rw_cen", gn], writes=["rw_cen"])
        P.op("dve", lambda e, GB=GB: e.tensor_tensor(out=ob[0:ntok, :], in0=cen[0:ntok, :], in1=GB[0:ntok, 0:64], op=ALU.mult),
             reads=["rw_cen", gn], writes=["rw_ob"])
        P.dma("sp", lambda e, b=b: e.dma_start(out=out_dst(b, n), in_=ob[0:ntok, :]), reads=["rw_ob"], writes=["rw_out"])


def _rw_pair(P, c, R, ntiles, ntok, cur_src, prev_src, rows_scr, S0_src, out_dst, ST_dst):
    S = R["S"]
    if S0_src is None:
        P.op("dve", lambda e: e.memset(S[:, :], 0.0), writes=["rw_S"])
    else:
        P.dma("sp", lambda e: e.dma_start(out=S[:, :], in_=S0_src), writes=["rw_S"])
    _rw_prep(P, c, R, 0, ntok, cur_src, prev_src, rows_scr)
    for n in range(ntiles):
        if n + 1 < ntiles:
            _rw_prep(P, c, R, n + 1, ntok, cur_src, prev_src, rows_scr)
        _rw_scan(P, c, R, n, ntok, rows_scr)
        _rw_post(P, c, R, n, ntok, out_dst)
    P.dma("sp", lambda e: e.dma_start(out=ST_dst, in_=S[:, :]), reads=["rw_S"], writes=["rw_STout"])


def build_phase2(n_prompt=2, n_sample=NSEQ_S, ngroups=32, rw_tiles=128, rw_pairs=8, do_fox=True):
    nc = bass.Bass("TRN2", target_bir_lowering=False)
    qTp = _din(nc, "qTp", [2, 64, TP])
    kTp = _din(nc, "kTp", [2, 64, TP])
    vp = _din(nc, "vp", [2, 128, 128, 64])
    lfp = _din(nc, "lfp", [2, 128, 128])
    qTs = _din(nc, "qTs", [NSEQ_S, 64, 16])
    kTs = _din(nc, "kTs", [NSEQ_S, 64, TS])
    vs = _din(nc, "vs", [NSEQ_S, 128, 17, 64])
    lfs = _din(nc, "lfs", [NSEQ_S, 128, 17])
    op_ = _dout(nc, "o_p", [2, TP, 64])
    os_ = _dout(nc, "o_s", [NSEQ_S, 16, 64])
    curp = _din(nc, "rw_curp", [2, TP, 352])
    prvp = _din(nc, "rw_prvp", [2, TP, 352])
    curs = _din(nc, "rw_curs", [NSEQ_S, 16, 352])
    prvs = _din(nc, "rw_prvs", [NSEQ_S, 16, 352])
    S0s = _din(nc, "rw_S0s", [8, 128, 64])
    rwp = _dout(nc, "rw_p", [2, TP, 64])
    rws = _dout(nc, "rw_s", [NSEQ_S, 16, 64])
    STp = _dout(nc, "ST_p", [128, 64])
    STs = _dout(nc, "ST_s", [8, 128, 64])
    rows_p = nc.dram_tensor("rows_p", [2, 3, TP, 320], BF16).ap()
    rows_s = nc.dram_tensor("rows_s", [8, 2, 3, 16, 320], BF16).ap()
    P = Prog(nc)
    c = _fox_consts(P, nc)
    B = _fox_bufs(P)
    if do_fox:
        for b in range(n_prompt):
            groups = [(512 * g, 512, 4 * g + 4, 4 * g, 4 * g + 2) for g in range(ngroups)]

            def out_fn(P, osb, q0, nq, nqi, wmax, b=b):
                P.dma("sp", lambda e: e.dma_start(
                    out=op_[b, q0:q0 + nq, :].rearrange("(a p) d -> p a d", p=128), in_=osb[:, 0:nqi, :]),
                    reads=["osb"], writes=["o_out"])
            _fox_seq(P, c, B, "p%d" % b, 128, qTp[b], TP, kTp[b], vp[b], lfp[b], groups, out_fn)
        for s in range(n_sample):
            groups = [(0, 16, 17, 16, 16)]

            def out_fn(P, osb, q0, nq, nqi, wmax, s=s):
                P.dma("sp", lambda e: e.dma_start(out=os_[s, :, :], in_=osb[0:16, 0, :]), reads=["osb"], writes=["o_out"])
            _fox_seq(P, c, B, "s%d" % s, 17, qTs[s], 16, kTs[s], vs[s], lfs[s], groups, out_fn)
    R = _rw_setup(P, nc, B)
    if rw_tiles > 0:
        _rw_pair(P, c, R, rw_tiles, 128,
                 lambda b, n: curp[b, n * 128:(n + 1) * 128, :], lambda b, n: prvp[b, n * 128:(n + 1) * 128, :],
                 rows_p, None, lambda b, n: rwp[b, n * 128:(n + 1) * 128, :], STp)
    for pr in range(rw_pairs):
        _rw_pair(P, c, R, 1, 16,
                 lambda b, n, pr=pr: curs[2 * pr + b, :, :], lambda b, n, pr=pr: prvs[2 * pr + b, :, :],
                 rows_s[pr], S0s[pr], lambda b, n, pr=pr: rws[2 * pr + b, :, :], STs[pr])
    P.emit()
    return nc


def rw_inputs(pp, psm, state_rwkv, state_shift, prm):
    ppb = pp.reshape(2, TP, IN_COLS)[:, :, FOX_COLS:]
    pss = psm.reshape(NSEQ_S, 16, IN_COLS)[:, :, FOX_COLS:]
    prev_p = np.zeros_like(ppb)
    prev_p[:, 1:] = ppb[:, :-1]
    prev_s = np.empty_like(pss)
    prev_s[:, 1:] = pss[:, :-1]
    prev_s[:, 0] = state_shift[:, 0, :]
    maps = []
    sel = np.zeros((6, 128), np.float32)
    sel[0:3, :64] = 1.0
    sel[3:6, 64:] = 1.0
    for h in range(NCORE):
        cols = np.concatenate([np.arange(h * 64, (h + 1) * 64), 512 + np.arange(h * 64, (h + 1) * 64),
                               1024 + np.arange(h * 64, (h + 1) * 64), np.arange(1536, 1696)])
        hs = slice(h * 64, (h + 1) * 64)
        par = np.concatenate([prm["rwkv_mu"][cols], prm["rwkv_w0"][hs], prm["rwkv_a0"][hs], prm["rwkv_k_k"][hs],
                              prm["rwkv_k_a"][hs], prm["rwkv_r_k"][h], prm["rwkv_lnx_w"][hs], prm["rwkv_lnx_b"][hs]])
        m = {
            "rw_curp": np.ascontiguousarray(ppb[:, :, cols]), "rw_prvp": np.ascontiguousarray(prev_p[:, :, cols]),
            "rw_curs": np.ascontiguousarray(pss[:, :, cols]), "rw_prvs": np.ascontiguousarray(prev_s[:, :, cols]),
            "rw_S0s": np.ascontiguousarray(state_rwkv[:, h].reshape(8, 128, 64)),
            "rw_par": np.ascontiguousarray(np.broadcast_to(par[None, :], (128, NPAR))).astype(np.float32),
            "rw_w2": np.ascontiguousarray(prm["rwkv_w2"][:, hs]), "rw_a2": np.ascontiguousarray(prm["rwkv_a2"][:, hs]),
            "rw_g2": np.ascontiguousarray(prm["rwkv_g2"][:, hs]), "rw_sel": sel,
        }
        maps.append(m)
    return maps


STAGE = 99


def build_phase3(NT, nslots=128):
    nc = bass.Bass("TRN2", target_bir_lowering=False)
    x = _din(nc, "x", [NT * 128, D])
    mixT = _din(nc, "mixT", [NT, 128, 8, 128])
    wout = _din(nc, "w_out", [D, D])
    wq = _din(nc, "w_q", [D, D])
    skT_d = _din(nc, "skT", [64, 16 * 128])
    g1 = _din(nc, "gffn", [128, D])
    g2 = _din(nc, "gfin", [128, D])
    identd = _din(nc, "ident", [128, 128])
    iota_d = _din(nc, "iota", [128, 256])
    pu = _din(nc, "peer_u", [16384, D])
    pv = _din(nc, "peer_v", [16384, D])
    y = _dout(nc, "y", [NT * 128, D])
    P = Prog(nc)
    wst = P.sb("wst", [128, D], F32)
    wo_bf = P.sb("wo_bf", [128, 8, D], BF16)
    wq_bf = P.sb("wq_bf", [128, 8, D], BF16)
    sk_bf = P.sb("sk_bf", [64, 16, 128], BF16)
    g1s = P.sb("g1s", [128, D], F32)
    g2s = P.sb("g2s", [128, D], F32)
    id_f = P.sb("id_f", [128, 128], F32)
    id_b = P.sb("id_b", [128, 128], BF16)
    P.dma("sp", lambda e: e.dma_start(out=g1s[:, :], in_=g1), writes=["g1"])
    P.dma("sp", lambda e: e.dma_start(out=g2s[:, :], in_=g2), writes=["g2"])
    P.dma("sp", lambda e: e.dma_start(out=id_f[:, :], in_=identd), writes=["idf"])
    P.op("dve", lambda e: e.tensor_copy(out=id_b[:, :], in_=id_f[:, :]), reads=["idf"], writes=["idb"])
    for (src, dst, nm) in ((wout, wo_bf, "wo"), (wq, wq_bf, "wq")):
        for dc in range(8):
            P.dma("sp", lambda e, src=src, dc=dc: e.dma_start(out=wst[:, :], in_=src[dc * 128:(dc + 1) * 128, :]), writes=["wst"])
            P.op("act", lambda e, dst=dst, dc=dc: e.copy(out=dst[:, dc, :], in_=wst[:, :]), reads=["wst"], writes=[nm])
    for hf in range(2):
        P.dma("sp", lambda e, hf=hf: e.dma_start(out=wst[0:64, :], in_=skT_d[:, hf * 1024:(hf + 1) * 1024]), writes=["wst"])
        P.op("act", lambda e, hf=hf: e.copy(out=sk_bf[:, hf * 8:(hf + 1) * 8, :],
                                            in_=wst[0:64, :].rearrange("p (h n) -> p h n", h=8)), reads=["wst"], writes=["sk"])

    xt = P.sb("xt", [128, D], F32)
    mt = P.sb("mt", [128, 8, 128], F32)
    mtb = P.sb("mtb", [128, 8, 128], BF16)
    x2 = P.sb("x2", [128, D], F32)
    xn = P.sb("xn", [128, D], F32)
    xnb = P.sb("xnb", [128, D], BF16)
    xnT = P.sb("xnT", [128, D], BF16)
    qT = P.sb("qT", [64, 16 * 128], BF16)
    junkb = P.sb("junkb", [128, D], BF16)
    junkD = P.sb("junkD", [128, D], F32)
    ss = P.sb("ss", [128, 1], F32)
    s_sb = P.sb("s_sb", [128, 16, 128], F32)
    s2 = P.sb("s2", [128, 128], F32)
    sv = P.sb("sv", [128, 16, 16], F32)
    si = P.sb("si", [128, 16, 16], U32)
    sif = P.sb("sif", [128, 16, 16], F32)
    cand = P.sb("cand", [128, 8, 256], F32)
    cidx = P.sb("cidx", [128, 8, 256], F32)
    c2 = P.sb("c2", [128, 256], F32)
    j256 = P.sb("j256", [128, 256], F32)
    tv = P.sb("tv", [128, 8, 16], F32)
    pi = P.sb("pi", [128, 8, 16], U32)
    pif = P.sb("pif", [128, 8, 16], F32)
    iota = P.sb("iota", [128, 256], F32)
    P.dma("sp", lambda e: e.dma_start(out=iota[:, :], in_=iota_d), writes=["iota"])
    negm = P.sb("negm", [128, 8], F32)
    gt = P.sb("gt", [128, 8, 16], F32)
    Z = P.sb("Z", [128, 8], F32)
    ef = P.sb("ef", [128, 128], F32)
    ei = P.sb("ei", [128, 128], I32)
    apre = P.sb("apre", [128, 128], F32)
    coef = P.sb("coef", [128, 128], F32)
    acc = P.sb("acc", [128, D], F32)
    yo = P.sb("yo", [128, D], F32)
    NG = 4
    gb = [P.sb("gbuf%d" % i, [128, D], F32) for i in range(NG)]
    psA = P.ps("psA", [128, D], F32)
    psT = P.ps("psT", [128, D], BF16)
    psS = P.ps("psS", [128, 16, 128], F32)
    gk = 0

    def rms(src, srcn, dst, dstn, gs, gn):
        P.op("act", lambda e: e.activation(out=junkb[:, :], in_=src[:, :], func=AF.Square, accum_out=ss[:, 0:1]),
             reads=[srcn], writes=["junkb", "ss"])
        P.op("act", lambda e: e.activation(out=ss[:, :], in_=ss[:, :], func=AF.Sqrt, bias=1e-6, scale=1.0 / D),
             reads=["ss"], writes=["ss"])
        P.op("dve", lambda e: e.reciprocal(out=ss[:, :], in_=ss[:, :]), reads=["ss"], writes=["ss"])
        P.op("dve", lambda e: e.scalar_tensor_tensor(out=dst[:, :], in0=src[:, :], scalar=ss[:, 0:1], in1=gs[:, :],
                                                     op0=ALU.mult, op1=ALU.mult), reads=[srcn, "ss", gn], writes=[dstn])

    for i in range(NT):
        P.dma("sp", lambda e, i=i: e.dma_start(out=xt[:, :], in_=x[i * 128:(i + 1) * 128, :]), writes=["xt"])
        P.dma("sp", lambda e, i=i: e.dma_start(out=mt[:, :, :], in_=mixT[i]), writes=["mt"])
        P.op("act", lambda e: e.copy(out=mtb[:, :, :], in_=mt[:, :, :]), reads=["mt"], writes=["mtb"])
        for g in range(2):
            for kc in range(8):
                P.op("pe", lambda e, g=g, kc=kc: e.matmul(psA[:, g * 512:(g + 1) * 512], lhsT=mtb[:, kc, :],
                                                          rhs=wo_bf[:, kc, g * 512:(g + 1) * 512], start=(kc == 0), stop=(kc == 7)),
                     reads=["mtb", "wo"], writes=["psA%d" % g])
        for g in range(2):
            P.op("dve", lambda e, g=g: e.tensor_tensor(out=x2[:, g * 512:(g + 1) * 512], in0=psA[:, g * 512:(g + 1) * 512],
                                                       in1=xt[:, g * 512:(g + 1) * 512], op=ALU.add),
                 reads=["psA%d" % g, "xt"], writes=["x2"])
        rms(x2, "x2", xn, "xn", g1s, "g1")
        P.op("act", lambda e: e.copy(out=xnb[:, :], in_=xn[:, :]), reads=["xn"], writes=["xnb"])
        for dc in range(8 if STAGE >= 2 else 0):
            P.op("pe", lambda e, dc=dc: e.transpose(out=psT[:, dc * 128:(dc + 1) * 128], in_=xnb[:, dc * 128:(dc + 1) * 128],
                                                    identity=id_b[:, :]), reads=["xnb", "idb"], writes=["psT"])
        P.op("act", lambda e: e.copy(out=xnT[:, :], in_=psT[:, :]), reads=["psT"], writes=["xnT"])
        for hc in range(16 if STAGE >= 3 else 0):
            bank = hc % 2
            for dc in range(8):
                P.op("pe", lambda e, hc=hc, dc=dc, bank=bank: e.matmul(
                    psA[0:64, bank * 512:bank * 512 + 128], lhsT=wq_bf[:, dc, hc * 64:(hc + 1) * 64],
                    rhs=xnT[:, dc * 128:(dc + 1) * 128], start=(dc == 0), stop=(dc == 7)),
                    reads=["xnT", "wq"], writes=["psA%d" % bank])
            if hc % 2 == 0:
                P.op("act", lambda e, hc=hc, bank=bank: e.copy(out=qT[0:64, hc * 128:(hc + 1) * 128],
                                                               in_=psA[0:64, bank * 512:bank * 512 + 128]),
                     reads=["psA%d" % bank], writes=["qT"])
            else:
                P.op("dve", lambda e, hc=hc, bank=bank: e.tensor_copy(out=qT[0:64, hc * 128:(hc + 1) * 128],
                                                                      in_=psA[0:64, bank * 512:bank * 512 + 128]),
                     reads=["psA%d" % bank], writes=["qT"])
        for hc in range(16 if STAGE >= 4 else 0):
            P.op("pe", lambda e, hc=hc: e.matmul(psS[:, hc, :], lhsT=qT[0:64, hc * 128:(hc + 1) * 128],
                                                 rhs=sk_bf[0:64, hc, :], start=True, stop=True),
                 reads=["qT", "sk"], writes=["psS"])
        for bk in range(4):
            if bk % 2 == 0:
                P.op("act", lambda e, bk=bk: e.copy(out=s_sb[:, 4 * bk:4 * bk + 4, :], in_=psS[:, 4 * bk:4 * bk + 4, :]),
                     reads=["psS"], writes=["s_sb"])
            else:
                P.op("dve", lambda e, bk=bk: e.tensor_copy(out=s_sb[:, 4 * bk:4 * bk + 4, :], in_=psS[:, 4 * bk:4 * bk + 4, :]),
                     reads=["psS"], writes=["s_sb"])
        for hc in range(16 if STAGE >= 5 else 0):
            P.op("dve", lambda e, hc=hc: e.max(out=sv[:, hc, 0:8], in_=s_sb[:, hc, :]), reads=["s_sb"], writes=["sv"])
            P.op("dve", lambda e, hc=hc: e.max_index(out=si[:, hc, 0:8], in_max=sv[:, hc, 0:8], in_values=s_sb[:, hc, :]),
                 reads=["s_sb", "sv"], writes=["si"])
            P.op("dve", lambda e, hc=hc: e.match_replace(out=s2[:, :], in_to_replace=sv[:, hc, 0:8], in_values=s_sb[:, hc, :],
                                                         imm_value=-1e30), reads=["s_sb", "sv"], writes=["s2"])
            P.op("dve", lambda e, hc=hc: e.max(out=sv[:, hc, 8:16], in_=s2[:, :]), reads=["s2"], writes=["sv"])
            P.op("dve", lambda e, hc=hc: e.max_index(out=si[:, hc, 8:16], in_max=sv[:, hc, 8:16], in_values=s2[:, :]),
                 reads=["s2", "sv"], writes=["si"])
        P.op("dve", lambda e: e.tensor_copy(out=sif[:, :, :], in_=si[:, :, :]), reads=["si"], writes=["sif"])
        for h in range(8 if STAGE >= 6 else 0):
            P.op("dve", lambda e, h=h: e.tensor_single_scalar(out=sif[:, 2 * h, :], in_=sif[:, 2 * h, :], scalar=128.0, op=ALU.mult),
                 reads=["sif"], writes=["sif"])
            P.op("dve", lambda e, h=h: e.tensor_tensor(
                out=cand[:, h, :].rearrange("p (a b) -> p a b", a=16),
                in0=sv[:, 2 * h, :].unsqueeze(2).to_broadcast([128, 16, 16]),
                in1=sv[:, 2 * h + 1, :].unsqueeze(1).to_broadcast([128, 16, 16]), op=ALU.add), reads=["sv"], writes=["cand"])
            P.op("dve", lambda e, h=h: e.tensor_tensor(
                out=cidx[:, h, :].rearrange("p (a b) -> p a b", a=16),
                in0=sif[:, 2 * h, :].unsqueeze(2).to_broadcast([128, 16, 16]),
                in1=sif[:, 2 * h + 1, :].unsqueeze(1).to_broadcast([128, 16, 16]), op=ALU.add), reads=["sif"], writes=["cidx"])
            P.op("dve", lambda e, h=h: e.max(out=tv[:, h, 0:8], in_=cand[:, h, :]), reads=["cand"], writes=["tv"])
            P.op("dve", lambda e, h=h: e.max_index(out=pi[:, h, 0:8], in_max=tv[:, h, 0:8], in_values=cand[:, h, :]),
                 reads=["cand", "tv"], writes=["pi"])
            P.op("dve", lambda e, h=h: e.match_replace(out=c2[:, :], in_to_replace=tv[:, h, 0:8], in_values=cand[:, h, :],
                                                       imm_value=-1e30), reads=["cand", "tv"], writes=["c2"])
            P.op("dve", lambda e, h=h: e.max(out=tv[:, h, 8:16], in_=c2[:, :]), reads=["c2"], writes=["tv"])
            P.op("dve", lambda e, h=h: e.max_index(out=pi[:, h, 8:16], in_max=tv[:, h, 8:16], in_values=c2[:, :]),
                 reads=["c2", "tv"], writes=["pi"])
            P.op("dve", lambda e, h=h: e.tensor_copy(out=pif[:, h, :], in_=pi[:, h, :]), reads=["pi"], writes=["pif"])
            for k in range(16):
                P.op("dve", lambda e, h=h, k=k: e.scalar_tensor_tensor(
                    out=j256[:, :], in0=iota[:, :], scalar=pif[:, h, k:k + 1], in1=cidx[:, h, :], op0=ALU.is_equal,
                    op1=ALU.mult, accum_out=ef[:, h * 16 + k:h * 16 + k + 1]), reads=["iota", "cidx", "pif"], writes=["j256", "ef"])
        P.op("dve", lambda e: e.tensor_copy(out=ei[:, :], in_=ef[:, :]), reads=["ef"], writes=["ei"])
        P.op("dve", lambda e: e.tensor_single_scalar(out=negm[:, :], in_=tv[:, :, 0], scalar=-1.0, op=ALU.mult),
             reads=["tv"], writes=["negm"])
        for h in range(8 if STAGE >= 7 else 0):
            P.op("act", lambda e, h=h: e.activation(out=gt[:, h, :], in_=tv[:, h, :], func=AF.Exp, bias=negm[:, h:h + 1],
                                                    scale=1.0, accum_out=Z[:, h:h + 1]), reads=["tv", "negm"], writes=["gt", "Z"])
        P.fence("act", ["Z", "gt"])
        P.op("dve", lambda e: e.reciprocal(out=Z[:, :], in_=Z[:, :]), reads=["Z"], writes=["Z"])
        for h in range(8):
            P.op("dve", lambda e, h=h: e.tensor_scalar_mul(out=gt[:, h, :], in0=gt[:, h, :], scalar1=Z[:, h:h + 1]),
                 reads=["gt", "Z"], writes=["gt"])
        for sl in range(nslots):
            r = gk % NG
            gk += 1
            P.dma("pool", lambda e, r=r, sl=sl: e.indirect_dma_start(
                out=gb[r][:, :], out_offset=None, in_=pu[:, :],
                in_offset=bass.IndirectOffsetOnAxis(ap=ei[:, sl:sl + 1], axis=0)), reads=["ei"], writes=["gbuf%d" % r])
            P.op("dve", lambda e, r=r, sl=sl: e.scalar_tensor_tensor(
                out=junkD[:, :], in0=gb[r][:, :], scalar=1.0, in1=xn[:, :], op0=ALU.mult, op1=ALU.mult,
                accum_out=apre[:, sl:sl + 1]), reads=["gbuf%d" % r, "xn"], writes=["junkD", "apre"])
        if nslots > 0:
            P.fence("dve", ["apre"])
            P.op("act", lambda e: e.activation(out=coef[:, 0:nslots], in_=apre[:, 0:nslots], func=AF.Gelu),
                 reads=["apre"], writes=["coef"])
            P.op("dve", lambda e: e.tensor_tensor(out=coef[:, 0:nslots], in0=coef[:, 0:nslots],
                                                  in1=gt[:, :, :].rearrange("p h k -> p (h k)")[:, 0:nslots], op=ALU.mult),
                 reads=["coef", "gt"], writes=["coef"])
        for sl in range(nslots):
            r = gk % NG
            gk += 1
            P.dma("pool", lambda e, r=r, sl=sl: e.indirect_dma_start(
                out=gb[r][:, :], out_offset=None, in_=pv[:, :],
                in_offset=bass.IndirectOffsetOnAxis(ap=ei[:, sl:sl + 1], axis=0)), reads=["ei"], writes=["gbuf%d" % r])
            if sl == 0:
                P.op("dve", lambda e, r=r: e.tensor_scalar_mul(out=acc[:, :], in0=gb[r][:, :], scalar1=coef[:, 0:1]),
                     reads=["gbuf%d" % r, "coef"], writes=["acc"])
            else:
                P.op("dve", lambda e, r=r, sl=sl: e.scalar_tensor_tensor(
                    out=acc[:, :], in0=gb[r][:, :], scalar=coef[:, sl:sl + 1], in1=acc[:, :], op0=ALU.mult, op1=ALU.add),
                    reads=["gbuf%d" % r, "coef", "acc"], writes=["acc"])
        if nslots > 0:
            P.op("dve", lambda e: e.tensor_tensor(out=x2[:, :], in0=x2[:, :], in1=acc[:, :], op=ALU.add),
                 reads=["x2", "acc"], writes=["x2"])
        rms(x2, "x2", yo, "yo", g2s, "g2")
        if nslots == 0:
            P.op("dve", lambda e: e.tensor_copy(out=yo[:, 0:128], in_=ef[:, :]), reads=["ef", "yo"], writes=["yo"])
            P.op("dve", lambda e: e.tensor_copy(out=yo[:, 128:256], in_=gt[:, :, :].rearrange("p h k -> p (h k)")),
                 reads=["gt", "yo"], writes=["yo"])
        P.dma("sp", lambda e, i=i: e.dma_start(out=y[i * 128:(i + 1) * 128, :], in_=yo[:, :]), reads=["yo"], writes=["yout"])
    P.emit()
    return nc


def run_phase3(x_prompt, x_sample, mix_p, mix_s, w_out, norm_ffn_g, peer_w_q, peer_sub_keys, peer_u, peer_v, norm_final_g,
               NT=33, nslots=128):
    xp = np.ascontiguousarray(x_prompt).reshape(-1, D)
    xs = np.ascontiguousarray(x_sample).reshape(-1, D)
    nc = build_phase3(NT, nslots)
    g1 = np.ascontiguousarray(np.broadcast_to(norm_ffn_g.reshape(1, D), (128, D))).astype(np.float32)
    g2 = np.ascontiguousarray(np.broadcast_to(norm_final_g.reshape(1, D), (128, D))).astype(np.float32)
    skT = np.ascontiguousarray(np.transpose(peer_sub_keys, (3, 0, 1, 2)).reshape(64, 16 * 128))
    iota_h = np.ascontiguousarray(np.broadcast_to(np.arange(256, dtype=np.float32)[None, :], (128, 256)))
    in_maps = []
    for c in range(NCORE):
        xc = np.zeros((33 * 128, D), np.float32)
        mc = np.zeros((33 * 128, D), np.float32)
        xc[:4096] = xp[c * 4096:(c + 1) * 4096]
        xc[4096:4128] = xs[c * 32:(c + 1) * 32]
        mc[:4096] = mix_p[c * 4096:(c + 1) * 4096]
        mc[4096:4128] = mix_s[c * 32:(c + 1) * 32]
        mT = np.ascontiguousarray(np.transpose(mc.reshape(33, 128, 8, 128), (0, 3, 2, 1)))
        in_maps.append({"x": xc[:NT * 128], "mixT": mT[:NT], "w_out": np.ascontiguousarray(w_out), "w_q": np.ascontiguousarray(peer_w_q),
                        "skT": skT, "iota": iota_h, "gffn": g1, "gfin": g2, "ident": _ident(), "peer_u": np.ascontiguousarray(peer_u),
                        "peer_v": np.ascontiguousarray(peer_v)})
    res = run_bass_kernel_spmd(nc, in_maps, core_ids=list(range(NCORE)))
    yp = np.concatenate([res.results[c]["y"][:4096] for c in range(NCORE)], axis=0) if NT == 33 else None
    ysm = np.concatenate([res.results[c]["y"][4096:4128] for c in range(NCORE)], axis=0) if NT == 33 else None
    return yp, ysm, res


def kernel(x_prompt, x_sample, cache_fox_k, cache_fox_v, cache_fox_logf, state_rwkv, state_shift,
           norm_mix_g, w_in, fox_b_f, rwkv_mu, rwkv_w0, rwkv_w2, rwkv_a0, rwkv_a2, rwkv_g2,
           rwkv_k_k, rwkv_k_a, rwkv_r_k, rwkv_lnx_w, rwkv_lnx_b, w_out, norm_ffn_g,
           peer_w_q, peer_sub_keys, peer_u, peer_v, norm_final_g):
    f = lambda a: np.asarray(a, dtype=np.float32)
    x_prompt, x_sample = f(x_prompt), f(x_sample)
    pp, psm = run_phase1(x_prompt, x_sample, f(norm_mix_g)[0], f(w_in)[0], f(fox_b_f)[0])
    prm = {"rwkv_mu": f(rwkv_mu)[0], "rwkv_w0": f(rwkv_w0)[0], "rwkv_w2": f(rwkv_w2)[0], "rwkv_a0": f(rwkv_a0)[0],
           "rwkv_a2": f(rwkv_a2)[0], "rwkv_g2": f(rwkv_g2)[0], "rwkv_k_k": f(rwkv_k_k)[0], "rwkv_k_a": f(rwkv_k_a)[0],
           "rwkv_r_k": f(rwkv_r_k)[0], "rwkv_lnx_w": f(rwkv_lnx_w)[0], "rwkv_lnx_b": f(rwkv_lnx_b)[0]}
    maps = fox_inputs(pp, psm, f(cache_fox_k)[0], f(cache_fox_v)[0], f(cache_fox_logf)[0])
    rmaps = rw_inputs(pp, psm, f(state_rwkv)[0], f(state_shift)[0], prm)
    for m, r in zip(maps, rmaps):
        m.update(r)
    nc2 = build_phase2()
    res2 = run_bass_kernel_spmd(nc2, maps, core_ids=list(range(NCORE)))
    del maps, rmaps
    R2 = res2.results
    mix_p = np.empty((2, TP, 1024), np.float32)
    mix_s = np.empty((NSEQ_S, 16, 1024), np.float32)
    S_p = np.empty((1, 2, 8, 64, 64), np.float32)
    S_s = np.empty((1, NSEQ_S, 8, 64, 64), np.float32)
    for h in range(NCORE):
        mix_p[:, :, h * 64:(h + 1) * 64] = R2[h]["o_p"]
        mix_p[:, :, 512 + h * 64:512 + (h + 1) * 64] = R2[h]["rw_p"]
        mix_s[:, :, h * 64:(h + 1) * 64] = R2[h]["o_s"]
        mix_s[:, :, 512 + h * 64:512 + (h + 1) * 64] = R2[h]["rw_s"]
        S_p[0, :, h] = R2[h]["ST_p"].reshape(2, 64, 64)
        S_s[0, :, h] = R2[h]["ST_s"].reshape(NSEQ_S, 64, 64)
    yp, ysm, _ = run_phase3(x_prompt, x_sample, mix_p.reshape(-1, 1024), mix_s.reshape(-1, 1024), f(w_out)[0],
                            f(norm_ffn_g)[0], f(peer_w_q)[0], f(peer_sub_keys)[0], f(peer_u)[0], f(peer_v)[0],
                            f(norm_final_g))
    ppb = pp.reshape(2, TP, IN_COLS)
    pss = psm.reshape(NSEQ_S, 16, IN_COLS)
    c = np.ascontiguousarray
    return (
        c(yp.reshape(2, TP, 1024)), c(ysm.reshape(NSEQ_S, 16, 1024)),
        c(ppb[:, :, 512:1024].reshape(1, 2, TP, 8, 64)), c(ppb[:, :, 1024:1536].reshape(1, 2, TP, 8, 64)),
        c(ppb[:, :, 1536:1544].reshape(1, 2, TP, 8)), S_p, c(ppb[:, -1:, FOX_COLS:].reshape(1, 2, 1, RW_COLS)),
        c(pss[:, :, 512:1024].reshape(1, NSEQ_S, 16, 8, 64)), c(pss[:, :, 1024:1536].reshape(1, NSEQ_S, 16, 8, 64)),
        c(pss[:, :, 1536:1544].reshape(1, NSEQ_S, 16, 8)), S_s, c(pss[:, -1:, FOX_COLS:].reshape(1, NSEQ_S, 1, RW_COLS)),
    )
```

```python
from contextlib import ExitStack
import math
import numpy as np
import concourse.bass as bass
import concourse.mybir as mybir
from concourse.bass_utils import run_bass_kernel_spmd

F32 = mybir.dt.float32
BF16 = mybir.dt.bfloat16
I32 = mybir.dt.int32
U32 = mybir.dt.uint32
ALU = mybir.AluOpType
AF = mybir.ActivationFunctionType
AX = mybir.AxisListType

D = 1024
IN_COLS = 3240
FOX_COLS = 1544
RW_COLS = 1696
NCORE = 8


class Prog:
    ENGS = ("pe", "act", "dve", "pool", "sp")

    def __init__(self, nc):
        self.nc = nc
        self.st = ExitStack()
        self.ops = {e: [] for e in self.ENGS}
        self.cnt = {}
        self.waited = {e: {} for e in self.ENGS}
        self.lastw = {}
        self.readers = {}
        self.ndma = {e: 0 for e in self.ENGS}
        self.NS = 8
        self.nosame = set()
        self.fence_t = {}
        self.uid = 0

    def sb(self, name, shape, dt):
        return self.st.enter_context(self.nc.sbuf_tensor("sb_" + name, list(shape), dt))

    def ps(self, name, shape, dt):
        return self.st.enter_context(self.nc.psum_tensor("ps_" + name, list(shape), dt))

    def _deps(self, eng, reads, writes):
        deps = []
        for b in reads:
            if b in self.lastw:
                deps.extend(self.lastw[b].items())
        for b in writes:
            if b in self.lastw:
                deps.extend(self.lastw[b].items())
            deps.extend(self.readers.get(b, ()))
        best = {}
        for (k, v) in deps:
            if eng == "pe" and k == "pe":
                continue
            if k == eng and eng in self.nosame:
                continue
            if self.waited[eng].get(k, 0) >= v:
                continue
            best[k] = max(best.get(k, 0), v)
        for k, v in best.items():
            self.waited[eng][k] = v
        return list(best.items())

    def _record(self, tok, reads, writes):
        for b in reads:
            self.readers.setdefault(b, []).append(tok)
        for b in writes:
            self.lastw.setdefault(b, {})[tok[0]] = tok[1]
            self.readers[b] = []

    def op(self, eng, fn, reads=(), writes=()):
        waits = self._deps(eng, reads, writes)
        self.cnt[eng] = self.cnt.get(eng, 0) + 1
        self.ops[eng].append((waits, fn, eng, 1))
        self._record((eng, self.cnt[eng]), reads, writes)

    def fence(self, eng, names):
        if eng not in self.fence_t:
            self.fence_t[eng] = self.sb("fence_" + eng, [128, 2], F32)
        t = self.fence_t[eng]
        if eng == "act":
            self.op("act", lambda e: e.copy(out=t[:, 1:2], in_=t[:, 0:1]), reads=(), writes=list(names))
        else:
            self.op("dve", lambda e: e.tensor_copy(out=t[:, 1:2], in_=t[:, 0:1]), reads=(), writes=list(names))

    def dma(self, q, fn, reads=(), writes=()):
        waits = self._deps(q, reads, writes)
        k = "d_%s_%d" % (q, self.ndma[q] % self.NS)
        self.ndma[q] += 1
        prev = self.cnt.get(k, 0)
        if prev and self.waited[q].get(k, 0) < prev:
            self.waited[q][k] = prev
            waits = [w for w in waits if w[0] != k] + [(k, prev)]
        self.cnt[k] = self.cnt.get(k, 0) + 16
        self.ops[q].append((waits, fn, k, 16))
        self._record((k, self.cnt[k]), reads, writes)

    def emit(self):
        nc = self.nc
        keys = sorted(self.cnt.keys())
        sems = {k: self.st.enter_context(nc.semaphore("s_" + k)) for k in keys}
        final = [(k, self.cnt[k]) for k in keys]
        ops = self.ops

        def run(name, e):
            for (waits, fn, k, inc) in ops[name]:
                for (wk, wv) in waits:
                    e.wait_ge(sems[wk], wv)
                fn(e).then_inc(sems[k], inc)
            if name == "sp":
                for (k, v) in final:
                    e.wait_ge(sems[k], v)

        with nc.Block() as block:
            @block.tensor
            def _(e):
                run("pe", e)

            @block.scalar
            def _(e):
                run("act", e)

            @block.vector
            def _(e):
                run("dve", e)

            @block.gpsimd
            def _(e):
                run("pool", e)

            @block.sync
            def _(e):
                run("sp", e)
        self.st.close()


def _din(nc, name, shape, dt=F32):
    return nc.dram_tensor(name, list(shape), dt, kind="ExternalInput").ap()


def _dout(nc, name, shape, dt=F32):
    return nc.dram_tensor(name, list(shape), dt, kind="ExternalOutput").ap()


def _load_cast(P, name, dram_ap, shape, stage, stage_name, q="sp", eng="act"):
    t = P.sb(name, shape, BF16)
    p, n = shape
    P.dma(q, lambda e: e.dma_start(out=stage[0:p, 0:n], in_=dram_ap), writes=[stage_name])
    if eng == "act":
        P.op("act", lambda e: e.copy(out=t[:, :], in_=stage[0:p, 0:n]), reads=[stage_name], writes=[name])
    else:
        P.op("dve", lambda e: e.tensor_copy(out=t[:, :], in_=stage[0:p, 0:n]), reads=[stage_name], writes=[name])
    return t


def build_phase1(NT):
    nc = bass.Bass("TRN2", target_bir_lowering=False)
    x = _din(nc, "x", [NT * 128, D])
    gbc = _din(nc, "gbc", [128, D])
    w = _din(nc, "w_in", [D, IN_COLS])
    bfb = _din(nc, "bfb", [128, 8])
    identd = _din(nc, "ident", [128, 128])
    proj = _dout(nc, "proj", [NT * 128, IN_COLS])
    P = Prog(nc)
    wst = P.sb("wst", [128, IN_COLS], F32)
    w_bf = P.sb("w_bf", [128, 8, IN_COLS], BF16)
    g_sb = P.sb("g_sb", [128, D], F32)
    bf_sb = P.sb("bf_sb", [128, 8], F32)
    id_f = P.sb("id_f", [128, 128], F32)
    id_b = P.sb("id_b", [128, 128], BF16)
    P.dma("sp", lambda e: e.dma_start(out=g_sb[:, :], in_=gbc), writes=["g"])
    P.dma("sp", lambda e: e.dma_start(out=bf_sb[:, :], in_=bfb), writes=["bf"])
    P.dma("sp", lambda e: e.dma_start(out=id_f[:, :], in_=identd), writes=["idf"])
    P.op("dve", lambda e: e.tensor_copy(out=id_b[:, :], in_=id_f[:, :]), reads=["idf"], writes=["idb"])
    for dc in range(8):
        P.dma("sp", lambda e, dc=dc: e.dma_start(out=wst[:, :], in_=w[dc * 128:(dc + 1) * 128, :]),
              writes=["wst"])
        P.op("act", lambda e, dc=dc: e.copy(out=w_bf[:, dc, :], in_=wst[:, :]), reads=["wst"], writes=["w%d" % dc])
    wnames = ["w%d" % dc for dc in range(8)]
    xt = [P.sb("xt%d" % i, [128, D], F32) for i in range(2)]
    junk = P.sb("junk", [128, D], BF16)
    ss = [P.sb("ss%d" % i, [128, 1], F32) for i in range(2)]
    rstd = [P.sb("rstd%d" % i, [128, 1], F32) for i in range(2)]
    h = [P.sb("h%d" % i, [128, D], BF16) for i in range(2)]
    hT = [P.sb("hT%d" % i, [128, D], BF16) for i in range(2)]
    pr = [P.sb("pr%d" % i, [128, IN_COLS], F32) for i in range(2)]
    lz = P.sb("lz", [128, 8], F32)
    psT = [P.ps("psT%d" % i, [128, D], BF16) for i in range(2)]
    psP = [P.ps("psP%d" % i, [128, 512], F32) for i in range(4)]
    groups = [(c0, min(c0 + 512, IN_COLS)) for c0 in range(0, IN_COLS, 512)]
    gi = 0
    for i in range(NT):
        b = i % 2
        X, H, HT, PR = xt[b], h[b], hT[b], pr[b]
        P.dma("sp", lambda e, X=X, i=i: e.dma_start(out=X[:, :], in_=x[i * 128:(i + 1) * 128, :]),
              writes=["xt%d" % b])
        P.op("act", lambda e, X=X, b=b: e.activation(out=junk[:, :], in_=X[:, :], func=AF.Square,
                                                      accum_out=ss[b][:, 0:1]),
             reads=["xt%d" % b], writes=["junk", "ss%d" % b])
        P.op("act", lambda e, b=b: e.activation(out=rstd[b][:, :], in_=ss[b][:, :], func=AF.Sqrt, bias=1e-6,
                                                scale=1.0 / D),
             reads=["ss%d" % b], writes=["rstd%d" % b])
        P.op("dve", lambda e, b=b: e.reciprocal(out=rstd[b][:, :], in_=rstd[b][:, :]),
             reads=["rstd%d" % b], writes=["rstd%d" % b])
        P.op("dve", lambda e, X=X, H=H, b=b: e.scalar_tensor_tensor(
            out=H[:, :], in0=X[:, :], scalar=rstd[b][:, 0:1], in1=g_sb[:, :], op0=ALU.mult, op1=ALU.mult),
            reads=["xt%d" % b, "rstd%d" % b, "g"], writes=["h%d" % b])
        for dc in range(8):
            P.op("pe", lambda e, H=H, b=b, dc=dc: e.transpose(
                out=psT[b][:, dc * 128:(dc + 1) * 128], in_=H[:, dc * 128:(dc + 1) * 128], identity=id_b[:, :]),
                reads=["h%d" % b, "idb"], writes=["psT%d" % b])
        P.op("act", lambda e, HT=HT, b=b: e.copy(out=HT[:, :], in_=psT[b][:, :]),
             reads=["psT%d" % b], writes=["hT%d" % b])
        for (c0, c1) in groups:
            pp = gi % 4
            gi += 1
            n = c1 - c0
            for dc in range(8):
                P.op("pe", lambda e, HT=HT, pp=pp, dc=dc, c0=c0, c1=c1, n=n: e.matmul(
                    psP[pp][:, 0:n], lhsT=HT[:, dc * 128:(dc + 1) * 128], rhs=w_bf[:, dc, c0:c1],
                    start=(dc == 0), stop=(dc == 7)),
                    reads=["hT%d" % b] + wnames, writes=["psP%d" % pp])
            if gi % 2 == 0:
                P.op("act", lambda e, PR=PR, pp=pp, c0=c0, c1=c1, n=n: e.copy(out=PR[:, c0:c1], in_=psP[pp][:, 0:n]),
                     reads=["psP%d" % pp], writes=["pr%d" % b])
            else:
                P.op("dve", lambda e, PR=PR, pp=pp, c0=c0, c1=c1, n=n: e.tensor_copy(out=PR[:, c0:c1],
                                                                                     in_=psP[pp][:, 0:n]),
                     reads=["psP%d" % pp], writes=["pr%d" % b])
        P.op("dve", lambda e, PR=PR: e.tensor_tensor(out=lz[:, :], in0=PR[:, 1536:1544], in1=bf_sb[:, :], op=ALU.add),
             reads=["pr%d" % b, "bf"], writes=["lz"])
        P.op("act", lambda e: e.activation(out=lz[:, :], in_=lz[:, :], func=AF.Exp, scale=-1.0),
             reads=["lz"], writes=["lz"])
        P.op("act", lambda e: e.activation(out=lz[:, :], in_=lz[:, :], func=AF.Ln, bias=1.0, scale=1.0),
             reads=["lz"], writes=["lz"])
        P.op("dve", lambda e, PR=PR: e.tensor_single_scalar(out=PR[:, 1536:1544], in_=lz[:, :], scalar=-1.0,
                                                            op=ALU.mult),
             reads=["lz"], writes=["pr%d" % b])
        P.dma("sp", lambda e, PR=PR, i=i: e.dma_start(out=proj[i * 128:(i + 1) * 128, :], in_=PR[:, :]),
              reads=["pr%d" % b], writes=["out%d" % i])
    P.emit()
    return nc


def _ident():
    return np.eye(128, dtype=np.float32)


def run_phase1(x_prompt, x_sample, norm_mix_g, w_in, fox_b_f):
    NT = 33
    xp = np.ascontiguousarray(x_prompt).reshape(-1, D)
    xs = np.ascontiguousarray(x_sample).reshape(-1, D)
    nc = build_phase1(NT)
    gbc = np.ascontiguousarray(np.broadcast_to(norm_mix_g.reshape(1, D), (128, D))).astype(np.float32)
    bfb = np.ascontiguousarray(np.broadcast_to(fox_b_f.reshape(1, 8), (128, 8))).astype(np.float32)
    w = np.ascontiguousarray(w_in.reshape(D, IN_COLS))
    in_maps = []
    for c in range(NCORE):
        xc = np.zeros((NT * 128, D), np.float32)
        xc[:4096] = xp[c * 4096:(c + 1) * 4096]
        xc[4096:4128] = xs[c * 32:(c + 1) * 32]
        in_maps.append({"x": xc, "gbc": gbc, "w_in": w, "bfb": bfb, "ident": _ident()})
    res = run_bass_kernel_spmd(nc, in_maps, core_ids=list(range(NCORE)))
    pp = np.concatenate([res.results[c]["proj"][:4096] for c in range(NCORE)], axis=0)
    psm = np.concatenate([res.results[c]["proj"][4096:4128] for c in range(NCORE)], axis=0)
    return pp, psm


TP = 16384
TS = 2176
NSEQ_S = 16


def _fox_consts(P, nc):
    c = {}
    tri_d = _din(nc, "tri", [128, 128])
    ones_d = _din(nc, "ones", [128, 128])
    id_d = _din(nc, "ident", [128, 128])
    mask_d = _din(nc, "mask", [128, 4 * 512])
    c["tri"] = P.sb("tri", [128, 128], F32)
    c["ones"] = P.sb("ones", [128, 128], F32)
    c["idf"] = P.sb("idf", [128, 128], F32)
    c["idb"] = P.sb("idb", [128, 128], BF16)
    c["maskf"] = P.sb("maskf", [128, 2048], F32)
    c["mask"] = P.sb("maskb", [128, 4, 512], BF16)
    P.dma("sp", lambda e: e.dma_start(out=c["tri"][:, :], in_=tri_d), writes=["tri"])
    P.dma("sp", lambda e: e.dma_start(out=c["ones"][:, :], in_=ones_d), writes=["ones"])
    P.dma("sp", lambda e: e.dma_start(out=c["idf"][:, :], in_=id_d), writes=["idf"])
    P.dma("sp", lambda e: e.dma_start(out=c["maskf"][:, :], in_=mask_d), writes=["maskf"])
    P.op("dve", lambda e: e.tensor_copy(out=c["idb"][:, :], in_=c["idf"][:, :]), reads=["idf"], writes=["idb"])
    P.op("dve", lambda e: e.tensor_copy(out=c["mask"][:, :, :], in_=c["maskf"][:, :].rearrange("p (a b) -> p a b", a=4)),
         reads=["maskf"], writes=["maskb"])
    return c


def _fox_seq(P, c, B, tag, NT, qT_src, nq_tot, kT_src, v_src, lf_src, groups, out_fn):
    T = NT * 128
    qT, kT, vv, stage = B["qT"], B["kT"], B["vv"], B["stage"]
    k = 0
    for (dst, src, n, nm) in ((qT, qT_src, nq_tot, "qT"), (kT, kT_src, T, "kT")):
        for c0 in range(0, n, 2048):
            w = min(2048, n - c0)
            s = k % 2
            k += 1
            P.dma("sp", lambda e, s=s, src=src, c0=c0, w=w: e.dma_start(out=stage[s][0:64, 0:w], in_=src[:, c0:c0 + w]),
                  writes=["stage%d" % s])
            eng = "act" if k % 2 else "dve"
            if eng == "act":
                P.op("act", lambda e, s=s, dst=dst, c0=c0, w=w: e.copy(out=dst[:, c0:c0 + w], in_=stage[s][0:64, 0:w]),
                     reads=["stage%d" % s], writes=[nm])
            else:
                P.op("dve", lambda e, s=s, dst=dst, c0=c0, w=w: e.tensor_copy(out=dst[:, c0:c0 + w], in_=stage[s][0:64, 0:w]),
                     reads=["stage%d" % s], writes=[nm])
    for j0 in range(0, NT, 32):
        nj = min(32, NT - j0)
        s = k % 2
        k += 1
        P.dma("sp", lambda e, s=s, j0=j0, nj=nj: e.dma_start(
            out=stage[s][:, 0:nj * 64].rearrange("p (j d) -> p j d", d=64), in_=v_src[:, j0:j0 + nj, :]),
            writes=["stage%d" % s])
        P.op("dve", lambda e, s=s, j0=j0, nj=nj: e.tensor_copy(
            out=vv[:, j0:j0 + nj, 0:64], in_=stage[s][:, 0:nj * 64].rearrange("p (j d) -> p j d", d=64)),
            reads=["stage%d" % s], writes=["vv"])
    L = B["L"]
    P.dma("sp", lambda e: e.dma_start(out=L[:, 0:NT], in_=lf_src), writes=["L"])
    cl_ps, tot_ps = B["cl_ps"], B["tot_ps"]
    P.op("pe", lambda e: e.matmul(cl_ps[:, 0:NT], lhsT=c["tri"][:, :], rhs=L[:, 0:NT], start=True, stop=True),
         reads=["L", "tri"], writes=["cl_ps"])
    P.op("pe", lambda e: e.matmul(tot_ps[:, 0:NT], lhsT=c["ones"][:, :], rhs=L[:, 0:NT], start=True, stop=True),
         reads=["L", "ones"], writes=["tot_ps"])
    sa, sbb = B["scanA"], B["scanB"]
    P.op("dve", lambda e: e.tensor_copy(out=sa[:, 0:NT], in_=tot_ps[:, 0:NT]), reads=["tot_ps"], writes=["scanA"])
    cur, nxt, cn, nn = sa, sbb, "scanA", "scanB"
    sh = 1
    while sh < NT:
        P.op("dve", lambda e, cur=cur, nxt=nxt, sh=sh: e.tensor_tensor(
            out=nxt[:, sh:NT], in0=cur[:, sh:NT], in1=cur[:, 0:NT - sh], op=ALU.add), reads=[cn], writes=[nn])
        P.op("dve", lambda e, cur=cur, nxt=nxt, sh=sh: e.tensor_copy(out=nxt[:, 0:sh], in_=cur[:, 0:sh]),
             reads=[cn], writes=[nn])
        cur, nxt, cn, nn = nxt, cur, nn, cn
        sh *= 2
    pex, negC = B["pex"], B["negC"]
    P.op("dve", lambda e, cur=cur: e.tensor_tensor(out=pex[:, 0:NT], in0=cur[:, 0:NT], in1=tot_ps[:, 0:NT],
                                                   op=ALU.subtract), reads=[cn, "tot_ps"], writes=["pex"])
    P.op("dve", lambda e: e.scalar_tensor_tensor(out=negC[:, 0:NT], in0=pex[:, 0:NT], scalar=-1.0, in1=cl_ps[:, 0:NT],
                                                 op0=ALU.mult, op1=ALU.subtract),
         reads=["pex", "cl_ps"], writes=["negC"])
    bias = B["bias"]
    for gi, (q0, nq, nk, d0, ct) in enumerate(groups):
        P.op("dve", lambda e, gi=gi, nk=nk, ct=ct: e.tensor_scalar(
            out=bias[:, gi, 0:nk], in0=negC[:, 0:nk], scalar1=pex[:, ct:ct + 1], scalar2=0.0,
            op0=ALU.add, op1=ALU.add), reads=["negC", "pex"], writes=["bias"])
    it = B["it"]
    for gi, (q0, nq, nk, d0, ct) in enumerate(groups):
        ob = B["gcount"] % 2
        B["gcount"] += 1
        OT = B["OT"][ob]
        def emit_score(j, it_):
            sb_ = it_ % 2
            sT = B["sT"][sb_]
            diag = j >= d0
            P.op("pe", lambda e, sT=sT, j=j, q0=q0, nq=nq, diag=diag: e.matmul(
                sT[:, 0:nq], lhsT=kT[:, j * 128:(j + 1) * 128], rhs=qT[:, q0:q0 + nq], start=True, stop=(not diag)),
                reads=["kT", "qT"], writes=["sT%d" % sb_])
            if diag:
                jl = j - d0
                P.op("pe", lambda e, sT=sT, jl=jl, nq=nq: e.matmul(
                    sT[:, 0:nq], lhsT=c["idb"][:, :], rhs=c["mask"][:, jl, 0:nq], start=False, stop=True),
                    reads=["idb", "maskb"], writes=["sT%d" % sb_])

        emit_score(0, it)
        for j in range(nk):
            sb_ = it % 2
            pb = it % 3
            sT = B["sT"][sb_]
            pT = B["pT"][pb]
            if j + 1 < nk:
                emit_score(j + 1, it + 1)
            it += 1
            P.op("act", lambda e, sT=sT, pT=pT, gi=gi, j=j, nq=nq: e.activation(
                out=pT[:, 0:nq], in_=sT[:, 0:nq], func=AF.Exp, bias=bias[:, gi, j:j + 1], scale=0.125),
                reads=["sT%d" % sb_, "bias"], writes=["pT%d" % pb])
            P.op("pe", lambda e, OT=OT, pT=pT, j=j, nq=nq, nk=nk: e.matmul(
                OT[0:65, 0:nq], lhsT=vv[:, j, :], rhs=pT[:, 0:nq], start=(j == 0), stop=(j == nk - 1)),
                reads=["vv", "pT%d" % pb], writes=["OT%d" % ob])
        oT = B["oT"]
        P.op("act", lambda e, OT=OT, nq=nq: e.copy(out=oT[0:65, 0:nq], in_=OT[0:65, 0:nq]),
             reads=["OT%d" % ob], writes=["oT"])
        oq, rec, osb = B["oq"], B["rec"], B["osb"]
        nqi = (nq + 127) // 128
        for qi in range(nqi):
            w = min(128, nq - qi * 128)
            P.op("pe", lambda e, qi=qi, w=w: e.transpose(out=oq[0:w, qi, :], in_=oT[0:65, qi * 128:qi * 128 + w],
                                                         identity=c["idf"][0:65, 0:65]),
                 reads=["oT", "idf"], writes=["oq"])
        wmax = min(128, nq)
        for qi in range(nqi):
            P.op("dve", lambda e, qi=qi: e.reciprocal(out=rec[0:wmax, qi:qi + 1], in_=oq[0:wmax, qi, 64:65]),
                 reads=["oq"], writes=["rec"])
            P.op("dve", lambda e, qi=qi: e.tensor_scalar_mul(out=osb[0:wmax, qi, :], in0=oq[0:wmax, qi, 0:64],
                                                             scalar1=rec[0:wmax, qi:qi + 1]),
                 reads=["oq", "rec"], writes=["osb"])
        out_fn(P, osb, q0, nq, nqi, wmax)
    B["it"] = it


def _fox_bufs(P):
    B = {}
    B["qT"] = P.sb("qT", [64, TP], BF16)
    B["kT"] = P.sb("kT", [64, TP], BF16)
    B["vv"] = P.sb("vv", [128, 128, 65], BF16)
    B["stage"] = [P.sb("stage%d" % i, [128, 2048], F32) for i in range(2)]
    B["L"] = P.sb("L", [128, 128], F32)
    B["scanA"] = P.sb("scanA", [128, 128], F32)
    B["scanB"] = P.sb("scanB", [128, 128], F32)
    B["pex"] = P.sb("pex", [128, 128], F32)
    B["negC"] = P.sb("negC", [128, 128], F32)
    B["bias"] = P.sb("bias", [128, 32, 128], F32)
    B["pT"] = [P.sb("pT%d" % i, [128, 512], BF16) for i in range(3)]
    B["oT"] = P.sb("oT", [65, 512], F32)
    B["rec"] = P.sb("rec", [128, 4], F32)
    B["osb"] = P.sb("osb", [128, 4, 64], F32)
    B["cl_ps"] = P.ps("cl_ps", [128, 128], F32)
    B["tot_ps"] = P.ps("tot_ps", [128, 128], F32)
    B["sT"] = [P.ps("sT%d" % i, [128, 512], F32) for i in range(2)]
    B["OT"] = [P.ps("OT%d" % i, [128, 512], F32) for i in range(2)]
    B["oq"] = P.ps("oq", [128, 4, 65], F32)
    B["it"] = 0
    B["gcount"] = 0
    P.op("pool", lambda e: e.memset(B["vv"][:, :, 64:65], 1.0), writes=["vv"])
    return B


def build_phase2_fox(n_prompt=2, n_sample=NSEQ_S, ngroups=32):
    nc = bass.Bass("TRN2", target_bir_lowering=False)
    qTp = _din(nc, "qTp", [2, 64, TP])
    kTp = _din(nc, "kTp", [2, 64, TP])
    vp = _din(nc, "vp", [2, 128, 128, 64])
    lfp = _din(nc, "lfp", [2, 128, 128])
    qTs = _din(nc, "qTs", [NSEQ_S, 64, 16])
    kTs = _din(nc, "kTs", [NSEQ_S, 64, TS])
    vs = _din(nc, "vs", [NSEQ_S, 128, 17, 64])
    lfs = _din(nc, "lfs", [NSEQ_S, 128, 17])
    op_ = _dout(nc, "o_p", [2, TP, 64])
    os_ = _dout(nc, "o_s", [NSEQ_S, 16, 64])
    P = Prog(nc)
    c = _fox_consts(P, nc)
    B = _fox_bufs(P)
    for b in range(n_prompt):
        groups = [(512 * g, 512, 4 * g + 4, 4 * g, 4 * g + 2) for g in range(ngroups)]

        def out_fn(P, osb, q0, nq, nqi, wmax, b=b):
            P.dma("sp", lambda e: e.dma_start(
                out=op_[b, q0:q0 + nq, :].rearrange("(a p) d -> p a d", p=128), in_=osb[:, 0:nqi, :]),
                reads=["osb"], writes=["o_out"])
        _fox_seq(P, c, B, "p%d" % b, 128, qTp[b], TP, kTp[b], vp[b], lfp[b], groups, out_fn)
    for s in range(n_sample):
        groups = [(0, 16, 17, 16, 16)]

        def out_fn(P, osb, q0, nq, nqi, wmax, s=s):
            P.dma("sp", lambda e: e.dma_start(out=os_[s, :, :], in_=osb[0:16, 0, :]), reads=["osb"], writes=["o_out"])
        _fox_seq(P, c, B, "s%d" % s, 17, qTs[s], 16, kTs[s], vs[s], lfs[s], groups, out_fn)
    P.emit()
    return nc


def _fox_const_inputs():
    p = np.arange(128)
    tri = (p[:, None] <= p[None, :]).astype(np.float32)
    ones = np.ones((128, 128), np.float32)
    col = np.arange(512)
    mask = np.zeros((128, 4, 512), np.float32)
    for jl in range(4):
        mask[:, jl, :] = np.where(jl * 128 + p[:, None] > col[None, :], -30000.0, 0.0)
    return {"tri": tri, "ones": ones, "ident": _ident(), "mask": mask.reshape(128, 2048)}


def _tile_major(a, nt):
    return np.ascontiguousarray(np.swapaxes(a.reshape((nt, 128) + a.shape[1:]), 0, 1))


def fox_inputs(pp, psm, cache_k, cache_v, cache_lf):
    ppb = pp.reshape(2, TP, IN_COLS)
    pss = psm.reshape(NSEQ_S, 16, IN_COLS)
    maps = []
    for h in range(NCORE):
        m = dict(_fox_const_inputs())
        m["qTp"] = np.ascontiguousarray(np.swapaxes(ppb[:, :, h * 64:(h + 1) * 64], 1, 2))
        m["kTp"] = np.ascontiguousarray(np.swapaxes(ppb[:, :, 512 + h * 64:512 + (h + 1) * 64], 1, 2))
        m["vp"] = np.stack([_tile_major(ppb[b, :, 1024 + h * 64:1024 + (h + 1) * 64], 128) for b in range(2)])
        m["lfp"] = np.stack([_tile_major(ppb[b, :, 1536 + h], 128) for b in range(2)])
        kfull = np.zeros((NSEQ_S, TS, 64), np.float32)
        vfull = np.zeros((NSEQ_S, TS, 64), np.float32)
        lfull = np.zeros((NSEQ_S, TS), np.float32)
        kfull[:, :2048] = cache_k[:, :, h, :]
        vfull[:, :2048] = cache_v[:, :, h, :]
        lfull[:, :2048] = cache_lf[:, :, h]
        kfull[:, 2048:2064] = pss[:, :, 512 + h * 64:512 + (h + 1) * 64]
        vfull[:, 2048:2064] = pss[:, :, 1024 + h * 64:1024 + (h + 1) * 64]
        lfull[:, 2048:2064] = pss[:, :, 1536 + h]
        m["qTs"] = np.ascontiguousarray(np.swapaxes(pss[:, :, h * 64:(h + 1) * 64], 1, 2))
        m["kTs"] = np.ascontiguousarray(np.swapaxes(kfull, 1, 2))
        m["vs"] = np.stack([_tile_major(vfull[s], 17) for s in range(NSEQ_S)])
        m["lfs"] = np.stack([_tile_major(lfull[s], 17) for s in range(NSEQ_S)])
        maps.append(m)
    return maps


NPAR = 352 + 7 * 64
EXPM05 = math.exp(-0.5)


def _rw_setup(P, nc, B):
    R = {}
    par_d = _din(nc, "rw_par", [128, NPAR])
    w2_d = _din(nc, "rw_w2", [32, 64])
    a2_d = _din(nc, "rw_a2", [32, 64])
    g2_d = _din(nc, "rw_g2", [96, 64])
    sel_d = _din(nc, "rw_sel", [6, 128])
    R["par"] = P.sb("rw_par", [128, NPAR], F32)
    R["w2"] = P.sb("rw_w2", [32, 64], F32)
    R["a2"] = P.sb("rw_a2", [32, 64], F32)
    R["g2"] = P.sb("rw_g2", [96, 64], F32)
    R["sel"] = P.sb("rw_sel", [6, 128], F32)
    R["selb"] = P.sb("rw_selb", [6, 128], BF16)
    R["omk"] = P.sb("rw_omk", [128, 64], F32)
    for nm, d_ in (("par", par_d), ("w2", w2_d), ("a2", a2_d), ("g2", g2_d), ("sel", sel_d)):
        P.dma("sp", lambda e, nm=nm, d_=d_: e.dma_start(out=R[nm][:, :], in_=d_), writes=["rwc_" + nm])
    P.op("dve", lambda e: e.tensor_copy(out=R["selb"][:, :], in_=R["sel"][:, :]), reads=["rwc_sel"], writes=["rwc_selb"])
    R["R3"] = [[P.sb("rw_R3_%d_%d" % (b, t), [128, 3, 320], BF16) for t in range(2)] for b in range(2)]
    R["r1"] = P.sb("rw_r1", [128, 320], F32)
    R["r2"] = P.sb("rw_r2", [128, 320], F32)
    o = 352
    R["mu"] = R["par"][:, 0:352]
    names = ["w0", "a0", "kk", "ka", "rk", "lnw", "lnb"]
    for i, nm in enumerate(names):
        R[nm] = R["par"][:, o + i * 64:o + (i + 1) * 64]
    P.op("dve", lambda e: e.tensor_scalar(out=R["omk"][:, :], in0=R["ka"], scalar1=-1.0, scalar2=1.0,
                                          op0=ALU.mult, op1=ALU.add), reads=["rwc_par"], writes=["rwc_omk"])
    R["cur"] = [P.sb("rw_cur%d" % b, [128, 352], F32) for b in range(2)]
    R["prv"] = [P.sb("rw_prv%d" % b, [128, 352], F32) for b in range(2)]
    R["R"] = [[P.sb("rw_R%d_%d" % (b, t), [128, 320], F32) for t in range(2)] for b in range(2)]
    R["GB"] = [[P.sb("rw_GB%d_%d" % (b, t), [128, 128], F32) for t in range(2)] for b in range(2)]
    R["VV"] = [P.sb("rw_VV%d" % t, [128, 128], F32) for t in range(2)]
    R["vT"] = [P.sb("rw_vT%d" % t, [128, 128], F32) for t in range(2)]
    R["yT"] = [P.sb("rw_yT%d" % t, [128, 128], F32) for t in range(2)]
    R["twl"] = P.sb("rw_twl", [32, 128], F32)
    R["alT"] = P.sb("rw_alT", [32, 128], F32)
    R["sgl"] = P.sb("rw_sgl", [96, 128], F32)
    for nm in ("zt", "at", "kkt", "tmp", "t1", "junk", "cen", "ob"):
        R[nm] = P.sb("rw_" + nm, [128, 64], F32)
    for nm in ("ssq", "rks", "mean", "var", "sk"):
        R[nm] = P.sb("rw_" + nm, [128, 1], F32)
    R["S"] = P.sb("rw_S", [128, 64], F32)
    R["stmp"] = P.sb("rw_stmp", [128, 64], F32)
    R["rowbuf"] = [P.sb("rw_rowbuf%d" % i, [6, 16 * 320], BF16) for i in range(2)]
    R["rowp"] = [B["sT"][0], B["sT"][1]]
    R["trp"] = B["OT"][0]
    R["lop"] = B["OT"][1]
    R["vtp"] = B["cl_ps"]
    R["ytp"] = B["tot_ps"]
    R["k"] = 0
    R["step"] = 0
    return R


def _rw_prep(P, c, R, n, ntok, cur_src, prev_src, rows_scr):
    tp = n % 2
    idf = c["idf"]
    t0 = n * ntok
    for b in range(2):
        cur, prv = R["cur"][b], R["prv"][b]
        cn, pn = "rw_cur%d" % b, "rw_prv%d" % b
        Rt, GB = R["R"][b][tp], R["GB"][b][tp]
        rn, gn = "rw_R%d_%d" % (b, tp), "rw_GB%d_%d" % (b, tp)
        P.dma("sp", lambda e, cur=cur, b=b: e.dma_start(out=cur[0:ntok, :], in_=cur_src(b, n)), writes=[cn])
        P.dma("sp", lambda e, prv=prv, b=b: e.dma_start(out=prv[0:ntok, :], in_=prev_src(b, n)), writes=[pn])
        P.op("dve", lambda e, cur=cur, prv=prv: e.tensor_tensor(out=prv[0:ntok, :], in0=prv[0:ntok, :], in1=cur[0:ntok, :],
                                                                op=ALU.subtract), reads=[cn, pn], writes=[pn])
        P.op("dve", lambda e, prv=prv: e.tensor_tensor(out=prv[0:ntok, :], in0=prv[0:ntok, :], in1=R["mu"][0:ntok, :],
                                                       op=ALU.mult), reads=[pn, "rwc_par"], writes=[pn])
        P.op("dve", lambda e, cur=cur, prv=prv: e.tensor_tensor(out=cur[0:ntok, :], in0=cur[0:ntok, :], in1=prv[0:ntok, :],
                                                                op=ALU.add), reads=[cn, pn], writes=[cn])
        trp, lop = R["trp"], R["lop"]
        for (o0, c0, c1, m) in ((0, 192, 224, 32), (128, 224, 256, 32), (256, 256, 352, 96)):
            P.op("pe", lambda e, cur=cur, o0=o0, c0=c0, c1=c1, m=m: e.transpose(
                out=trp[0:m, o0:o0 + ntok], in_=cur[0:ntok, c0:c1], identity=idf[0:ntok, 0:ntok]),
                reads=[cn, "idf"], writes=["OT0"])
        P.op("act", lambda e: e.activation(out=R["twl"][0:32, 0:ntok], in_=trp[0:32, 0:ntok], func=AF.Tanh),
             reads=["OT0"], writes=["rw_twl"])
        P.op("act", lambda e: e.copy(out=R["alT"][0:32, 0:ntok], in_=trp[0:32, 128:128 + ntok]),
             reads=["OT0"], writes=["rw_alT"])
        P.op("act", lambda e: e.activation(out=R["sgl"][0:96, 0:ntok], in_=trp[0:96, 256:256 + ntok], func=AF.Sigmoid),
             reads=["OT0"], writes=["rw_sgl"])
        P.op("pe", lambda e: e.matmul(lop[0:ntok, 0:64], lhsT=R["twl"][0:32, 0:ntok], rhs=R["w2"][:, :], start=True, stop=True),
             reads=["rw_twl", "rwc_w2"], writes=["OT1"])
        P.op("pe", lambda e: e.matmul(lop[0:ntok, 64:128], lhsT=R["alT"][0:32, 0:ntok], rhs=R["a2"][:, :], start=True, stop=True),
             reads=["rw_alT", "rwc_a2"], writes=["OT1"])
        P.op("pe", lambda e: e.matmul(lop[0:ntok, 128:192], lhsT=R["sgl"][0:96, 0:ntok], rhs=R["g2"][:, :], start=True, stop=True),
             reads=["rw_sgl", "rwc_g2"], writes=["OT1"])
        zt, at, kkt, tmp, t1, junk = R["zt"], R["at"], R["kkt"], R["tmp"], R["t1"], R["junk"]
        ssq, rks = R["ssq"], R["rks"]
        P.op("dve", lambda e: e.tensor_tensor(out=zt[0:ntok, :], in0=lop[0:ntok, 0:64], in1=R["w0"][0:ntok, :], op=ALU.add),
             reads=["OT1", "rwc_par"], writes=["rw_zt"])
        P.op("act", lambda e: e.activation(out=zt[0:ntok, :], in_=zt[0:ntok, :], func=AF.Sigmoid),
             reads=["rw_zt"], writes=["rw_zt"])
        P.op("act", lambda e, Rt=Rt: e.activation(out=Rt[0:ntok, 0:64], in_=zt[0:ntok, :], func=AF.Exp, scale=-EXPM05),
             reads=["rw_zt"], writes=[rn])
        P.op("dve", lambda e: e.tensor_tensor(out=at[0:ntok, :], in0=lop[0:ntok, 64:128], in1=R["a0"][0:ntok, :], op=ALU.add),
             reads=["OT1", "rwc_par"], writes=["rw_at"])
        P.op("act", lambda e: e.activation(out=at[0:ntok, :], in_=at[0:ntok, :], func=AF.Sigmoid),
             reads=["rw_at"], writes=["rw_at"])
        P.op("act", lambda e, GB=GB: e.copy(out=GB[0:ntok, 0:64], in_=lop[0:ntok, 128:192]), reads=["OT1"], writes=[gn])
        P.op("dve", lambda e, cur=cur: e.tensor_tensor(out=kkt[0:ntok, :], in0=cur[0:ntok, 64:128], in1=R["kk"][0:ntok, :],
                                                       op=ALU.mult), reads=[cn, "rwc_par"], writes=["rw_kkt"])
        P.op("dve", lambda e: e.scalar_tensor_tensor(out=junk[0:ntok, :], in0=kkt[0:ntok, :], scalar=1.0, in1=kkt[0:ntok, :],
                                                     op0=ALU.mult, op1=ALU.mult, accum_out=ssq[0:ntok, 0:1]),
             reads=["rw_kkt"], writes=["rw_junk", "rw_ssq"])
        P.fence("dve", ["rw_ssq"])
        P.op("act", lambda e: e.activation(out=ssq[0:ntok, :], in_=ssq[0:ntok, :], func=AF.Sqrt), reads=["rw_ssq"], writes=["rw_ssq"])
        P.op("dve", lambda e: e.tensor_scalar_max(out=ssq[0:ntok, :], in0=ssq[0:ntok, :], scalar1=1e-12),
             reads=["rw_ssq"], writes=["rw_ssq"])
        P.op("dve", lambda e: e.reciprocal(out=ssq[0:ntok, :], in_=ssq[0:ntok, :]), reads=["rw_ssq"], writes=["rw_ssq"])
        P.op("dve", lambda e: e.tensor_scalar_mul(out=kkt[0:ntok, :], in0=kkt[0:ntok, :], scalar1=ssq[0:ntok, 0:1]),
             reads=["rw_kkt", "rw_ssq"], writes=["rw_kkt"])
        P.op("dve", lambda e, Rt=Rt: e.tensor_single_scalar(out=Rt[0:ntok, 64:128], in_=kkt[0:ntok, :], scalar=-1.0, op=ALU.mult),
             reads=["rw_kkt"], writes=[rn])
        P.op("dve", lambda e, Rt=Rt: e.tensor_tensor(out=Rt[0:ntok, 128:192], in0=kkt[0:ntok, :], in1=at[0:ntok, :], op=ALU.mult),
             reads=["rw_kkt", "rw_at"], writes=[rn])
        P.op("dve", lambda e: e.tensor_tensor(out=tmp[0:ntok, :], in0=at[0:ntok, :], in1=R["ka"][0:ntok, :], op=ALU.mult),
             reads=["rw_at", "rwc_par"], writes=["rw_tmp"])
        P.op("dve", lambda e: e.tensor_tensor(out=tmp[0:ntok, :], in0=tmp[0:ntok, :], in1=R["omk"][0:ntok, :], op=ALU.add),
             reads=["rw_tmp", "rwc_omk"], writes=["rw_tmp"])
        P.op("dve", lambda e, cur=cur, Rt=Rt: e.tensor_tensor(out=Rt[0:ntok, 192:256], in0=cur[0:ntok, 64:128], in1=tmp[0:ntok, :],
                                                              op=ALU.mult), reads=[cn, "rw_tmp"], writes=[rn])
        P.op("act", lambda e, cur=cur, Rt=Rt: e.copy(out=Rt[0:ntok, 256:320], in_=cur[0:ntok, 0:64]), reads=[cn], writes=[rn])
        P.op("dve", lambda e, cur=cur, Rt=Rt: e.tensor_tensor(out=t1[0:ntok, :], in0=cur[0:ntok, 0:64], in1=Rt[0:ntok, 192:256],
                                                              op=ALU.mult), reads=[cn, rn], writes=["rw_t1"])
        P.op("dve", lambda e: e.scalar_tensor_tensor(out=junk[0:ntok, :], in0=t1[0:ntok, :], scalar=1.0, in1=R["rk"][0:ntok, :],
                                                     op0=ALU.mult, op1=ALU.mult, accum_out=rks[0:ntok, 0:1]),
             reads=["rw_t1", "rwc_par"], writes=["rw_junk", "rw_rks"])
        P.op("dve", lambda e, cur=cur, GB=GB: e.tensor_scalar_mul(out=GB[0:ntok, 64:128], in0=cur[0:ntok, 128:192],
                                                                  scalar1=rks[0:ntok, 0:1]),
             reads=[cn, "rw_rks"], writes=[gn])
        P.op("act", lambda e, cur=cur, b=b: e.copy(out=R["VV"][tp][0:ntok, b * 64:(b + 1) * 64], in_=cur[0:ntok, 128:192]),
             reads=[cn], writes=["rw_VV%d" % tp])
        R3 = R["R3"][b][tp]
        r3n = "rw_R3_%d_%d" % (b, tp)
        r1, r2 = R["r1"], R["r2"]
        P.op("act", lambda e, Rt=Rt, R3=R3: e.copy(out=R3[0:ntok, 0, :], in_=Rt[0:ntok, :]), reads=[rn], writes=[r3n])
        P.op("dve", lambda e, Rt=Rt, R3=R3: e.tensor_tensor(out=r1[0:ntok, :], in0=Rt[0:ntok, :], in1=R3[0:ntok, 0, :],
                                                            op=ALU.subtract), reads=[rn, r3n], writes=["rw_r1"])
        P.op("act", lambda e, R3=R3: e.copy(out=R3[0:ntok, 1, :], in_=r1[0:ntok, :]), reads=["rw_r1"], writes=[r3n])
        P.op("dve", lambda e, R3=R3: e.tensor_tensor(out=r2[0:ntok, :], in0=r1[0:ntok, :], in1=R3[0:ntok, 1, :],
                                                     op=ALU.subtract), reads=["rw_r1", r3n], writes=["rw_r2"])
        P.op("act", lambda e, R3=R3: e.copy(out=R3[0:ntok, 2, :], in_=r2[0:ntok, :]), reads=["rw_r2"], writes=[r3n])
        P.dma("sp", lambda e, R3=R3, b=b: e.dma_start(out=rows_scr[b, :, t0:t0 + ntok, :].rearrange("p t c -> t p c"),
                                                      in_=R3[0:ntok, :, :]),
              reads=[r3n], writes=["rows%d_%d" % (b, tp)])
    P.op("pe", lambda e: e.transpose(out=R["vtp"][:, 0:ntok], in_=R["VV"][tp][0:ntok, :], identity=idf[0:ntok, 0:ntok]),
         reads=["rw_VV%d" % tp, "idf"], writes=["cl_ps"])
    P.op("act", lambda e: e.copy(out=R["vT"][tp][:, 0:ntok], in_=R["vtp"][:, 0:ntok]), reads=["cl_ps"], writes=["rw_vT%d" % tp])


def _rw_scan(P, c, R, n, ntok, rows_scr):
    P.nosame = {"dve"}
    _rw_scan_body(P, c, R, n, ntok, rows_scr)
    P.nosame = set()


def _rw_scan_body(P, c, R, n, ntok, rows_scr):
    tp = n % 2
    t0 = n * ntok
    S, stmp, sk = R["S"], R["stmp"], R["sk"]
    vT, yT = R["vT"][tp], R["yT"][tp]
    vn, yn = "rw_vT%d" % tp, "rw_yT%d" % tp
    for blk in range(0, ntok, 16):
        nb = min(16, ntok - blk)
        rb = R["k"] % 2
        R["k"] += 1
        rowbuf = R["rowbuf"][rb]
        P.dma("sp", lambda e, rowbuf=rowbuf, blk=blk, nb=nb: e.dma_start(
            out=rowbuf[0:6, 0:nb * 320].rearrange("q (s c) -> q s c", c=320),
            in_=rows_scr[:, :, t0 + blk:t0 + blk + nb, :].rearrange("b p s c -> (b p) s c")),
            reads=["rows0_%d" % tp, "rows1_%d" % tp], writes=["rw_rowbuf%d" % rb])
        for s in range(nb):
            pb = R["step"] % 2
            R["step"] += 1
            rowp = R["rowp"][pb]
            pn = "sT%d" % pb
            t = blk + s
            P.op("pe", lambda e, rowp=rowp, rowbuf=rowbuf, s=s: e.matmul(
                rowp[:, 0:320], lhsT=R["selb"][0:6, :], rhs=rowbuf[0:6, s * 320:(s + 1) * 320], start=True, stop=True),
                reads=["rw_rowbuf%d" % rb, "rwc_selb"], writes=[pn])
            P.op("dve", lambda e, rowp=rowp: e.scalar_tensor_tensor(
                out=stmp[:, :], in0=S[:, :], scalar=1.0, in1=rowp[:, 64:128], op0=ALU.mult, op1=ALU.mult,
                accum_out=sk[:, 0:1]), reads=["rw_S", pn], writes=["rw_stmp", "rw_sk"])
            P.op("dve", lambda e, rowp=rowp: e.tensor_tensor(out=S[:, :], in0=S[:, :], in1=rowp[:, 0:64], op=ALU.mult),
                 reads=["rw_S", pn], writes=["rw_S"])
            P.op("dve", lambda e, rowp=rowp: e.scalar_tensor_tensor(
                out=S[:, :], in0=rowp[:, 128:192], scalar=sk[:, 0:1], in1=S[:, :], op0=ALU.mult, op1=ALU.add),
                reads=["rw_S", "rw_sk", pn], writes=["rw_S"])
            P.op("dve", lambda e, rowp=rowp, t=t: e.scalar_tensor_tensor(
                out=S[:, :], in0=rowp[:, 192:256], scalar=vT[:, t:t + 1], in1=S[:, :], op0=ALU.mult, op1=ALU.add),
                reads=["rw_S", vn, pn], writes=["rw_S"])
            P.op("dve", lambda e, rowp=rowp, t=t: e.scalar_tensor_tensor(
                out=stmp[:, :], in0=S[:, :], scalar=1.0, in1=rowp[:, 256:320], op0=ALU.mult, op1=ALU.mult,
                accum_out=yT[:, t:t + 1]), reads=["rw_S", pn], writes=["rw_stmp", yn])


def _rw_post(P, c, R, n, ntok, out_dst):
    tp = n % 2
    idf = c["idf"]
    ytp = R["ytp"]
    cen, ob, junk, mean, var = R["cen"], R["ob"], R["junk"], R["mean"], R["var"]
    P.op("pe", lambda e: e.transpose(out=ytp[0:ntok, 0:128], in_=R["yT"][tp][:, 0:ntok], identity=idf[:, :]),
         reads=["rw_yT%d" % tp, "idf"], writes=["tot_ps"])
    for b in range(2):
        GB = R["GB"][b][tp]
        gn = "rw_GB%d_%d" % (b, tp)
        ysl = ytp[0:ntok, b * 64:(b + 1) * 64]
        P.op("dve", lambda e, ysl=ysl: e.tensor_reduce(out=mean[0:ntok, :], in_=ysl, axis=AX.X, op=ALU.add),
             reads=["tot_ps"], writes=["rw_mean"])
        P.op("dve", lambda e: e.tensor_single_scalar(out=mean[0:ntok, :], in_=mean[0:ntok, :], scalar=1.0 / 64, op=ALU.mult),
             reads=["rw_mean"], writes=["rw_mean"])
        P.op("dve", lambda e, ysl=ysl: e.tensor_scalar(out=cen[0:ntok, :], in0=ysl, scalar1=mean[0:ntok, 0:1], scalar2=0.0,
                                                       op0=ALU.subtract, op1=ALU.add),
             reads=["tot_ps", "rw_mean"], writes=["rw_cen"])
        P.op("dve", lambda e: e.scalar_tensor_tensor(out=junk[0:ntok, :], in0=cen[0:ntok, :], scalar=1.0, in1=cen[0:ntok, :],
                                                     op0=ALU.mult, op1=ALU.mult, accum_out=var[0:ntok, 0:1]),
             reads=["rw_cen"], writes=["rw_junk", "rw_var"])
        P.fence("dve", ["rw_var"])
        P.op("act", lambda e: e.activation(out=var[0:ntok, :], in_=var[0:ntok, :], func=AF.Sqrt, bias=64e-5, scale=1.0 / 64),
             reads=["rw_var"], writes=["rw_var"])
        P.op("dve", lambda e: e.reciprocal(out=var[0:ntok, :], in_=var[0:ntok, :]), reads=["rw_var"], writes=["rw_var"])
        P.op("dve", lambda e: e.scalar_tensor_tensor(out=cen[0:ntok, :], in0=cen[0:ntok, :], scalar=var[0:ntok, 0:1],
                                                     in1=R["lnw"][0:ntok, :], op0=ALU.mult, op1=ALU.mult),
             reads=["rw_cen", "rw_var", "rwc_par"], writes=["rw_cen"])
        P.op("dve", lambda e: e.tensor_tensor(out=cen[0:ntok, :], in0=cen[0:ntok, :], in1=R["lnb"][0:ntok, :], op=ALU.add),
             reads=["rw_cen", "rwc_par"], writes=["rw_cen"])
        P.op("dve", lambda e, GB=GB: e.tensor_tensor(out=cen[0:ntok, :], in0=cen[0:ntok, :], in1=GB[0:ntok, 64:128], op=ALU.add),
             reads=["rw_cen", gn], writes=["rw_cen"])
        P.op("dve", lambda e, GB=GB: e.tensor_tensor(out=ob[0:ntok, :], in0=cen[0:ntok, :], in1=GB[0:ntok, 0:64], op=ALU.mult),
             reads=["rw_cen", gn], writes=["rw_ob"])
        P.dma("sp", lambda e, b=b: e.dma_start(out=out_dst(b, n), in_=ob[0:ntok, :]), reads=["rw_ob"], writes=["rw_out"])


def _rw_pair(P, c, R, ntiles, ntok, cur_src, prev_src, rows_scr, S0_src, out_dst, ST_dst):
    S = R["S"]
    if S0_src is None:
        P.op("dve", lambda e: e.memset(S[:, :], 0.0), writes=["rw_S"])
    else:
        P.dma("sp", lambda e: e.dma_start(out=S[:, :], in_=S0_src), writes=["rw_S"])
    _rw_prep(P, c, R, 0, ntok, cur_src, prev_src, rows_scr)
    for n in range(ntiles):
        if n + 1 < ntiles:
            _rw_prep(P, c, R, n + 1, ntok, cur_src, prev_src, rows_scr)
        _rw_scan(P, c, R, n, ntok, rows_scr)
        _rw_post(P, c, R, n, ntok, out_dst)
    P.dma("sp", lambda e: e.dma_start(out=ST_dst, in_=S[:, :]), reads=["rw_S"], writes=["rw_STout"])


def build_phase2(n_prompt=2, n_sample=NSEQ_S, ngroups=32, rw_tiles=128, rw_pairs=8, do_fox=True):
    nc = bass.Bass("TRN2", target_bir_lowering=False)
    qTp = _din(nc, "qTp", [2, 64, TP])
    kTp = _din(nc, "kTp", [2, 64, TP])
    vp = _din(nc, "vp", [2, 128, 128, 64])
    lfp = _din(nc, "lfp", [2, 128, 128])
    qTs = _din(nc, "qTs", [NSEQ_S, 64, 16])
    kTs = _din(nc, "kTs", [NSEQ_S, 64, TS])
    vs = _din(nc, "vs", [NSEQ_S, 128, 17, 64])
    lfs = _din(nc, "lfs", [NSEQ_S, 128, 17])
    op_ = _dout(nc, "o_p", [2, TP, 64])
    os_ = _dout(nc, "o_s", [NSEQ_S, 16, 64])
    curp = _din(nc, "rw_curp", [2, TP, 352])
    prvp = _din(nc, "rw_prvp", [2, TP, 352])
    curs = _din(nc, "rw_curs", [NSEQ_S, 16, 352])
    prvs = _din(nc, "rw_prvs", [NSEQ_S, 16, 352])
    S0s = _din(nc, "rw_S0s", [8, 128, 64])
    rwp = _dout(nc, "rw_p", [2, TP, 64])
    rws = _dout(nc, "rw_s", [NSEQ_S, 16, 64])
    STp = _dout(nc, "ST_p", [128, 64])
    STs = _dout(nc, "ST_s", [8, 128, 64])
    rows_p = nc.dram_tensor("rows_p", [2, 3, TP, 320], BF16).ap()
    rows_s = nc.dram_tensor("rows_s", [8, 2, 3, 16, 320], BF16).ap()
    P = Prog(nc)
    c = _fox_consts(P, nc)
    B = _fox_bufs(P)
    if do_fox:
        for b in range(n_prompt):
            groups = [(512 * g, 512, 4 * g + 4, 4 * g, 4 * g + 2) for g in range(ngroups)]

            def out_fn(P, osb, q0, nq, nqi, wmax, b=b):
                P.dma("sp", lambda e: e.dma_start(
                    out=op_[b, q0:q0 + nq, :].rearrange("(a p) d -> p a d", p=128), in_=osb[:, 0:nqi, :]),
                    reads=["osb"], writes=["o_out"])
            _fox_seq(P, c, B, "p%d" % b, 128, qTp[b], TP, kTp[b], vp[b], lfp[b], groups, out_fn)
        for s in range(n_sample):
            groups = [(0, 16, 17, 16, 16)]

            def out_fn(P, osb, q0, nq, nqi, wmax, s=s):
                P.dma("sp", lambda e: e.dma_start(out=os_[s, :, :], in_=osb[0:16, 0, :]), reads=["osb"], writes=["o_out"])
            _fox_seq(P, c, B, "s%d" % s, 17, qTs[s], 16, kTs[s], vs[s], lfs[s], groups, out_fn)
    R = _rw_setup(P, nc, B)
    if rw_tiles > 0:
        _rw_pair(P, c, R, rw_tiles, 128,
                 lambda b, n: curp[b, n * 128:(n + 1) * 128, :], lambda b, n: prvp[b, n * 128:(n + 1) * 128, :],
                 rows_p, None, lambda b, n: rwp[b, n * 128:(n + 1) * 128, :], STp)
    for pr in range(rw_pairs):
        _rw_pair(P, c, R, 1, 16,
                 lambda b, n, pr=pr: curs[2 * pr + b, :, :], lambda b, n, pr=pr: prvs[2 * pr + b, :, :],
                 rows_s[pr], S0s[pr], lambda b, n, pr=pr: rws[2 * pr + b, :, :], STs[pr])
    P.emit()
    return nc


def rw_inputs(pp, psm, state_rwkv, state_shift, prm):
    ppb = pp.reshape(2, TP, IN_COLS)[:, :, FOX_COLS:]
    pss = psm.reshape(NSEQ_S, 16, IN_COLS)[:, :, FOX_COLS:]
    prev_p = np.zeros_like(ppb)
    prev_p[:, 1:] = ppb[:, :-1]
    prev_s = np.empty_like(pss)
    prev_s[:, 1:] = pss[:, :-1]
    prev_s[:, 0] = state_shift[:, 0, :]
    maps = []
    sel = np.zeros((6, 128), np.float32)
    sel[0:3, :64] = 1.0
    sel[3:6, 64:] = 1.0
    for h in range(NCORE):
        cols = np.concatenate([np.arange(h * 64, (h + 1) * 64), 512 + np.arange(h * 64, (h + 1) * 64),
                               1024 + np.arange(h * 64, (h + 1) * 64), np.arange(1536, 1696)])
        hs = slice(h * 64, (h + 1) * 64)
        par = np.concatenate([prm["rwkv_mu"][cols], prm["rwkv_w0"][hs], prm["rwkv_a0"][hs], prm["rwkv_k_k"][hs],
                              prm["rwkv_k_a"][hs], prm["rwkv_r_k"][h], prm["rwkv_lnx_w"][hs], prm["rwkv_lnx_b"][hs]])
        m = {
            "rw_curp": np.ascontiguousarray(ppb[:, :, cols]), "rw_prvp": np.ascontiguousarray(prev_p[:, :, cols]),
            "rw_curs": np.ascontiguousarray(pss[:, :, cols]), "rw_prvs": np.ascontiguousarray(prev_s[:, :, cols]),
            "rw_S0s": np.ascontiguousarray(state_rwkv[:, h].reshape(8, 128, 64)),
            "rw_par": np.ascontiguousarray(np.broadcast_to(par[None, :], (128, NPAR))).astype(np.float32),
            "rw_w2": np.ascontiguousarray(prm["rwkv_w2"][:, hs]), "rw_a2": np.ascontiguousarray(prm["rwkv_a2"][:, hs]),
            "rw_g2": np.ascontiguousarray(prm["rwkv_g2"][:, hs]), "rw_sel": sel,
        }
        maps.append(m)
    return maps


STAGE = 99


def build_phase3(NT, nslots=128):
    nc = bass.Bass("TRN2", target_bir_lowering=False)
    x = _din(nc, "x", [NT * 128, D])
    mixT = _din(nc, "mixT", [NT, 128, 8, 128])
    wout = _din(nc, "w_out", [D, D])
    wq = _din(nc, "w_q", [D, D])
    skT_d = _din(nc, "skT", [64, 16 * 128])
    g1 = _din(nc, "gffn", [128, D])
    g2 = _din(nc, "gfin", [128, D])
    identd = _din(nc, "ident", [128, 128])
    iota_d = _din(nc, "iota", [128, 256])
    pu = _din(nc, "peer_u", [16384, D])
    pv = _din(nc, "peer_v", [16384, D])
    y = _dout(nc, "y", [NT * 128, D])
    pu16 = nc.dram_tensor("pu16", [16384, D], BF16).ap()
    pv16 = nc.dram_tensor("pv16", [16384, D], BF16).ap()
    P = Prog(nc)
    cst = [P.sb("cst%d" % i, [128, 4096], F32) for i in range(2)]
    cbf = [P.sb("cbf%d" % i, [128, 4096], BF16) for i in range(2)]
    ck = 0
    if nslots > 0:
        for (src, dst, nm) in ((pu, pu16, "pu16"), (pv, pv16, "pv16")):
            sv_ = src.rearrange("(p r) d -> p (r d)", p=128)
            dv_ = dst.rearrange("(p r) d -> p (r d)", p=128)
            for pc in range(32):
                i2 = ck % 2
                P.dma("sp", lambda e, i2=i2, sv_=sv_, pc=pc: e.dma_start(out=cst[i2][:, :], in_=sv_[:, pc * 4096:(pc + 1) * 4096]),
                      writes=["cst%d" % i2])
                eng = ("act", "pool", "dve")[ck % 3]
                if eng == "act":
                    P.op("act", lambda e, i2=i2: e.copy(out=cbf[i2][:, :], in_=cst[i2][:, :]), reads=["cst%d" % i2], writes=["cbf%d" % i2])
                else:
                    P.op(eng, lambda e, i2=i2: e.tensor_copy(out=cbf[i2][:, :], in_=cst[i2][:, :]), reads=["cst%d" % i2],
                         writes=["cbf%d" % i2])
                P.dma("sp", lambda e, i2=i2, dv_=dv_, pc=pc: e.dma_start(out=dv_[:, pc * 4096:(pc + 1) * 4096], in_=cbf[i2][:, :]),
                      reads=["cbf%d" % i2], writes=[nm])
                ck += 1
    wst = P.sb("wst", [128, D], F32)
    wo_bf = P.sb("wo_bf", [128, 8, D], BF16)
    wq_bf = P.sb("wq_bf", [128, 8, D], BF16)
    sk_bf = P.sb("sk_bf", [64, 16, 128], BF16)
    g1s = P.sb("g1s", [128, D], F32)
    g2s = P.sb("g2s", [128, D], F32)
    id_f = P.sb("id_f", [128, 128], F32)
    id_b = P.sb("id_b", [128, 128], BF16)
    P.dma("sp", lambda e: e.dma_start(out=g1s[:, :], in_=g1), writes=["g1"])
    P.dma("sp", lambda e: e.dma_start(out=g2s[:, :], in_=g2), writes=["g2"])
    P.dma("sp", lambda e: e.dma_start(out=id_f[:, :], in_=identd), writes=["idf"])
    P.op("dve", lambda e: e.tensor_copy(out=id_b[:, :], in_=id_f[:, :]), reads=["idf"], writes=["idb"])
    for (src, dst, nm) in ((wout, wo_bf, "wo"), (wq, wq_bf, "wq")):
        for dc in range(8):
            P.dma("sp", lambda e, src=src, dc=dc: e.dma_start(out=wst[:, :], in_=src[dc * 128:(dc + 1) * 128, :]), writes=["wst"])
            P.op("act", lambda e, dst=dst, dc=dc: e.copy(out=dst[:, dc, :], in_=wst[:, :]), reads=["wst"], writes=[nm])
    for hf in range(2):
        P.dma("sp", lambda e, hf=hf: e.dma_start(out=wst[0:64, :], in_=skT_d[:, hf * 1024:(hf + 1) * 1024]), writes=["wst"])
        P.op("act", lambda e, hf=hf: e.copy(out=sk_bf[:, hf * 8:(hf + 1) * 8, :],
                                            in_=wst[0:64, :].rearrange("p (h n) -> p h n", h=8)), reads=["wst"], writes=["sk"])

    xt = P.sb("xt", [128, D], F32)
    mt = P.sb("mt", [128, 8, 128], F32)
    mtb = P.sb("mtb", [128, 8, 128], BF16)
    x2 = P.sb("x2", [128, D], F32)
    xn = P.sb("xn", [128, D], F32)
    xnb = P.sb("xnb", [128, D], BF16)
    xnT = P.sb("xnT", [128, D], BF16)
    qT = P.sb("qT", [64, 16 * 128], BF16)
    junkb = P.sb("junkb", [128, D], BF16)
    junkD = P.sb("junkD", [128, D], F32)
    ss = P.sb("ss", [128, 1], F32)
    s_sb = P.sb("s_sb", [128, 16, 128], F32)
    s2 = P.sb("s2", [128, 128], F32)
    sv = P.sb("sv", [128, 16, 16], F32)
    si = P.sb("si", [128, 16, 16], U32)
    sif = P.sb("sif", [128, 16, 16], F32)
    cand = P.sb("cand", [128, 8, 256], F32)
    cidx = P.sb("cidx", [128, 8, 256], F32)
    c2 = P.sb("c2", [128, 256], F32)
    j256 = P.sb("j256", [128, 256], F32)
    tv = P.sb("tv", [128, 8, 16], F32)
    pi = P.sb("pi", [128, 8, 16], U32)
    pif = P.sb("pif", [128, 8, 16], F32)
    iota = P.sb("iota", [128, 256], F32)
    P.dma("sp", lambda e: e.dma_start(out=iota[:, :], in_=iota_d), writes=["iota"])
    negm = P.sb("negm", [128, 8], F32)
    gt = P.sb("gt", [128, 8, 16], F32)
    Z = P.sb("Z", [128, 8], F32)
    ef = P.sb("ef", [128, 128], F32)
    ei = P.sb("ei", [128, 128], I32)
    apre = P.sb("apre", [128, 128], F32)
    coef = P.sb("coef", [128, 128], F32)
    acc = P.sb("acc", [128, D], F32)
    yo = P.sb("yo", [128, D], F32)
    NG = 4
    gb = [P.sb("gbuf%d" % i, [128, D], BF16) for i in range(NG)]
    psA = P.ps("psA", [128, D], F32)
    psT = P.ps("psT", [128, D], BF16)
    psS = P.ps("psS", [128, 16, 128], F32)
    gk = 0

    def rms(src, srcn, dst, dstn, gs, gn):
        P.op("act", lambda e: e.activation(out=junkb[:, :], in_=src[:, :], func=AF.Square, accum_out=ss[:, 0:1]),
             reads=[srcn], writes=["junkb", "ss"])
        P.op("act", lambda e: e.activation(out=ss[:, :], in_=ss[:, :], func=AF.Sqrt, bias=1e-6, scale=1.0 / D),
             reads=["ss"], writes=["ss"])
        P.op("dve", lambda e: e.reciprocal(out=ss[:, :], in_=ss[:, :]), reads=["ss"], writes=["ss"])
        P.op("dve", lambda e: e.scalar_tensor_tensor(out=dst[:, :], in0=src[:, :], scalar=ss[:, 0:1], in1=gs[:, :],
                                                     op0=ALU.mult, op1=ALU.mult), reads=[srcn, "ss", gn], writes=[dstn])

    for i in range(NT):
        P.dma("sp", lambda e, i=i: e.dma_start(out=xt[:, :], in_=x[i * 128:(i + 1) * 128, :]), writes=["xt"])
        P.dma("sp", lambda e, i=i: e.dma_start(out=mt[:, :, :], in_=mixT[i]), writes=["mt"])
        P.op("act", lambda e: e.copy(out=mtb[:, :, :], in_=mt[:, :, :]), reads=["mt"], writes=["mtb"])
        for g in range(2):
            for kc in range(8):
                P.op("pe", lambda e, g=g, kc=kc: e.matmul(psA[:, g * 512:(g + 1) * 512], lhsT=mtb[:, kc, :],
                                                          rhs=wo_bf[:, kc, g * 512:(g + 1) * 512], start=(kc == 0), stop=(kc == 7)),
                     reads=["mtb", "wo"], writes=["psA%d" % g])
        for g in range(2):
            P.op("dve", lambda e, g=g: e.tensor_tensor(out=x2[:, g * 512:(g + 1) * 512], in0=psA[:, g * 512:(g + 1) * 512],
                                                       in1=xt[:, g * 512:(g + 1) * 512], op=ALU.add),
                 reads=["psA%d" % g, "xt"], writes=["x2"])
        rms(x2, "x2", xn, "xn", g1s, "g1")
        P.op("act", lambda e: e.copy(out=xnb[:, :], in_=xn[:, :]), reads=["xn"], writes=["xnb"])
        for dc in range(8 if STAGE >= 2 else 0):
            P.op("pe", lambda e, dc=dc: e.transpose(out=psT[:, dc * 128:(dc + 1) * 128], in_=xnb[:, dc * 128:(dc + 1) * 128],
                                                    identity=id_b[:, :]), reads=["xnb", "idb"], writes=["psT"])
        P.op("act", lambda e: e.copy(out=xnT[:, :], in_=psT[:, :]), reads=["psT"], writes=["xnT"])
        for hc in range(16 if STAGE >= 3 else 0):
            bank = hc % 2
            for dc in range(8):
                P.op("pe", lambda e, hc=hc, dc=dc, bank=bank: e.matmul(
                    psA[0:64, bank * 512:bank * 512 + 128], lhsT=wq_bf[:, dc, hc * 64:(hc + 1) * 64],
                    rhs=xnT[:, dc * 128:(dc + 1) * 128], start=(dc == 0), stop=(dc == 7)),
                    reads=["xnT", "wq"], writes=["psA%d" % bank])
            if hc % 2 == 0:
                P.op("act", lambda e, hc=hc, bank=bank: e.copy(out=qT[0:64, hc * 128:(hc + 1) * 128],
                                                               in_=psA[0:64, bank * 512:bank * 512 + 128]),
                     reads=["psA%d" % bank], writes=["qT"])
            else:
                P.op("dve", lambda e, hc=hc, bank=bank: e.tensor_copy(out=qT[0:64, hc * 128:(hc + 1) * 128],
                                                                      in_=psA[0:64, bank * 512:bank * 512 + 128]),
                     reads=["psA%d" % bank], writes=["qT"])
        for hc in range(16 if STAGE >= 4 else 0):
            P.op("pe", lambda e, hc=hc: e.matmul(psS[:, hc, :], lhsT=qT[0:64, hc * 128:(hc + 1) * 128],
                                                 rhs=sk_bf[0:64, hc, :], start=True, stop=True),
                 reads=["qT", "sk"], writes=["psS"])
        for bk in range(4):
            if bk % 2 == 0:
                P.op("act", lambda e, bk=bk: e.copy(out=s_sb[:, 4 * bk:4 * bk + 4, :], in_=psS[:, 4 * bk:4 * bk + 4, :]),
                     reads=["psS"], writes=["s_sb"])
            else:
                P.op("dve", lambda e, bk=bk: e.tensor_copy(out=s_sb[:, 4 * bk:4 * bk + 4, :], in_=psS[:, 4 * bk:4 * bk + 4, :]),
                     reads=["psS"], writes=["s_sb"])
        for hc in range(16 if STAGE >= 5 else 0):
            P.op("dve", lambda e, hc=hc: e.max(out=sv[:, hc, 0:8], in_=s_sb[:, hc, :]), reads=["s_sb"], writes=["sv"])
            P.op("dve", lambda e, hc=hc: e.max_index(out=si[:, hc, 0:8], in_max=sv[:, hc, 0:8], in_values=s_sb[:, hc, :]),
                 reads=["s_sb", "sv"], writes=["si"])
            P.op("dve", lambda e, hc=hc: e.match_replace(out=s2[:, :], in_to_replace=sv[:, hc, 0:8], in_values=s_sb[:, hc, :],
                                                         imm_value=-1e30), reads=["s_sb", "sv"], writes=["s2"])
            P.op("dve", lambda e, hc=hc: e.max(out=sv[:, hc, 8:16], in_=s2[:, :]), reads=["s2"], writes=["sv"])
            P.op("dve", lambda e, hc=hc: e.max_index(out=si[:, hc, 8:16], in_max=sv[:, hc, 8:16], in_values=s2[:, :]),
                 reads=["s2", "sv"], writes=["si"])
        P.op("dve", lambda e: e.tensor_copy(out=sif[:, :, :], in_=si[:, :, :]), reads=["si"], writes=["sif"])
        for h in range(8 if STAGE >= 6 else 0):
            P.op("dve", lambda e, h=h: e.tensor_single_scalar(out=sif[:, 2 * h, :], in_=sif[:, 2 * h, :], scalar=128.0, op=ALU.mult),
                 reads=["sif"], writes=["sif"])
            P.op("dve", lambda e, h=h: e.tensor_tensor(
                out=cand[:, h, :].rearrange("p (a b) -> p a b", a=16),
                in0=sv[:, 2 * h, :].unsqueeze(2).to_broadcast([128, 16, 16]),
                in1=sv[:, 2 * h + 1, :].unsqueeze(1).to_broadcast([128, 16, 16]), op=ALU.add), reads=["sv"], writes=["cand"])
            P.op("dve", lambda e, h=h: e.tensor_tensor(
                out=cidx[:, h, :].rearrange("p (a b) -> p a b", a=16),
                in0=sif[:, 2 * h, :].unsqueeze(2).to_broadcast([128, 16, 16]),
                in1=sif[:, 2 * h + 1, :].unsqueeze(1).to_broadcast([128, 16, 16]), op=ALU.add), reads=["sif"], writes=["cidx"])
            P.op("dve", lambda e, h=h: e.max(out=tv[:, h, 0:8], in_=cand[:, h, :]), reads=["cand"], writes=["tv"])
            P.op("dve", lambda e, h=h: e.max_index(out=pi[:, h, 0:8], in_max=tv[:, h, 0:8], in_values=cand[:, h, :]),
                 reads=["cand", "tv"], writes=["pi"])
            P.op("dve", lambda e, h=h: e.match_replace(out=c2[:, :], in_to_replace=tv[:, h, 0:8], in_values=cand[:, h, :],
                                                       imm_value=-1e30), reads=["cand", "tv"], writes=["c2"])
            P.op("dve", lambda e, h=h: e.max(out=tv[:, h, 8:16], in_=c2[:, :]), reads=["c2"], writes=["tv"])
            P.op("dve", lambda e, h=h: e.max_index(out=pi[:, h, 8:16], in_max=tv[:, h, 8:16], in_values=c2[:, :]),
                 reads=["c2", "tv"], writes=["pi"])
            P.op("dve", lambda e, h=h: e.tensor_copy(out=pif[:, h, :], in_=pi[:, h, :]), reads=["pi"], writes=["pif"])
            for k in range(16):
                P.op("dve", lambda e, h=h, k=k: e.scalar_tensor_tensor(
                    out=j256[:, :], in0=iota[:, :], scalar=pif[:, h, k:k + 1], in1=cidx[:, h, :], op0=ALU.is_equal,
                    op1=ALU.mult, accum_out=ef[:, h * 16 + k:h * 16 + k + 1]), reads=["iota", "cidx", "pif"], writes=["j256", "ef"])
        P.op("dve", lambda e: e.tensor_copy(out=ei[:, :], in_=ef[:, :]), reads=["ef"], writes=["ei"])
        P.op("dve", lambda e: e.tensor_single_scalar(out=negm[:, :], in_=tv[:, :, 0], scalar=-1.0, op=ALU.mult),
             reads=["tv"], writes=["negm"])
        for h in range(8 if STAGE >= 7 else 0):
            P.op("act", lambda e, h=h: e.activation(out=gt[:, h, :], in_=tv[:, h, :], func=AF.Exp, bias=negm[:, h:h + 1],
                                                    scale=1.0, accum_out=Z[:, h:h + 1]), reads=["tv", "negm"], writes=["gt", "Z"])
        P.fence("act", ["Z", "gt"])
        P.op("dve", lambda e: e.reciprocal(out=Z[:, :], in_=Z[:, :]), reads=["Z"], writes=["Z"])
        for h in range(8):
            P.op("dve", lambda e, h=h: e.tensor_scalar_mul(out=gt[:, h, :], in0=gt[:, h, :], scalar1=Z[:, h:h + 1]),
                 reads=["gt", "Z"], writes=["gt"])
        for sl in range(nslots):
            r = gk % NG
            gk += 1
            P.dma("pool", lambda e, r=r, sl=sl: e.indirect_dma_start(
                out=gb[r][:, :], out_offset=None, in_=pu16[:, :],
                in_offset=bass.IndirectOffsetOnAxis(ap=ei[:, sl:sl + 1], axis=0)), reads=["ei", "pu16"], writes=["gbuf%d" % r])
            P.op("dve", lambda e, r=r, sl=sl: e.scalar_tensor_tensor(
                out=junkD[:, :], in0=gb[r][:, :], scalar=1.0, in1=xn[:, :], op0=ALU.mult, op1=ALU.mult,
                accum_out=apre[:, sl:sl + 1]), reads=["gbuf%d" % r, "xn"], writes=["junkD", "apre"])
        if nslots > 0:
            P.fence("dve", ["apre"])
            P.op("act", lambda e: e.activation(out=coef[:, 0:nslots], in_=apre[:, 0:nslots], func=AF.Gelu),
                 reads=["apre"], writes=["coef"])
            P.op("dve", lambda e: e.tensor_tensor(out=coef[:, 0:nslots], in0=coef[:, 0:nslots],
                                                  in1=gt[:, :, :].rearrange("p h k -> p (h k)")[:, 0:nslots], op=ALU.mult),
                 reads=["coef", "gt"], writes=["coef"])
        for sl in range(nslots):
            r = gk % NG
            gk += 1
            P.dma("pool", lambda e, r=r, sl=sl: e.indirect_dma_start(
                out=gb[r][:, :], out_offset=None, in_=pv16[:, :],
                in_offset=bass.IndirectOffsetOnAxis(ap=ei[:, sl:sl + 1], axis=0)), reads=["ei", "pv16"], writes=["gbuf%d" % r])
            if sl == 0:
                P.op("dve", lambda e, r=r: e.tensor_scalar_mul(out=acc[:, :], in0=gb[r][:, :], scalar1=coef[:, 0:1]),
                     reads=["gbuf%d" % r, "coef"], writes=["acc"])
            else:
                P.op("dve", lambda e, r=r, sl=sl: e.scalar_tensor_tensor(
                    out=acc[:, :], in0=gb[r][:, :], scalar=coef[:, sl:sl + 1], in1=acc[:, :], op0=ALU.mult, op1=ALU.add),
                    reads=["gbuf%d" % r, "coef", "acc"], writes=["acc"])
        if nslots > 0:
            P.op("dve", lambda e: e.tensor_tensor(out=x2[:, :], in0=x2[:, :], in1=acc[:, :], op=ALU.add),
                 reads=["x2", "acc"], writes=["x2"])
        rms(x2, "x2", yo, "yo", g2s, "g2")
        if nslots == 0:
            P.op("dve", lambda e: e.tensor_copy(out=yo[:, 0:128], in_=ef[:, :]), reads=["ef", "yo"], writes=["yo"])
            P.op("dve", lambda e: e.tensor_copy(out=yo[:, 128:256], in_=gt[:, :, :].rearrange("p h k -> p (h k)")),
                 reads=["gt", "yo"], writes=["yo"])
        P.dma("sp", lambda e, i=i: e.dma_start(out=y[i * 128:(i + 1) * 128, :], in_=yo[:, :]), reads=["yo"], writes=["yout"])
    P.emit()
    return nc


def run_phase3(x_prompt, x_sample, mix_p, mix_s, w_out, norm_ffn_g, peer_w_q, peer_sub_keys, peer_u, peer_v, norm_final_g,
               NT=33, nslots=128):
    xp = np.ascontiguousarray(x_prompt).reshape(-1, D)
    xs = np.ascontiguousarray(x_sample).reshape(-1, D)
    nc = build_phase3(NT, nslots)
    g1 = np.ascontiguousarray(np.broadcast_to(norm_ffn_g.reshape(1, D), (128, D))).astype(np.float32)
    g2 = np.ascontiguousarray(np.broadcast_to(norm_final_g.reshape(1, D), (128, D))).astype(np.float32)
    skT = np.ascontiguousarray(np.transpose(peer_sub_keys, (3, 0, 1, 2)).reshape(64, 16 * 128))
    iota_h = np.ascontiguousarray(np.broadcast_to(np.arange(256, dtype=np.float32)[None, :], (128, 256)))
    in_maps = []
    for c in range(NCORE):
        xc = np.zeros((33 * 128, D), np.float32)
        mc = np.zeros((33 * 128, D), np.float32)
        xc[:4096] = xp[c * 4096:(c + 1) * 4096]
        xc[4096:4128] = xs[c * 32:(c + 1) * 32]
        mc[:4096] = mix_p[c * 4096:(c + 1) * 4096]
        mc[4096:4128] = mix_s[c * 32:(c + 1) * 32]
        mT = np.ascontiguousarray(np.transpose(mc.reshape(33, 128, 8, 128), (0, 3, 2, 1)))
        in_maps.append({"x": xc[:NT * 128], "mixT": mT[:NT], "w_out": np.ascontiguousarray(w_out), "w_q": np.ascontiguousarray(peer_w_q),
                        "skT": skT, "iota": iota_h, "gffn": g1, "gfin": g2, "ident": _ident(), "peer_u": np.ascontiguousarray(peer_u),
                        "peer_v": np.ascontiguousarray(peer_v)})
    res = run_bass_kernel_spmd(nc, in_maps, core_ids=list(range(NCORE)))
    yp = np.concatenate([res.results[c]["y"][:4096] for c in range(NCORE)], axis=0) if NT == 33 else None
    ysm = np.concatenate([res.results[c]["y"][4096:4128] for c in range(NCORE)], axis=0) if NT == 33 else None
    return yp, ysm, res


def kernel(x_prompt, x_sample, cache_fox_k, cache_fox_v, cache_fox_logf, state_rwkv, state_shift,
           norm_mix_g, w_in, fox_b_f, rwkv_mu, rwkv_w0, rwkv_w2, rwkv_a0, rwkv_a2, rwkv_g2,
           rwkv_k_k, rwkv_k_a, rwkv_r_k, rwkv_lnx_w, rwkv_lnx_b, w_out, norm_ffn_g,
           peer_w_q, peer_sub_keys, peer_u, peer_v, norm_final_g):
    f = lambda a: np.asarray(a, dtype=np.float32)
    x_prompt, x_sample = f(x_prompt), f(x_sample)
    pp, psm = run_phase1(x_prompt, x_sample, f(norm_mix_g)[0], f(w_in)[0], f(fox_b_f)[0])
    prm = {"rwkv_mu": f(rwkv_mu)[0], "rwkv_w0": f(rwkv_w0)[0], "rwkv_w2": f(rwkv_w2)[0], "rwkv_a0": f(rwkv_a0)[0],
           "rwkv_a2": f(rwkv_a2)[0], "rwkv_g2": f(rwkv_g2)[0], "rwkv_k_k": f(rwkv_k_k)[0], "rwkv_k_a": f(rwkv_k_a)[0],
           "rwkv_r_k": f(rwkv_r_k)[0], "rwkv_lnx_w": f(rwkv_lnx_w)[0], "rwkv_lnx_b": f(rwkv_lnx_b)[0]}
    maps = fox_inputs(pp, psm, f(cache_fox_k)[0], f(cache_fox_v)[0], f(cache_fox_logf)[0])
    rmaps = rw_inputs(pp, psm, f(state_rwkv)[0], f(state_shift)[0], prm)
    for m, r in zip(maps, rmaps):
        m.update(r)
    nc2 = build_phase2()
    res2 = run_bass_kernel_spmd(nc2, maps, core_ids=list(range(NCORE)))
    del maps, rmaps
    R2 = res2.results
    mix_p = np.empty((2, TP, 1024), np.float32)
    mix_s = np.empty((NSEQ_S, 16, 1024), np.float32)
    S_p = np.empty((1, 2, 8, 64, 64), np.float32)
    S_s = np.empty((1, NSEQ_S, 8, 64, 64), np.float32)
    for h in range(NCORE):
        mix_p[:, :, h * 64:(h + 1) * 64] = R2[h]["o_p"]
        mix_p[:, :, 512 + h * 64:512 + (h + 1) * 64] = R2[h]["rw_p"]
        mix_s[:, :, h * 64:(h + 1) * 64] = R2[h]["o_s"]
        mix_s[:, :, 512 + h * 64:512 + (h + 1) * 64] = R2[h]["rw_s"]
        S_p[0, :, h] = R2[h]["ST_p"].reshape(2, 64, 64)
        S_s[0, :, h] = R2[h]["ST_s"].reshape(NSEQ_S, 64, 64)
    yp, ysm, _ = run_phase3(x_prompt, x_sample, mix_p.reshape(-1, 1024), mix_s.reshape(-1, 1024), f(w_out)[0],
                            f(norm_ffn_g)[0], f(peer_w_q)[0], f(peer_sub_keys)[0], f(peer_u)[0], f(peer_v)[0],
                            f(norm_final_g))
    ppb = pp.reshape(2, TP, IN_COLS)
    pss = psm.reshape(NSEQ_S, 16, IN_COLS)
    c = np.ascontiguousarray
    return (
        c(yp.reshape(2, TP, 1024)), c(ysm.reshape(NSEQ_S, 16, 1024)),
        c(ppb[:, :, 512:1024].reshape(1, 2, TP, 8, 64)), c(ppb[:, :, 1024:1536].reshape(1, 2, TP, 8, 64)),
        c(ppb[:, :, 1536:1544].reshape(1, 2, TP, 8)), S_p, c(ppb[:, -1:, FOX_COLS:].reshape(1, 2, 1, RW_COLS)),
        c(pss[:, :, 512:1024].reshape(1, NSEQ_S, 16, 8, 64)), c(pss[:, :, 1024:1536].reshape(1, NSEQ_S, 16, 8, 64)),
        c(pss[:, :, 1536:1544].reshape(1, NSEQ_S, 16, 8)), S_s, c(pss[:, -1:, FOX_COLS:].reshape(1, NSEQ_S, 1, RW_COLS)),
    )
```

```python
from contextlib import ExitStack
import math
import numpy as np
import concourse.bass as bass
import concourse.mybir as mybir
from concourse.bass_utils import run_bass_kernel_spmd

F32 = mybir.dt.float32
BF16 = mybir.dt.bfloat16
I32 = mybir.dt.int32
U32 = mybir.dt.uint32
ALU = mybir.AluOpType
AF = mybir.ActivationFunctionType
AX = mybir.AxisListType

D = 1024
IN_COLS = 3240
FOX_COLS = 1544
RW_COLS = 1696
NCORE = 8


class Prog:
    ENGS = ("pe", "act", "dve", "pool", "sp")

    def __init__(self, nc):
        self.nc = nc
        self.st = ExitStack()
        self.ops = {e: [] for e in self.ENGS}
        self.cnt = {}
        self.waited = {e: {} for e in self.ENGS}
        self.lastw = {}
        self.readers = {}
        self.ndma = {e: 0 for e in self.ENGS}
        self.NS = 8
        self.nosame = set()
        self.fence_t = {}
        self.uid = 0

    def sb(self, name, shape, dt):
        return self.st.enter_context(self.nc.sbuf_tensor("sb_" + name, list(shape), dt))

    def ps(self, name, shape, dt):
        return self.st.enter_context(self.nc.psum_tensor("ps_" + name, list(shape), dt))

    def _deps(self, eng, reads, writes):
        deps = []
        for b in reads:
            if b in self.lastw:
                deps.extend(self.lastw[b].items())
        for b in writes:
            if b in self.lastw:
                deps.extend(self.lastw[b].items())
            deps.extend(self.readers.get(b, ()))
        best = {}
        for (k, v) in deps:
            if eng == "pe" and k == "pe":
                continue
            if k == eng and eng in self.nosame:
                continue
            if self.waited[eng].get(k, 0) >= v:
                continue
            best[k] = max(best.get(k, 0), v)
        for k, v in best.items():
            self.waited[eng][k] = v
        return list(best.items())

    def _record(self, tok, reads, writes):
        for b in reads:
            self.readers.setdefault(b, []).append(tok)
        for b in writes:
            self.lastw.setdefault(b, {})[tok[0]] = tok[1]
            self.readers[b] = []

    def op(self, eng, fn, reads=(), writes=()):
        waits = self._deps(eng, reads, writes)
        self.cnt[eng] = self.cnt.get(eng, 0) + 1
        self.ops[eng].append((waits, fn, eng, 1))
        self._record((eng, self.cnt[eng]), reads, writes)

    def fence(self, eng, names):
        if eng not in self.fence_t:
            self.fence_t[eng] = self.sb("fence_" + eng, [128, 2], F32)
        t = self.fence_t[eng]
        if eng == "act":
            self.op("act", lambda e: e.copy(out=t[:, 1:2], in_=t[:, 0:1]), reads=(), writes=list(names))
        else:
            self.op("dve", lambda e: e.tensor_copy(out=t[:, 1:2], in_=t[:, 0:1]), reads=(), writes=list(names))

    def dma(self, q, fn, reads=(), writes=()):
        waits = self._deps(q, reads, writes)
        k = "d_%s_%d" % (q, self.ndma[q] % (16 if q == "pool" else self.NS))
        self.ndma[q] += 1
        prev = self.cnt.get(k, 0)
        if prev and self.waited[q].get(k, 0) < prev:
            self.waited[q][k] = prev
            waits = [w for w in waits if w[0] != k] + [(k, prev)]
        self.cnt[k] = self.cnt.get(k, 0) + 16
        self.ops[q].append((waits, fn, k, 16))
        self._record((k, self.cnt[k]), reads, writes)

    def emit(self):
        nc = self.nc
        keys = sorted(self.cnt.keys())
        sems = {k: self.st.enter_context(nc.semaphore("s_" + k)) for k in keys}
        final = [(k, self.cnt[k]) for k in keys]
        ops = self.ops

        def run(name, e):
            for (waits, fn, k, inc) in ops[name]:
                for (wk, wv) in waits:
                    e.wait_ge(sems[wk], wv)
                fn(e).then_inc(sems[k], inc)
            if name == "sp":
                for (k, v) in final:
                    e.wait_ge(sems[k], v)

        with nc.Block() as block:
            @block.tensor
            def _(e):
                run("pe", e)

            @block.scalar
            def _(e):
                run("act", e)

            @block.vector
            def _(e):
                run("dve", e)

            @block.gpsimd
            def _(e):
                run("pool", e)

            @block.sync
            def _(e):
                run("sp", e)
        self.st.close()


def _din(nc, name, shape, dt=F32):
    return nc.dram_tensor(name, list(shape), dt, kind="ExternalInput").ap()


def _dout(nc, name, shape, dt=F32):
    return nc.dram_tensor(name, list(shape), dt, kind="ExternalOutput").ap()


def _load_cast(P, name, dram_ap, shape, stage, stage_name, q="sp", eng="act"):
    t = P.sb(name, shape, BF16)
    p, n = shape
    P.dma(q, lambda e: e.dma_start(out=stage[0:p, 0:n], in_=dram_ap), writes=[stage_name])
    if eng == "act":
        P.op("act", lambda e: e.copy(out=t[:, :], in_=stage[0:p, 0:n]), reads=[stage_name], writes=[name])
    else:
        P.op("dve", lambda e: e.tensor_copy(out=t[:, :], in_=stage[0:p, 0:n]), reads=[stage_name], writes=[name])
    return t


def build_phase1(NT):
    nc = bass.Bass("TRN2", target_bir_lowering=False)
    x = _din(nc, "x", [NT * 128, D])
    gbc = _din(nc, "gbc", [128, D])
    w = _din(nc, "w_in", [D, IN_COLS])
    bfb = _din(nc, "bfb", [128, 8])
    identd = _din(nc, "ident", [128, 128])
    proj = _dout(nc, "proj", [NT * 128, IN_COLS])
    P = Prog(nc)
    wst = P.sb("wst", [128, IN_COLS], F32)
    w_bf = P.sb("w_bf", [128, 8, IN_COLS], BF16)
    g_sb = P.sb("g_sb", [128, D], F32)
    bf_sb = P.sb("bf_sb", [128, 8], F32)
    id_f = P.sb("id_f", [128, 128], F32)
    id_b = P.sb("id_b", [128, 128], BF16)
    P.dma("sp", lambda e: e.dma_start(out=g_sb[:, :], in_=gbc), writes=["g"])
    P.dma("sp", lambda e: e.dma_start(out=bf_sb[:, :], in_=bfb), writes=["bf"])
    P.dma("sp", lambda e: e.dma_start(out=id_f[:, :], in_=identd), writes=["idf"])
    P.op("dve", lambda e: e.tensor_copy(out=id_b[:, :], in_=id_f[:, :]), reads=["idf"], writes=["idb"])
    for dc in range(8):
        P.dma("sp", lambda e, dc=dc: e.dma_start(out=wst[:, :], in_=w[dc * 128:(dc + 1) * 128, :]),
              writes=["wst"])
        P.op("act", lambda e, dc=dc: e.copy(out=w_bf[:, dc, :], in_=wst[:, :]), reads=["wst"], writes=["w%d" % dc])
    wnames = ["w%d" % dc for dc in range(8)]
    xt = [P.sb("xt%d" % i, [128, D], F32) for i in range(2)]
    junk = P.sb("junk", [128, D], BF16)
    ss = [P.sb("ss%d" % i, [128, 1], F32) for i in range(2)]
    rstd = [P.sb("rstd%d" % i, [128, 1], F32) for i in range(2)]
    h = [P.sb("h%d" % i, [128, D], BF16) for i in range(2)]
    hT = [P.sb("hT%d" % i, [128, D], BF16) for i in range(2)]
    pr = [P.sb("pr%d" % i, [128, IN_COLS], F32) for i in range(2)]
    lz = P.sb("lz", [128, 8], F32)
    psT = [P.ps("psT%d" % i, [128, D], BF16) for i in range(2)]
    psP = [P.ps("psP%d" % i, [128, 512], F32) for i in range(4)]
    groups = [(c0, min(c0 + 512, IN_COLS)) for c0 in range(0, IN_COLS, 512)]
    gi = 0
    for i in range(NT):
        b = i % 2
        X, H, HT, PR = xt[b], h[b], hT[b], pr[b]
        P.dma("sp", lambda e, X=X, i=i: e.dma_start(out=X[:, :], in_=x[i * 128:(i + 1) * 128, :]),
              writes=["xt%d" % b])
        P.op("act", lambda e, X=X, b=b: e.activation(out=junk[:, :], in_=X[:, :], func=AF.Square,
                                                      accum_out=ss[b][:, 0:1]),
             reads=["xt%d" % b], writes=["junk", "ss%d" % b])
        P.op("act", lambda e, b=b: e.activation(out=rstd[b][:, :], in_=ss[b][:, :], func=AF.Sqrt, bias=1e-6,
                                                scale=1.0 / D),
             reads=["ss%d" % b], writes=["rstd%d" % b])
        P.op("dve", lambda e, b=b: e.reciprocal(out=rstd[b][:, :], in_=rstd[b][:, :]),
             reads=["rstd%d" % b], writes=["rstd%d" % b])
        P.op("dve", lambda e, X=X, H=H, b=b: e.scalar_tensor_tensor(
            out=H[:, :], in0=X[:, :], scalar=rstd[b][:, 0:1], in1=g_sb[:, :], op0=ALU.mult, op1=ALU.mult),
            reads=["xt%d" % b, "rstd%d" % b, "g"], writes=["h%d" % b])
        for dc in range(8):
            P.op("pe", lambda e, H=H, b=b, dc=dc: e.transpose(
                out=psT[b][:, dc * 128:(dc + 1) * 128], in_=H[:, dc * 128:(dc + 1) * 128], identity=id_b[:, :]),
                reads=["h%d" % b, "idb"], writes=["psT%d" % b])
        P.op("act", lambda e, HT=HT, b=b: e.copy(out=HT[:, :], in_=psT[b][:, :]),
             reads=["psT%d" % b], writes=["hT%d" % b])
        for (c0, c1) in groups:
            pp = gi % 4
            gi += 1
            n = c1 - c0
            for dc in range(8):
                P.op("pe", lambda e, HT=HT, pp=pp, dc=dc, c0=c0, c1=c1, n=n: e.matmul(
                    psP[pp][:, 0:n], lhsT=HT[:, dc * 128:(dc + 1) * 128], rhs=w_bf[:, dc, c0:c1],
                    start=(dc == 0), stop=(dc == 7)),
                    reads=["hT%d" % b] + wnames, writes=["psP%d" % pp])
            if gi % 2 == 0:
                P.op("act", lambda e, PR=PR, pp=pp, c0=c0, c1=c1, n=n: e.copy(out=PR[:, c0:c1], in_=psP[pp][:, 0:n]),
                     reads=["psP%d" % pp], writes=["pr%d" % b])
            else:
                P.op("dve", lambda e, PR=PR, pp=pp, c0=c0, c1=c1, n=n: e.tensor_copy(out=PR[:, c0:c1],
                                                                                     in_=psP[pp][:, 0:n]),
                     reads=["psP%d" % pp], writes=["pr%d" % b])
        P.op("dve", lambda e, PR=PR: e.tensor_tensor(out=lz[:, :], in0=PR[:, 1536:1544], in1=bf_sb[:, :], op=ALU.add),
             reads=["pr%d" % b, "bf"], writes=["lz"])
        P.op("act", lambda e: e.activation(out=lz[:, :], in_=lz[:, :], func=AF.Exp, scale=-1.0),
             reads=["lz"], writes=["lz"])
        P.op("act", lambda e: e.activation(out=lz[:, :], in_=lz[:, :], func=AF.Ln, bias=1.0, scale=1.0),
             reads=["lz"], writes=["lz"])
        P.op("dve", lambda e, PR=PR: e.tensor_single_scalar(out=PR[:, 1536:1544], in_=lz[:, :], scalar=-1.0,
                                                            op=ALU.mult),
             reads=["lz"], writes=["pr%d" % b])
        P.dma("sp", lambda e, PR=PR, i=i: e.dma_start(out=proj[i * 128:(i + 1) * 128, :], in_=PR[:, :]),
              reads=["pr%d" % b], writes=["out%d" % i])
    P.emit()
    return nc


def _ident():
    return np.eye(128, dtype=np.float32)


def run_phase1(x_prompt, x_sample, norm_mix_g, w_in, fox_b_f):
    NT = 33
    xp = np.ascontiguousarray(x_prompt).reshape(-1, D)
    xs = np.ascontiguousarray(x_sample).reshape(-1, D)
    nc = build_phase1(NT)
    gbc = np.ascontiguousarray(np.broadcast_to(norm_mix_g.reshape(1, D), (128, D))).astype(np.float32)
    bfb = np.ascontiguousarray(np.broadcast_to(fox_b_f.reshape(1, 8), (128, 8))).astype(np.float32)
    w = np.ascontiguousarray(w_in.reshape(D, IN_COLS))
    in_maps = []
    for c in range(NCORE):
        xc = np.zeros((NT * 128, D), np.float32)
        xc[:4096] = xp[c * 4096:(c + 1) * 4096]
        xc[4096:4128] = xs[c * 32:(c + 1) * 32]
        in_maps.append({"x": xc, "gbc": gbc, "w_in": w, "bfb": bfb, "ident": _ident()})
    res = run_bass_kernel_spmd(nc, in_maps, core_ids=list(range(NCORE)))
    pp = np.concatenate([res.results[c]["proj"][:4096] for c in range(NCORE)], axis=0)
    psm = np.concatenate([res.results[c]["proj"][4096:4128] for c in range(NCORE)], axis=0)
    return pp, psm


TP = 16384
TS = 2176
NSEQ_S = 16


def _fox_consts(P, nc):
    c = {}
    tri_d = _din(nc, "tri", [128, 128])
    ones_d = _din(nc, "ones", [128, 128])
    id_d = _din(nc, "ident", [128, 128])
    mask_d = _din(nc, "mask", [128, 4 * 512])
    c["tri"] = P.sb("tri", [128, 128], F32)
    c["ones"] = P.sb("ones", [128, 128], F32)
    c["idf"] = P.sb("idf", [128, 128], F32)
    c["idb"] = P.sb("idb", [128, 128], BF16)
    c["maskf"] = P.sb("maskf", [128, 2048], F32)
    c["mask"] = P.sb("maskb", [128, 4, 512], BF16)
    P.dma("sp", lambda e: e.dma_start(out=c["tri"][:, :], in_=tri_d), writes=["tri"])
    P.dma("sp", lambda e: e.dma_start(out=c["ones"][:, :], in_=ones_d), writes=["ones"])
    P.dma("sp", lambda e: e.dma_start(out=c["idf"][:, :], in_=id_d), writes=["idf"])
    P.dma("sp", lambda e: e.dma_start(out=c["maskf"][:, :], in_=mask_d), writes=["maskf"])
    P.op("dve", lambda e: e.tensor_copy(out=c["idb"][:, :], in_=c["idf"][:, :]), reads=["idf"], writes=["idb"])
    P.op("dve", lambda e: e.tensor_copy(out=c["mask"][:, :, :], in_=c["maskf"][:, :].rearrange("p (a b) -> p a b", a=4)),
         reads=["maskf"], writes=["maskb"])
    return c


def _fox_seq(P, c, B, tag, NT, qT_src, nq_tot, kT_src, v_src, lf_src, groups, out_fn):
    T = NT * 128
    qT, kT, vv, stage = B["qT"], B["kT"], B["vv"], B["stage"]
    k = 0
    for (dst, src, n, nm) in ((qT, qT_src, nq_tot, "qT"), (kT, kT_src, T, "kT")):
        for c0 in range(0, n, 2048):
            w = min(2048, n - c0)
            s = k % 2
            k += 1
            P.dma("sp", lambda e, s=s, src=src, c0=c0, w=w: e.dma_start(out=stage[s][0:64, 0:w], in_=src[:, c0:c0 + w]),
                  writes=["stage%d" % s])
            eng = "act" if k % 2 else "dve"
            if eng == "act":
                P.op("act", lambda e, s=s, dst=dst, c0=c0, w=w: e.copy(out=dst[:, c0:c0 + w], in_=stage[s][0:64, 0:w]),
                     reads=["stage%d" % s], writes=[nm])
            else:
                P.op("dve", lambda e, s=s, dst=dst, c0=c0, w=w: e.tensor_copy(out=dst[:, c0:c0 + w], in_=stage[s][0:64, 0:w]),
                     reads=["stage%d" % s], writes=[nm])
    for j0 in range(0, NT, 32):
        nj = min(32, NT - j0)
        s = k % 2
        k += 1
        P.dma("sp", lambda e, s=s, j0=j0, nj=nj: e.dma_start(
            out=stage[s][:, 0:nj * 64].rearrange("p (j d) -> p j d", d=64), in_=v_src[:, j0:j0 + nj, :]),
            writes=["stage%d" % s])
        P.op("dve", lambda e, s=s, j0=j0, nj=nj: e.tensor_copy(
            out=vv[:, j0:j0 + nj, 0:64], in_=stage[s][:, 0:nj * 64].rearrange("p (j d) -> p j d", d=64)),
            reads=["stage%d" % s], writes=["vv"])
    L = B["L"]
    P.dma("sp", lambda e: e.dma_start(out=L[:, 0:NT], in_=lf_src), writes=["L"])
    cl_ps, tot_ps = B["cl_ps"], B["tot_ps"]
    P.op("pe", lambda e: e.matmul(cl_ps[:, 0:NT], lhsT=c["tri"][:, :], rhs=L[:, 0:NT], start=True, stop=True),
         reads=["L", "tri"], writes=["cl_ps"])
    P.op("pe", lambda e: e.matmul(tot_ps[:, 0:NT], lhsT=c["ones"][:, :], rhs=L[:, 0:NT], start=True, stop=True),
         reads=["L", "ones"], writes=["tot_ps"])
    sa, sbb = B["scanA"], B["scanB"]
    P.op("dve", lambda e: e.tensor_copy(out=sa[:, 0:NT], in_=tot_ps[:, 0:NT]), reads=["tot_ps"], writes=["scanA"])
    cur, nxt, cn, nn = sa, sbb, "scanA", "scanB"
    sh = 1
    while sh < NT:
        P.op("dve", lambda e, cur=cur, nxt=nxt, sh=sh: e.tensor_tensor(
            out=nxt[:, sh:NT], in0=cur[:, sh:NT], in1=cur[:, 0:NT - sh], op=ALU.add), reads=[cn], writes=[nn])
        P.op("dve", lambda e, cur=cur, nxt=nxt, sh=sh: e.tensor_copy(out=nxt[:, 0:sh], in_=cur[:, 0:sh]),
             reads=[cn], writes=[nn])
        cur, nxt, cn, nn = nxt, cur, nn, cn
        sh *= 2
    pex, negC = B["pex"], B["negC"]
    P.op("dve", lambda e, cur=cur: e.tensor_tensor(out=pex[:, 0:NT], in0=cur[:, 0:NT], in1=tot_ps[:, 0:NT],
                                                   op=ALU.subtract), reads=[cn, "tot_ps"], writes=["pex"])
    P.op("dve", lambda e: e.scalar_tensor_tensor(out=negC[:, 0:NT], in0=pex[:, 0:NT], scalar=-1.0, in1=cl_ps[:, 0:NT],
                                                 op0=ALU.mult, op1=ALU.subtract),
         reads=["pex", "cl_ps"], writes=["negC"])
    bias = B["bias"]
    for gi, (q0, nq, nk, d0, ct) in enumerate(groups):
        P.op("dve", lambda e, gi=gi, nk=nk, ct=ct: e.tensor_scalar(
            out=bias[:, gi, 0:nk], in0=negC[:, 0:nk], scalar1=pex[:, ct:ct + 1], scalar2=0.0,
            op0=ALU.add, op1=ALU.add), reads=["negC", "pex"], writes=["bias"])
    it = B["it"]
    for gi, (q0, nq, nk, d0, ct) in enumerate(groups):
        ob = B["gcount"] % 2
        B["gcount"] += 1
        OT = B["OT"][ob]
        def emit_score(j, it_):
            sb_ = it_ % 2
            sT = B["sT"][sb_]
            diag = j >= d0
            P.op("pe", lambda e, sT=sT, j=j, q0=q0, nq=nq, diag=diag: e.matmul(
                sT[:, 0:nq], lhsT=kT[:, j * 128:(j + 1) * 128], rhs=qT[:, q0:q0 + nq], start=True, stop=(not diag)),
                reads=["kT", "qT"], writes=["sT%d" % sb_])
            if diag:
                jl = j - d0
                P.op("pe", lambda e, sT=sT, jl=jl, nq=nq: e.matmul(
                    sT[:, 0:nq], lhsT=c["idb"][:, :], rhs=c["mask"][:, jl, 0:nq], start=False, stop=True),
                    reads=["idb", "maskb"], writes=["sT%d" % sb_])

        emit_score(0, it)
        for j in range(nk):
            sb_ = it % 2
            pb = it % 3
            sT = B["sT"][sb_]
            pT = B["pT"][pb]
            if j + 1 < nk:
                emit_score(j + 1, it + 1)
            it += 1
            P.op("act", lambda e, sT=sT, pT=pT, gi=gi, j=j, nq=nq: e.activation(
                out=pT[:, 0:nq], in_=sT[:, 0:nq], func=AF.Exp, bias=bias[:, gi, j:j + 1], scale=0.125),
                reads=["sT%d" % sb_, "bias"], writes=["pT%d" % pb])
            P.op("pe", lambda e, OT=OT, pT=pT, j=j, nq=nq, nk=nk: e.matmul(
                OT[0:65, 0:nq], lhsT=vv[:, j, :], rhs=pT[:, 0:nq], start=(j == 0), stop=(j == nk - 1)),
                reads=["vv", "pT%d" % pb], writes=["OT%d" % ob])
        oT = B["oT"]
        P.op("act", lambda e, OT=OT, nq=nq: e.copy(out=oT[0:65, 0:nq], in_=OT[0:65, 0:nq]),
             reads=["OT%d" % ob], writes=["oT"])
        oq, rec, osb = B["oq"], B["rec"], B["osb"]
        nqi = (nq + 127) // 128
        for qi in range(nqi):
            w = min(128, nq - qi * 128)
            P.op("pe", lambda e, qi=qi, w=w: e.transpose(out=oq[0:w, qi, :], in_=oT[0:65, qi * 128:qi * 128 + w],
                                                         identity=c["idf"][0:65, 0:65]),
                 reads=["oT", "idf"], writes=["oq"])
        wmax = min(128, nq)
        for qi in range(nqi):
            P.op("dve", lambda e, qi=qi: e.reciprocal(out=rec[0:wmax, qi:qi + 1], in_=oq[0:wmax, qi, 64:65]),
                 reads=["oq"], writes=["rec"])
            P.op("dve", lambda e, qi=qi: e.tensor_scalar_mul(out=osb[0:wmax, qi, :], in0=oq[0:wmax, qi, 0:64],
                                                             scalar1=rec[0:wmax, qi:qi + 1]),
                 reads=["oq", "rec"], writes=["osb"])
        out_fn(P, osb, q0, nq, nqi, wmax)
    B["it"] = it


def _fox_bufs(P):
    B = {}
    B["qT"] = P.sb("qT", [64, TP], BF16)
    B["kT"] = P.sb("kT", [64, TP], BF16)
    B["vv"] = P.sb("vv", [128, 128, 65], BF16)
    B["stage"] = [P.sb("stage%d" % i, [128, 2048], F32) for i in range(2)]
    B["L"] = P.sb("L", [128, 128], F32)
    B["scanA"] = P.sb("scanA", [128, 128], F32)
    B["scanB"] = P.sb("scanB", [128, 128], F32)
    B["pex"] = P.sb("pex", [128, 128], F32)
    B["negC"] = P.sb("negC", [128, 128], F32)
    B["bias"] = P.sb("bias", [128, 32, 128], F32)
    B["pT"] = [P.sb("pT%d" % i, [128, 512], BF16) for i in range(3)]
    B["oT"] = P.sb("oT", [65, 512], F32)
    B["rec"] = P.sb("rec", [128, 4], F32)
    B["osb"] = P.sb("osb", [128, 4, 64], F32)
    B["cl_ps"] = P.ps("cl_ps", [128, 128], F32)
    B["tot_ps"] = P.ps("tot_ps", [128, 128], F32)
    B["sT"] = [P.ps("sT%d" % i, [128, 512], F32) for i in range(2)]
    B["OT"] = [P.ps("OT%d" % i, [128, 512], F32) for i in range(2)]
    B["oq"] = P.ps("oq", [128, 4, 65], F32)
    B["it"] = 0
    B["gcount"] = 0
    P.op("pool", lambda e: e.memset(B["vv"][:, :, 64:65], 1.0), writes=["vv"])
    return B


def build_phase2_fox(n_prompt=2, n_sample=NSEQ_S, ngroups=32):
    nc = bass.Bass("TRN2", target_bir_lowering=False)
    qTp = _din(nc, "qTp", [2, 64, TP])
    kTp = _din(nc, "kTp", [2, 64, TP])
    vp = _din(nc, "vp", [2, 128, 128, 64])
    lfp = _din(nc, "lfp", [2, 128, 128])
    qTs = _din(nc, "qTs", [NSEQ_S, 64, 16])
    kTs = _din(nc, "kTs", [NSEQ_S, 64, TS])
    vs = _din(nc, "vs", [NSEQ_S, 128, 17, 64])
    lfs = _din(nc, "lfs", [NSEQ_S, 128, 17])
    op_ = _dout(nc, "o_p", [2, TP, 64])
    os_ = _dout(nc, "o_s", [NSEQ_S, 16, 64])
    P = Prog(nc)
    c = _fox_consts(P, nc)
    B = _fox_bufs(P)
    for b in range(n_prompt):
        groups = [(512 * g, 512, 4 * g + 4, 4 * g, 4 * g + 2) for g in range(ngroups)]

        def out_fn(P, osb, q0, nq, nqi, wmax, b=b):
            P.dma("sp", lambda e: e.dma_start(
                out=op_[b, q0:q0 + nq, :].rearrange("(a p) d -> p a d", p=128), in_=osb[:, 0:nqi, :]),
                reads=["osb"], writes=["o_out"])
        _fox_seq(P, c, B, "p%d" % b, 128, qTp[b], TP, kTp[b], vp[b], lfp[b], groups, out_fn)
    for s in range(n_sample):
        groups = [(0, 16, 17, 16, 16)]

        def out_fn(P, osb, q0, nq, nqi, wmax, s=s):
            P.dma("sp", lambda e: e.dma_start(out=os_[s, :, :], in_=osb[0:16, 0, :]), reads=["osb"], writes=["o_out"])
        _fox_seq(P, c, B, "s%d" % s, 17, qTs[s], 16, kTs[s], vs[s], lfs[s], groups, out_fn)
    P.emit()
    return nc


def _fox_const_inputs():
    p = np.arange(128)
    tri = (p[:, None] <= p[None, :]).astype(np.float32)
    ones = np.ones((128, 128), np.float32)
    col = np.arange(512)
    mask = np.zeros((128, 4, 512), np.float32)
    for jl in range(4):
        mask[:, jl, :] = np.where(jl * 128 + p[:, None] > col[None, :], -30000.0, 0.0)
    return {"tri": tri, "ones": ones, "ident": _ident(), "mask": mask.reshape(128, 2048)}


def _tile_major(a, nt):
    return np.ascontiguousarray(np.swapaxes(a.reshape((nt, 128) + a.shape[1:]), 0, 1))


def fox_inputs(pp, psm, cache_k, cache_v, cache_lf):
    ppb = pp.reshape(2, TP, IN_COLS)
    pss = psm.reshape(NSEQ_S, 16, IN_COLS)
    maps = []
    for h in range(NCORE):
        m = dict(_fox_const_inputs())
        m["qTp"] = np.ascontiguousarray(np.swapaxes(ppb[:, :, h * 64:(h + 1) * 64], 1, 2))
        m["kTp"] = np.ascontiguousarray(np.swapaxes(ppb[:, :, 512 + h * 64:512 + (h + 1) * 64], 1, 2))
        m["vp"] = np.stack([_tile_major(ppb[b, :, 1024 + h * 64:1024 + (h + 1) * 64], 128) for b in range(2)])
        m["lfp"] = np.stack([_tile_major(ppb[b, :, 1536 + h], 128) for b in range(2)])
        kfull = np.zeros((NSEQ_S, TS, 64), np.float32)
        vfull = np.zeros((NSEQ_S, TS, 64), np.float32)
        lfull = np.zeros((NSEQ_S, TS), np.float32)
        kfull[:, :2048] = cache_k[:, :, h, :]
        vfull[:, :2048] = cache_v[:, :, h, :]
        lfull[:, :2048] = cache_lf[:, :, h]
        kfull[:, 2048:2064] = pss[:, :, 512 + h * 64:512 + (h + 1) * 64]
        vfull[:, 2048:2064] = pss[:, :, 1024 + h * 64:1024 + (h + 1) * 64]
        lfull[:, 2048:2064] = pss[:, :, 1536 + h]
        m["qTs"] = np.ascontiguousarray(np.swapaxes(pss[:, :, h * 64:(h + 1) * 64], 1, 2))
        m["kTs"] = np.ascontiguousarray(np.swapaxes(kfull, 1, 2))
        m["vs"] = np.stack([_tile_major(vfull[s], 17) for s in range(NSEQ_S)])
        m["lfs"] = np.stack([_tile_major(lfull[s], 17) for s in range(NSEQ_S)])
        maps.append(m)
    return maps


NPAR = 352 + 7 * 64
EXPM05 = math.exp(-0.5)


def _rw_setup(P, nc, B):
    R = {}
    par_d = _din(nc, "rw_par", [128, NPAR])
    w2_d = _din(nc, "rw_w2", [32, 64])
    a2_d = _din(nc, "rw_a2", [32, 64])
    g2_d = _din(nc, "rw_g2", [96, 64])
    sel_d = _din(nc, "rw_sel", [6, 128])
    R["par"] = P.sb("rw_par", [128, NPAR], F32)
    R["w2"] = P.sb("rw_w2", [32, 64], F32)
    R["a2"] = P.sb("rw_a2", [32, 64], F32)
    R["g2"] = P.sb("rw_g2", [96, 64], F32)
    R["sel"] = P.sb("rw_sel", [6, 128], F32)
    R["selb"] = P.sb("rw_selb", [6, 128], BF16)
    R["omk"] = P.sb("rw_omk", [128, 64], F32)
    for nm, d_ in (("par", par_d), ("w2", w2_d), ("a2", a2_d), ("g2", g2_d), ("sel", sel_d)):
        P.dma("sp", lambda e, nm=nm, d_=d_: e.dma_start(out=R[nm][:, :], in_=d_), writes=["rwc_" + nm])
    P.op("dve", lambda e: e.tensor_copy(out=R["selb"][:, :], in_=R["sel"][:, :]), reads=["rwc_sel"], writes=["rwc_selb"])
    R["R3"] = [[P.sb("rw_R3_%d_%d" % (b, t), [128, 3, 320], BF16) for t in range(2)] for b in range(2)]
    R["r1"] = P.sb("rw_r1", [128, 320], F32)
    R["r2"] = P.sb("rw_r2", [128, 320], F32)
    o = 352
    R["mu"] = R["par"][:, 0:352]
    names = ["w0", "a0", "kk", "ka", "rk", "lnw", "lnb"]
    for i, nm in enumerate(names):
        R[nm] = R["par"][:, o + i * 64:o + (i + 1) * 64]
    P.op("dve", lambda e: e.tensor_scalar(out=R["omk"][:, :], in0=R["ka"], scalar1=-1.0, scalar2=1.0,
                                          op0=ALU.mult, op1=ALU.add), reads=["rwc_par"], writes=["rwc_omk"])
    R["cur"] = [P.sb("rw_cur%d" % b, [128, 352], F32) for b in range(2)]
    R["prv"] = [P.sb("rw_prv%d" % b, [128, 352], F32) for b in range(2)]
    R["R"] = [[P.sb("rw_R%d_%d" % (b, t), [128, 320], F32) for t in range(2)] for b in range(2)]
    R["GB"] = [[P.sb("rw_GB%d_%d" % (b, t), [128, 128], F32) for t in range(2)] for b in range(2)]
    R["VV"] = [P.sb("rw_VV%d" % t, [128, 128], F32) for t in range(2)]
    R["vT"] = [P.sb("rw_vT%d" % t, [128, 128], F32) for t in range(2)]
    R["yT"] = [P.sb("rw_yT%d" % t, [128, 128], F32) for t in range(2)]
    R["twl"] = P.sb("rw_twl", [32, 128], F32)
    R["alT"] = P.sb("rw_alT", [32, 128], F32)
    R["sgl"] = P.sb("rw_sgl", [96, 128], F32)
    for nm in ("zt", "at", "kkt", "tmp", "t1", "junk", "cen", "ob"):
        R[nm] = P.sb("rw_" + nm, [128, 64], F32)
    for nm in ("ssq", "rks", "mean", "var", "sk"):
        R[nm] = P.sb("rw_" + nm, [128, 1], F32)
    R["S"] = P.sb("rw_S", [128, 64], F32)
    R["stmp"] = P.sb("rw_stmp", [128, 64], F32)
    R["rowbuf"] = [P.sb("rw_rowbuf%d" % i, [6, 16 * 320], BF16) for i in range(2)]
    R["rowp"] = [B["sT"][0], B["sT"][1]]
    R["trp"] = B["OT"][0]
    R["lop"] = B["OT"][1]
    R["vtp"] = B["cl_ps"]
    R["ytp"] = B["tot_ps"]
    R["k"] = 0
    R["step"] = 0
    return R


def _rw_prep(P, c, R, n, ntok, cur_src, prev_src, rows_scr):
    tp = n % 2
    idf = c["idf"]
    t0 = n * ntok
    for b in range(2):
        cur, prv = R["cur"][b], R["prv"][b]
        cn, pn = "rw_cur%d" % b, "rw_prv%d" % b
        Rt, GB = R["R"][b][tp], R["GB"][b][tp]
        rn, gn = "rw_R%d_%d" % (b, tp), "rw_GB%d_%d" % (b, tp)
        P.dma("sp", lambda e, cur=cur, b=b: e.dma_start(out=cur[0:ntok, :], in_=cur_src(b, n)), writes=[cn])
        P.dma("sp", lambda e, prv=prv, b=b: e.dma_start(out=prv[0:ntok, :], in_=prev_src(b, n)), writes=[pn])
        P.op("dve", lambda e, cur=cur, prv=prv: e.tensor_tensor(out=prv[0:ntok, :], in0=prv[0:ntok, :], in1=cur[0:ntok, :],
                                                                op=ALU.subtract), reads=[cn, pn], writes=[pn])
        P.op("dve", lambda e, prv=prv: e.tensor_tensor(out=prv[0:ntok, :], in0=prv[0:ntok, :], in1=R["mu"][0:ntok, :],
                                                       op=ALU.mult), reads=[pn, "rwc_par"], writes=[pn])
        P.op("dve", lambda e, cur=cur, prv=prv: e.tensor_tensor(out=cur[0:ntok, :], in0=cur[0:ntok, :], in1=prv[0:ntok, :],
                                                                op=ALU.add), reads=[cn, pn], writes=[cn])
        trp, lop = R["trp"], R["lop"]
        for (o0, c0, c1, m) in ((0, 192, 224, 32), (128, 224, 256, 32), (256, 256, 352, 96)):
            P.op("pe", lambda e, cur=cur, o0=o0, c0=c0, c1=c1, m=m: e.transpose(
                out=trp[0:m, o0:o0 + ntok], in_=cur[0:ntok, c0:c1], identity=idf[0:ntok, 0:ntok]),
                reads=[cn, "idf"], writes=["OT0"])
        P.op("act", lambda e: e.activation(out=R["twl"][0:32, 0:ntok], in_=trp[0:32, 0:ntok], func=AF.Tanh),
             reads=["OT0"], writes=["rw_twl"])
        P.op("act", lambda e: e.copy(out=R["alT"][0:32, 0:ntok], in_=trp[0:32, 128:128 + ntok]),
             reads=["OT0"], writes=["rw_alT"])
        P.op("act", lambda e: e.activation(out=R["sgl"][0:96, 0:ntok], in_=trp[0:96, 256:256 + ntok], func=AF.Sigmoid),
             reads=["OT0"], writes=["rw_sgl"])
        P.op("pe", lambda e: e.matmul(lop[0:ntok, 0:64], lhsT=R["twl"][0:32, 0:ntok], rhs=R["w2"][:, :], start=True, stop=True),
             reads=["rw_twl", "rwc_w2"], writes=["OT1"])
        P.op("pe", lambda e: e.matmul(lop[0:ntok, 64:128], lhsT=R["alT"][0:32, 0:ntok], rhs=R["a2"][:, :], start=True, stop=True),
             reads=["rw_alT", "rwc_a2"], writes=["OT1"])
        P.op("pe", lambda e: e.matmul(lop[0:ntok, 128:192], lhsT=R["sgl"][0:96, 0:ntok], rhs=R["g2"][:, :], start=True, stop=True),
             reads=["rw_sgl", "rwc_g2"], writes=["OT1"])
        zt, at, kkt, tmp, t1, junk = R["zt"], R["at"], R["kkt"], R["tmp"], R["t1"], R["junk"]
        ssq, rks = R["ssq"], R["rks"]
        P.op("dve", lambda e: e.tensor_tensor(out=zt[0:ntok, :], in0=lop[0:ntok, 0:64], in1=R["w0"][0:ntok, :], op=ALU.add),
             reads=["OT1", "rwc_par"], writes=["rw_zt"])
        P.op("act", lambda e: e.activation(out=zt[0:ntok, :], in_=zt[0:ntok, :], func=AF.Sigmoid),
             reads=["rw_zt"], writes=["rw_zt"])
        P.op("act", lambda e, Rt=Rt: e.activation(out=Rt[0:ntok, 0:64], in_=zt[0:ntok, :], func=AF.Exp, scale=-EXPM05),
             reads=["rw_zt"], writes=[rn])
        P.op("dve", lambda e: e.tensor_tensor(out=at[0:ntok, :], in0=lop[0:ntok, 64:128], in1=R["a0"][0:ntok, :], op=ALU.add),
             reads=["OT1", "rwc_par"], writes=["rw_at"])
        P.op("act", lambda e: e.activation(out=at[0:ntok, :], in_=at[0:ntok, :], func=AF.Sigmoid),
             reads=["rw_at"], writes=["rw_at"])
        P.op("act", lambda e, GB=GB: e.copy(out=GB[0:ntok, 0:64], in_=lop[0:ntok, 128:192]), reads=["OT1"], writes=[gn])
        P.op("dve", lambda e, cur=cur: e.tensor_tensor(out=kkt[0:ntok, :], in0=cur[0:ntok, 64:128], in1=R["kk"][0:ntok, :],
                                                       op=ALU.mult), reads=[cn, "rwc_par"], writes=["rw_kkt"])
        P.op("dve", lambda e: e.scalar_tensor_tensor(out=junk[0:ntok, :], in0=kkt[0:ntok, :], scalar=1.0, in1=kkt[0:ntok, :],
                                                     op0=ALU.mult, op1=ALU.mult, accum_out=ssq[0:ntok, 0:1]),
             reads=["rw_kkt"], writes=["rw_junk", "rw_ssq"])
        P.fence("dve", ["rw_ssq"])
        P.op("act", lambda e: e.activation(out=ssq[0:ntok, :], in_=ssq[0:ntok, :], func=AF.Sqrt), reads=["rw_ssq"], writes=["rw_ssq"])
        P.op("dve", lambda e: e.tensor_scalar_max(out=ssq[0:ntok, :], in0=ssq[0:ntok, :], scalar1=1e-12),
             reads=["rw_ssq"], writes=["rw_ssq"])
        P.op("dve", lambda e: e.reciprocal(out=ssq[0:ntok, :], in_=ssq[0:ntok, :]), reads=["rw_ssq"], writes=["rw_ssq"])
        P.op("dve", lambda e: e.tensor_scalar_mul(out=kkt[0:ntok, :], in0=kkt[0:ntok, :], scalar1=ssq[0:ntok, 0:1]),
             reads=["rw_kkt", "rw_ssq"], writes=["rw_kkt"])
        P.op("dve", lambda e, Rt=Rt: e.tensor_single_scalar(out=Rt[0:ntok, 64:128], in_=kkt[0:ntok, :], scalar=-1.0, op=ALU.mult),
             reads=["rw_kkt"], writes=[rn])
        P.op("dve", lambda e, Rt=Rt: e.tensor_tensor(out=Rt[0:ntok, 128:192], in0=kkt[0:ntok, :], in1=at[0:ntok, :], op=ALU.mult),
             reads=["rw_kkt", "rw_at"], writes=[rn])
        P.op("dve", lambda e: e.tensor_tensor(out=tmp[0:ntok, :], in0=at[0:ntok, :], in1=R["ka"][0:ntok, :], op=ALU.mult),
             reads=["rw_at", "rwc_par"], writes=["rw_tmp"])
        P.op("dve", lambda e: e.tensor_tensor(out=tmp[0:ntok, :], in0=tmp[0:ntok, :], in1=R["omk"][0:ntok, :], op=ALU.add),
             reads=["rw_tmp", "rwc_omk"], writes=["rw_tmp"])
        P.op("dve", lambda e, cur=cur, Rt=Rt: e.tensor_tensor(out=Rt[0:ntok, 192:256], in0=cur[0:ntok, 64:128], in1=tmp[0:ntok, :],
                                                              op=ALU.mult), reads=[cn, "rw_tmp"], writes=[rn])
        P.op("act", lambda e, cur=cur, Rt=Rt: e.copy(out=Rt[0:ntok, 256:320], in_=cur[0:ntok, 0:64]), reads=[cn], writes=[rn])
        P.op("dve", lambda e, cur=cur, Rt=Rt: e.tensor_tensor(out=t1[0:ntok, :], in0=cur[0:ntok, 0:64], in1=Rt[0:ntok, 192:256],
                                                              op=ALU.mult), reads=[cn, rn], writes=["rw_t1"])
        P.op("dve", lambda e: e.scalar_tensor_tensor(out=junk[0:ntok, :], in0=t1[0:ntok, :], scalar=1.0, in1=R["rk"][0:ntok, :],
                                                     op0=ALU.mult, op1=ALU.mult, accum_out=rks[0:ntok, 0:1]),
             reads=["rw_t1", "rwc_par"], writes=["rw_junk", "rw_rks"])
        P.op("dve", lambda e, cur=cur, GB=GB: e.tensor_scalar_mul(out=GB[0:ntok, 64:128], in0=cur[0:ntok, 128:192],
                                                                  scalar1=rks[0:ntok, 0:1]),
             reads=[cn, "rw_rks"], writes=[gn])
        P.op("act", lambda e, cur=cur, b=b: e.copy(out=R["VV"][tp][0:ntok, b * 64:(b + 1) * 64], in_=cur[0:ntok, 128:192]),
             reads=[cn], writes=["rw_VV%d" % tp])
        R3 = R["R3"][b][tp]
        r3n = "rw_R3_%d_%d" % (b, tp)
        r1, r2 = R["r1"], R["r2"]
        P.op("act", lambda e, Rt=Rt, R3=R3: e.copy(out=R3[0:ntok, 0, :], in_=Rt[0:ntok, :]), reads=[rn], writes=[r3n])
        P.op("dve", lambda e, Rt=Rt, R3=R3: e.tensor_tensor(out=r1[0:ntok, :], in0=Rt[0:ntok, :], in1=R3[0:ntok, 0, :],
                                                            op=ALU.subtract), reads=[rn, r3n], writes=["rw_r1"])
        P.op("act", lambda e, R3=R3: e.copy(out=R3[0:ntok, 1, :], in_=r1[0:ntok, :]), reads=["rw_r1"], writes=[r3n])
        P.op("dve", lambda e, R3=R3: e.tensor_tensor(out=r2[0:ntok, :], in0=r1[0:ntok, :], in1=R3[0:ntok, 1, :],
                                                     op=ALU.subtract), reads=["rw_r1", r3n], writes=["rw_r2"])
        P.op("act", lambda e, R3=R3: e.copy(out=R3[0:ntok, 2, :], in_=r2[0:ntok, :]), reads=["rw_r2"], writes=[r3n])
        P.dma("sp", lambda e, R3=R3, b=b: e.dma_start(out=rows_scr[b, :, t0:t0 + ntok, :].rearrange("p t c -> t p c"),
                                                      in_=R3[0:ntok, :, :]),
              reads=[r3n], writes=["rows%d_%d" % (b, tp)])
    P.op("pe", lambda e: e.transpose(out=R["vtp"][:, 0:ntok], in_=R["VV"][tp][0:ntok, :], identity=idf[0:ntok, 0:ntok]),
         reads=["rw_VV%d" % tp, "idf"], writes=["cl_ps"])
    P.op("act", lambda e: e.copy(out=R["vT"][tp][:, 0:ntok], in_=R["vtp"][:, 0:ntok]), reads=["cl_ps"], writes=["rw_vT%d" % tp])


def _rw_scan(P, c, R, n, ntok, rows_scr):
    P.nosame = {"dve"}
    _rw_scan_body(P, c, R, n, ntok, rows_scr)
    P.nosame = set()


def _rw_scan_body(P, c, R, n, ntok, rows_scr):
    tp = n % 2
    t0 = n * ntok
    S, stmp, sk = R["S"], R["stmp"], R["sk"]
    vT, yT = R["vT"][tp], R["yT"][tp]
    vn, yn = "rw_vT%d" % tp, "rw_yT%d" % tp
    for blk in range(0, ntok, 16):
        nb = min(16, ntok - blk)
        rb = R["k"] % 2
        R["k"] += 1
        rowbuf = R["rowbuf"][rb]
        P.dma("sp", lambda e, rowbuf=rowbuf, blk=blk, nb=nb: e.dma_start(
            out=rowbuf[0:6, 0:nb * 320].rearrange("q (s c) -> q s c", c=320),
            in_=rows_scr[:, :, t0 + blk:t0 + blk + nb, :].rearrange("b p s c -> (b p) s c")),
            reads=["rows0_%d" % tp, "rows1_%d" % tp], writes=["rw_rowbuf%d" % rb])
        for s in range(nb):
            pb = R["step"] % 2
            R["step"] += 1
            rowp = R["rowp"][pb]
            pn = "sT%d" % pb
            t = blk + s
            P.op("pe", lambda e, rowp=rowp, rowbuf=rowbuf, s=s: e.matmul(
                rowp[:, 0:320], lhsT=R["selb"][0:6, :], rhs=rowbuf[0:6, s * 320:(s + 1) * 320], start=True, stop=True),
                reads=["rw_rowbuf%d" % rb, "rwc_selb"], writes=[pn])
            P.op("dve", lambda e, rowp=rowp: e.scalar_tensor_tensor(
                out=stmp[:, :], in0=S[:, :], scalar=1.0, in1=rowp[:, 64:128], op0=ALU.mult, op1=ALU.mult,
                accum_out=sk[:, 0:1]), reads=["rw_S", pn], writes=["rw_stmp", "rw_sk"])
            P.op("dve", lambda e, rowp=rowp: e.tensor_tensor(out=S[:, :], in0=S[:, :], in1=rowp[:, 0:64], op=ALU.mult),
                 reads=["rw_S", pn], writes=["rw_S"])
            P.op("dve", lambda e, rowp=rowp: e.scalar_tensor_tensor(
                out=S[:, :], in0=rowp[:, 128:192], scalar=sk[:, 0:1], in1=S[:, :], op0=ALU.mult, op1=ALU.add),
                reads=["rw_S", "rw_sk", pn], writes=["rw_S"])
            P.op("dve", lambda e, rowp=rowp, t=t: e.scalar_tensor_tensor(
                out=S[:, :], in0=rowp[:, 192:256], scalar=vT[:, t:t + 1], in1=S[:, :], op0=ALU.mult, op1=ALU.add),
                reads=["rw_S", vn, pn], writes=["rw_S"])
            P.op("dve", lambda e, rowp=rowp, t=t: e.scalar_tensor_tensor(
                out=stmp[:, :], in0=S[:, :], scalar=1.0, in1=rowp[:, 256:320], op0=ALU.mult, op1=ALU.mult,
                accum_out=yT[:, t:t + 1]), reads=["rw_S", pn], writes=["rw_stmp", yn])


def _rw_post(P, c, R, n, ntok, out_dst):
    tp = n % 2
    idf = c["idf"]
    ytp = R["ytp"]
    cen, ob, junk, mean, var = R["cen"], R["ob"], R["junk"], R["mean"], R["var"]
    P.op("pe", lambda e: e.transpose(out=ytp[0:ntok, 0:128], in_=R["yT"][tp][:, 0:ntok], identity=idf[:, :]),
         reads=["rw_yT%d" % tp, "idf"], writes=["tot_ps"])
    for b in range(2):
        GB = R["GB"][b][tp]
        gn = "rw_GB%d_%d" % (b, tp)
        ysl = ytp[0:ntok, b * 64:(b + 1) * 64]
        P.op("dve", lambda e, ysl=ysl: e.tensor_reduce(out=mean[0:ntok, :], in_=ysl, axis=AX.X, op=ALU.add),
             reads=["tot_ps"], writes=["rw_mean"])
        P.op("dve", lambda e: e.tensor_single_scalar(out=mean[0:ntok, :], in_=mean[0:ntok, :], scalar=1.0 / 64, op=ALU.mult),
             reads=["rw_mean"], writes=["rw_mean"])
        P.op("dve", lambda e, ysl=ysl: e.tensor_scalar(out=cen[0:ntok, :], in0=ysl, scalar1=mean[0:ntok, 0:1], scalar2=0.0,
                                                       op0=ALU.subtract, op1=ALU.add),
             reads=["tot_ps", "rw_mean"], writes=["rw_cen"])
        P.op("dve", lambda e: e.scalar_tensor_tensor(out=junk[0:ntok, :], in0=cen[0:ntok, :], scalar=1.0, in1=cen[0:ntok, :],
                                                     op0=ALU.mult, op1=ALU.mult, accum_out=var[0:ntok, 0:1]),
             reads=["rw_cen"], writes=["rw_junk", "rw_var"])
        P.fence("dve", ["rw_var"])
        P.op("act", lambda e: e.activation(out=var[0:ntok, :], in_=var[0:ntok, :], func=AF.Sqrt, bias=64e-5, scale=1.0 / 64),
             reads=["rw_var"], writes=["rw_var"])
        P.op("dve", lambda e: e.reciprocal(out=var[0:ntok, :], in_=var[0:ntok, :]), reads=["rw_var"], writes=["rw_var"])
        P.op("dve", lambda e: e.scalar_tensor_tensor(out=cen[0:ntok, :], in0=cen[0:ntok, :], scalar=var[0:ntok, 0:1],
                                                     in1=R["lnw"][0:ntok, :], op0=ALU.mult, op1=ALU.mult),
             reads=["rw_cen", "rw_var", "rwc_par"], writes=["rw_cen"])
        P.op("dve", lambda e: e.tensor_tensor(out=cen[0:ntok, :], in0=cen[0:ntok, :], in1=R["lnb"][0:ntok, :], op=ALU.add),
             reads=["rw_cen", "rwc_par"], writes=["rw_cen"])
        P.op("dve", lambda e, GB=GB: e.tensor_tensor(out=cen[0:ntok, :], in0=cen[0:ntok, :], in1=GB[0:ntok, 64:128], op=ALU.add),
             reads=["rw_cen", gn], writes=["rw_cen"])
        P.op("dve", lambda e, GB=GB: e.tensor_tensor(out=ob[0:ntok, :], in0=cen[0:ntok, :], in1=GB[0:ntok, 0:64], op=ALU.mult),
             reads=["rw_cen", gn], writes=["rw_ob"])
        P.dma("sp", lambda e, b=b: e.dma_start(out=out_dst(b, n), in_=ob[0:ntok, :]), reads=["rw_ob"], writes=["rw_out"])


def _rw_pair(P, c, R, ntiles, ntok, cur_src, prev_src, rows_scr, S0_src, out_dst, ST_dst):
    S = R["S"]
    if S0_src is None:
        P.op("dve", lambda e: e.memset(S[:, :], 0.0), writes=["rw_S"])
    else:
        P.dma("sp", lambda e: e.dma_start(out=S[:, :], in_=S0_src), writes=["rw_S"])
    _rw_prep(P, c, R, 0, ntok, cur_src, prev_src, rows_scr)
    for n in range(ntiles):
        if n + 1 < ntiles:
            _rw_prep(P, c, R, n + 1, ntok, cur_src, prev_src, rows_scr)
        _rw_scan(P, c, R, n, ntok, rows_scr)
        _rw_post(P, c, R, n, ntok, out_dst)
    P.dma("sp", lambda e: e.dma_start(out=ST_dst, in_=S[:, :]), reads=["rw_S"], writes=["rw_STout"])


def build_phase2(n_prompt=2, n_sample=NSEQ_S, ngroups=32, rw_tiles=128, rw_pairs=8, do_fox=True):
    nc = bass.Bass("TRN2", target_bir_lowering=False)
    qTp = _din(nc, "qTp", [2, 64, TP])
    kTp = _din(nc, "kTp", [2, 64, TP])
    vp = _din(nc, "vp", [2, 128, 128, 64])
    lfp = _din(nc, "lfp", [2, 128, 128])
    qTs = _din(nc, "qTs", [NSEQ_S, 64, 16])
    kTs = _din(nc, "kTs", [NSEQ_S, 64, TS])
    vs = _din(nc, "vs", [NSEQ_S, 128, 17, 64])
    lfs = _din(nc, "lfs", [NSEQ_S, 128, 17])
    op_ = _dout(nc, "o_p", [2, TP, 64])
    os_ = _dout(nc, "o_s", [NSEQ_S, 16, 64])
    curp = _din(nc, "rw_curp", [2, TP, 352])
    prvp = _din(nc, "rw_prvp", [2, TP, 352])
    curs = _din(nc, "rw_curs", [NSEQ_S, 16, 352])
    prvs = _din(nc, "rw_prvs", [NSEQ_S, 16, 352])
    S0s = _din(nc, "rw_S0s", [8, 128, 64])
    rwp = _dout(nc, "rw_p", [2, TP, 64])
    rws = _dout(nc, "rw_s", [NSEQ_S, 16, 64])
    STp = _dout(nc, "ST_p", [128, 64])
    STs = _dout(nc, "ST_s", [8, 128, 64])
    rows_p = nc.dram_tensor("rows_p", [2, 3, TP, 320], BF16).ap()
    rows_s = nc.dram_tensor("rows_s", [8, 2, 3, 16, 320], BF16).ap()
    P = Prog(nc)
    c = _fox_consts(P, nc)
    B = _fox_bufs(P)
    if do_fox:
        for b in range(n_prompt):
            groups = [(512 * g, 512, 4 * g + 4, 4 * g, 4 * g + 2) for g in range(ngroups)]

            def out_fn(P, osb, q0, nq, nqi, wmax, b=b):
                P.dma("sp", lambda e: e.dma_start(
                    out=op_[b, q0:q0 + nq, :].rearrange("(a p) d -> p a d", p=128), in_=osb[:, 0:nqi, :]),
                    reads=["osb"], writes=["o_out"])
            _fox_seq(P, c, B, "p%d" % b, 128, qTp[b], TP, kTp[b], vp[b], lfp[b], groups, out_fn)
        for s in range(n_sample):
            groups = [(0, 16, 17, 16, 16)]

            def out_fn(P, osb, q0, nq, nqi, wmax, s=s):
                P.dma("sp", lambda e: e.dma_start(out=os_[s, :, :], in_=osb[0:16, 0, :]), reads=["osb"], writes=["o_out"])
            _fox_seq(P, c, B, "s%d" % s, 17, qTs[s], 16, kTs[s], vs[s], lfs[s], groups, out_fn)
    R = _rw_setup(P, nc, B)
    if rw_tiles > 0:
        _rw_pair(P, c, R, rw_tiles, 128,
                 lambda b, n: curp[b, n * 128:(n + 1) * 128, :], lambda b, n: prvp[b, n * 128:(n + 1) * 128, :],
                 rows_p, None, lambda b, n: rwp[b, n * 128:(n + 1) * 128, :], STp)
    for pr in range(rw_pairs):
        _rw_pair(P, c, R, 1, 16,
                 lambda b, n, pr=pr: curs[2 * pr + b, :, :], lambda b, n, pr=pr: prvs[2 * pr + b, :, :],
                 rows_s[pr], S0s[pr], lambda b, n, pr=pr: rws[2 * pr + b, :, :], STs[pr])
    P.emit()
    return nc


def rw_inputs(pp, psm, state_rwkv, state_shift, prm):
    ppb = pp.reshape(2, TP, IN_COLS)[:, :, FOX_COLS:]
    pss = psm.reshape(NSEQ_S, 16, IN_COLS)[:, :, FOX_COLS:]
    prev_p = np.zeros_like(ppb)
    prev_p[:, 1:] = ppb[:, :-1]
    prev_s = np.empty_like(pss)
    prev_s[:, 1:] = pss[:, :-1]
    prev_s[:, 0] = state_shift[:, 0, :]
    maps = []
    sel = np.zeros((6, 128), np.float32)
    sel[0:3, :64] = 1.0
    sel[3:6, 64:] = 1.0
    for h in range(NCORE):
        cols = np.concatenate([np.arange(h * 64, (h + 1) * 64), 512 + np.arange(h * 64, (h + 1) * 64),
                               1024 + np.arange(h * 64, (h + 1) * 64), np.arange(1536, 1696)])
        hs = slice(h * 64, (h + 1) * 64)
        par = np.concatenate([prm["rwkv_mu"][cols], prm["rwkv_w0"][hs], prm["rwkv_a0"][hs], prm["rwkv_k_k"][hs],
                              prm["rwkv_k_a"][hs], prm["rwkv_r_k"][h], prm["rwkv_lnx_w"][hs], prm["rwkv_lnx_b"][hs]])
        m = {
            "rw_curp": np.ascontiguousarray(ppb[:, :, cols]), "rw_prvp": np.ascontiguousarray(prev_p[:, :, cols]),
            "rw_curs": np.ascontiguousarray(pss[:, :, cols]), "rw_prvs": np.ascontiguousarray(prev_s[:, :, cols]),
            "rw_S0s": np.ascontiguousarray(state_rwkv[:, h].reshape(8, 128, 64)),
            "rw_par": np.ascontiguousarray(np.broadcast_to(par[None, :], (128, NPAR))).astype(np.float32),
            "rw_w2": np.ascontiguousarray(prm["rwkv_w2"][:, hs]), "rw_a2": np.ascontiguousarray(prm["rwkv_a2"][:, hs]),
            "rw_g2": np.ascontiguousarray(prm["rwkv_g2"][:, hs]), "rw_sel": sel,
        }
        maps.append(m)
    return maps


STAGE = 99


def build_phase3(NT, nslots=128):
    nc = bass.Bass("TRN2", target_bir_lowering=False)
    x = _din(nc, "x", [NT * 128, D])
    mixT = _din(nc, "mixT", [NT, 128, 8, 128])
    wout = _din(nc, "w_out", [D, D])
    wq = _din(nc, "w_q", [D, D])
    skT_d = _din(nc, "skT", [64, 16 * 128])
    g1 = _din(nc, "gffn", [128, D])
    g2 = _din(nc, "gfin", [128, D])
    identd = _din(nc, "ident", [128, 128])
    iota_d = _din(nc, "iota", [128, 256])
    pu = _din(nc, "peer_u", [16384, D])
    pv = _din(nc, "peer_v", [16384, D])
    y = _dout(nc, "y", [NT * 128, D])
    puv16 = nc.dram_tensor("puv16", [16384, 2 * D], BF16).ap()
    P = Prog(nc)
    cst = [P.sb("cst%d" % i, [128, 4096], F32) for i in range(2)]
    cbf = [P.sb("cbf%d" % i, [128, 4096], BF16) for i in range(2)]
    ck = 0
    if nslots > 0:
        for (src, half) in ((pu, 0), (pv, 1)):
            sv_ = src.rearrange("(p r) d -> p (r d)", p=128)
            dv_ = puv16.rearrange("(p r) d -> p r d", p=128)
            for pc in range(32):
                i2 = ck % 2
                P.dma("sp", lambda e, i2=i2, sv_=sv_, pc=pc: e.dma_start(out=cst[i2][:, :], in_=sv_[:, pc * 4096:(pc + 1) * 4096]),
                      writes=["cst%d" % i2])
                eng = ("act", "pool", "dve")[ck % 3]
                if eng == "act":
                    P.op("act", lambda e, i2=i2: e.copy(out=cbf[i2][:, :], in_=cst[i2][:, :]), reads=["cst%d" % i2], writes=["cbf%d" % i2])
                else:
                    P.op(eng, lambda e, i2=i2: e.tensor_copy(out=cbf[i2][:, :], in_=cst[i2][:, :]), reads=["cst%d" % i2],
                         writes=["cbf%d" % i2])
                P.dma("sp", lambda e, i2=i2, dv_=dv_, pc=pc, half=half: e.dma_start(
                    out=dv_[:, pc * 4:(pc + 1) * 4, half * D:(half + 1) * D], in_=cbf[i2][:, :].rearrange("p (r d) -> p r d", d=D)),
                    reads=["cbf%d" % i2], writes=["puv16"])
                ck += 1
    wst = P.sb("wst", [128, D], F32)
    wo_bf = P.sb("wo_bf", [128, 8, D], BF16)
    wq_bf = P.sb("wq_bf", [128, 8, D], BF16)
    sk_bf = P.sb("sk_bf", [64, 16, 128], BF16)
    g1s = P.sb("g1s", [128, D], F32)
    g2s = P.sb("g2s", [128, D], F32)
    id_f = P.sb("id_f", [128, 128], F32)
    id_b = P.sb("id_b", [128, 128], BF16)
    P.dma("sp", lambda e: e.dma_start(out=g1s[:, :], in_=g1), writes=["g1"])
    P.dma("sp", lambda e: e.dma_start(out=g2s[:, :], in_=g2), writes=["g2"])
    P.dma("sp", lambda e: e.dma_start(out=id_f[:, :], in_=identd), writes=["idf"])
    P.op("dve", lambda e: e.tensor_copy(out=id_b[:, :], in_=id_f[:, :]), reads=["idf"], writes=["idb"])
    for (src, dst, nm) in ((wout, wo_bf, "wo"), (wq, wq_bf, "wq")):
        for dc in range(8):
            P.dma("sp", lambda e, src=src, dc=dc: e.dma_start(out=wst[:, :], in_=src[dc * 128:(dc + 1) * 128, :]), writes=["wst"])
            P.op("act", lambda e, dst=dst, dc=dc: e.copy(out=dst[:, dc, :], in_=wst[:, :]), reads=["wst"], writes=[nm])
    for hf in range(2):
        P.dma("sp", lambda e, hf=hf: e.dma_start(out=wst[0:64, :], in_=skT_d[:, hf * 1024:(hf + 1) * 1024]), writes=["wst"])
        P.op("act", lambda e, hf=hf: e.copy(out=sk_bf[:, hf * 8:(hf + 1) * 8, :],
                                            in_=wst[0:64, :].rearrange("p (h n) -> p h n", h=8)), reads=["wst"], writes=["sk"])

    xt = P.sb("xt", [128, D], F32)
    mt = P.sb("mt", [128, 8, 128], F32)
    mtb = P.sb("mtb", [128, 8, 128], BF16)
    x2 = P.sb("x2", [128, D], F32)
    xn = P.sb("xn", [128, D], F32)
    xnb = P.sb("xnb", [128, D], BF16)
    xnT = P.sb("xnT", [128, D], BF16)
    qT = P.sb("qT", [64, 16 * 128], BF16)
    junkb = P.sb("junkb", [128, D], BF16)
    junkD = P.sb("junkD", [128, D], F32)
    ss = P.sb("ss", [128, 1], F32)
    s_sb = P.sb("s_sb", [128, 16, 128], F32)
    s2 = P.sb("s2", [128, 128], F32)
    sv = P.sb("sv", [128, 16, 16], F32)
    si = P.sb("si", [128, 16, 16], U32)
    sif = P.sb("sif", [128, 16, 16], F32)
    cand = P.sb("cand", [128, 8, 256], F32)
    cidx = P.sb("cidx", [128, 8, 256], F32)
    c2 = P.sb("c2", [128, 256], F32)
    j256 = P.sb("j256", [128, 256], F32)
    tv = P.sb("tv", [128, 8, 16], F32)
    pi = P.sb("pi", [128, 8, 16], U32)
    pif = P.sb("pif", [128, 8, 16], F32)
    iota = P.sb("iota", [128, 256], F32)
    P.dma("sp", lambda e: e.dma_start(out=iota[:, :], in_=iota_d), writes=["iota"])
    negm = P.sb("negm", [128, 8], F32)
    gt = P.sb("gt", [128, 8, 16], F32)
    Z = P.sb("Z", [128, 8], F32)
    ef = P.sb("ef", [128, 128], F32)
    ei = P.sb("ei", [128, 128], I32)
    apre = P.sb("apre", [128, 128], F32)
    coef = P.sb("coef", [128, 128], F32)
    acc = P.sb("acc", [128, D], F32)
    yo = P.sb("yo", [128, D], F32)
    NG = 8
    gb = [P.sb("gbuf%d" % i, [128, 2 * D], BF16) for i in range(NG)]
    gl = P.sb("gl", [128, 128], F32)
    psA = P.ps("psA", [128, D], F32)
    psT = P.ps("psT", [128, D], BF16)
    psS = P.ps("psS", [128, 16, 128], F32)
    gk = 0

    def rms(src, srcn, dst, dstn, gs, gn):
        P.op("act", lambda e: e.activation(out=junkb[:, :], in_=src[:, :], func=AF.Square, accum_out=ss[:, 0:1]),
             reads=[srcn], writes=["junkb", "ss"])
        P.op("act", lambda e: e.activation(out=ss[:, :], in_=ss[:, :], func=AF.Sqrt, bias=1e-6, scale=1.0 / D),
             reads=["ss"], writes=["ss"])
        P.op("dve", lambda e: e.reciprocal(out=ss[:, :], in_=ss[:, :]), reads=["ss"], writes=["ss"])
        P.op("dve", lambda e: e.scalar_tensor_tensor(out=dst[:, :], in0=src[:, :], scalar=ss[:, 0:1], in1=gs[:, :],
                                                     op0=ALU.mult, op1=ALU.mult), reads=[srcn, "ss", gn], writes=[dstn])

    for i in range(NT):
        P.dma("sp", lambda e, i=i: e.dma_start(out=xt[:, :], in_=x[i * 128:(i + 1) * 128, :]), writes=["xt"])
        P.dma("sp", lambda e, i=i: e.dma_start(out=mt[:, :, :], in_=mixT[i]), writes=["mt"])
        P.op("act", lambda e: e.copy(out=mtb[:, :, :], in_=mt[:, :, :]), reads=["mt"], writes=["mtb"])
        for g in range(2):
            for kc in range(8):
                P.op("pe", lambda e, g=g, kc=kc: e.matmul(psA[:, g * 512:(g + 1) * 512], lhsT=mtb[:, kc, :],
                                                          rhs=wo_bf[:, kc, g * 512:(g + 1) * 512], start=(kc == 0), stop=(kc == 7)),
                     reads=["mtb", "wo"], writes=["psA%d" % g])
        for g in range(2):
            P.op("dve", lambda e, g=g: e.tensor_tensor(out=x2[:, g * 512:(g + 1) * 512], in0=psA[:, g * 512:(g + 1) * 512],
                                                       in1=xt[:, g * 512:(g + 1) * 512], op=ALU.add),
                 reads=["psA%d" % g, "xt"], writes=["x2"])
        rms(x2, "x2", xn, "xn", g1s, "g1")
        P.op("act", lambda e: e.copy(out=xnb[:, :], in_=xn[:, :]), reads=["xn"], writes=["xnb"])
        for dc in range(8 if STAGE >= 2 else 0):
            P.op("pe", lambda e, dc=dc: e.transpose(out=psT[:, dc * 128:(dc + 1) * 128], in_=xnb[:, dc * 128:(dc + 1) * 128],
                                                    identity=id_b[:, :]), reads=["xnb", "idb"], writes=["psT"])
        P.op("act", lambda e: e.copy(out=xnT[:, :], in_=psT[:, :]), reads=["psT"], writes=["xnT"])
        for hc in range(16 if STAGE >= 3 else 0):
            bank = hc % 2
            for dc in range(8):
                P.op("pe", lambda e, hc=hc, dc=dc, bank=bank: e.matmul(
                    psA[0:64, bank * 512:bank * 512 + 128], lhsT=wq_bf[:, dc, hc * 64:(hc + 1) * 64],
                    rhs=xnT[:, dc * 128:(dc + 1) * 128], start=(dc == 0), stop=(dc == 7)),
                    reads=["xnT", "wq"], writes=["psA%d" % bank])
            if hc % 2 == 0:
                P.op("act", lambda e, hc=hc, bank=bank: e.copy(out=qT[0:64, hc * 128:(hc + 1) * 128],
                                                               in_=psA[0:64, bank * 512:bank * 512 + 128]),
                     reads=["psA%d" % bank], writes=["qT"])
            else:
                P.op("dve", lambda e, hc=hc, bank=bank: e.tensor_copy(out=qT[0:64, hc * 128:(hc + 1) * 128],
                                                                      in_=psA[0:64, bank * 512:bank * 512 + 128]),
                     reads=["psA%d" % bank], writes=["qT"])
        for hc in range(16 if STAGE >= 4 else 0):
            P.op("pe", lambda e, hc=hc: e.matmul(psS[:, hc, :], lhsT=qT[0:64, hc * 128:(hc + 1) * 128],
                                                 rhs=sk_bf[0:64, hc, :], start=True, stop=True),
                 reads=["qT", "sk"], writes=["psS"])
        for bk in range(4):
            if bk % 2 == 0:
                P.op("act", lambda e, bk=bk: e.copy(out=s_sb[:, 4 * bk:4 * bk + 4, :], in_=psS[:, 4 * bk:4 * bk + 4, :]),
                     reads=["psS"], writes=["s_sb"])
            else:
                P.op("dve", lambda e, bk=bk: e.tensor_copy(out=s_sb[:, 4 * bk:4 * bk + 4, :], in_=psS[:, 4 * bk:4 * bk + 4, :]),
                     reads=["psS"], writes=["s_sb"])
        for hc in range(16 if STAGE >= 5 else 0):
            P.op("dve", lambda e, hc=hc: e.max(out=sv[:, hc, 0:8], in_=s_sb[:, hc, :]), reads=["s_sb"], writes=["sv"])
            P.op("dve", lambda e, hc=hc: e.max_index(out=si[:, hc, 0:8], in_max=sv[:, hc, 0:8], in_values=s_sb[:, hc, :]),
                 reads=["s_sb", "sv"], writes=["si"])
            P.op("dve", lambda e, hc=hc: e.match_replace(out=s2[:, :], in_to_replace=sv[:, hc, 0:8], in_values=s_sb[:, hc, :],
                                                         imm_value=-1e30), reads=["s_sb", "sv"], writes=["s2"])
            P.op("dve", lambda e, hc=hc: e.max(out=sv[:, hc, 8:16], in_=s2[:, :]), reads=["s2"], writes=["sv"])
            P.op("dve", lambda e, hc=hc: e.max_index(out=si[:, hc, 8:16], in_max=sv[:, hc, 8:16], in_values=s2[:, :]),
                 reads=["s2", "sv"], writes=["si"])
        P.op("dve", lambda e: e.tensor_copy(out=sif[:, :, :], in_=si[:, :, :]), reads=["si"], writes=["sif"])
        for h in range(8 if STAGE >= 6 else 0):
            P.op("dve", lambda e, h=h: e.tensor_single_scalar(out=sif[:, 2 * h, :], in_=sif[:, 2 * h, :], scalar=128.0, op=ALU.mult),
                 reads=["sif"], writes=["sif"])
            P.op("dve", lambda e, h=h: e.tensor_tensor(
                out=cand[:, h, :].rearrange("p (a b) -> p a b", a=16),
                in0=sv[:, 2 * h, :].unsqueeze(2).to_broadcast([128, 16, 16]),
                in1=sv[:, 2 * h + 1, :].unsqueeze(1).to_broadcast([128, 16, 16]), op=ALU.add), reads=["sv"], writes=["cand"])
            P.op("dve", lambda e, h=h: e.tensor_tensor(
                out=cidx[:, h, :].rearrange("p (a b) -> p a b", a=16),
                in0=sif[:, 2 * h, :].unsqueeze(2).to_broadcast([128, 16, 16]),
                in1=sif[:, 2 * h + 1, :].unsqueeze(1).to_broadcast([128, 16, 16]), op=ALU.add), reads=["sif"], writes=["cidx"])
            P.op("dve", lambda e, h=h: e.max(out=tv[:, h, 0:8], in_=cand[:, h, :]), reads=["cand"], writes=["tv"])
            P.op("dve", lambda e, h=h: e.max_index(out=pi[:, h, 0:8], in_max=tv[:, h, 0:8], in_values=cand[:, h, :]),
                 reads=["cand", "tv"], writes=["pi"])
            P.op("dve", lambda e, h=h: e.match_replace(out=c2[:, :], in_to_replace=tv[:, h, 0:8], in_values=cand[:, h, :],
                                                       imm_value=-1e30), reads=["cand", "tv"], writes=["c2"])
            P.op("dve", lambda e, h=h: e.max(out=tv[:, h, 8:16], in_=c2[:, :]), reads=["c2"], writes=["tv"])
            P.op("dve", lambda e, h=h: e.max_index(out=pi[:, h, 8:16], in_max=tv[:, h, 8:16], in_values=c2[:, :]),
                 reads=["c2", "tv"], writes=["pi"])
            P.op("dve", lambda e, h=h: e.tensor_copy(out=pif[:, h, :], in_=pi[:, h, :]), reads=["pi"], writes=["pif"])
            for k in range(16):
                P.op("dve", lambda e, h=h, k=k: e.scalar_tensor_tensor(
                    out=j256[:, :], in0=iota[:, :], scalar=pif[:, h, k:k + 1], in1=cidx[:, h, :], op0=ALU.is_equal,
                    op1=ALU.mult, accum_out=ef[:, h * 16 + k:h * 16 + k + 1]), reads=["iota", "cidx", "pif"], writes=["j256", "ef"])
        P.op("dve", lambda e: e.tensor_copy(out=ei[:, :], in_=ef[:, :]), reads=["ef"], writes=["ei"])
        P.op("dve", lambda e: e.tensor_single_scalar(out=negm[:, :], in_=tv[:, :, 0], scalar=-1.0, op=ALU.mult),
             reads=["tv"], writes=["negm"])
        for h in range(8 if STAGE >= 7 else 0):
            P.op("act", lambda e, h=h: e.activation(out=gt[:, h, :], in_=tv[:, h, :], func=AF.Exp, bias=negm[:, h:h + 1],
                                                    scale=1.0, accum_out=Z[:, h:h + 1]), reads=["tv", "negm"], writes=["gt", "Z"])
        P.fence("act", ["Z", "gt"])
        P.op("dve", lambda e: e.reciprocal(out=Z[:, :], in_=Z[:, :]), reads=["Z"], writes=["Z"])
        for h in range(8):
            P.op("dve", lambda e, h=h: e.tensor_scalar_mul(out=gt[:, h, :], in0=gt[:, h, :], scalar1=Z[:, h:h + 1]),
                 reads=["gt", "Z"], writes=["gt"])
        gtf = gt[:, :, :].rearrange("p h k -> p (h k)")
        slot_buf = {}

        def vacc(sl):
            r = slot_buf[sl]
            P.op("dve", lambda e, sl=sl: e.tensor_tensor(out=coef[:, sl:sl + 1], in0=gl[:, sl:sl + 1], in1=gtf[:, sl:sl + 1], op=ALU.mult),
                 reads=["gl%d" % sl, "gt"], writes=["coef%d" % sl])
            if sl == 0:
                P.op("dve", lambda e, r=r: e.tensor_scalar_mul(out=acc[:, :], in0=gb[r][:, D:2 * D], scalar1=coef[:, 0:1]),
                     reads=["gbuf%d" % r, "coef0"], writes=["acc"])
            else:
                P.op("dve", lambda e, r=r, sl=sl: e.scalar_tensor_tensor(
                    out=acc[:, :], in0=gb[r][:, D:2 * D], scalar=coef[:, sl:sl + 1], in1=acc[:, :], op0=ALU.mult, op1=ALU.add),
                    reads=["gbuf%d" % r, "coef%d" % sl, "acc"], writes=["acc"])

        LAG = 3
        for sl in range(nslots):
            r = gk % NG
            gk += 1
            slot_buf[sl] = r
            P.dma("pool", lambda e, r=r, sl=sl: e.indirect_dma_start(
                out=gb[r][:, :], out_offset=None, in_=puv16[:, :],
                in_offset=bass.IndirectOffsetOnAxis(ap=ei[:, sl:sl + 1], axis=0)), reads=["ei", "puv16"], writes=["gbuf%d" % r])
            P.op("dve", lambda e, r=r, sl=sl: e.scalar_tensor_tensor(
                out=junkD[:, :], in0=gb[r][:, 0:D], scalar=1.0, in1=xn[:, :], op0=ALU.mult, op1=ALU.mult,
                accum_out=apre[:, sl:sl + 1]), reads=["gbuf%d" % r, "xn"], writes=["junkD", "apre_raw%d" % sl])
            P.fence("dve", ["apre%d" % sl])
            P.op("act", lambda e, sl=sl: e.activation(out=gl[:, sl:sl + 1], in_=apre[:, sl:sl + 1], func=AF.Gelu),
                 reads=["apre%d" % sl], writes=["gl%d" % sl])
            if sl >= LAG:
                vacc(sl - LAG)
        for sl in range(max(0, nslots - LAG), nslots):
            vacc(sl)
        if nslots > 0:
            P.op("dve", lambda e: e.tensor_tensor(out=x2[:, :], in0=x2[:, :], in1=acc[:, :], op=ALU.add),
                 reads=["x2", "acc"], writes=["x2"])
        rms(x2, "x2", yo, "yo", g2s, "g2")
        if nslots == 0:
            P.op("dve", lambda e: e.tensor_copy(out=yo[:, 0:128], in_=ef[:, :]), reads=["ef", "yo"], writes=["yo"])
            P.op("dve", lambda e: e.tensor_copy(out=yo[:, 128:256], in_=gt[:, :, :].rearrange("p h k -> p (h k)")),
                 reads=["gt", "yo"], writes=["yo"])
        P.dma("sp", lambda e, i=i: e.dma_start(out=y[i * 128:(i + 1) * 128, :], in_=yo[:, :]), reads=["yo"], writes=["yout"])
    P.emit()
    return nc


def run_phase3(x_prompt, x_sample, mix_p, mix_s, w_out, norm_ffn_g, peer_w_q, peer_sub_keys, peer_u, peer_v, norm_final_g,
               NT=33, nslots=128):
    xp = np.ascontiguousarray(x_prompt).reshape(-1, D)
    xs = np.ascontiguousarray(x_sample).reshape(-1, D)
    nc = build_phase3(NT, nslots)
    g1 = np.ascontiguousarray(np.broadcast_to(norm_ffn_g.reshape(1, D), (128, D))).astype(np.float32)
    g2 = np.ascontiguousarray(np.broadcast_to(norm_final_g.reshape(1, D), (128, D))).astype(np.float32)
    skT = np.ascontiguousarray(np.transpose(peer_sub_keys, (3, 0, 1, 2)).reshape(64, 16 * 128))
    iota_h = np.ascontiguousarray(np.broadcast_to(np.arange(256, dtype=np.float32)[None, :], (128, 256)))
    in_maps = []
    for c in range(NCORE):
        xc = np.zeros((33 * 128, D), np.float32)
        mc = np.zeros((33 * 128, D), np.float32)
        xc[:4096] = xp[c * 4096:(c + 1) * 4096]
        xc[4096:4128] = xs[c * 32:(c + 1) * 32]
        mc[:4096] = mix_p[c * 4096:(c + 1) * 4096]
        mc[4096:4128] = mix_s[c * 32:(c + 1) * 32]
        mT = np.ascontiguousarray(np.transpose(mc.reshape(33, 128, 8, 128), (0, 3, 2, 1)))
        in_maps.append({"x": xc[:NT * 128], "mixT": mT[:NT], "w_out": np.ascontiguousarray(w_out), "w_q": np.ascontiguousarray(peer_w_q),
                        "skT": skT, "iota": iota_h, "gffn": g1, "gfin": g2, "ident": _ident(), "peer_u": np.ascontiguousarray(peer_u),
                        "peer_v": np.ascontiguousarray(peer_v)})
    res = run_bass_kernel_spmd(nc, in_maps, core_ids=list(range(NCORE)))
    yp = np.concatenate([res.results[c]["y"][:4096] for c in range(NCORE)], axis=0) if NT == 33 else None
    ysm = np.concatenate([res.results[c]["y"][4096:4128] for c in range(NCORE)], axis=0) if NT == 33 else None
    return yp, ysm, res


def kernel(x_prompt, x_sample, cache_fox_k, cache_fox_v, cache_fox_logf, state_rwkv, state_shift,
           norm_mix_g, w_in, fox_b_f, rwkv_mu, rwkv_w0, rwkv_w2, rwkv_a0, rwkv_a2, rwkv_g2,
           rwkv_k_k, rwkv_k_a, rwkv_r_k, rwkv_lnx_w, rwkv_lnx_b, w_out, norm_ffn_g,
           peer_w_q, peer_sub_keys, peer_u, peer_v, norm_final_g):
    f = lambda a: np.asarray(a, dtype=np.float32)
    x_prompt, x_sample = f(x_prompt), f(x_sample)
    pp, psm = run_phase1(x_prompt, x_sample, f(norm_mix_g)[0], f(w_in)[0], f(fox_b_f)[0])
    prm = {"rwkv_mu": f(rwkv_mu)[0], "rwkv_w0": f(rwkv_w0)[0], "rwkv_w2": f(rwkv_w2)[0], "rwkv_a0": f(rwkv_a0)[0],
           "rwkv_a2": f(rwkv_a2)[0], "rwkv_g2": f(rwkv_g2)[0], "rwkv_k_k": f(rwkv_k_k)[0], "rwkv_k_a": f(rwkv_k_a)[0],
           "rwkv_r_k": f(rwkv_r_k)[0], "rwkv_lnx_w": f(rwkv_lnx_w)[0], "rwkv_lnx_b": f(rwkv_lnx_b)[0]}
    maps = fox_inputs(pp, psm, f(cache_fox_k)[0], f(cache_fox_v)[0], f(cache_fox_logf)[0])
    rmaps = rw_inputs(pp, psm, f(state_rwkv)[0], f(state_shift)[0], prm)
    for m, r in zip(maps, rmaps):
        m.update(r)
    nc2 = build_phase2()
    res2 = run_bass_kernel_spmd(nc2, maps, core_ids=list(range(NCORE)))
    del maps, rmaps
    R2 = res2.results
    mix_p = np.empty((2, TP, 1024), np.float32)
    mix_s = np.empty((NSEQ_S, 16, 1024), np.float32)
    S_p = np.empty((1, 2, 8, 64, 64), np.float32)
    S_s = np.empty((1, NSEQ_S, 8, 64, 64), np.float32)
    for h in range(NCORE):
        mix_p[:, :, h * 64:(h + 1) * 64] = R2[h]["o_p"]
        mix_p[:, :, 512 + h * 64:512 + (h + 1) * 64] = R2[h]["rw_p"]
        mix_s[:, :, h * 64:(h + 1) * 64] = R2[h]["o_s"]
        mix_s[:, :, 512 + h * 64:512 + (h + 1) * 64] = R2[h]["rw_s"]
        S_p[0, :, h] = R2[h]["ST_p"].reshape(2, 64, 64)
        S_s[0, :, h] = R2[h]["ST_s"].reshape(NSEQ_S, 64, 64)
    yp, ysm, _ = run_phase3(x_prompt, x_sample, mix_p.reshape(-1, 1024), mix_s.reshape(-1, 1024), f(w_out)[0],
                            f(norm_ffn_g)[0], f(peer_w_q)[0], f(peer_sub_keys)[0], f(peer_u)[0], f(peer_v)[0],
                            f(norm_final_g))
    ppb = pp.reshape(2, TP, IN_COLS)
    pss = psm.reshape(NSEQ_S, 16, IN_COLS)
    c = np.ascontiguousarray
    return (
        c(yp.reshape(2, TP, 1024)), c(ysm.reshape(NSEQ_S, 16, 1024)),
        c(ppb[:, :, 512:1024].reshape(1, 2, TP, 8, 64)), c(ppb[:, :, 1024:1536].reshape(1, 2, TP, 8, 64)),
        c(ppb[:, :, 1536:1544].reshape(1, 2, TP, 8)), S_p, c(ppb[:, -1:, FOX_COLS:].reshape(1, 2, 1, RW_COLS)),
        c(pss[:, :, 512:1024].reshape(1, NSEQ_S, 16, 8, 64)), c(pss[:, :, 1024:1536].reshape(1, NSEQ_S, 16, 8, 64)),
        c(pss[:, :, 1536:1544].reshape(1, NSEQ_S, 16, 8)), S_s, c(pss[:, -1:, FOX_COLS:].reshape(1, NSEQ_S, 1, RW_COLS)),
    )
```

```python
from contextlib import ExitStack
import math
import numpy as np
import concourse.bass as bass
import concourse.mybir as mybir
from concourse.bass_utils import run_bass_kernel_spmd

F32 = mybir.dt.float32
BF16 = mybir.dt.bfloat16
I32 = mybir.dt.int32
U32 = mybir.dt.uint32
ALU = mybir.AluOpType
AF = mybir.ActivationFunctionType
AX = mybir.AxisListType

D = 1024
IN_COLS = 3240
FOX_COLS = 1544
RW_COLS = 1696
NCORE = 8


class Prog:
    ENGS = ("pe", "act", "dve", "pool", "sp")

    def __init__(self, nc):
        self.nc = nc
        self.st = ExitStack()
        self.ops = {e: [] for e in self.ENGS}
        self.cnt = {}
        self.waited = {e: {} for e in self.ENGS}
        self.lastw = {}
        self.readers = {}
        self.ndma = {e: 0 for e in self.ENGS}
        self.NS = 8
        self.nosame = set()
        self.defer = None
        self.fence_t = {}
        self.uid = 0

    def sb(self, name, shape, dt):
        return self.st.enter_context(self.nc.sbuf_tensor("sb_" + name, list(shape), dt))

    def ps(self, name, shape, dt):
        return self.st.enter_context(self.nc.psum_tensor("ps_" + name, list(shape), dt))

    def _deps(self, eng, reads, writes):
        deps = []
        for b in reads:
            if b in self.lastw:
                deps.extend(self.lastw[b].items())
        for b in writes:
            if b in self.lastw:
                deps.extend(self.lastw[b].items())
            deps.extend(self.readers.get(b, ()))
        best = {}
        for (k, v) in deps:
            if eng == "pe" and k == "pe":
                continue
            if k == eng and eng in self.nosame:
                continue
            if self.waited[eng].get(k, 0) >= v:
                continue
            best[k] = max(best.get(k, 0), v)
        for k, v in best.items():
            self.waited[eng][k] = v
        return list(best.items())

    def _record(self, tok, reads, writes):
        for b in reads:
            self.readers.setdefault(b, []).append(tok)
        for b in writes:
            self.lastw.setdefault(b, {})[tok[0]] = tok[1]
            self.readers[b] = []

    def drain(self, q, k):
        saved, self.nosame = self.nosame, set()
        d, self.defer = self.defer, None
        for _ in range(min(k, len(q))):
            kind, a = q.pop(0)
            getattr(self, kind)(*a)
        self.defer, self.nosame = d, saved

    def op(self, eng, fn, reads=(), writes=()):
        if self.defer is not None:
            self.defer.append(("op", (eng, fn, tuple(reads), tuple(writes))))
            return
        waits = self._deps(eng, reads, writes)
        self.cnt[eng] = self.cnt.get(eng, 0) + 1
        self.ops[eng].append((waits, fn, eng, 1))
        self._record((eng, self.cnt[eng]), reads, writes)

    def fence(self, eng, names):
        if eng not in self.fence_t:
            self.fence_t[eng] = self.sb("fence_" + eng, [128, 2], F32)
        t = self.fence_t[eng]
        if self.defer is not None:
            self.defer.append(("fence", (eng, tuple(names))))
            return
        if eng == "act":
            self.op("act", lambda e: e.copy(out=t[:, 1:2], in_=t[:, 0:1]), reads=(), writes=list(names))
        else:
            self.op("dve", lambda e: e.tensor_copy(out=t[:, 1:2], in_=t[:, 0:1]), reads=(), writes=list(names))

    def dma(self, q, fn, reads=(), writes=()):
        if self.defer is not None:
            self.defer.append(("dma", (q, fn, tuple(reads), tuple(writes))))
            return
        waits = self._deps(q, reads, writes)
        k = "d_%s_%d" % (q, self.ndma[q] % (16 if q == "pool" else self.NS))
        self.ndma[q] += 1
        prev = self.cnt.get(k, 0)
        if prev and self.waited[q].get(k, 0) < prev:
            self.waited[q][k] = prev
            waits = [w for w in waits if w[0] != k] + [(k, prev)]
        self.cnt[k] = self.cnt.get(k, 0) + 16
        self.ops[q].append((waits, fn, k, 16))
        self._record((k, self.cnt[k]), reads, writes)

    def emit(self):
        nc = self.nc
        keys = sorted(self.cnt.keys())
        sems = {k: self.st.enter_context(nc.semaphore("s_" + k)) for k in keys}
        final = [(k, self.cnt[k]) for k in keys]
        ops = self.ops

        def run(name, e):
            for (waits, fn, k, inc) in ops[name]:
                for (wk, wv) in waits:
                    e.wait_ge(sems[wk], wv)
                fn(e).then_inc(sems[k], inc)
            if name == "sp":
                for (k, v) in final:
                    e.wait_ge(sems[k], v)

        with nc.Block() as block:
            @block.tensor
            def _(e):
                run("pe", e)

            @block.scalar
            def _(e):
                run("act", e)

            @block.vector
            def _(e):
                run("dve", e)

            @block.gpsimd
            def _(e):
                run("pool", e)

            @block.sync
            def _(e):
                run("sp", e)
        self.st.close()


def _din(nc, name, shape, dt=F32):
    return nc.dram_tensor(name, list(shape), dt, kind="ExternalInput").ap()


def _dout(nc, name, shape, dt=F32):
    return nc.dram_tensor(name, list(shape), dt, kind="ExternalOutput").ap()


def _load_cast(P, name, dram_ap, shape, stage, stage_name, q="sp", eng="act"):
    t = P.sb(name, shape, BF16)
    p, n = shape
    P.dma(q, lambda e: e.dma_start(out=stage[0:p, 0:n], in_=dram_ap), writes=[stage_name])
    if eng == "act":
        P.op("act", lambda e: e.copy(out=t[:, :], in_=stage[0:p, 0:n]), reads=[stage_name], writes=[name])
    else:
        P.op("dve", lambda e: e.tensor_copy(out=t[:, :], in_=stage[0:p, 0:n]), reads=[stage_name], writes=[name])
    return t


def build_phase1(NT):
    nc = bass.Bass("TRN2", target_bir_lowering=False)
    x = _din(nc, "x", [NT * 128, D])
    gbc = _din(nc, "gbc", [128, D])
    w = _din(nc, "w_in", [D, IN_COLS])
    bfb = _din(nc, "bfb", [128, 8])
    identd = _din(nc, "ident", [128, 128])
    proj = _dout(nc, "proj", [NT * 128, IN_COLS])
    P = Prog(nc)
    wst = P.sb("wst", [128, IN_COLS], F32)
    w_bf = P.sb("w_bf", [128, 8, IN_COLS], BF16)
    g_sb = P.sb("g_sb", [128, D], F32)
    bf_sb = P.sb("bf_sb", [128, 8], F32)
    id_f = P.sb("id_f", [128, 128], F32)
    id_b = P.sb("id_b", [128, 128], BF16)
    P.dma("sp", lambda e: e.dma_start(out=g_sb[:, :], in_=gbc), writes=["g"])
    P.dma("sp", lambda e: e.dma_start(out=bf_sb[:, :], in_=bfb), writes=["bf"])
    P.dma("sp", lambda e: e.dma_start(out=id_f[:, :], in_=identd), writes=["idf"])
    P.op("dve", lambda e: e.tensor_copy(out=id_b[:, :], in_=id_f[:, :]), reads=["idf"], writes=["idb"])
    for dc in range(8):
        P.dma("sp", lambda e, dc=dc: e.dma_start(out=wst[:, :], in_=w[dc * 128:(dc + 1) * 128, :]),
              writes=["wst"])
        P.op("act", lambda e, dc=dc: e.copy(out=w_bf[:, dc, :], in_=wst[:, :]), reads=["wst"], writes=["w%d" % dc])
    wnames = ["w%d" % dc for dc in range(8)]
    xt = [P.sb("xt%d" % i, [128, D], F32) for i in range(2)]
    junk = P.sb("junk", [128, D], BF16)
    ss = [P.sb("ss%d" % i, [128, 1], F32) for i in range(2)]
    rstd = [P.sb("rstd%d" % i, [128, 1], F32) for i in range(2)]
    h = [P.sb("h%d" % i, [128, D], BF16) for i in range(2)]
    hT = [P.sb("hT%d" % i, [128, D], BF16) for i in range(2)]
    pr = [P.sb("pr%d" % i, [128, IN_COLS], F32) for i in range(2)]
    lz = P.sb("lz", [128, 8], F32)
    psT = [P.ps("psT%d" % i, [128, D], BF16) for i in range(2)]
    psP = [P.ps("psP%d" % i, [128, 512], F32) for i in range(4)]
    groups = [(c0, min(c0 + 512, IN_COLS)) for c0 in range(0, IN_COLS, 512)]
    gi = 0
    for i in range(NT):
        b = i % 2
        X, H, HT, PR = xt[b], h[b], hT[b], pr[b]
        P.dma("sp", lambda e, X=X, i=i: e.dma_start(out=X[:, :], in_=x[i * 128:(i + 1) * 128, :]),
              writes=["xt%d" % b])
        P.op("act", lambda e, X=X, b=b: e.activation(out=junk[:, :], in_=X[:, :], func=AF.Square,
                                                      accum_out=ss[b][:, 0:1]),
             reads=["xt%d" % b], writes=["junk", "ss%d" % b])
        P.op("act", lambda e, b=b: e.activation(out=rstd[b][:, :], in_=ss[b][:, :], func=AF.Sqrt, bias=1e-6,
                                                scale=1.0 / D),
             reads=["ss%d" % b], writes=["rstd%d" % b])
        P.op("dve", lambda e, b=b: e.reciprocal(out=rstd[b][:, :], in_=rstd[b][:, :]),
             reads=["rstd%d" % b], writes=["rstd%d" % b])
        P.op("dve", lambda e, X=X, H=H, b=b: e.scalar_tensor_tensor(
            out=H[:, :], in0=X[:, :], scalar=rstd[b][:, 0:1], in1=g_sb[:, :], op0=ALU.mult, op1=ALU.mult),
            reads=["xt%d" % b, "rstd%d" % b, "g"], writes=["h%d" % b])
        for dc in range(8):
            P.op("pe", lambda e, H=H, b=b, dc=dc: e.transpose(
                out=psT[b][:, dc * 128:(dc + 1) * 128], in_=H[:, dc * 128:(dc + 1) * 128], identity=id_b[:, :]),
                reads=["h%d" % b, "idb"], writes=["psT%d" % b])
        P.op("act", lambda e, HT=HT, b=b: e.copy(out=HT[:, :], in_=psT[b][:, :]),
             reads=["psT%d" % b], writes=["hT%d" % b])
        for (c0, c1) in groups:
            pp = gi % 4
            gi += 1
            n = c1 - c0
            for dc in range(8):
                P.op("pe", lambda e, HT=HT, pp=pp, dc=dc, c0=c0, c1=c1, n=n: e.matmul(
                    psP[pp][:, 0:n], lhsT=HT[:, dc * 128:(dc + 1) * 128], rhs=w_bf[:, dc, c0:c1],
                    start=(dc == 0), stop=(dc == 7)),
                    reads=["hT%d" % b] + wnames, writes=["psP%d" % pp])
            if gi % 2 == 0:
                P.op("act", lambda e, PR=PR, pp=pp, c0=c0, c1=c1, n=n: e.copy(out=PR[:, c0:c1], in_=psP[pp][:, 0:n]),
                     reads=["psP%d" % pp], writes=["pr%d" % b])
            else:
                P.op("dve", lambda e, PR=PR, pp=pp, c0=c0, c1=c1, n=n: e.tensor_copy(out=PR[:, c0:c1],
                                                                                     in_=psP[pp][:, 0:n]),
                     reads=["psP%d" % pp], writes=["pr%d" % b])
        P.op("dve", lambda e, PR=PR: e.tensor_tensor(out=lz[:, :], in0=PR[:, 1536:1544], in1=bf_sb[:, :], op=ALU.add),
             reads=["pr%d" % b, "bf"], writes=["lz"])
        P.op("act", lambda e: e.activation(out=lz[:, :], in_=lz[:, :], func=AF.Exp, scale=-1.0),
             reads=["lz"], writes=["lz"])
        P.op("act", lambda e: e.activation(out=lz[:, :], in_=lz[:, :], func=AF.Ln, bias=1.0, scale=1.0),
             reads=["lz"], writes=["lz"])
        P.op("dve", lambda e, PR=PR: e.tensor_single_scalar(out=PR[:, 1536:1544], in_=lz[:, :], scalar=-1.0,
                                                            op=ALU.mult),
             reads=["lz"], writes=["pr%d" % b])
        P.dma("sp", lambda e, PR=PR, i=i: e.dma_start(out=proj[i * 128:(i + 1) * 128, :], in_=PR[:, :]),
              reads=["pr%d" % b], writes=["out%d" % i])
    P.emit()
    return nc


def _ident():
    return np.eye(128, dtype=np.float32)


def run_phase1(x_prompt, x_sample, norm_mix_g, w_in, fox_b_f):
    NT = 33
    xp = np.ascontiguousarray(x_prompt).reshape(-1, D)
    xs = np.ascontiguousarray(x_sample).reshape(-1, D)
    nc = build_phase1(NT)
    gbc = np.ascontiguousarray(np.broadcast_to(norm_mix_g.reshape(1, D), (128, D))).astype(np.float32)
    bfb = np.ascontiguousarray(np.broadcast_to(fox_b_f.reshape(1, 8), (128, 8))).astype(np.float32)
    w = np.ascontiguousarray(w_in.reshape(D, IN_COLS))
    in_maps = []
    for c in range(NCORE):
        xc = np.zeros((NT * 128, D), np.float32)
        xc[:4096] = xp[c * 4096:(c + 1) * 4096]
        xc[4096:4128] = xs[c * 32:(c + 1) * 32]
        in_maps.append({"x": xc, "gbc": gbc, "w_in": w, "bfb": bfb, "ident": _ident()})
    res = run_bass_kernel_spmd(nc, in_maps, core_ids=list(range(NCORE)))
    pp = np.concatenate([res.results[c]["proj"][:4096] for c in range(NCORE)], axis=0)
    psm = np.concatenate([res.results[c]["proj"][4096:4128] for c in range(NCORE)], axis=0)
    return pp, psm


TP = 16384
TS = 2176
NSEQ_S = 16


def _fox_consts(P, nc):
    c = {}
    tri_d = _din(nc, "tri", [128, 128])
    ones_d = _din(nc, "ones", [128, 128])
    id_d = _din(nc, "ident", [128, 128])
    mask_d = _din(nc, "mask", [128, 4 * 512])
    c["tri"] = P.sb("tri", [128, 128], F32)
    c["ones"] = P.sb("ones", [128, 128], F32)
    c["idf"] = P.sb("idf", [128, 128], F32)
    c["idb"] = P.sb("idb", [128, 128], BF16)
    c["maskf"] = P.sb("maskf", [128, 2048], F32)
    c["mask"] = P.sb("maskb", [128, 4, 512], BF16)
    P.dma("sp", lambda e: e.dma_start(out=c["tri"][:, :], in_=tri_d), writes=["tri"])
    P.dma("sp", lambda e: e.dma_start(out=c["ones"][:, :], in_=ones_d), writes=["ones"])
    P.dma("sp", lambda e: e.dma_start(out=c["idf"][:, :], in_=id_d), writes=["idf"])
    P.dma("sp", lambda e: e.dma_start(out=c["maskf"][:, :], in_=mask_d), writes=["maskf"])
    P.op("dve", lambda e: e.tensor_copy(out=c["idb"][:, :], in_=c["idf"][:, :]), reads=["idf"], writes=["idb"])
    P.op("dve", lambda e: e.tensor_copy(out=c["mask"][:, :, :], in_=c["maskf"][:, :].rearrange("p (a b) -> p a b", a=4)),
         reads=["maskf"], writes=["maskb"])
    return c


def _fox_seq(P, c, B, tag, NT, qT_src, nq_tot, kT_src, v_src, lf_src, groups, out_fn):
    T = NT * 128
    qT, kT, vv, stage = B["qT"], B["kT"], B["vv"], B["stage"]
    k = 0
    for (dst, src, n, nm) in ((qT, qT_src, nq_tot, "qT"), (kT, kT_src, T, "kT")):
        for c0 in range(0, n, 2048):
            w = min(2048, n - c0)
            s = k % 2
            k += 1
            P.dma("sp", lambda e, s=s, src=src, c0=c0, w=w: e.dma_start(out=stage[s][0:64, 0:w], in_=src[:, c0:c0 + w]),
                  writes=["stage%d" % s])
            eng = "act" if k % 2 else "dve"
            if eng == "act":
                P.op("act", lambda e, s=s, dst=dst, c0=c0, w=w: e.copy(out=dst[:, c0:c0 + w], in_=stage[s][0:64, 0:w]),
                     reads=["stage%d" % s], writes=[nm])
            else:
                P.op("dve", lambda e, s=s, dst=dst, c0=c0, w=w: e.tensor_copy(out=dst[:, c0:c0 + w], in_=stage[s][0:64, 0:w]),
                     reads=["stage%d" % s], writes=[nm])
    for j0 in range(0, NT, 32):
        nj = min(32, NT - j0)
        s = k % 2
        k += 1
        P.dma("sp", lambda e, s=s, j0=j0, nj=nj: e.dma_start(
            out=stage[s][:, 0:nj * 64].rearrange("p (j d) -> p j d", d=64), in_=v_src[:, j0:j0 + nj, :]),
            writes=["stage%d" % s])
        P.op("dve", lambda e, s=s, j0=j0, nj=nj: e.tensor_copy(
            out=vv[:, j0:j0 + nj, 0:64], in_=stage[s][:, 0:nj * 64].rearrange("p (j d) -> p j d", d=64)),
            reads=["stage%d" % s], writes=["vv"])
    L = B["L"]
    P.dma("sp", lambda e: e.dma_start(out=L[:, 0:NT], in_=lf_src), writes=["L"])
    cl_ps, tot_ps = B["cl_ps"], B["tot_ps"]
    P.op("pe", lambda e: e.matmul(cl_ps[:, 0:NT], lhsT=c["tri"][:, :], rhs=L[:, 0:NT], start=True, stop=True),
         reads=["L", "tri"], writes=["cl_ps"])
    P.op("pe", lambda e: e.matmul(tot_ps[:, 0:NT], lhsT=c["ones"][:, :], rhs=L[:, 0:NT], start=True, stop=True),
         reads=["L", "ones"], writes=["tot_ps"])
    sa, sbb = B["scanA"], B["scanB"]
    P.op("dve", lambda e: e.tensor_copy(out=sa[:, 0:NT], in_=tot_ps[:, 0:NT]), reads=["tot_ps"], writes=["scanA"])
    cur, nxt, cn, nn = sa, sbb, "scanA", "scanB"
    sh = 1
    while sh < NT:
        P.op("dve", lambda e, cur=cur, nxt=nxt, sh=sh: e.tensor_tensor(
            out=nxt[:, sh:NT], in0=cur[:, sh:NT], in1=cur[:, 0:NT - sh], op=ALU.add), reads=[cn], writes=[nn])
        P.op("dve", lambda e, cur=cur, nxt=nxt, sh=sh: e.tensor_copy(out=nxt[:, 0:sh], in_=cur[:, 0:sh]),
             reads=[cn], writes=[nn])
        cur, nxt, cn, nn = nxt, cur, nn, cn
        sh *= 2
    pex, negC = B["pex"], B["negC"]
    P.op("dve", lambda e, cur=cur: e.tensor_tensor(out=pex[:, 0:NT], in0=cur[:, 0:NT], in1=tot_ps[:, 0:NT],
                                                   op=ALU.subtract), reads=[cn, "tot_ps"], writes=["pex"])
    P.op("dve", lambda e: e.scalar_tensor_tensor(out=negC[:, 0:NT], in0=pex[:, 0:NT], scalar=-1.0, in1=cl_ps[:, 0:NT],
                                                 op0=ALU.mult, op1=ALU.subtract),
         reads=["pex", "cl_ps"], writes=["negC"])
    bias = B["bias"]
    for gi, (q0, nq, nk, d0, ct) in enumerate(groups):
        P.op("dve", lambda e, gi=gi, nk=nk, ct=ct: e.tensor_scalar(
            out=bias[:, gi, 0:nk], in0=negC[:, 0:nk], scalar1=pex[:, ct:ct + 1], scalar2=0.0,
            op0=ALU.add, op1=ALU.add), reads=["negC", "pex"], writes=["bias"])
    it = B["it"]
    for gi, (q0, nq, nk, d0, ct) in enumerate(groups):
        ob = B["gcount"] % 2
        B["gcount"] += 1
        OT = B["OT"][ob]
        def emit_score(j, it_):
            sb_ = it_ % 2
            sT = B["sT"][sb_]
            diag = j >= d0
            P.op("pe", lambda e, sT=sT, j=j, q0=q0, nq=nq, diag=diag: e.matmul(
                sT[:, 0:nq], lhsT=kT[:, j * 128:(j + 1) * 128], rhs=qT[:, q0:q0 + nq], start=True, stop=(not diag)),
                reads=["kT", "qT"], writes=["sT%d" % sb_])
            if diag:
                jl = j - d0
                P.op("pe", lambda e, sT=sT, jl=jl, nq=nq: e.matmul(
                    sT[:, 0:nq], lhsT=c["idb"][:, :], rhs=c["mask"][:, jl, 0:nq], start=False, stop=True),
                    reads=["idb", "maskb"], writes=["sT%d" % sb_])

        emit_score(0, it)
        for j in range(nk):
            sb_ = it % 2
            pb = it % 3
            sT = B["sT"][sb_]
            pT = B["pT"][pb]
            if j + 1 < nk:
                emit_score(j + 1, it + 1)
            it += 1
            P.op("act", lambda e, sT=sT, pT=pT, gi=gi, j=j, nq=nq: e.activation(
                out=pT[:, 0:nq], in_=sT[:, 0:nq], func=AF.Exp, bias=bias[:, gi, j:j + 1], scale=0.125),
                reads=["sT%d" % sb_, "bias"], writes=["pT%d" % pb])
            P.op("pe", lambda e, OT=OT, pT=pT, j=j, nq=nq, nk=nk: e.matmul(
                OT[0:65, 0:nq], lhsT=vv[:, j, :], rhs=pT[:, 0:nq], start=(j == 0), stop=(j == nk - 1)),
                reads=["vv", "pT%d" % pb], writes=["OT%d" % ob])
        oT = B["oT"]
        P.op("act", lambda e, OT=OT, nq=nq: e.copy(out=oT[0:65, 0:nq], in_=OT[0:65, 0:nq]),
             reads=["OT%d" % ob], writes=["oT"])
        oq, rec, osb = B["oq"], B["rec"], B["osb"]
        nqi = (nq + 127) // 128
        for qi in range(nqi):
            w = min(128, nq - qi * 128)
            P.op("pe", lambda e, qi=qi, w=w: e.transpose(out=oq[0:w, qi, :], in_=oT[0:65, qi * 128:qi * 128 + w],
                                                         identity=c["idf"][0:65, 0:65]),
                 reads=["oT", "idf"], writes=["oq"])
        wmax = min(128, nq)
        for qi in range(nqi):
            P.op("dve", lambda e, qi=qi: e.reciprocal(out=rec[0:wmax, qi:qi + 1], in_=oq[0:wmax, qi, 64:65]),
                 reads=["oq"], writes=["rec"])
            P.op("dve", lambda e, qi=qi: e.tensor_scalar_mul(out=osb[0:wmax, qi, :], in0=oq[0:wmax, qi, 0:64],
                                                             scalar1=rec[0:wmax, qi:qi + 1]),
                 reads=["oq", "rec"], writes=["osb"])
        out_fn(P, osb, q0, nq, nqi, wmax)
    B["it"] = it


def _fox_bufs(P):
    B = {}
    B["qT"] = P.sb("qT", [64, TP], BF16)
    B["kT"] = P.sb("kT", [64, TP], BF16)
    B["vv"] = P.sb("vv", [128, 128, 65], BF16)
    B["stage"] = [P.sb("stage%d" % i, [128, 2048], F32) for i in range(2)]
    B["L"] = P.sb("L", [128, 128], F32)
    B["scanA"] = P.sb("scanA", [128, 128], F32)
    B["scanB"] = P.sb("scanB", [128, 128], F32)
    B["pex"] = P.sb("pex", [128, 128], F32)
    B["negC"] = P.sb("negC", [128, 128], F32)
    B["bias"] = P.sb("bias", [128, 32, 128], F32)
    B["pT"] = [P.sb("pT%d" % i, [128, 512], BF16) for i in range(3)]
    B["oT"] = P.sb("oT", [65, 512], F32)
    B["rec"] = P.sb("rec", [128, 4], F32)
    B["osb"] = P.sb("osb", [128, 4, 64], F32)
    B["cl_ps"] = P.ps("cl_ps", [128, 128], F32)
    B["tot_ps"] = P.ps("tot_ps", [128, 128], F32)
    B["sT"] = [P.ps("sT%d" % i, [128, 512], F32) for i in range(2)]
    B["OT"] = [P.ps("OT%d" % i, [128, 512], F32) for i in range(2)]
    B["oq"] = P.ps("oq", [128, 4, 65], F32)
    B["it"] = 0
    B["gcount"] = 0
    P.op("pool", lambda e: e.memset(B["vv"][:, :, 64:65], 1.0), writes=["vv"])
    return B


def build_phase2_fox(n_prompt=2, n_sample=NSEQ_S, ngroups=32):
    nc = bass.Bass("TRN2", target_bir_lowering=False)
    qTp = _din(nc, "qTp", [2, 64, TP])
    kTp = _din(nc, "kTp", [2, 64, TP])
    vp = _din(nc, "vp", [2, 128, 128, 64])
    lfp = _din(nc, "lfp", [2, 128, 128])
    qTs = _din(nc, "qTs", [NSEQ_S, 64, 16])
    kTs = _din(nc, "kTs", [NSEQ_S, 64, TS])
    vs = _din(nc, "vs", [NSEQ_S, 128, 17, 64])
    lfs = _din(nc, "lfs", [NSEQ_S, 128, 17])
    op_ = _dout(nc, "o_p", [2, TP, 64])
    os_ = _dout(nc, "o_s", [NSEQ_S, 16, 64])
    P = Prog(nc)
    c = _fox_consts(P, nc)
    B = _fox_bufs(P)
    for b in range(n_prompt):
        groups = [(512 * g, 512, 4 * g + 4, 4 * g, 4 * g + 2) for g in range(ngroups)]

        def out_fn(P, osb, q0, nq, nqi, wmax, b=b):
            P.dma("sp", lambda e: e.dma_start(
                out=op_[b, q0:q0 + nq, :].rearrange("(a p) d -> p a d", p=128), in_=osb[:, 0:nqi, :]),
                reads=["osb"], writes=["o_out"])
        _fox_seq(P, c, B, "p%d" % b, 128, qTp[b], TP, kTp[b], vp[b], lfp[b], groups, out_fn)
    for s in range(n_sample):
        groups = [(0, 16, 17, 16, 16)]

        def out_fn(P, osb, q0, nq, nqi, wmax, s=s):
            P.dma("sp", lambda e: e.dma_start(out=os_[s, :, :], in_=osb[0:16, 0, :]), reads=["osb"], writes=["o_out"])
        _fox_seq(P, c, B, "s%d" % s, 17, qTs[s], 16, kTs[s], vs[s], lfs[s], groups, out_fn)
    P.emit()
    return nc


def _fox_const_inputs():
    p = np.arange(128)
    tri = (p[:, None] <= p[None, :]).astype(np.float32)
    ones = np.ones((128, 128), np.float32)
    col = np.arange(512)
    mask = np.zeros((128, 4, 512), np.float32)
    for jl in range(4):
        mask[:, jl, :] = np.where(jl * 128 + p[:, None] > col[None, :], -30000.0, 0.0)
    return {"tri": tri, "ones": ones, "ident": _ident(), "mask": mask.reshape(128, 2048)}


def _tile_major(a, nt):
    return np.ascontiguousarray(np.swapaxes(a.reshape((nt, 128) + a.shape[1:]), 0, 1))


def fox_inputs(pp, psm, cache_k, cache_v, cache_lf):
    ppb = pp.reshape(2, TP, IN_COLS)
    pss = psm.reshape(NSEQ_S, 16, IN_COLS)
    maps = []
    for h in range(NCORE):
        m = dict(_fox_const_inputs())
        m["qTp"] = np.ascontiguousarray(np.swapaxes(ppb[:, :, h * 64:(h + 1) * 64], 1, 2))
        m["kTp"] = np.ascontiguousarray(np.swapaxes(ppb[:, :, 512 + h * 64:512 + (h + 1) * 64], 1, 2))
        m["vp"] = np.stack([_tile_major(ppb[b, :, 1024 + h * 64:1024 + (h + 1) * 64], 128) for b in range(2)])
        m["lfp"] = np.stack([_tile_major(ppb[b, :, 1536 + h], 128) for b in range(2)])
        kfull = np.zeros((NSEQ_S, TS, 64), np.float32)
        vfull = np.zeros((NSEQ_S, TS, 64), np.float32)
        lfull = np.zeros((NSEQ_S, TS), np.float32)
        kfull[:, :2048] = cache_k[:, :, h, :]
        vfull[:, :2048] = cache_v[:, :, h, :]
        lfull[:, :2048] = cache_lf[:, :, h]
        kfull[:, 2048:2064] = pss[:, :, 512 + h * 64:512 + (h + 1) * 64]
        vfull[:, 2048:2064] = pss[:, :, 1024 + h * 64:1024 + (h + 1) * 64]
        lfull[:, 2048:2064] = pss[:, :, 1536 + h]
        m["qTs"] = np.ascontiguousarray(np.swapaxes(pss[:, :, h * 64:(h + 1) * 64], 1, 2))
        m["kTs"] = np.ascontiguousarray(np.swapaxes(kfull, 1, 2))
        m["vs"] = np.stack([_tile_major(vfull[s], 17) for s in range(NSEQ_S)])
        m["lfs"] = np.stack([_tile_major(lfull[s], 17) for s in range(NSEQ_S)])
        maps.append(m)
    return maps


NPAR = 352 + 7 * 64
EXPM05 = math.exp(-0.5)


def _rw_setup(P, nc, B):
    R = {}
    par_d = _din(nc, "rw_par", [128, NPAR])
    w2_d = _din(nc, "rw_w2", [32, 64])
    a2_d = _din(nc, "rw_a2", [32, 64])
    g2_d = _din(nc, "rw_g2", [96, 64])
    sel_d = _din(nc, "rw_sel", [6, 128])
    R["par"] = P.sb("rw_par", [128, NPAR], F32)
    R["w2"] = P.sb("rw_w2", [32, 64], F32)
    R["a2"] = P.sb("rw_a2", [32, 64], F32)
    R["g2"] = P.sb("rw_g2", [96, 64], F32)
    R["sel"] = P.sb("rw_sel", [6, 128], F32)
    R["selb"] = P.sb("rw_selb", [6, 128], BF16)
    R["omk"] = P.sb("rw_omk", [128, 64], F32)
    for nm, d_ in (("par", par_d), ("w2", w2_d), ("a2", a2_d), ("g2", g2_d), ("sel", sel_d)):
        P.dma("sp", lambda e, nm=nm, d_=d_: e.dma_start(out=R[nm][:, :], in_=d_), writes=["rwc_" + nm])
    P.op("dve", lambda e: e.tensor_copy(out=R["selb"][:, :], in_=R["sel"][:, :]), reads=["rwc_sel"], writes=["rwc_selb"])
    R["R3"] = [[P.sb("rw_R3_%d_%d" % (b, t), [128, 3, 320], BF16) for t in range(2)] for b in range(2)]
    R["r1"] = P.sb("rw_r1", [128, 320], F32)
    R["r2"] = P.sb("rw_r2", [128, 320], F32)
    o = 352
    R["mu"] = R["par"][:, 0:352]
    names = ["w0", "a0", "kk", "ka", "rk", "lnw", "lnb"]
    for i, nm in enumerate(names):
        R[nm] = R["par"][:, o + i * 64:o + (i + 1) * 64]
    P.op("dve", lambda e: e.tensor_scalar(out=R["omk"][:, :], in0=R["ka"], scalar1=-1.0, scalar2=1.0,
                                          op0=ALU.mult, op1=ALU.add), reads=["rwc_par"], writes=["rwc_omk"])
    R["cur"] = [P.sb("rw_cur%d" % b, [128, 352], F32) for b in range(2)]
    R["prv"] = [P.sb("rw_prv%d" % b, [128, 352], F32) for b in range(2)]
    R["R"] = [[P.sb("rw_R%d_%d" % (b, t), [128, 320], F32) for t in range(2)] for b in range(2)]
    R["GB"] = [[P.sb("rw_GB%d_%d" % (b, t), [128, 128], F32) for t in range(2)] for b in range(2)]
    R["VV"] = [P.sb("rw_VV%d" % t, [128, 128], F32) for t in range(2)]
    R["vT"] = [P.sb("rw_vT%d" % t, [128, 128], F32) for t in range(2)]
    R["yT"] = [P.sb("rw_yT%d" % t, [128, 128], F32) for t in range(2)]
    R["twl"] = P.sb("rw_twl", [32, 128], F32)
    R["alT"] = P.sb("rw_alT", [32, 128], F32)
    R["sgl"] = P.sb("rw_sgl", [96, 128], F32)
    for nm in ("zt", "at", "kkt", "tmp", "t1", "junk", "cen", "ob"):
        R[nm] = P.sb("rw_" + nm, [128, 64], F32)
    for nm in ("ssq", "rks", "mean", "var", "sk"):
        R[nm] = P.sb("rw_" + nm, [128, 1], F32)
    R["ysb"] = P.sb("rw_ysb", [128, 128], F32)
    R["S"] = P.sb("rw_S", [128, 64], F32)
    R["stmp"] = P.sb("rw_stmp", [128, 64], F32)
    R["rowbuf"] = [P.sb("rw_rowbuf%d" % i, [6, 16 * 320], BF16) for i in range(2)]
    R["rowp"] = [B["sT"][0], B["sT"][1]]
    R["trp"] = B["OT"][0]
    R["lop"] = B["OT"][1]
    R["vtp"] = B["cl_ps"]
    R["ytp"] = B["tot_ps"]
    R["k"] = 0
    R["step"] = 0
    return R


def _rw_prep(P, c, R, n, ntok, cur_src, prev_src, rows_scr):
    tp = n % 2
    idf = c["idf"]
    t0 = n * ntok
    for b in range(2):
        cur, prv = R["cur"][b], R["prv"][b]
        cn, pn = "rw_cur%d" % b, "rw_prv%d" % b
        Rt, GB = R["R"][b][tp], R["GB"][b][tp]
        rn, gn = "rw_R%d_%d" % (b, tp), "rw_GB%d_%d" % (b, tp)
        P.dma("sp", lambda e, cur=cur, b=b: e.dma_start(out=cur[0:ntok, :], in_=cur_src(b, n)), writes=[cn])
        P.dma("sp", lambda e, prv=prv, b=b: e.dma_start(out=prv[0:ntok, :], in_=prev_src(b, n)), writes=[pn])
        P.op("pool", lambda e, cur=cur, prv=prv: e.tensor_tensor(out=prv[0:ntok, :], in0=prv[0:ntok, :], in1=cur[0:ntok, :],
                                                                op=ALU.subtract), reads=[cn, pn], writes=[pn])
        P.op("pool", lambda e, prv=prv: e.tensor_tensor(out=prv[0:ntok, :], in0=prv[0:ntok, :], in1=R["mu"][0:ntok, :],
                                                       op=ALU.mult), reads=[pn, "rwc_par"], writes=[pn])
        P.op("pool", lambda e, cur=cur, prv=prv: e.tensor_tensor(out=cur[0:ntok, :], in0=cur[0:ntok, :], in1=prv[0:ntok, :],
                                                                op=ALU.add), reads=[cn, pn], writes=[cn])
        trp, lop = R["trp"], R["lop"]
        for (o0, c0, c1, m) in ((0, 192, 224, 32), (128, 224, 256, 32), (256, 256, 352, 96)):
            P.op("pe", lambda e, cur=cur, o0=o0, c0=c0, c1=c1, m=m: e.transpose(
                out=trp[0:m, o0:o0 + ntok], in_=cur[0:ntok, c0:c1], identity=idf[0:ntok, 0:ntok]),
                reads=[cn, "idf"], writes=["OT0"])
        P.op("act", lambda e: e.activation(out=R["twl"][0:32, 0:ntok], in_=trp[0:32, 0:ntok], func=AF.Tanh),
             reads=["OT0"], writes=["rw_twl"])
        P.op("act", lambda e: e.copy(out=R["alT"][0:32, 0:ntok], in_=trp[0:32, 128:128 + ntok]),
             reads=["OT0"], writes=["rw_alT"])
        P.op("act", lambda e: e.activation(out=R["sgl"][0:96, 0:ntok], in_=trp[0:96, 256:256 + ntok], func=AF.Sigmoid),
             reads=["OT0"], writes=["rw_sgl"])
        P.op("pe", lambda e: e.matmul(lop[0:ntok, 0:64], lhsT=R["twl"][0:32, 0:ntok], rhs=R["w2"][:, :], start=True, stop=True),
             reads=["rw_twl", "rwc_w2"], writes=["OT1"])
        P.op("pe", lambda e: e.matmul(lop[0:ntok, 64:128], lhsT=R["alT"][0:32, 0:ntok], rhs=R["a2"][:, :], start=True, stop=True),
             reads=["rw_alT", "rwc_a2"], writes=["OT1"])
        P.op("pe", lambda e: e.matmul(lop[0:ntok, 128:192], lhsT=R["sgl"][0:96, 0:ntok], rhs=R["g2"][:, :], start=True, stop=True),
             reads=["rw_sgl", "rwc_g2"], writes=["OT1"])
        zt, at, kkt, tmp, t1, junk = R["zt"], R["at"], R["kkt"], R["tmp"], R["t1"], R["junk"]
        ssq, rks = R["ssq"], R["rks"]
        P.op("dve", lambda e: e.tensor_tensor(out=zt[0:ntok, :], in0=lop[0:ntok, 0:64], in1=R["w0"][0:ntok, :], op=ALU.add),
             reads=["OT1", "rwc_par"], writes=["rw_zt"])
        P.op("act", lambda e: e.activation(out=zt[0:ntok, :], in_=zt[0:ntok, :], func=AF.Sigmoid),
             reads=["rw_zt"], writes=["rw_zt"])
        P.op("act", lambda e, Rt=Rt: e.activation(out=Rt[0:ntok, 0:64], in_=zt[0:ntok, :], func=AF.Exp, scale=-EXPM05),
             reads=["rw_zt"], writes=[rn])
        P.op("dve", lambda e: e.tensor_tensor(out=at[0:ntok, :], in0=lop[0:ntok, 64:128], in1=R["a0"][0:ntok, :], op=ALU.add),
             reads=["OT1", "rwc_par"], writes=["rw_at"])
        P.op("act", lambda e: e.activation(out=at[0:ntok, :], in_=at[0:ntok, :], func=AF.Sigmoid),
             reads=["rw_at"], writes=["rw_at"])
        P.op("act", lambda e, GB=GB: e.copy(out=GB[0:ntok, 0:64], in_=lop[0:ntok, 128:192]), reads=["OT1"], writes=[gn])
        P.op("pool", lambda e, cur=cur: e.tensor_tensor(out=kkt[0:ntok, :], in0=cur[0:ntok, 64:128], in1=R["kk"][0:ntok, :],
                                                       op=ALU.mult), reads=[cn, "rwc_par"], writes=["rw_kkt"])
        P.op("dve", lambda e: e.scalar_tensor_tensor(out=junk[0:ntok, :], in0=kkt[0:ntok, :], scalar=1.0, in1=kkt[0:ntok, :],
                                                     op0=ALU.mult, op1=ALU.mult, accum_out=ssq[0:ntok, 0:1]),
             reads=["rw_kkt"], writes=["rw_junk", "rw_ssq"])
        P.fence("dve", ["rw_ssq"])
        P.op("act", lambda e: e.activation(out=ssq[0:ntok, :], in_=ssq[0:ntok, :], func=AF.Sqrt), reads=["rw_ssq"], writes=["rw_ssq"])
        P.op("dve", lambda e: e.tensor_scalar_max(out=ssq[0:ntok, :], in0=ssq[0:ntok, :], scalar1=1e-12),
             reads=["rw_ssq"], writes=["rw_ssq"])
        P.op("dve", lambda e: e.reciprocal(out=ssq[0:ntok, :], in_=ssq[0:ntok, :]), reads=["rw_ssq"], writes=["rw_ssq"])
        P.op("pool", lambda e: e.tensor_scalar_mul(out=kkt[0:ntok, :], in0=kkt[0:ntok, :], scalar1=ssq[0:ntok, 0:1]),
             reads=["rw_kkt", "rw_ssq"], writes=["rw_kkt"])
        P.op("pool", lambda e, Rt=Rt: e.tensor_single_scalar(out=Rt[0:ntok, 64:128], in_=kkt[0:ntok, :], scalar=-1.0, op=ALU.mult),
             reads=["rw_kkt"], writes=[rn])
        P.op("pool", lambda e, Rt=Rt: e.tensor_tensor(out=Rt[0:ntok, 128:192], in0=kkt[0:ntok, :], in1=at[0:ntok, :], op=ALU.mult),
             reads=["rw_kkt", "rw_at"], writes=[rn])
        P.op("pool", lambda e: e.tensor_tensor(out=tmp[0:ntok, :], in0=at[0:ntok, :], in1=R["ka"][0:ntok, :], op=ALU.mult),
             reads=["rw_at", "rwc_par"], writes=["rw_tmp"])
        P.op("pool", lambda e: e.tensor_tensor(out=tmp[0:ntok, :], in0=tmp[0:ntok, :], in1=R["omk"][0:ntok, :], op=ALU.add),
             reads=["rw_tmp", "rwc_omk"], writes=["rw_tmp"])
        P.op("pool", lambda e, cur=cur, Rt=Rt: e.tensor_tensor(out=Rt[0:ntok, 192:256], in0=cur[0:ntok, 64:128], in1=tmp[0:ntok, :],
                                                              op=ALU.mult), reads=[cn, "rw_tmp"], writes=[rn])
        P.op("act", lambda e, cur=cur, Rt=Rt: e.copy(out=Rt[0:ntok, 256:320], in_=cur[0:ntok, 0:64]), reads=[cn], writes=[rn])
        P.op("pool", lambda e, cur=cur, Rt=Rt: e.tensor_tensor(out=t1[0:ntok, :], in0=cur[0:ntok, 0:64], in1=Rt[0:ntok, 192:256],
                                                              op=ALU.mult), reads=[cn, rn], writes=["rw_t1"])
        P.op("dve", lambda e: e.scalar_tensor_tensor(out=junk[0:ntok, :], in0=t1[0:ntok, :], scalar=1.0, in1=R["rk"][0:ntok, :],
                                                     op0=ALU.mult, op1=ALU.mult, accum_out=rks[0:ntok, 0:1]),
             reads=["rw_t1", "rwc_par"], writes=["rw_junk", "rw_rks"])
        P.op("pool", lambda e, cur=cur, GB=GB: e.tensor_scalar_mul(out=GB[0:ntok, 64:128], in0=cur[0:ntok, 128:192],
                                                                  scalar1=rks[0:ntok, 0:1]),
             reads=[cn, "rw_rks"], writes=[gn])
        P.op("act", lambda e, cur=cur, b=b: e.copy(out=R["VV"][tp][0:ntok, b * 64:(b + 1) * 64], in_=cur[0:ntok, 128:192]),
             reads=[cn], writes=["rw_VV%d" % tp])
        R3 = R["R3"][b][tp]
        r3n = "rw_R3_%d_%d" % (b, tp)
        r1, r2 = R["r1"], R["r2"]
        P.op("act", lambda e, Rt=Rt, R3=R3: e.copy(out=R3[0:ntok, 0, :], in_=Rt[0:ntok, :]), reads=[rn], writes=[r3n])
        P.op("pool", lambda e, Rt=Rt, R3=R3: e.tensor_tensor(out=r1[0:ntok, :], in0=Rt[0:ntok, :], in1=R3[0:ntok, 0, :],
                                                            op=ALU.subtract), reads=[rn, r3n], writes=["rw_r1"])
        P.op("act", lambda e, R3=R3: e.copy(out=R3[0:ntok, 1, :], in_=r1[0:ntok, :]), reads=["rw_r1"], writes=[r3n])
        P.op("pool", lambda e, R3=R3: e.tensor_tensor(out=r2[0:ntok, :], in0=r1[0:ntok, :], in1=R3[0:ntok, 1, :],
                                                     op=ALU.subtract), reads=["rw_r1", r3n], writes=["rw_r2"])
        P.op("act", lambda e, R3=R3: e.copy(out=R3[0:ntok, 2, :], in_=r2[0:ntok, :]), reads=["rw_r2"], writes=[r3n])
        P.dma("sp", lambda e, R3=R3, b=b: e.dma_start(out=rows_scr[b, :, t0:t0 + ntok, :].rearrange("p t c -> t p c"),
                                                      in_=R3[0:ntok, :, :]),
              reads=[r3n], writes=["rows%d_%d" % (b, tp)])
    P.op("pe", lambda e: e.transpose(out=R["vtp"][:, 0:ntok], in_=R["VV"][tp][0:ntok, :], identity=idf[0:ntok, 0:ntok]),
         reads=["rw_VV%d" % tp, "idf"], writes=["cl_ps"])
    P.op("act", lambda e: e.copy(out=R["vT"][tp][:, 0:ntok], in_=R["vtp"][:, 0:ntok]), reads=["cl_ps"], writes=["rw_vT%d" % tp])


def _rw_scan(P, c, R, n, ntok, rows_scr, q=None):
    P.nosame = {"dve"}
    R["q"] = q if q is not None else []
    R["per"] = -(-len(R["q"]) // max(1, ntok - 8))
    _rw_scan_body(P, c, R, n, ntok, rows_scr)
    P.nosame = set()
    P.drain(R["q"], len(R["q"]))


def _rw_scan_body(P, c, R, n, ntok, rows_scr):
    tp = n % 2
    t0 = n * ntok
    S, stmp, sk = R["S"], R["stmp"], R["sk"]
    vT, yT = R["vT"][tp], R["yT"][tp]
    vn, yn = "rw_vT%d" % tp, "rw_yT%d" % tp
    for blk in range(0, ntok, 16):
        nb = min(16, ntok - blk)
        rb = R["k"] % 2
        R["k"] += 1
        rowbuf = R["rowbuf"][rb]
        P.dma("sp", lambda e, rowbuf=rowbuf, blk=blk, nb=nb: e.dma_start(
            out=rowbuf[0:6, 0:nb * 320].rearrange("q (s c) -> q s c", c=320),
            in_=rows_scr[:, :, t0 + blk:t0 + blk + nb, :].rearrange("b p s c -> (b p) s c")),
            reads=["rows0_%d" % tp, "rows1_%d" % tp], writes=["rw_rowbuf%d" % rb])
        for s in range(nb):
            pb = R["step"] % 2
            R["step"] += 1
            rowp = R["rowp"][pb]
            pn = "sT%d" % pb
            t = blk + s
            P.op("pe", lambda e, rowp=rowp, rowbuf=rowbuf, s=s: e.matmul(
                rowp[:, 0:320], lhsT=R["selb"][0:6, :], rhs=rowbuf[0:6, s * 320:(s + 1) * 320], start=True, stop=True),
                reads=["rw_rowbuf%d" % rb, "rwc_selb"], writes=[pn])
            P.op("dve", lambda e, rowp=rowp: e.scalar_tensor_tensor(
                out=stmp[:, :], in0=S[:, :], scalar=1.0, in1=rowp[:, 64:128], op0=ALU.mult, op1=ALU.mult,
                accum_out=sk[:, 0:1]), reads=["rw_S", pn], writes=["rw_stmp", "rw_sk"])
            P.op("dve", lambda e, rowp=rowp: e.tensor_tensor(out=S[:, :], in0=S[:, :], in1=rowp[:, 0:64], op=ALU.mult),
                 reads=["rw_S", pn], writes=["rw_S"])
            P.op("dve", lambda e, rowp=rowp: e.scalar_tensor_tensor(
                out=S[:, :], in0=rowp[:, 128:192], scalar=sk[:, 0:1], in1=S[:, :], op0=ALU.mult, op1=ALU.add),
                reads=["rw_S", "rw_sk", pn], writes=["rw_S"])
            P.op("dve", lambda e, rowp=rowp, t=t: e.scalar_tensor_tensor(
                out=S[:, :], in0=rowp[:, 192:256], scalar=vT[:, t:t + 1], in1=S[:, :], op0=ALU.mult, op1=ALU.add),
                reads=["rw_S", vn, pn], writes=["rw_S"])
            P.op("dve", lambda e, rowp=rowp, t=t: e.scalar_tensor_tensor(
                out=stmp[:, :], in0=S[:, :], scalar=1.0, in1=rowp[:, 256:320], op0=ALU.mult, op1=ALU.mult,
                accum_out=yT[:, t:t + 1]), reads=["rw_S", pn], writes=["rw_stmp", yn])
            if R["q"]:
                P.drain(R["q"], R["per"])


def _rw_post(P, c, R, n, ntok, out_dst):
    tp = n % 2
    idf = c["idf"]
    ytp = R["ytp"]
    cen, ob, junk, mean, var = R["cen"], R["ob"], R["junk"], R["mean"], R["var"]
    ysb = R["ysb"]
    P.op("pe", lambda e: e.transpose(out=ytp[0:ntok, 0:128], in_=R["yT"][tp][:, 0:ntok], identity=idf[:, :]),
         reads=["rw_yT%d" % tp, "idf"], writes=["tot_ps"])
    P.op("act", lambda e: e.copy(out=ysb[0:ntok, :], in_=ytp[0:ntok, 0:128]), reads=["tot_ps"], writes=["rw_ysb"])
    for b in range(2):
        GB = R["GB"][b][tp]
        gn = "rw_GB%d_%d" % (b, tp)
        ysl = ysb[0:ntok, b * 64:(b + 1) * 64]
        P.op("dve", lambda e, ysl=ysl: e.tensor_reduce(out=mean[0:ntok, :], in_=ysl, axis=AX.X, op=ALU.add),
             reads=["rw_ysb"], writes=["rw_mean"])
        P.op("pool", lambda e: e.tensor_single_scalar(out=mean[0:ntok, :], in_=mean[0:ntok, :], scalar=1.0 / 64, op=ALU.mult),
             reads=["rw_mean"], writes=["rw_mean"])
        P.op("pool", lambda e, ysl=ysl: e.tensor_scalar(out=cen[0:ntok, :], in0=ysl, scalar1=mean[0:ntok, 0:1], scalar2=0.0,
                                                        op0=ALU.subtract, op1=ALU.add),
             reads=["rw_ysb", "rw_mean"], writes=["rw_cen"])
        P.op("dve", lambda e: e.scalar_tensor_tensor(out=junk[0:ntok, :], in0=cen[0:ntok, :], scalar=1.0, in1=cen[0:ntok, :],
                                                     op0=ALU.mult, op1=ALU.mult, accum_out=var[0:ntok, 0:1]),
             reads=["rw_cen"], writes=["rw_junk", "rw_var"])
        P.fence("dve", ["rw_var"])
        P.op("act", lambda e: e.activation(out=var[0:ntok, :], in_=var[0:ntok, :], func=AF.Sqrt, bias=64e-5, scale=1.0 / 64),
             reads=["rw_var"], writes=["rw_var"])
        P.op("dve", lambda e: e.reciprocal(out=var[0:ntok, :], in_=var[0:ntok, :]), reads=["rw_var"], writes=["rw_var"])
        P.op("pool", lambda e: e.tensor_scalar_mul(out=cen[0:ntok, :], in0=cen[0:ntok, :], scalar1=var[0:ntok, 0:1]),
             reads=["rw_cen", "rw_var"], writes=["rw_cen"])
        P.op("pool", lambda e: e.tensor_tensor(out=cen[0:ntok, :], in0=cen[0:ntok, :], in1=R["lnw"][0:ntok, :], op=ALU.mult),
             reads=["rw_cen", "rwc_par"], writes=["rw_cen"])
        P.op("pool", lambda e: e.tensor_tensor(out=cen[0:ntok, :], in0=cen[0:ntok, :], in1=R["lnb"][0:ntok, :], op=ALU.add),
             reads=["rw_cen", "rwc_par"], writes=["rw_cen"])
        P.op("pool", lambda e, GB=GB: e.tensor_tensor(out=cen[0:ntok, :], in0=cen[0:ntok, :], in1=GB[0:ntok, 64:128], op=ALU.add),
             reads=["rw_cen", gn], writes=["rw_cen"])
        P.op("pool", lambda e, GB=GB: e.tensor_tensor(out=ob[0:ntok, :], in0=cen[0:ntok, :], in1=GB[0:ntok, 0:64], op=ALU.mult),
             reads=["rw_cen", gn], writes=["rw_ob"])
        P.dma("sp", lambda e, b=b: e.dma_start(out=out_dst(b, n), in_=ob[0:ntok, :]), reads=["rw_ob"], writes=["rw_out"])


def _rw_pair(P, c, R, ntiles, ntok, cur_src, prev_src, rows_scr, S0_src, out_dst, ST_dst):
    S = R["S"]
    if S0_src is None:
        P.op("dve", lambda e: e.memset(S[:, :], 0.0), writes=["rw_S"])
    else:
        P.dma("sp", lambda e: e.dma_start(out=S[:, :], in_=S0_src), writes=["rw_S"])
    _rw_prep(P, c, R, 0, ntok, cur_src, prev_src, rows_scr)
    pend = []
    for n in range(ntiles):
        P.defer = pend
        if n + 1 < ntiles:
            _rw_prep(P, c, R, n + 1, ntok, cur_src, prev_src, rows_scr)
        P.defer = None
        _rw_scan(P, c, R, n, ntok, rows_scr, pend)
        pend = []
        P.defer = pend
        _rw_post(P, c, R, n, ntok, out_dst)
        P.defer = None
    P.drain(pend, len(pend))
    P.dma("sp", lambda e: e.dma_start(out=ST_dst, in_=S[:, :]), reads=["rw_S"], writes=["rw_STout"])


def build_phase2(n_prompt=2, n_sample=NSEQ_S, ngroups=32, rw_tiles=128, rw_pairs=8, do_fox=True):
    nc = bass.Bass("TRN2", target_bir_lowering=False)
    qTp = _din(nc, "qTp", [2, 64, TP])
    kTp = _din(nc, "kTp", [2, 64, TP])
    vp = _din(nc, "vp", [2, 128, 128, 64])
    lfp = _din(nc, "lfp", [2, 128, 128])
    qTs = _din(nc, "qTs", [NSEQ_S, 64, 16])
    kTs = _din(nc, "kTs", [NSEQ_S, 64, TS])
    vs = _din(nc, "vs", [NSEQ_S, 128, 17, 64])
    lfs = _din(nc, "lfs", [NSEQ_S, 128, 17])
    op_ = _dout(nc, "o_p", [2, TP, 64])
    os_ = _dout(nc, "o_s", [NSEQ_S, 16, 64])
    curp = _din(nc, "rw_curp", [2, TP, 352])
    prvp = _din(nc, "rw_prvp", [2, TP, 352])
    curs = _din(nc, "rw_curs", [NSEQ_S, 16, 352])
    prvs = _din(nc, "rw_prvs", [NSEQ_S, 16, 352])
    S0s = _din(nc, "rw_S0s", [8, 128, 64])
    rwp = _dout(nc, "rw_p", [2, TP, 64])
    rws = _dout(nc, "rw_s", [NSEQ_S, 16, 64])
    STp = _dout(nc, "ST_p", [128, 64])
    STs = _dout(nc, "ST_s", [8, 128, 64])
    rows_p = nc.dram_tensor("rows_p", [2, 3, TP, 320], BF16).ap()
    rows_s = nc.dram_tensor("rows_s", [8, 2, 3, 16, 320], BF16).ap()
    P = Prog(nc)
    c = _fox_consts(P, nc)
    B = _fox_bufs(P)
    if do_fox:
        for b in range(n_prompt):
            groups = [(512 * g, 512, 4 * g + 4, 4 * g, 4 * g + 2) for g in range(ngroups)]

            def out_fn(P, osb, q0, nq, nqi, wmax, b=b):
                P.dma("sp", lambda e: e.dma_start(
                    out=op_[b, q0:q0 + nq, :].rearrange("(a p) d -> p a d", p=128), in_=osb[:, 0:nqi, :]),
                    reads=["osb"], writes=["o_out"])
            _fox_seq(P, c, B, "p%d" % b, 128, qTp[b], TP, kTp[b], vp[b], lfp[b], groups, out_fn)
        for s in range(n_sample):
            groups = [(0, 16, 17, 16, 16)]

            def out_fn(P, osb, q0, nq, nqi, wmax, s=s):
                P.dma("sp", lambda e: e.dma_start(out=os_[s, :, :], in_=osb[0:16, 0, :]), reads=["osb"], writes=["o_out"])
            _fox_seq(P, c, B, "s%d" % s, 17, qTs[s], 16, kTs[s], vs[s], lfs[s], groups, out_fn)
    R = _rw_setup(P, nc, B)
    if rw_tiles > 0:
        _rw_pair(P, c, R, rw_tiles, 128,
                 lambda b, n: curp[b, n * 128:(n + 1) * 128, :], lambda b, n: prvp[b, n * 128:(n + 1) * 128, :],
                 rows_p, None, lambda b, n: rwp[b, n * 128:(n + 1) * 128, :], STp)
    for pr in range(rw_pairs):
        _rw_pair(P, c, R, 1, 16,
                 lambda b, n, pr=pr: curs[2 * pr + b, :, :], lambda b, n, pr=pr: prvs[2 * pr + b, :, :],
                 rows_s[pr], S0s[pr], lambda b, n, pr=pr: rws[2 * pr + b, :, :], STs[pr])
    P.emit()
    return nc


def rw_inputs(pp, psm, state_rwkv, state_shift, prm):
    ppb = pp.reshape(2, TP, IN_COLS)[:, :, FOX_COLS:]
    pss = psm.reshape(NSEQ_S, 16, IN_COLS)[:, :, FOX_COLS:]
    prev_p = np.zeros_like(ppb)
    prev_p[:, 1:] = ppb[:, :-1]
    prev_s = np.empty_like(pss)
    prev_s[:, 1:] = pss[:, :-1]
    prev_s[:, 0] = state_shift[:, 0, :]
    maps = []
    sel = np.zeros((6, 128), np.float32)
    sel[0:3, :64] = 1.0
    sel[3:6, 64:] = 1.0
    for h in range(NCORE):
        cols = np.concatenate([np.arange(h * 64, (h + 1) * 64), 512 + np.arange(h * 64, (h + 1) * 64),
                               1024 + np.arange(h * 64, (h + 1) * 64), np.arange(1536, 1696)])
        hs = slice(h * 64, (h + 1) * 64)
        par = np.concatenate([prm["rwkv_mu"][cols], prm["rwkv_w0"][hs], prm["rwkv_a0"][hs], prm["rwkv_k_k"][hs],
                              prm["rwkv_k_a"][hs], prm["rwkv_r_k"][h], prm["rwkv_lnx_w"][hs], prm["rwkv_lnx_b"][hs]])
        m = {
            "rw_curp": np.ascontiguousarray(ppb[:, :, cols]), "rw_prvp": np.ascontiguousarray(prev_p[:, :, cols]),
            "rw_curs": np.ascontiguousarray(pss[:, :, cols]), "rw_prvs": np.ascontiguousarray(prev_s[:, :, cols]),
            "rw_S0s": np.ascontiguousarray(state_rwkv[:, h].reshape(8, 128, 64)),
            "rw_par": np.ascontiguousarray(np.broadcast_to(par[None, :], (128, NPAR))).astype(np.float32),
            "rw_w2": np.ascontiguousarray(prm["rwkv_w2"][:, hs]), "rw_a2": np.ascontiguousarray(prm["rwkv_a2"][:, hs]),
            "rw_g2": np.ascontiguousarray(prm["rwkv_g2"][:, hs]), "rw_sel": sel,
        }
        maps.append(m)
    return maps


STAGE = 99


def build_phase3(NT, nslots=128):
    nc = bass.Bass("TRN2", target_bir_lowering=False)
    x = _din(nc, "x", [NT * 128, D])
    mixT = _din(nc, "mixT", [NT, 128, 8, 128])
    wout = _din(nc, "w_out", [D, D])
    wq = _din(nc, "w_q", [D, D])
    skT_d = _din(nc, "skT", [64, 16 * 128])
    g1 = _din(nc, "gffn", [128, D])
    g2 = _din(nc, "gfin", [128, D])
    identd = _din(nc, "ident", [128, 128])
    iota_d = _din(nc, "iota", [128, 256])
    pu = _din(nc, "peer_u", [16384, D])
    pv = _din(nc, "peer_v", [16384, D])
    y = _dout(nc, "y", [NT * 128, D])
    puv16 = nc.dram_tensor("puv16", [16384, 2 * D], BF16).ap()
    P = Prog(nc)
    cst = [P.sb("cst%d" % i, [128, 4096], F32) for i in range(2)]
    cbf = [P.sb("cbf%d" % i, [128, 4096], BF16) for i in range(2)]
    ck = 0
    if nslots > 0:
        for (src, half) in ((pu, 0), (pv, 1)):
            sv_ = src.rearrange("(p r) d -> p (r d)", p=128)
            dv_ = puv16.rearrange("(p r) d -> p r d", p=128)
            for pc in range(32):
                i2 = ck % 2
                P.dma("sp", lambda e, i2=i2, sv_=sv_, pc=pc: e.dma_start(out=cst[i2][:, :], in_=sv_[:, pc * 4096:(pc + 1) * 4096]),
                      writes=["cst%d" % i2])
                eng = ("act", "pool", "dve")[ck % 3]
                if eng == "act":
                    P.op("act", lambda e, i2=i2: e.copy(out=cbf[i2][:, :], in_=cst[i2][:, :]), reads=["cst%d" % i2], writes=["cbf%d" % i2])
                else:
                    P.op(eng, lambda e, i2=i2: e.tensor_copy(out=cbf[i2][:, :], in_=cst[i2][:, :]), reads=["cst%d" % i2],
                         writes=["cbf%d" % i2])
                P.dma("sp", lambda e, i2=i2, dv_=dv_, pc=pc, half=half: e.dma_start(
                    out=dv_[:, pc * 4:(pc + 1) * 4, half * D:(half + 1) * D], in_=cbf[i2][:, :].rearrange("p (r d) -> p r d", d=D)),
                    reads=["cbf%d" % i2], writes=["puv16"])
                ck += 1
    wst = P.sb("wst", [128, D], F32)
    wo_bf = P.sb("wo_bf", [128, 8, D], BF16)
    wq_bf = P.sb("wq_bf", [128, 8, D], BF16)
    sk_bf = P.sb("sk_bf", [64, 16, 128], BF16)
    g1s = P.sb("g1s", [128, D], F32)
    g2s = P.sb("g2s", [128, D], F32)
    id_f = P.sb("id_f", [128, 128], F32)
    id_b = P.sb("id_b", [128, 128], BF16)
    P.dma("sp", lambda e: e.dma_start(out=g1s[:, :], in_=g1), writes=["g1"])
    P.dma("sp", lambda e: e.dma_start(out=g2s[:, :], in_=g2), writes=["g2"])
    P.dma("sp", lambda e: e.dma_start(out=id_f[:, :], in_=identd), writes=["idf"])
    P.op("dve", lambda e: e.tensor_copy(out=id_b[:, :], in_=id_f[:, :]), reads=["idf"], writes=["idb"])
    for (src, dst, nm) in ((wout, wo_bf, "wo"), (wq, wq_bf, "wq")):
        for dc in range(8):
            P.dma("sp", lambda e, src=src, dc=dc: e.dma_start(out=wst[:, :], in_=src[dc * 128:(dc + 1) * 128, :]), writes=["wst"])
            P.op("act", lambda e, dst=dst, dc=dc: e.copy(out=dst[:, dc, :], in_=wst[:, :]), reads=["wst"], writes=[nm])
    for hf in range(2):
        P.dma("sp", lambda e, hf=hf: e.dma_start(out=wst[0:64, :], in_=skT_d[:, hf * 1024:(hf + 1) * 1024]), writes=["wst"])
        P.op("act", lambda e, hf=hf: e.copy(out=sk_bf[:, hf * 8:(hf + 1) * 8, :],
                                            in_=wst[0:64, :].rearrange("p (h n) -> p h n", h=8)), reads=["wst"], writes=["sk"])

    xt = P.sb("xt", [128, D], F32)
    mt = P.sb("mt", [128, 8, 128], F32)
    mtb = P.sb("mtb", [128, 8, 128], BF16)
    x2 = P.sb("x2", [128, D], F32)
    xn = P.sb("xn", [128, D], F32)
    xnb = P.sb("xnb", [128, D], BF16)
    xnT = P.sb("xnT", [128, D], BF16)
    qT = P.sb("qT", [64, 16 * 128], BF16)
    junkb = P.sb("junkb", [128, D], BF16)
    junkD = P.sb("junkD", [128, D], F32)
    ss = P.sb("ss", [128, 1], F32)
    s_sb = P.sb("s_sb", [128, 16, 128], F32)
    s2 = P.sb("s2", [128, 128], F32)
    sv = P.sb("sv", [128, 16, 16], F32)
    si = P.sb("si", [128, 16, 16], U32)
    sif = P.sb("sif", [128, 16, 16], F32)
    cand = P.sb("cand", [128, 8, 256], F32)
    cidx = P.sb("cidx", [128, 8, 256], F32)
    c2 = P.sb("c2", [128, 256], F32)
    j256 = P.sb("j256", [128, 256], F32)
    tv = P.sb("tv", [128, 8, 16], F32)
    pi = P.sb("pi", [128, 8, 16], U32)
    pif = P.sb("pif", [128, 8, 16], F32)
    iota = P.sb("iota", [128, 256], F32)
    P.dma("sp", lambda e: e.dma_start(out=iota[:, :], in_=iota_d), writes=["iota"])
    negm = P.sb("negm", [128, 8], F32)
    gt = P.sb("gt", [128, 8, 16], F32)
    Z = P.sb("Z", [128, 8], F32)
    ef = P.sb("ef", [128, 128], F32)
    ei = P.sb("ei", [128, 128], I32)
    apre = P.sb("apre", [128, 128], F32)
    coef = P.sb("coef", [128, 128], F32)
    acc = P.sb("acc", [128, D], F32)
    yo = P.sb("yo", [128, D], F32)
    NG = 8
    gb = [P.sb("gbuf%d" % i, [128, 2 * D], BF16) for i in range(NG)]
    gl = P.sb("gl", [128, 128], F32)
    psA = P.ps("psA", [128, D], F32)
    psT = P.ps("psT", [128, D], BF16)
    psS = P.ps("psS", [128, 16, 128], F32)
    gk = 0

    def rms(src, srcn, dst, dstn, gs, gn):
        P.op("act", lambda e: e.activation(out=junkb[:, :], in_=src[:, :], func=AF.Square, accum_out=ss[:, 0:1]),
             reads=[srcn], writes=["junkb", "ss"])
        P.op("act", lambda e: e.activation(out=ss[:, :], in_=ss[:, :], func=AF.Sqrt, bias=1e-6, scale=1.0 / D),
             reads=["ss"], writes=["ss"])
        P.op("dve", lambda e: e.reciprocal(out=ss[:, :], in_=ss[:, :]), reads=["ss"], writes=["ss"])
        P.op("dve", lambda e: e.scalar_tensor_tensor(out=dst[:, :], in0=src[:, :], scalar=ss[:, 0:1], in1=gs[:, :],
                                                     op0=ALU.mult, op1=ALU.mult), reads=[srcn, "ss", gn], writes=[dstn])

    for i in range(NT):
        P.dma("sp", lambda e, i=i: e.dma_start(out=xt[:, :], in_=x[i * 128:(i + 1) * 128, :]), writes=["xt"])
        P.dma("sp", lambda e, i=i: e.dma_start(out=mt[:, :, :], in_=mixT[i]), writes=["mt"])
        P.op("act", lambda e: e.copy(out=mtb[:, :, :], in_=mt[:, :, :]), reads=["mt"], writes=["mtb"])
        for g in range(2):
            for kc in range(8):
                P.op("pe", lambda e, g=g, kc=kc: e.matmul(psA[:, g * 512:(g + 1) * 512], lhsT=mtb[:, kc, :],
                                                          rhs=wo_bf[:, kc, g * 512:(g + 1) * 512], start=(kc == 0), stop=(kc == 7)),
                     reads=["mtb", "wo"], writes=["psA%d" % g])
        for g in range(2):
            P.op("dve", lambda e, g=g: e.tensor_tensor(out=x2[:, g * 512:(g + 1) * 512], in0=psA[:, g * 512:(g + 1) * 512],
                                                       in1=xt[:, g * 512:(g + 1) * 512], op=ALU.add),
                 reads=["psA%d" % g, "xt"], writes=["x2"])
        rms(x2, "x2", xn, "xn", g1s, "g1")
        P.op("act", lambda e: e.copy(out=xnb[:, :], in_=xn[:, :]), reads=["xn"], writes=["xnb"])
        for dc in range(8 if STAGE >= 2 else 0):
            P.op("pe", lambda e, dc=dc: e.transpose(out=psT[:, dc * 128:(dc + 1) * 128], in_=xnb[:, dc * 128:(dc + 1) * 128],
                                                    identity=id_b[:, :]), reads=["xnb", "idb"], writes=["psT"])
        P.op("act", lambda e: e.copy(out=xnT[:, :], in_=psT[:, :]), reads=["psT"], writes=["xnT"])
        for hc in range(16 if STAGE >= 3 else 0):
            bank = hc % 2
            for dc in range(8):
                P.op("pe", lambda e, hc=hc, dc=dc, bank=bank: e.matmul(
                    psA[0:64, bank * 512:bank * 512 + 128], lhsT=wq_bf[:, dc, hc * 64:(hc + 1) * 64],
                    rhs=xnT[:, dc * 128:(dc + 1) * 128], start=(dc == 0), stop=(dc == 7)),
                    reads=["xnT", "wq"], writes=["psA%d" % bank])
            if hc % 2 == 0:
                P.op("act", lambda e, hc=hc, bank=bank: e.copy(out=qT[0:64, hc * 128:(hc + 1) * 128],
                                                               in_=psA[0:64, bank * 512:bank * 512 + 128]),
                     reads=["psA%d" % bank], writes=["qT"])
            else:
                P.op("dve", lambda e, hc=hc, bank=bank: e.tensor_copy(out=qT[0:64, hc * 128:(hc + 1) * 128],
                                                                      in_=psA[0:64, bank * 512:bank * 512 + 128]),
                     reads=["psA%d" % bank], writes=["qT"])
        for hc in range(16 if STAGE >= 4 else 0):
            P.op("pe", lambda e, hc=hc: e.matmul(psS[:, hc, :], lhsT=qT[0:64, hc * 128:(hc + 1) * 128],
                                                 rhs=sk_bf[0:64, hc, :], start=True, stop=True),
                 reads=["qT", "sk"], writes=["psS"])
        for bk in range(4):
            if bk % 2 == 0:
                P.op("act", lambda e, bk=bk: e.copy(out=s_sb[:, 4 * bk:4 * bk + 4, :], in_=psS[:, 4 * bk:4 * bk + 4, :]),
                     reads=["psS"], writes=["s_sb"])
            else:
                P.op("dve", lambda e, bk=bk: e.tensor_copy(out=s_sb[:, 4 * bk:4 * bk + 4, :], in_=psS[:, 4 * bk:4 * bk + 4, :]),
                     reads=["psS"], writes=["s_sb"])
        for hc in range(16 if STAGE >= 5 else 0):
            P.op("dve", lambda e, hc=hc: e.max(out=sv[:, hc, 0:8], in_=s_sb[:, hc, :]), reads=["s_sb"], writes=["sv"])
            P.op("dve", lambda e, hc=hc: e.max_index(out=si[:, hc, 0:8], in_max=sv[:, hc, 0:8], in_values=s_sb[:, hc, :]),
                 reads=["s_sb", "sv"], writes=["si"])
            P.op("dve", lambda e, hc=hc: e.match_replace(out=s2[:, :], in_to_replace=sv[:, hc, 0:8], in_values=s_sb[:, hc, :],
                                                         imm_value=-1e30), reads=["s_sb", "sv"], writes=["s2"])
            P.op("dve", lambda e, hc=hc: e.max(out=sv[:, hc, 8:16], in_=s2[:, :]), reads=["s2"], writes=["sv"])
            P.op("dve", lambda e, hc=hc: e.max_index(out=si[:, hc, 8:16], in_max=sv[:, hc, 8:16], in_values=s2[:, :]),
                 reads=["s2", "sv"], writes=["si"])
        P.op("dve", lambda e: e.tensor_copy(out=sif[:, :, :], in_=si[:, :, :]), reads=["si"], writes=["sif"])
        for h in range(8 if STAGE >= 6 else 0):
            P.op("dve", lambda e, h=h: e.tensor_single_scalar(out=sif[:, 2 * h, :], in_=sif[:, 2 * h, :], scalar=128.0, op=ALU.mult),
                 reads=["sif"], writes=["sif"])
            P.op("dve", lambda e, h=h: e.tensor_tensor(
                out=cand[:, h, :].rearrange("p (a b) -> p a b", a=16),
                in0=sv[:, 2 * h, :].unsqueeze(2).to_broadcast([128, 16, 16]),
                in1=sv[:, 2 * h + 1, :].unsqueeze(1).to_broadcast([128, 16, 16]), op=ALU.add), reads=["sv"], writes=["cand"])
            P.op("dve", lambda e, h=h: e.tensor_tensor(
                out=cidx[:, h, :].rearrange("p (a b) -> p a b", a=16),
                in0=sif[:, 2 * h, :].unsqueeze(2).to_broadcast([128, 16, 16]),
                in1=sif[:, 2 * h + 1, :].unsqueeze(1).to_broadcast([128, 16, 16]), op=ALU.add), reads=["sif"], writes=["cidx"])
            P.op("dve", lambda e, h=h: e.max(out=tv[:, h, 0:8], in_=cand[:, h, :]), reads=["cand"], writes=["tv"])
            P.op("dve", lambda e, h=h: e.max_index(out=pi[:, h, 0:8], in_max=tv[:, h, 0:8], in_values=cand[:, h, :]),
                 reads=["cand", "tv"], writes=["pi"])
            P.op("dve", lambda e, h=h: e.match_replace(out=c2[:, :], in_to_replace=tv[:, h, 0:8], in_values=cand[:, h, :],
                                                       imm_value=-1e30), reads=["cand", "tv"], writes=["c2"])
            P.op("dve", lambda e, h=h: e.max(out=tv[:, h, 8:16], in_=c2[:, :]), reads=["c2"], writes=["tv"])
            P.op("dve", lambda e, h=h: e.max_index(out=pi[:, h, 8:16], in_max=tv[:, h, 8:16], in_values=c2[:, :]),
                 reads=["c2", "tv"], writes=["pi"])
            P.op("dve", lambda e, h=h: e.tensor_copy(out=pif[:, h, :], in_=pi[:, h, :]), reads=["pi"], writes=["pif"])
            for k in range(16):
                P.op("dve", lambda e, h=h, k=k: e.scalar_tensor_tensor(
                    out=j256[:, :], in0=iota[:, :], scalar=pif[:, h, k:k + 1], in1=cidx[:, h, :], op0=ALU.is_equal,
                    op1=ALU.mult, accum_out=ef[:, h * 16 + k:h * 16 + k + 1]), reads=["iota", "cidx", "pif"], writes=["j256", "ef"])
        P.op("dve", lambda e: e.tensor_copy(out=ei[:, :], in_=ef[:, :]), reads=["ef"], writes=["ei"])
        P.op("dve", lambda e: e.tensor_single_scalar(out=negm[:, :], in_=tv[:, :, 0], scalar=-1.0, op=ALU.mult),
             reads=["tv"], writes=["negm"])
        for h in range(8 if STAGE >= 7 else 0):
            P.op("act", lambda e, h=h: e.activation(out=gt[:, h, :], in_=tv[:, h, :], func=AF.Exp, bias=negm[:, h:h + 1],
                                                    scale=1.0, accum_out=Z[:, h:h + 1]), reads=["tv", "negm"], writes=["gt", "Z"])
        P.fence("act", ["Z", "gt"])
        P.op("dve", lambda e: e.reciprocal(out=Z[:, :], in_=Z[:, :]), reads=["Z"], writes=["Z"])
        for h in range(8):
            P.op("dve", lambda e, h=h: e.tensor_scalar_mul(out=gt[:, h, :], in0=gt[:, h, :], scalar1=Z[:, h:h + 1]),
                 reads=["gt", "Z"], writes=["gt"])
        gtf = gt[:, :, :].rearrange("p h k -> p (h k)")
        slot_buf = {}

        def vacc(sl):
            r = slot_buf[sl]
            P.op("dve", lambda e, sl=sl: e.tensor_tensor(out=coef[:, sl:sl + 1], in0=gl[:, sl:sl + 1], in1=gtf[:, sl:sl + 1], op=ALU.mult),
                 reads=["gl%d" % sl, "gt"], writes=["coef%d" % sl])
            if sl == 0:
                P.op("dve", lambda e, r=r: e.tensor_scalar_mul(out=acc[:, :], in0=gb[r][:, D:2 * D], scalar1=coef[:, 0:1]),
                     reads=["gbuf%d" % r, "coef0"], writes=["acc"])
            else:
                P.op("dve", lambda e, r=r, sl=sl: e.scalar_tensor_tensor(
                    out=acc[:, :], in0=gb[r][:, D:2 * D], scalar=coef[:, sl:sl + 1], in1=acc[:, :], op0=ALU.mult, op1=ALU.add),
                    reads=["gbuf%d" % r, "coef%d" % sl, "acc"], writes=["acc"])

        LAG = 3
        for sl in range(nslots):
            r = gk % NG
            gk += 1
            slot_buf[sl] = r
            P.dma("pool", lambda e, r=r, sl=sl: e.indirect_dma_start(
                out=gb[r][:, :], out_offset=None, in_=puv16[:, :],
                in_offset=bass.IndirectOffsetOnAxis(ap=ei[:, sl:sl + 1], axis=0)), reads=["ei", "puv16"], writes=["gbuf%d" % r])
            P.op("dve", lambda e, r=r, sl=sl: e.scalar_tensor_tensor(
                out=junkD[:, :], in0=gb[r][:, 0:D], scalar=1.0, in1=xn[:, :], op0=ALU.mult, op1=ALU.mult,
                accum_out=apre[:, sl:sl + 1]), reads=["gbuf%d" % r, "xn"], writes=["junkD", "apre_raw%d" % sl])
            P.fence("dve", ["apre%d" % sl])
            P.op("act", lambda e, sl=sl: e.activation(out=gl[:, sl:sl + 1], in_=apre[:, sl:sl + 1], func=AF.Gelu),
                 reads=["apre%d" % sl], writes=["gl%d" % sl])
            if sl >= LAG:
                vacc(sl - LAG)
        for sl in range(max(0, nslots - LAG), nslots):
            vacc(sl)
        if nslots > 0:
            P.op("dve", lambda e: e.tensor_tensor(out=x2[:, :], in0=x2[:, :], in1=acc[:, :], op=ALU.add),
                 reads=["x2", "acc"], writes=["x2"])
        rms(x2, "x2", yo, "yo", g2s, "g2")
        if nslots == 0:
            P.op("dve", lambda e: e.tensor_copy(out=yo[:, 0:128], in_=ef[:, :]), reads=["ef", "yo"], writes=["yo"])
            P.op("dve", lambda e: e.tensor_copy(out=yo[:, 128:256], in_=gt[:, :, :].rearrange("p h k -> p (h k)")),
                 reads=["gt", "yo"], writes=["yo"])
        P.dma("sp", lambda e, i=i: e.dma_start(out=y[i * 128:(i + 1) * 128, :], in_=yo[:, :]), reads=["yo"], writes=["yout"])
    P.emit()
    return nc


def run_phase3(x_prompt, x_sample, mix_p, mix_s, w_out, norm_ffn_g, peer_w_q, peer_sub_keys, peer_u, peer_v, norm_final_g,
               NT=33, nslots=128):
    xp = np.ascontiguousarray(x_prompt).reshape(-1, D)
    xs = np.ascontiguousarray(x_sample).reshape(-1, D)
    nc = build_phase3(NT, nslots)
    g1 = np.ascontiguousarray(np.broadcast_to(norm_ffn_g.reshape(1, D), (128, D))).astype(np.float32)
    g2 = np.ascontiguousarray(np.broadcast_to(norm_final_g.reshape(1, D), (128, D))).astype(np.float32)
    skT = np.ascontiguousarray(np.transpose(peer_sub_keys, (3, 0, 1, 2)).reshape(64, 16 * 128))
    iota_h = np.ascontiguousarray(np.broadcast_to(np.arange(256, dtype=np.float32)[None, :], (128, 256)))
    in_maps = []
    for c in range(NCORE):
        xc = np.zeros((33 * 128, D), np.float32)
        mc = np.zeros((33 * 128, D), np.float32)
        xc[:4096] = xp[c * 4096:(c + 1) * 4096]
        xc[4096:4128] = xs[c * 32:(c + 1) * 32]
        mc[:4096] = mix_p[c * 4096:(c + 1) * 4096]
        mc[4096:4128] = mix_s[c * 32:(c + 1) * 32]
        mT = np.ascontiguousarray(np.transpose(mc.reshape(33, 128, 8, 128), (0, 3, 2, 1)))
        in_maps.append({"x": xc[:NT * 128], "mixT": mT[:NT], "w_out": np.ascontiguousarray(w_out), "w_q": np.ascontiguousarray(peer_w_q),
                        "skT": skT, "iota": iota_h, "gffn": g1, "gfin": g2, "ident": _ident(), "peer_u": np.ascontiguousarray(peer_u),
                        "peer_v": np.ascontiguousarray(peer_v)})
    res = run_bass_kernel_spmd(nc, in_maps, core_ids=list(range(NCORE)))
    yp = np.concatenate([res.results[c]["y"][:4096] for c in range(NCORE)], axis=0) if NT == 33 else None
    ysm = np.concatenate([res.results[c]["y"][4096:4128] for c in range(NCORE)], axis=0) if NT == 33 else None
    return yp, ysm, res


def kernel(x_prompt, x_sample, cache_fox_k, cache_fox_v, cache_fox_logf, state_rwkv, state_shift,
           norm_mix_g, w_in, fox_b_f, rwkv_mu, rwkv_w0, rwkv_w2, rwkv_a0, rwkv_a2, rwkv_g2,
           rwkv_k_k, rwkv_k_a, rwkv_r_k, rwkv_lnx_w, rwkv_lnx_b, w_out, norm_ffn_g,
           peer_w_q, peer_sub_keys, peer_u, peer_v, norm_final_g):
    f = lambda a: np.asarray(a, dtype=np.float32)
    x_prompt, x_sample = f(x_prompt), f(x_sample)
    pp, psm = run_phase1(x_prompt, x_sample, f(norm_mix_g)[0], f(w_in)[0], f(fox_b_f)[0])
    prm = {"rwkv_mu": f(rwkv_mu)[0], "rwkv_w0": f(rwkv_w0)[0], "rwkv_w2": f(rwkv_w2)[0], "rwkv_a0": f(rwkv_a0)[0],
           "rwkv_a2": f(rwkv_a2)[0], "rwkv_g2": f(rwkv_g2)[0], "rwkv_k_k": f(rwkv_k_k)[0], "rwkv_k_a": f(rwkv_k_a)[0],
           "rwkv_r_k": f(rwkv_r_k)[0], "rwkv_lnx_w": f(rwkv_lnx_w)[0], "rwkv_lnx_b": f(rwkv_lnx_b)[0]}
    maps = fox_inputs(pp, psm, f(cache_fox_k)[0], f(cache_fox_v)[0], f(cache_fox_logf)[0])
    rmaps = rw_inputs(pp, psm, f(state_rwkv)[0], f(state_shift)[0], prm)
    for m, r in zip(maps, rmaps):
        m.update(r)
    nc2 = build_phase2()
    res2 = run_bass_kernel_spmd(nc2, maps, core_ids=list(range(NCORE)))
    del maps, rmaps
    R2 = res2.results
    mix_p = np.empty((2, TP, 1024), np.float32)
    mix_s = np.empty((NSEQ_S, 16, 1024), np.float32)
    S_p = np.empty((1, 2, 8, 64, 64), np.float32)
    S_s = np.empty((1, NSEQ_S, 8, 64, 64), np.float32)
    for h in range(NCORE):
        mix_p[:, :, h * 64:(h + 1) * 64] = R2[h]["o_p"]
        mix_p[:, :, 512 + h * 64:512 + (h + 1) * 64] = R2[h]["rw_p"]
        mix_s[:, :, h * 64:(h + 1) * 64] = R2[h]["o_s"]
        mix_s[:, :, 512 + h * 64:512 + (h + 1) * 64] = R2[h]["rw_s"]
        S_p[0, :, h] = R2[h]["ST_p"].reshape(2, 64, 64)
        S_s[0, :, h] = R2[h]["ST_s"].reshape(NSEQ_S, 64, 64)
    yp, ysm, _ = run_phase3(x_prompt, x_sample, mix_p.reshape(-1, 1024), mix_s.reshape(-1, 1024), f(w_out)[0],
                            f(norm_ffn_g)[0], f(peer_w_q)[0], f(peer_sub_keys)[0], f(peer_u)[0], f(peer_v)[0],
                            f(norm_final_g))
    ppb = pp.reshape(2, TP, IN_COLS)
    pss = psm.reshape(NSEQ_S, 16, IN_COLS)
    c = np.ascontiguousarray
    return (
        c(yp.reshape(2, TP, 1024)), c(ysm.reshape(NSEQ_S, 16, 1024)),
        c(ppb[:, :, 512:1024].reshape(1, 2, TP, 8, 64)), c(ppb[:, :, 1024:1536].reshape(1, 2, TP, 8, 64)),
        c(ppb[:, :, 1536:1544].reshape(1, 2, TP, 8)), S_p, c(ppb[:, -1:, FOX_COLS:].reshape(1, 2, 1, RW_COLS)),
        c(pss[:, :, 512:1024].reshape(1, NSEQ_S, 16, 8, 64)), c(pss[:, :, 1024:1536].reshape(1, NSEQ_S, 16, 8, 64)),
        c(pss[:, :, 1536:1544].reshape(1, NSEQ_S, 16, 8)), S_s, c(pss[:, -1:, FOX_COLS:].reshape(1, NSEQ_S, 1, RW_COLS)),
    )
```

```python
from contextlib import ExitStack
import math
import numpy as np
import concourse.bass as bass
import concourse.mybir as mybir
from concourse.bass_utils import run_bass_kernel_spmd

F32 = mybir.dt.float32
BF16 = mybir.dt.bfloat16
I32 = mybir.dt.int32
U32 = mybir.dt.uint32
ALU = mybir.AluOpType
AF = mybir.ActivationFunctionType
AX = mybir.AxisListType

D = 1024
IN_COLS = 3240
FOX_COLS = 1544
RW_COLS = 1696
NCORE = 8


class Prog:
    ENGS = ("pe", "act", "dve", "pool", "sp")

    def __init__(self, nc):
        self.nc = nc
        self.st = ExitStack()
        self.ops = {e: [] for e in self.ENGS}
        self.cnt = {}
        self.waited = {e: {} for e in self.ENGS}
        self.lastw = {}
        self.readers = {}
        self.ndma = {e: 0 for e in self.ENGS}
        self.NS = 8
        self.nosame = set()
        self.defer = None
        self.fence_t = {}
        self.uid = 0

    def sb(self, name, shape, dt):
        return self.st.enter_context(self.nc.sbuf_tensor("sb_" + name, list(shape), dt))

    def ps(self, name, shape, dt):
        return self.st.enter_context(self.nc.psum_tensor("ps_" + name, list(shape), dt))

    def _deps(self, eng, reads, writes):
        deps = []
        for b in reads:
            if b in self.lastw:
                deps.extend(self.lastw[b].items())
        for b in writes:
            if b in self.lastw:
                deps.extend(self.lastw[b].items())
            deps.extend(self.readers.get(b, ()))
        best = {}
        for (k, v) in deps:
            if eng == "pe" and k == "pe":
                continue
            if k == eng and eng in self.nosame:
                continue
            if self.waited[eng].get(k, 0) >= v:
                continue
            best[k] = max(best.get(k, 0), v)
        for k, v in best.items():
            self.waited[eng][k] = v
        return list(best.items())

    def _record(self, tok, reads, writes):
        for b in reads:
            self.readers.setdefault(b, []).append(tok)
        for b in writes:
            self.lastw.setdefault(b, {})[tok[0]] = tok[1]
            self.readers[b] = []

    def drain(self, q, k):
        saved, self.nosame = self.nosame, set()
        d, self.defer = self.defer, None
        for _ in range(min(k, len(q))):
            kind, a = q.pop(0)
            getattr(self, kind)(*a)
        self.defer, self.nosame = d, saved

    def op(self, eng, fn, reads=(), writes=()):
        if self.defer is not None:
            self.defer.append(("op", (eng, fn, tuple(reads), tuple(writes))))
            return
        waits = self._deps(eng, reads, writes)
        self.cnt[eng] = self.cnt.get(eng, 0) + 1
        self.ops[eng].append((waits, fn, eng, 1))
        self._record((eng, self.cnt[eng]), reads, writes)

    def fence(self, eng, names):
        if eng not in self.fence_t:
            self.fence_t[eng] = self.sb("fence_" + eng, [128, 2], F32)
            t0_ = self.fence_t[eng]
            d_, self.defer = self.defer, None
            if eng == "act":
                self.op("act", lambda e: e.memzero(t0_[:, :]), writes=["fence_" + eng])
            else:
                self.op("dve", lambda e: e.memset(t0_[:, :], 0.0), writes=["fence_" + eng])
            self.defer = d_
        t = self.fence_t[eng]
        if self.defer is not None:
            self.defer.append(("fence", (eng, tuple(names))))
            return
        if eng == "act":
            self.op("act", lambda e: e.copy(out=t[:, 1:2], in_=t[:, 0:1]), reads=(), writes=list(names))
        else:
            self.op("dve", lambda e: e.tensor_copy(out=t[:, 1:2], in_=t[:, 0:1]), reads=(), writes=list(names))

    def dma(self, q, fn, reads=(), writes=()):
        if self.defer is not None:
            self.defer.append(("dma", (q, fn, tuple(reads), tuple(writes))))
            return
        waits = self._deps(q, reads, writes)
        k = "d_%s_%d" % (q, self.ndma[q] % (16 if q == "pool" else self.NS))
        self.ndma[q] += 1
        prev = self.cnt.get(k, 0)
        if prev and self.waited[q].get(k, 0) < prev:
            self.waited[q][k] = prev
            waits = [w for w in waits if w[0] != k] + [(k, prev)]
        self.cnt[k] = self.cnt.get(k, 0) + 16
        self.ops[q].append((waits, fn, k, 16))
        self._record((k, self.cnt[k]), reads, writes)

    def emit(self):
        nc = self.nc
        keys = sorted(self.cnt.keys())
        sems = {k: self.st.enter_context(nc.semaphore("s_" + k)) for k in keys}
        final = [(k, self.cnt[k]) for k in keys]
        ops = self.ops

        def run(name, e):
            for (waits, fn, k, inc) in ops[name]:
                for (wk, wv) in waits:
                    e.wait_ge(sems[wk], wv)
                fn(e).then_inc(sems[k], inc)
            if name == "sp":
                for (k, v) in final:
                    e.wait_ge(sems[k], v)

        with nc.Block() as block:
            @block.tensor
            def _(e):
                run("pe", e)

            @block.scalar
            def _(e):
                run("act", e)

            @block.vector
            def _(e):
                run("dve", e)

            @block.gpsimd
            def _(e):
                run("pool", e)

            @block.sync
            def _(e):
                run("sp", e)
        self.st.close()


def _din(nc, name, shape, dt=F32):
    return nc.dram_tensor(name, list(shape), dt, kind="ExternalInput").ap()


def _dout(nc, name, shape, dt=F32):
    return nc.dram_tensor(name, list(shape), dt, kind="ExternalOutput").ap()


def _load_cast(P, name, dram_ap, shape, stage, stage_name, q="sp", eng="act"):
    t = P.sb(name, shape, BF16)
    p, n = shape
    P.dma(q, lambda e: e.dma_start(out=stage[0:p, 0:n], in_=dram_ap), writes=[stage_name])
    if eng == "act":
        P.op("act", lambda e: e.copy(out=t[:, :], in_=stage[0:p, 0:n]), reads=[stage_name], writes=[name])
    else:
        P.op("dve", lambda e: e.tensor_copy(out=t[:, :], in_=stage[0:p, 0:n]), reads=[stage_name], writes=[name])
    return t


def build_phase1(NT):
    nc = bass.Bass("TRN2", target_bir_lowering=False)
    x = _din(nc, "x", [NT * 128, D])
    gbc = _din(nc, "gbc", [128, D])
    w = _din(nc, "w_in", [D, IN_COLS])
    bfb = _din(nc, "bfb", [128, 8])
    identd = _din(nc, "ident", [128, 128])
    proj = _dout(nc, "proj", [NT * 128, IN_COLS])
    P = Prog(nc)
    wst = P.sb("wst", [128, IN_COLS], F32)
    w_bf = P.sb("w_bf", [128, 8, IN_COLS], BF16)
    g_sb = P.sb("g_sb", [128, D], F32)
    bf_sb = P.sb("bf_sb", [128, 8], F32)
    id_f = P.sb("id_f", [128, 128], F32)
    id_b = P.sb("id_b", [128, 128], BF16)
    P.dma("sp", lambda e: e.dma_start(out=g_sb[:, :], in_=gbc), writes=["g"])
    P.dma("sp", lambda e: e.dma_start(out=bf_sb[:, :], in_=bfb), writes=["bf"])
    P.dma("sp", lambda e: e.dma_start(out=id_f[:, :], in_=identd), writes=["idf"])
    P.op("dve", lambda e: e.tensor_copy(out=id_b[:, :], in_=id_f[:, :]), reads=["idf"], writes=["idb"])
    for dc in range(8):
        P.dma("sp", lambda e, dc=dc: e.dma_start(out=wst[:, :], in_=w[dc * 128:(dc + 1) * 128, :]),
              writes=["wst"])
        P.op("act", lambda e, dc=dc: e.copy(out=w_bf[:, dc, :], in_=wst[:, :]), reads=["wst"], writes=["w%d" % dc])
    wnames = ["w%d" % dc for dc in range(8)]
    xt = [P.sb("xt%d" % i, [128, D], F32) for i in range(2)]
    junk = P.sb("junk", [128, D], BF16)
    ss = [P.sb("ss%d" % i, [128, 1], F32) for i in range(2)]
    rstd = [P.sb("rstd%d" % i, [128, 1], F32) for i in range(2)]
    h = [P.sb("h%d" % i, [128, D], BF16) for i in range(2)]
    hT = [P.sb("hT%d" % i, [128, D], BF16) for i in range(2)]
    pr = [P.sb("pr%d" % i, [128, IN_COLS], F32) for i in range(2)]
    lz = P.sb("lz", [128, 8], F32)
    psT = [P.ps("psT%d" % i, [128, D], BF16) for i in range(2)]
    psP = [P.ps("psP%d" % i, [128, 512], F32) for i in range(4)]
    groups = [(c0, min(c0 + 512, IN_COLS)) for c0 in range(0, IN_COLS, 512)]
    gi = 0
    for i in range(NT):
        b = i % 2
        X, H, HT, PR = xt[b], h[b], hT[b], pr[b]
        P.dma("sp", lambda e, X=X, i=i: e.dma_start(out=X[:, :], in_=x[i * 128:(i + 1) * 128, :]),
              writes=["xt%d" % b])
        P.op("act", lambda e, X=X, b=b: e.activation(out=junk[:, :], in_=X[:, :], func=AF.Square,
                                                      accum_out=ss[b][:, 0:1]),
             reads=["xt%d" % b], writes=["junk", "ss%d" % b])
        P.op("act", lambda e, b=b: e.activation(out=rstd[b][:, :], in_=ss[b][:, :], func=AF.Sqrt, bias=1e-6,
                                                scale=1.0 / D),
             reads=["ss%d" % b], writes=["rstd%d" % b])
        P.op("dve", lambda e, b=b: e.reciprocal(out=rstd[b][:, :], in_=rstd[b][:, :]),
             reads=["rstd%d" % b], writes=["rstd%d" % b])
        P.op("dve", lambda e, X=X, H=H, b=b: e.scalar_tensor_tensor(
            out=H[:, :], in0=X[:, :], scalar=rstd[b][:, 0:1], in1=g_sb[:, :], op0=ALU.mult, op1=ALU.mult),
            reads=["xt%d" % b, "rstd%d" % b, "g"], writes=["h%d" % b])
        for dc in range(8):
            P.op("pe", lambda e, H=H, b=b, dc=dc: e.transpose(
                out=psT[b][:, dc * 128:(dc + 1) * 128], in_=H[:, dc * 128:(dc + 1) * 128], identity=id_b[:, :]),
                reads=["h%d" % b, "idb"], writes=["psT%d" % b])
        P.op("act", lambda e, HT=HT, b=b: e.copy(out=HT[:, :], in_=psT[b][:, :]),
             reads=["psT%d" % b], writes=["hT%d" % b])
        for (c0, c1) in groups:
            pp = gi % 4
            gi += 1
            n = c1 - c0
            for dc in range(8):
                P.op("pe", lambda e, HT=HT, pp=pp, dc=dc, c0=c0, c1=c1, n=n: e.matmul(
                    psP[pp][:, 0:n], lhsT=HT[:, dc * 128:(dc + 1) * 128], rhs=w_bf[:, dc, c0:c1],
                    start=(dc == 0), stop=(dc == 7)),
                    reads=["hT%d" % b] + wnames, writes=["psP%d" % pp])
            if gi % 2 == 0:
                P.op("act", lambda e, PR=PR, pp=pp, c0=c0, c1=c1, n=n: e.copy(out=PR[:, c0:c1], in_=psP[pp][:, 0:n]),
                     reads=["psP%d" % pp], writes=["pr%d" % b])
            else:
                P.op("dve", lambda e, PR=PR, pp=pp, c0=c0, c1=c1, n=n: e.tensor_copy(out=PR[:, c0:c1],
                                                                                     in_=psP[pp][:, 0:n]),
                     reads=["psP%d" % pp], writes=["pr%d" % b])
        P.op("dve", lambda e, PR=PR: e.tensor_tensor(out=lz[:, :], in0=PR[:, 1536:1544], in1=bf_sb[:, :], op=ALU.add),
             reads=["pr%d" % b, "bf"], writes=["lz"])
        P.op("act", lambda e: e.activation(out=lz[:, :], in_=lz[:, :], func=AF.Exp, scale=-1.0),
             reads=["lz"], writes=["lz"])
        P.op("act", lambda e: e.activation(out=lz[:, :], in_=lz[:, :], func=AF.Ln, bias=1.0, scale=1.0),
             reads=["lz"], writes=["lz"])
        P.op("dve", lambda e, PR=PR: e.tensor_single_scalar(out=PR[:, 1536:1544], in_=lz[:, :], scalar=-1.0,
                                                            op=ALU.mult),
             reads=["lz"], writes=["pr%d" % b])
        P.dma("sp", lambda e, PR=PR, i=i: e.dma_start(out=proj[i * 128:(i + 1) * 128, :], in_=PR[:, :]),
              reads=["pr%d" % b], writes=["out%d" % i])
    P.emit()
    return nc


def _ident():
    return np.eye(128, dtype=np.float32)


def run_phase1(x_prompt, x_sample, norm_mix_g, w_in, fox_b_f):
    NT = 33
    xp = np.ascontiguousarray(x_prompt).reshape(-1, D)
    xs = np.ascontiguousarray(x_sample).reshape(-1, D)
    nc = build_phase1(NT)
    gbc = np.ascontiguousarray(np.broadcast_to(norm_mix_g.reshape(1, D), (128, D))).astype(np.float32)
    bfb = np.ascontiguousarray(np.broadcast_to(fox_b_f.reshape(1, 8), (128, 8))).astype(np.float32)
    w = np.ascontiguousarray(w_in.reshape(D, IN_COLS))
    in_maps = []
    for c in range(NCORE):
        xc = np.zeros((NT * 128, D), np.float32)
        xc[:4096] = xp[c * 4096:(c + 1) * 4096]
        xc[4096:4128] = xs[c * 32:(c + 1) * 32]
        in_maps.append({"x": xc, "gbc": gbc, "w_in": w, "bfb": bfb, "ident": _ident()})
    res = run_bass_kernel_spmd(nc, in_maps, core_ids=list(range(NCORE)))
    pp = np.concatenate([res.results[c]["proj"][:4096] for c in range(NCORE)], axis=0)
    psm = np.concatenate([res.results[c]["proj"][4096:4128] for c in range(NCORE)], axis=0)
    return pp, psm


TP = 16384
TS = 2176
NSEQ_S = 16


def _fox_consts(P, nc):
    c = {}
    tri_d = _din(nc, "tri", [128, 128])
    ones_d = _din(nc, "ones", [128, 128])
    id_d = _din(nc, "ident", [128, 128])
    mask_d = _din(nc, "mask", [128, 4 * 512])
    c["tri"] = P.sb("tri", [128, 128], F32)
    c["ones"] = P.sb("ones", [128, 128], F32)
    c["idf"] = P.sb("idf", [128, 128], F32)
    c["idb"] = P.sb("idb", [128, 128], BF16)
    c["maskf"] = P.sb("maskf", [128, 2048], F32)
    c["mask"] = P.sb("maskb", [128, 4, 512], BF16)
    P.dma("sp", lambda e: e.dma_start(out=c["tri"][:, :], in_=tri_d), writes=["tri"])
    P.dma("sp", lambda e: e.dma_start(out=c["ones"][:, :], in_=ones_d), writes=["ones"])
    P.dma("sp", lambda e: e.dma_start(out=c["idf"][:, :], in_=id_d), writes=["idf"])
    P.dma("sp", lambda e: e.dma_start(out=c["maskf"][:, :], in_=mask_d), writes=["maskf"])
    P.op("dve", lambda e: e.tensor_copy(out=c["idb"][:, :], in_=c["idf"][:, :]), reads=["idf"], writes=["idb"])
    P.op("dve", lambda e: e.tensor_copy(out=c["mask"][:, :, :], in_=c["maskf"][:, :].rearrange("p (a b) -> p a b", a=4)),
         reads=["maskf"], writes=["maskb"])
    return c


def _fox_seq(P, c, B, tag, NT, qT_src, nq_tot, kT_src, v_src, lf_src, groups, out_fn):
    T = NT * 128
    qT, kT, vv, stage = B["qT"], B["kT"], B["vv"], B["stage"]
    k = 0
    for (dst, src, n, nm) in ((qT, qT_src, nq_tot, "qT"), (kT, kT_src, T, "kT")):
        for c0 in range(0, n, 2048):
            w = min(2048, n - c0)
            s = k % 2
            k += 1
            P.dma("sp", lambda e, s=s, src=src, c0=c0, w=w: e.dma_start(out=stage[s][0:64, 0:w], in_=src[:, c0:c0 + w]),
                  writes=["stage%d" % s])
            eng = "act" if k % 2 else "dve"
            if eng == "act":
                P.op("act", lambda e, s=s, dst=dst, c0=c0, w=w: e.copy(out=dst[:, c0:c0 + w], in_=stage[s][0:64, 0:w]),
                     reads=["stage%d" % s], writes=[nm])
            else:
                P.op("dve", lambda e, s=s, dst=dst, c0=c0, w=w: e.tensor_copy(out=dst[:, c0:c0 + w], in_=stage[s][0:64, 0:w]),
                     reads=["stage%d" % s], writes=[nm])
    for j0 in range(0, NT, 32):
        nj = min(32, NT - j0)
        s = k % 2
        k += 1
        P.dma("sp", lambda e, s=s, j0=j0, nj=nj: e.dma_start(
            out=stage[s][:, 0:nj * 64].rearrange("p (j d) -> p j d", d=64), in_=v_src[:, j0:j0 + nj, :]),
            writes=["stage%d" % s])
        P.op("dve", lambda e, s=s, j0=j0, nj=nj: e.tensor_copy(
            out=vv[:, j0:j0 + nj, 0:64], in_=stage[s][:, 0:nj * 64].rearrange("p (j d) -> p j d", d=64)),
            reads=["stage%d" % s], writes=["vv"])
    L = B["L"]
    P.dma("sp", lambda e: e.dma_start(out=L[:, 0:NT], in_=lf_src), writes=["L"])
    cl_ps, tot_ps = B["cl_ps"], B["tot_ps"]
    P.op("pe", lambda e: e.matmul(cl_ps[:, 0:NT], lhsT=c["tri"][:, :], rhs=L[:, 0:NT], start=True, stop=True),
         reads=["L", "tri"], writes=["cl_ps"])
    P.op("pe", lambda e: e.matmul(tot_ps[:, 0:NT], lhsT=c["ones"][:, :], rhs=L[:, 0:NT], start=True, stop=True),
         reads=["L", "ones"], writes=["tot_ps"])
    sa, sbb = B["scanA"], B["scanB"]
    P.op("dve", lambda e: e.tensor_copy(out=sa[:, 0:NT], in_=tot_ps[:, 0:NT]), reads=["tot_ps"], writes=["scanA"])
    cur, nxt, cn, nn = sa, sbb, "scanA", "scanB"
    sh = 1
    while sh < NT:
        P.op("dve", lambda e, cur=cur, nxt=nxt, sh=sh: e.tensor_tensor(
            out=nxt[:, sh:NT], in0=cur[:, sh:NT], in1=cur[:, 0:NT - sh], op=ALU.add), reads=[cn], writes=[nn])
        P.op("dve", lambda e, cur=cur, nxt=nxt, sh=sh: e.tensor_copy(out=nxt[:, 0:sh], in_=cur[:, 0:sh]),
             reads=[cn], writes=[nn])
        cur, nxt, cn, nn = nxt, cur, nn, cn
        sh *= 2
    pex, negC = B["pex"], B["negC"]
    P.op("dve", lambda e, cur=cur: e.tensor_tensor(out=pex[:, 0:NT], in0=cur[:, 0:NT], in1=tot_ps[:, 0:NT],
                                                   op=ALU.subtract), reads=[cn, "tot_ps"], writes=["pex"])
    P.op("dve", lambda e: e.scalar_tensor_tensor(out=negC[:, 0:NT], in0=pex[:, 0:NT], scalar=-1.0, in1=cl_ps[:, 0:NT],
                                                 op0=ALU.mult, op1=ALU.subtract),
         reads=["pex", "cl_ps"], writes=["negC"])
    bias = B["bias"]
    for gi, (q0, nq, nk, d0, ct) in enumerate(groups):
        P.op("dve", lambda e, gi=gi, nk=nk, ct=ct: e.tensor_scalar(
            out=bias[:, gi, 0:nk], in0=negC[:, 0:nk], scalar1=pex[:, ct:ct + 1], scalar2=0.0,
            op0=ALU.add, op1=ALU.add), reads=["negC", "pex"], writes=["bias"])
    it = B["it"]
    for gi, (q0, nq, nk, d0, ct) in enumerate(groups):
        ob = B["gcount"] % 2
        B["gcount"] += 1
        OT = B["OT"][ob]
        def emit_score(j, it_):
            sb_ = it_ % 3
            sT = B["sT"][sb_]
            diag = j >= d0
            P.op("pe", lambda e, sT=sT, j=j, q0=q0, nq=nq, diag=diag: e.matmul(
                sT[:, 0:nq], lhsT=kT[:, j * 128:(j + 1) * 128], rhs=qT[:, q0:q0 + nq], start=True, stop=(not diag)),
                reads=["kT", "qT"], writes=["sT%d" % sb_])
            if diag:
                jl = j - d0
                P.op("pe", lambda e, sT=sT, jl=jl, nq=nq: e.matmul(
                    sT[:, 0:nq], lhsT=c["idb"][:, :], rhs=c["mask"][:, jl, 0:nq], start=False, stop=True),
                    reads=["idb", "maskb"], writes=["sT%d" % sb_])

        emit_score(0, it)
        if nk > 1:
            emit_score(1, it + 1)
        for j in range(nk):
            sb_ = it % 3
            pb = it % 3
            sT = B["sT"][sb_]
            pT = B["pT"][pb]
            if j + 2 < nk:
                emit_score(j + 2, it + 2)
            it += 1
            P.op("act", lambda e, sT=sT, pT=pT, gi=gi, j=j, nq=nq: e.activation(
                out=pT[:, 0:nq], in_=sT[:, 0:nq], func=AF.Exp, bias=bias[:, gi, j:j + 1], scale=0.125),
                reads=["sT%d" % sb_, "bias"], writes=["pT%d" % pb])
            P.op("pe", lambda e, OT=OT, pT=pT, j=j, nq=nq, nk=nk: e.matmul(
                OT[0:65, 0:nq], lhsT=vv[:, j, :], rhs=pT[:, 0:nq], start=(j == 0), stop=(j == nk - 1)),
                reads=["vv", "pT%d" % pb], writes=["OT%d" % ob])
        oT = B["oT"]
        P.op("act", lambda e, OT=OT, nq=nq: e.copy(out=oT[0:65, 0:nq], in_=OT[0:65, 0:nq]),
             reads=["OT%d" % ob], writes=["oT"])
        oq, rec, osb = B["oq"], B["rec"], B["osb"]
        nqi = (nq + 127) // 128
        for qi in range(nqi):
            w = min(128, nq - qi * 128)
            P.op("pe", lambda e, qi=qi, w=w: e.transpose(out=oq[0:w, qi, :], in_=oT[0:65, qi * 128:qi * 128 + w],
                                                         identity=c["idf"][0:65, 0:65]),
                 reads=["oT", "idf"], writes=["oq"])
        wmax = min(128, nq)
        for qi in range(nqi):
            P.op("dve", lambda e, qi=qi: e.reciprocal(out=rec[0:wmax, qi:qi + 1], in_=oq[0:wmax, qi, 64:65]),
                 reads=["oq"], writes=["rec"])
            P.op("dve", lambda e, qi=qi: e.tensor_scalar_mul(out=osb[0:wmax, qi, :], in0=oq[0:wmax, qi, 0:64],
                                                             scalar1=rec[0:wmax, qi:qi + 1]),
                 reads=["oq", "rec"], writes=["osb"])
        out_fn(P, osb, q0, nq, nqi, wmax)
    B["it"] = it


def _fox_bufs(P):
    B = {}
    B["qT"] = P.sb("qT", [64, TP], BF16)
    B["kT"] = P.sb("kT", [64, TP], BF16)
    B["vv"] = P.sb("vv", [128, 128, 65], BF16)
    B["stage"] = [P.sb("stage%d" % i, [128, 2048], F32) for i in range(2)]
    B["L"] = P.sb("L", [128, 128], F32)
    B["scanA"] = P.sb("scanA", [128, 128], F32)
    B["scanB"] = P.sb("scanB", [128, 128], F32)
    B["pex"] = P.sb("pex", [128, 128], F32)
    B["negC"] = P.sb("negC", [128, 128], F32)
    B["bias"] = P.sb("bias", [128, 32, 128], F32)
    B["pT"] = [P.sb("pT%d" % i, [128, 512], BF16) for i in range(3)]
    B["oT"] = P.sb("oT", [65, 512], F32)
    B["rec"] = P.sb("rec", [128, 4], F32)
    B["osb"] = P.sb("osb", [128, 4, 64], F32)
    B["cl_ps"] = P.ps("cl_ps", [128, 128], F32)
    B["tot_ps"] = P.ps("tot_ps", [128, 128], F32)
    B["sT"] = [P.ps("sT%d" % i, [128, 512], F32) for i in range(3)]
    B["OT"] = [P.ps("OT%d" % i, [128, 512], F32) for i in range(2)]
    B["oq"] = P.ps("oq", [128, 4, 65], F32)
    B["it"] = 0
    B["gcount"] = 0
    P.op("pool", lambda e: e.memset(B["vv"][:, :, 64:65], 1.0), writes=["vv"])
    return B


def build_phase2_fox(n_prompt=2, n_sample=NSEQ_S, ngroups=32):
    nc = bass.Bass("TRN2", target_bir_lowering=False)
    qTp = _din(nc, "qTp", [2, 64, TP])
    kTp = _din(nc, "kTp", [2, 64, TP])
    vp = _din(nc, "vp", [2, 128, 128, 64])
    lfp = _din(nc, "lfp", [2, 128, 128])
    qTs = _din(nc, "qTs", [NSEQ_S, 64, 16])
    kTs = _din(nc, "kTs", [NSEQ_S, 64, TS])
    vs = _din(nc, "vs", [NSEQ_S, 128, 17, 64])
    lfs = _din(nc, "lfs", [NSEQ_S, 128, 17])
    op_ = _dout(nc, "o_p", [2, TP, 64])
    os_ = _dout(nc, "o_s", [NSEQ_S, 16, 64])
    P = Prog(nc)
    c = _fox_consts(P, nc)
    B = _fox_bufs(P)
    for b in range(n_prompt):
        groups = [(512 * g, 512, 4 * g + 4, 4 * g, 4 * g + 2) for g in range(ngroups)]

        def out_fn(P, osb, q0, nq, nqi, wmax, b=b):
            P.dma("sp", lambda e: e.dma_start(
                out=op_[b, q0:q0 + nq, :].rearrange("(a p) d -> p a d", p=128), in_=osb[:, 0:nqi, :]),
                reads=["osb"], writes=["o_out"])
        _fox_seq(P, c, B, "p%d" % b, 128, qTp[b], TP, kTp[b], vp[b], lfp[b], groups, out_fn)
    for s in range(n_sample):
        groups = [(0, 16, 17, 16, 16)]

        def out_fn(P, osb, q0, nq, nqi, wmax, s=s):
            P.dma("sp", lambda e: e.dma_start(out=os_[s, :, :], in_=osb[0:16, 0, :]), reads=["osb"], writes=["o_out"])
        _fox_seq(P, c, B, "s%d" % s, 17, qTs[s], 16, kTs[s], vs[s], lfs[s], groups, out_fn)
    P.emit()
    return nc


def _fox_const_inputs():
    p = np.arange(128)
    tri = (p[:, None] <= p[None, :]).astype(np.float32)
    ones = np.ones((128, 128), np.float32)
    col = np.arange(512)
    mask = np.zeros((128, 4, 512), np.float32)
    for jl in range(4):
        mask[:, jl, :] = np.where(jl * 128 + p[:, None] > col[None, :], -30000.0, 0.0)
    return {"tri": tri, "ones": ones, "ident": _ident(), "mask": mask.reshape(128, 2048)}


def _tile_major(a, nt):
    return np.ascontiguousarray(np.swapaxes(a.reshape((nt, 128) + a.shape[1:]), 0, 1))


def fox_inputs(pp, psm, cache_k, cache_v, cache_lf):
    ppb = pp.reshape(2, TP, IN_COLS)
    pss = psm.reshape(NSEQ_S, 16, IN_COLS)
    maps = []
    for h in range(NCORE):
        m = dict(_fox_const_inputs())
        m["qTp"] = np.ascontiguousarray(np.swapaxes(ppb[:, :, h * 64:(h + 1) * 64], 1, 2))
        m["kTp"] = np.ascontiguousarray(np.swapaxes(ppb[:, :, 512 + h * 64:512 + (h + 1) * 64], 1, 2))
        m["vp"] = np.stack([_tile_major(ppb[b, :, 1024 + h * 64:1024 + (h + 1) * 64], 128) for b in range(2)])
        m["lfp"] = np.stack([_tile_major(ppb[b, :, 1536 + h], 128) for b in range(2)])
        kfull = np.zeros((NSEQ_S, TS, 64), np.float32)
        vfull = np.zeros((NSEQ_S, TS, 64), np.float32)
        lfull = np.zeros((NSEQ_S, TS), np.float32)
        kfull[:, :2048] = cache_k[:, :, h, :]
        vfull[:, :2048] = cache_v[:, :, h, :]
        lfull[:, :2048] = cache_lf[:, :, h]
        kfull[:, 2048:2064] = pss[:, :, 512 + h * 64:512 + (h + 1) * 64]
        vfull[:, 2048:2064] = pss[:, :, 1024 + h * 64:1024 + (h + 1) * 64]
        lfull[:, 2048:2064] = pss[:, :, 1536 + h]
        m["qTs"] = np.ascontiguousarray(np.swapaxes(pss[:, :, h * 64:(h + 1) * 64], 1, 2))
        m["kTs"] = np.ascontiguousarray(np.swapaxes(kfull, 1, 2))
        m["vs"] = np.stack([_tile_major(vfull[s], 17) for s in range(NSEQ_S)])
        m["lfs"] = np.stack([_tile_major(lfull[s], 17) for s in range(NSEQ_S)])
        maps.append(m)
    return maps


NPAR = 352 + 7 * 64
EXPM05 = math.exp(-0.5)


def _rw_setup(P, nc, B):
    R = {}
    par_d = _din(nc, "rw_par", [128, NPAR])
    w2_d = _din(nc, "rw_w2", [32, 64])
    a2_d = _din(nc, "rw_a2", [32, 64])
    g2_d = _din(nc, "rw_g2", [96, 64])
    sel_d = _din(nc, "rw_sel", [6, 128])
    R["par"] = P.sb("rw_par", [128, NPAR], F32)
    R["w2"] = P.sb("rw_w2", [32, 64], F32)
    R["a2"] = P.sb("rw_a2", [32, 64], F32)
    R["g2"] = P.sb("rw_g2", [96, 64], F32)
    R["sel"] = P.sb("rw_sel", [6, 128], F32)
    R["selb"] = P.sb("rw_selb", [6, 128], BF16)
    R["omk"] = P.sb("rw_omk", [128, 64], F32)
    for nm, d_ in (("par", par_d), ("w2", w2_d), ("a2", a2_d), ("g2", g2_d), ("sel", sel_d)):
        P.dma("sp", lambda e, nm=nm, d_=d_: e.dma_start(out=R[nm][:, :], in_=d_), writes=["rwc_" + nm])
    P.op("dve", lambda e: e.tensor_copy(out=R["selb"][:, :], in_=R["sel"][:, :]), reads=["rwc_sel"], writes=["rwc_selb"])
    R["R3"] = [[P.sb("rw_R3_%d_%d" % (b, t), [128, 3, 320], BF16) for t in range(2)] for b in range(2)]
    R["r1"] = P.sb("rw_r1", [128, 320], F32)
    R["r2"] = P.sb("rw_r2", [128, 320], F32)
    o = 352
    R["mu"] = R["par"][:, 0:352]
    names = ["w0", "a0", "kk", "ka", "rk", "lnw", "lnb"]
    for i, nm in enumerate(names):
        R[nm] = R["par"][:, o + i * 64:o + (i + 1) * 64]
    P.op("dve", lambda e: e.tensor_scalar(out=R["omk"][:, :], in0=R["ka"], scalar1=-1.0, scalar2=1.0,
                                          op0=ALU.mult, op1=ALU.add), reads=["rwc_par"], writes=["rwc_omk"])
    R["cur"] = [P.sb("rw_cur%d" % b, [128, 352], F32) for b in range(2)]
    R["prv"] = [P.sb("rw_prv%d" % b, [128, 352], F32) for b in range(2)]
    R["R"] = [[P.sb("rw_R%d_%d" % (b, t), [128, 320], F32) for t in range(2)] for b in range(2)]
    R["GB"] = [[P.sb("rw_GB%d_%d" % (b, t), [128, 128], F32) for t in range(2)] for b in range(2)]
    R["VV"] = [P.sb("rw_VV%d" % t, [128, 128], F32) for t in range(2)]
    R["vT"] = [P.sb("rw_vT%d" % t, [128, 128], F32) for t in range(2)]
    R["yT"] = [P.sb("rw_yT%d" % t, [128, 128], F32) for t in range(2)]
    R["twl"] = P.sb("rw_twl", [32, 128], F32)
    R["alT"] = P.sb("rw_alT", [32, 128], F32)
    R["sgl"] = P.sb("rw_sgl", [96, 128], F32)
    for nm in ("zt", "at", "kkt", "tmp", "t1", "junk", "cen", "ob"):
        R[nm] = P.sb("rw_" + nm, [128, 64], F32)
    for nm in ("ssq", "rks", "mean", "var", "sk"):
        R[nm] = P.sb("rw_" + nm, [128, 1], F32)
    R["ysb"] = P.sb("rw_ysb", [128, 128], F32)
    R["S"] = P.sb("rw_S", [128, 64], F32)
    R["stmp"] = P.sb("rw_stmp", [128, 64], F32)
    R["rowbuf"] = [P.sb("rw_rowbuf%d" % i, [6, 16 * 320], BF16) for i in range(2)]
    R["rowp"] = [B["sT"][0], B["sT"][1]]
    R["trp"] = B["OT"][0]
    R["lop"] = B["OT"][1]
    R["vtp"] = B["cl_ps"]
    R["ytp"] = B["tot_ps"]
    R["k"] = 0
    R["step"] = 0
    return R


def _rw_prep(P, c, R, n, ntok, cur_src, prev_src, rows_scr):
    tp = n % 2
    idf = c["idf"]
    t0 = n * ntok
    for b in range(2):
        cur, prv = R["cur"][b], R["prv"][b]
        cn, pn = "rw_cur%d" % b, "rw_prv%d" % b
        Rt, GB = R["R"][b][tp], R["GB"][b][tp]
        rn, gn = "rw_R%d_%d" % (b, tp), "rw_GB%d_%d" % (b, tp)
        P.dma("sp", lambda e, cur=cur, b=b: e.dma_start(out=cur[0:ntok, :], in_=cur_src(b, n)), writes=[cn])
        P.dma("sp", lambda e, prv=prv, b=b: e.dma_start(out=prv[0:ntok, :], in_=prev_src(b, n)), writes=[pn])
        P.op("pool", lambda e, cur=cur, prv=prv: e.tensor_tensor(out=prv[0:ntok, :], in0=prv[0:ntok, :], in1=cur[0:ntok, :],
                                                                op=ALU.subtract), reads=[cn, pn], writes=[pn])
        P.op("pool", lambda e, prv=prv: e.tensor_tensor(out=prv[0:ntok, :], in0=prv[0:ntok, :], in1=R["mu"][0:ntok, :],
                                                       op=ALU.mult), reads=[pn, "rwc_par"], writes=[pn])
        P.op("pool", lambda e, cur=cur, prv=prv: e.tensor_tensor(out=cur[0:ntok, :], in0=cur[0:ntok, :], in1=prv[0:ntok, :],
                                                                op=ALU.add), reads=[cn, pn], writes=[cn])
        trp, lop = R["trp"], R["lop"]
        for (o0, c0, c1, m) in ((0, 192, 224, 32), (128, 224, 256, 32), (256, 256, 352, 96)):
            P.op("pe", lambda e, cur=cur, o0=o0, c0=c0, c1=c1, m=m: e.transpose(
                out=trp[0:m, o0:o0 + ntok], in_=cur[0:ntok, c0:c1], identity=idf[0:ntok, 0:ntok]),
                reads=[cn, "idf"], writes=["OT0"])
        P.op("act", lambda e: e.activation(out=R["twl"][0:32, 0:ntok], in_=trp[0:32, 0:ntok], func=AF.Tanh),
             reads=["OT0"], writes=["rw_twl"])
        P.op("act", lambda e: e.copy(out=R["alT"][0:32, 0:ntok], in_=trp[0:32, 128:128 + ntok]),
             reads=["OT0"], writes=["rw_alT"])
        P.op("act", lambda e: e.activation(out=R["sgl"][0:96, 0:ntok], in_=trp[0:96, 256:256 + ntok], func=AF.Sigmoid),
             reads=["OT0"], writes=["rw_sgl"])
        P.op("pe", lambda e: e.matmul(lop[0:ntok, 0:64], lhsT=R["twl"][0:32, 0:ntok], rhs=R["w2"][:, :], start=True, stop=True),
             reads=["rw_twl", "rwc_w2"], writes=["OT1"])
        P.op("pe", lambda e: e.matmul(lop[0:ntok, 64:128], lhsT=R["alT"][0:32, 0:ntok], rhs=R["a2"][:, :], start=True, stop=True),
             reads=["rw_alT", "rwc_a2"], writes=["OT1"])
        P.op("pe", lambda e: e.matmul(lop[0:ntok, 128:192], lhsT=R["sgl"][0:96, 0:ntok], rhs=R["g2"][:, :], start=True, stop=True),
             reads=["rw_sgl", "rwc_g2"], writes=["OT1"])
        zt, at, kkt, tmp, t1, junk = R["zt"], R["at"], R["kkt"], R["tmp"], R["t1"], R["junk"]
        ssq, rks = R["ssq"], R["rks"]
        P.op("dve", lambda e: e.tensor_tensor(out=zt[0:ntok, :], in0=lop[0:ntok, 0:64], in1=R["w0"][0:ntok, :], op=ALU.add),
             reads=["OT1", "rwc_par"], writes=["rw_zt"])
        P.op("act", lambda e: e.activation(out=zt[0:ntok, :], in_=zt[0:ntok, :], func=AF.Sigmoid),
             reads=["rw_zt"], writes=["rw_zt"])
        P.op("act", lambda e, Rt=Rt: e.activation(out=Rt[0:ntok, 0:64], in_=zt[0:ntok, :], func=AF.Exp, scale=-EXPM05),
             reads=["rw_zt"], writes=[rn])
        P.op("dve", lambda e: e.tensor_tensor(out=at[0:ntok, :], in0=lop[0:ntok, 64:128], in1=R["a0"][0:ntok, :], op=ALU.add),
             reads=["OT1", "rwc_par"], writes=["rw_at"])
        P.op("act", lambda e: e.activation(out=at[0:ntok, :], in_=at[0:ntok, :], func=AF.Sigmoid),
             reads=["rw_at"], writes=["rw_at"])
        P.op("act", lambda e, GB=GB: e.copy(out=GB[0:ntok, 0:64], in_=lop[0:ntok, 128:192]), reads=["OT1"], writes=[gn])
        P.op("pool", lambda e, cur=cur: e.tensor_tensor(out=kkt[0:ntok, :], in0=cur[0:ntok, 64:128], in1=R["kk"][0:ntok, :],
                                                       op=ALU.mult), reads=[cn, "rwc_par"], writes=["rw_kkt"])
        P.op("dve", lambda e: e.scalar_tensor_tensor(out=junk[0:ntok, :], in0=kkt[0:ntok, :], scalar=1.0, in1=kkt[0:ntok, :],
                                                     op0=ALU.mult, op1=ALU.mult, accum_out=ssq[0:ntok, 0:1]),
             reads=["rw_kkt"], writes=["rw_junk", "rw_ssq"])
        P.fence("dve", ["rw_ssq"])
        P.op("act", lambda e: e.activation(out=ssq[0:ntok, :], in_=ssq[0:ntok, :], func=AF.Sqrt), reads=["rw_ssq"], writes=["rw_ssq"])
        P.op("dve", lambda e: e.tensor_scalar_max(out=ssq[0:ntok, :], in0=ssq[0:ntok, :], scalar1=1e-12),
             reads=["rw_ssq"], writes=["rw_ssq"])
        P.op("dve", lambda e: e.reciprocal(out=ssq[0:ntok, :], in_=ssq[0:ntok, :]), reads=["rw_ssq"], writes=["rw_ssq"])
        P.op("pool", lambda e: e.tensor_scalar_mul(out=kkt[0:ntok, :], in0=kkt[0:ntok, :], scalar1=ssq[0:ntok, 0:1]),
             reads=["rw_kkt", "rw_ssq"], writes=["rw_kkt"])
        P.op("pool", lambda e, Rt=Rt: e.tensor_single_scalar(out=Rt[0:ntok, 64:128], in_=kkt[0:ntok, :], scalar=-1.0, op=ALU.mult),
             reads=["rw_kkt"], writes=[rn])
        P.op("pool", lambda e, Rt=Rt: e.tensor_tensor(out=Rt[0:ntok, 128:192], in0=kkt[0:ntok, :], in1=at[0:ntok, :], op=ALU.mult),
             reads=["rw_kkt", "rw_at"], writes=[rn])
        P.op("pool", lambda e: e.tensor_tensor(out=tmp[0:ntok, :], in0=at[0:ntok, :], in1=R["ka"][0:ntok, :], op=ALU.mult),
             reads=["rw_at", "rwc_par"], writes=["rw_tmp"])
        P.op("pool", lambda e: e.tensor_tensor(out=tmp[0:ntok, :], in0=tmp[0:ntok, :], in1=R["omk"][0:ntok, :], op=ALU.add),
             reads=["rw_tmp", "rwc_omk"], writes=["rw_tmp"])
        P.op("pool", lambda e, cur=cur, Rt=Rt: e.tensor_tensor(out=Rt[0:ntok, 192:256], in0=cur[0:ntok, 64:128], in1=tmp[0:ntok, :],
                                                              op=ALU.mult), reads=[cn, "rw_tmp"], writes=[rn])
        P.op("act", lambda e, cur=cur, Rt=Rt: e.copy(out=Rt[0:ntok, 256:320], in_=cur[0:ntok, 0:64]), reads=[cn], writes=[rn])
        P.op("pool", lambda e, cur=cur, Rt=Rt: e.tensor_tensor(out=t1[0:ntok, :], in0=cur[0:ntok, 0:64], in1=Rt[0:ntok, 192:256],
                                                              op=ALU.mult), reads=[cn, rn], writes=["rw_t1"])
        P.op("dve", lambda e: e.scalar_tensor_tensor(out=junk[0:ntok, :], in0=t1[0:ntok, :], scalar=1.0, in1=R["rk"][0:ntok, :],
                                                     op0=ALU.mult, op1=ALU.mult, accum_out=rks[0:ntok, 0:1]),
             reads=["rw_t1", "rwc_par"], writes=["rw_junk", "rw_rks"])
        P.op("pool", lambda e, cur=cur, GB=GB: e.tensor_scalar_mul(out=GB[0:ntok, 64:128], in0=cur[0:ntok, 128:192],
                                                                  scalar1=rks[0:ntok, 0:1]),
             reads=[cn, "rw_rks"], writes=[gn])
        P.op("act", lambda e, cur=cur, b=b: e.copy(out=R["VV"][tp][0:ntok, b * 64:(b + 1) * 64], in_=cur[0:ntok, 128:192]),
             reads=[cn], writes=["rw_VV%d" % tp])
        R3 = R["R3"][b][tp]
        r3n = "rw_R3_%d_%d" % (b, tp)
        r1, r2 = R["r1"], R["r2"]
        P.op("act", lambda e, Rt=Rt, R3=R3: e.copy(out=R3[0:ntok, 0, :], in_=Rt[0:ntok, :]), reads=[rn], writes=[r3n])
        P.op("pool", lambda e, Rt=Rt, R3=R3: e.tensor_tensor(out=r1[0:ntok, :], in0=Rt[0:ntok, :], in1=R3[0:ntok, 0, :],
                                                            op=ALU.subtract), reads=[rn, r3n], writes=["rw_r1"])
        P.op("act", lambda e, R3=R3: e.copy(out=R3[0:ntok, 1, :], in_=r1[0:ntok, :]), reads=["rw_r1"], writes=[r3n])
        P.op("pool", lambda e, R3=R3: e.tensor_tensor(out=r2[0:ntok, :], in0=r1[0:ntok, :], in1=R3[0:ntok, 1, :],
                                                     op=ALU.subtract), reads=["rw_r1", r3n], writes=["rw_r2"])
        P.op("act", lambda e, R3=R3: e.copy(out=R3[0:ntok, 2, :], in_=r2[0:ntok, :]), reads=["rw_r2"], writes=[r3n])
        P.dma("sp", lambda e, R3=R3, b=b: e.dma_start(out=rows_scr[b, :, t0:t0 + ntok, :].rearrange("p t c -> t p c"),
                                                      in_=R3[0:ntok, :, :]),
              reads=[r3n], writes=["rows%d_%d" % (b, tp)])
    P.op("pe", lambda e: e.transpose(out=R["vtp"][:, 0:ntok], in_=R["VV"][tp][0:ntok, :], identity=idf[0:ntok, 0:ntok]),
         reads=["rw_VV%d" % tp, "idf"], writes=["cl_ps"])
    P.op("act", lambda e: e.copy(out=R["vT"][tp][:, 0:ntok], in_=R["vtp"][:, 0:ntok]), reads=["cl_ps"], writes=["rw_vT%d" % tp])


def _rw_scan(P, c, R, n, ntok, rows_scr, q=None):
    P.nosame = {"dve"}
    R["q"] = q if q is not None else []
    R["per"] = -(-len(R["q"]) // max(1, ntok - 8))
    _rw_scan_body(P, c, R, n, ntok, rows_scr)
    P.nosame = set()
    P.drain(R["q"], len(R["q"]))


def _rw_scan_body(P, c, R, n, ntok, rows_scr):
    tp = n % 2
    t0 = n * ntok
    S, stmp, sk = R["S"], R["stmp"], R["sk"]
    vT, yT = R["vT"][tp], R["yT"][tp]
    vn, yn = "rw_vT%d" % tp, "rw_yT%d" % tp
    for blk in range(0, ntok, 16):
        nb = min(16, ntok - blk)
        rb = R["k"] % 2
        R["k"] += 1
        rowbuf = R["rowbuf"][rb]
        P.dma("sp", lambda e, rowbuf=rowbuf, blk=blk, nb=nb: e.dma_start(
            out=rowbuf[0:6, 0:nb * 320].rearrange("q (s c) -> q s c", c=320),
            in_=rows_scr[:, :, t0 + blk:t0 + blk + nb, :].rearrange("b p s c -> (b p) s c")),
            reads=["rows0_%d" % tp, "rows1_%d" % tp], writes=["rw_rowbuf%d" % rb])
        for s in range(nb):
            pb = R["step"] % 2
            R["step"] += 1
            rowp = R["rowp"][pb]
            pn = "sT%d" % pb
            t = blk + s
            P.op("pe", lambda e, rowp=rowp, rowbuf=rowbuf, s=s: e.matmul(
                rowp[:, 0:320], lhsT=R["selb"][0:6, :], rhs=rowbuf[0:6, s * 320:(s + 1) * 320], start=True, stop=True),
                reads=["rw_rowbuf%d" % rb, "rwc_selb"], writes=[pn])
            P.op("dve", lambda e, rowp=rowp: e.scalar_tensor_tensor(
                out=stmp[:, :], in0=S[:, :], scalar=1.0, in1=rowp[:, 64:128], op0=ALU.mult, op1=ALU.mult,
                accum_out=sk[:, 0:1]), reads=["rw_S", pn], writes=["rw_stmp", "rw_sk"])
            P.op("dve", lambda e, rowp=rowp: e.tensor_tensor(out=S[:, :], in0=S[:, :], in1=rowp[:, 0:64], op=ALU.mult),
                 reads=["rw_S", pn], writes=["rw_S"])
            P.op("dve", lambda e, rowp=rowp: e.scalar_tensor_tensor(
                out=S[:, :], in0=rowp[:, 128:192], scalar=sk[:, 0:1], in1=S[:, :], op0=ALU.mult, op1=ALU.add),
                reads=["rw_S", "rw_sk", pn], writes=["rw_S"])
            P.op("dve", lambda e, rowp=rowp, t=t: e.scalar_tensor_tensor(
                out=S[:, :], in0=rowp[:, 192:256], scalar=vT[:, t:t + 1], in1=S[:, :], op0=ALU.mult, op1=ALU.add),
                reads=["rw_S", vn, pn], writes=["rw_S"])
            P.op("dve", lambda e, rowp=rowp, t=t: e.scalar_tensor_tensor(
                out=stmp[:, :], in0=S[:, :], scalar=1.0, in1=rowp[:, 256:320], op0=ALU.mult, op1=ALU.mult,
                accum_out=yT[:, t:t + 1]), reads=["rw_S", pn], writes=["rw_stmp", yn])
            if R["q"]:
                P.drain(R["q"], R["per"])


def _rw_post(P, c, R, n, ntok, out_dst):
    tp = n % 2
    idf = c["idf"]
    ytp = R["ytp"]
    cen, ob, junk, mean, var = R["cen"], R["ob"], R["junk"], R["mean"], R["var"]
    ysb = R["ysb"]
    P.op("pe", lambda e: e.transpose(out=ytp[0:ntok, 0:128], in_=R["yT"][tp][:, 0:ntok], identity=idf[:, :]),
         reads=["rw_yT%d" % tp, "idf"], writes=["tot_ps"])
    P.op("act", lambda e: e.copy(out=ysb[0:ntok, :], in_=ytp[0:ntok, 0:128]), reads=["tot_ps"], writes=["rw_ysb"])
    for b in range(2):
        GB = R["GB"][b][tp]
        gn = "rw_GB%d_%d" % (b, tp)
        ysl = ysb[0:ntok, b * 64:(b + 1) * 64]
        P.op("dve", lambda e, ysl=ysl: e.tensor_reduce(out=mean[0:ntok, :], in_=ysl, axis=AX.X, op=ALU.add),
             reads=["rw_ysb"], writes=["rw_mean"])
        P.op("pool", lambda e: e.tensor_single_scalar(out=mean[0:ntok, :], in_=mean[0:ntok, :], scalar=1.0 / 64, op=ALU.mult),
             reads=["rw_mean"], writes=["rw_mean"])
        P.op("pool", lambda e, ysl=ysl: e.tensor_scalar(out=cen[0:ntok, :], in0=ysl, scalar1=mean[0:ntok, 0:1], scalar2=0.0,
                                                        op0=ALU.subtract, op1=ALU.add),
             reads=["rw_ysb", "rw_mean"], writes=["rw_cen"])
        P.op("dve", lambda e: e.scalar_tensor_tensor(out=junk[0:ntok, :], in0=cen[0:ntok, :], scalar=1.0, in1=cen[0:ntok, :],
                                                     op0=ALU.mult, op1=ALU.mult, accum_out=var[0:ntok, 0:1]),
             reads=["rw_cen"], writes=["rw_junk", "rw_var"])
        P.fence("dve", ["rw_var"])
        P.op("act", lambda e: e.activation(out=var[0:ntok, :], in_=var[0:ntok, :], func=AF.Sqrt, bias=64e-5, scale=1.0 / 64),
             reads=["rw_var"], writes=["rw_var"])
        P.op("dve", lambda e: e.reciprocal(out=var[0:ntok, :], in_=var[0:ntok, :]), reads=["rw_var"], writes=["rw_var"])
        P.op("pool", lambda e: e.tensor_scalar_mul(out=cen[0:ntok, :], in0=cen[0:ntok, :], scalar1=var[0:ntok, 0:1]),
             reads=["rw_cen", "rw_var"], writes=["rw_cen"])
        P.op("pool", lambda e: e.tensor_tensor(out=cen[0:ntok, :], in0=cen[0:ntok, :], in1=R["lnw"][0:ntok, :], op=ALU.mult),
             reads=["rw_cen", "rwc_par"], writes=["rw_cen"])
        P.op("pool", lambda e: e.tensor_tensor(out=cen[0:ntok, :], in0=cen[0:ntok, :], in1=R["lnb"][0:ntok, :], op=ALU.add),
             reads=["rw_cen", "rwc_par"], writes=["rw_cen"])
        P.op("pool", lambda e, GB=GB: e.tensor_tensor(out=cen[0:ntok, :], in0=cen[0:ntok, :], in1=GB[0:ntok, 64:128], op=ALU.add),
             reads=["rw_cen", gn], writes=["rw_cen"])
        P.op("pool", lambda e, GB=GB: e.tensor_tensor(out=ob[0:ntok, :], in0=cen[0:ntok, :], in1=GB[0:ntok, 0:64], op=ALU.mult),
             reads=["rw_cen", gn], writes=["rw_ob"])
        P.dma("sp", lambda e, b=b: e.dma_start(out=out_dst(b, n), in_=ob[0:ntok, :]), reads=["rw_ob"], writes=["rw_out"])


def _rw_pair(P, c, R, ntiles, ntok, cur_src, prev_src, rows_scr, S0_src, out_dst, ST_dst):
    S = R["S"]
    if S0_src is None:
        P.op("dve", lambda e: e.memset(S[:, :], 0.0), writes=["rw_S"])
    else:
        P.dma("sp", lambda e: e.dma_start(out=S[:, :], in_=S0_src), writes=["rw_S"])
    _rw_prep(P, c, R, 0, ntok, cur_src, prev_src, rows_scr)
    pend = []
    for n in range(ntiles):
        P.defer = pend
        if n + 1 < ntiles:
            _rw_prep(P, c, R, n + 1, ntok, cur_src, prev_src, rows_scr)
        P.defer = None
        _rw_scan(P, c, R, n, ntok, rows_scr, pend)
        pend = []
        P.defer = pend
        _rw_post(P, c, R, n, ntok, out_dst)
        P.defer = None
    P.drain(pend, len(pend))
    P.dma("sp", lambda e: e.dma_start(out=ST_dst, in_=S[:, :]), reads=["rw_S"], writes=["rw_STout"])


def build_phase2(n_prompt=2, n_sample=NSEQ_S, ngroups=32, rw_tiles=128, rw_pairs=8, do_fox=True):
    nc = bass.Bass("TRN2", target_bir_lowering=False)
    qTp = _din(nc, "qTp", [2, 64, TP])
    kTp = _din(nc, "kTp", [2, 64, TP])
    vp = _din(nc, "vp", [2, 128, 128, 64])
    lfp = _din(nc, "lfp", [2, 128, 128])
    qTs = _din(nc, "qTs", [NSEQ_S, 64, 16])
    kTs = _din(nc, "kTs", [NSEQ_S, 64, TS])
    vs = _din(nc, "vs", [NSEQ_S, 128, 17, 64])
    lfs = _din(nc, "lfs", [NSEQ_S, 128, 17])
    op_ = _dout(nc, "o_p", [2, TP, 64])
    os_ = _dout(nc, "o_s", [NSEQ_S, 16, 64])
    curp = _din(nc, "rw_curp", [2, TP, 352])
    prvp = _din(nc, "rw_prvp", [2, TP, 352])
    curs = _din(nc, "rw_curs", [NSEQ_S, 16, 352])
    prvs = _din(nc, "rw_prvs", [NSEQ_S, 16, 352])
    S0s = _din(nc, "rw_S0s", [8, 128, 64])
    rwp = _dout(nc, "rw_p", [2, TP, 64])
    rws = _dout(nc, "rw_s", [NSEQ_S, 16, 64])
    STp = _dout(nc, "ST_p", [128, 64])
    STs = _dout(nc, "ST_s", [8, 128, 64])
    rows_p = nc.dram_tensor("rows_p", [2, 3, TP, 320], BF16).ap()
    rows_s = nc.dram_tensor("rows_s", [8, 2, 3, 16, 320], BF16).ap()
    P = Prog(nc)
    c = _fox_consts(P, nc)
    B = _fox_bufs(P)
    if do_fox:
        for b in range(n_prompt):
            groups = [(512 * g, 512, 4 * g + 4, 4 * g, 4 * g + 2) for g in range(ngroups)]

            def out_fn(P, osb, q0, nq, nqi, wmax, b=b):
                P.dma("sp", lambda e: e.dma_start(
                    out=op_[b, q0:q0 + nq, :].rearrange("(a p) d -> p a d", p=128), in_=osb[:, 0:nqi, :]),
                    reads=["osb"], writes=["o_out"])
            _fox_seq(P, c, B, "p%d" % b, 128, qTp[b], TP, kTp[b], vp[b], lfp[b], groups, out_fn)
        for s in range(n_sample):
            groups = [(0, 16, 17, 16, 16)]

            def out_fn(P, osb, q0, nq, nqi, wmax, s=s):
                P.dma("sp", lambda e: e.dma_start(out=os_[s, :, :], in_=osb[0:16, 0, :]), reads=["osb"], writes=["o_out"])
            _fox_seq(P, c, B, "s%d" % s, 17, qTs[s], 16, kTs[s], vs[s], lfs[s], groups, out_fn)
    R = _rw_setup(P, nc, B)
    if rw_tiles > 0:
        _rw_pair(P, c, R, rw_tiles, 128,
                 lambda b, n: curp[b, n * 128:(n + 1) * 128, :], lambda b, n: prvp[b, n * 128:(n + 1) * 128, :],
                 rows_p, None, lambda b, n: rwp[b, n * 128:(n + 1) * 128, :], STp)
    for pr in range(rw_pairs):
        _rw_pair(P, c, R, 1, 16,
                 lambda b, n, pr=pr: curs[2 * pr + b, :, :], lambda b, n, pr=pr: prvs[2 * pr + b, :, :],
                 rows_s[pr], S0s[pr], lambda b, n, pr=pr: rws[2 * pr + b, :, :], STs[pr])
    P.emit()
    return nc


def rw_inputs(pp, psm, state_rwkv, state_shift, prm):
    ppb = pp.reshape(2, TP, IN_COLS)[:, :, FOX_COLS:]
    pss = psm.reshape(NSEQ_S, 16, IN_COLS)[:, :, FOX_COLS:]
    prev_p = np.zeros_like(ppb)
    prev_p[:, 1:] = ppb[:, :-1]
    prev_s = np.empty_like(pss)
    prev_s[:, 1:] = pss[:, :-1]
    prev_s[:, 0] = state_shift[:, 0, :]
    maps = []
    sel = np.zeros((6, 128), np.float32)
    sel[0:3, :64] = 1.0
    sel[3:6, 64:] = 1.0
    for h in range(NCORE):
        cols = np.concatenate([np.arange(h * 64, (h + 1) * 64), 512 + np.arange(h * 64, (h + 1) * 64),
                               1024 + np.arange(h * 64, (h + 1) * 64), np.arange(1536, 1696)])
        hs = slice(h * 64, (h + 1) * 64)
        par = np.concatenate([prm["rwkv_mu"][cols], prm["rwkv_w0"][hs], prm["rwkv_a0"][hs], prm["rwkv_k_k"][hs],
                              prm["rwkv_k_a"][hs], prm["rwkv_r_k"][h], prm["rwkv_lnx_w"][hs], prm["rwkv_lnx_b"][hs]])
        m = {
            "rw_curp": np.ascontiguousarray(ppb[:, :, cols]), "rw_prvp": np.ascontiguousarray(prev_p[:, :, cols]),
            "rw_curs": np.ascontiguousarray(pss[:, :, cols]), "rw_prvs": np.ascontiguousarray(prev_s[:, :, cols]),
            "rw_S0s": np.ascontiguousarray(state_rwkv[:, h].reshape(8, 128, 64)),
            "rw_par": np.ascontiguousarray(np.broadcast_to(par[None, :], (128, NPAR))).astype(np.float32),
            "rw_w2": np.ascontiguousarray(prm["rwkv_w2"][:, hs]), "rw_a2": np.ascontiguousarray(prm["rwkv_a2"][:, hs]),
            "rw_g2": np.ascontiguousarray(prm["rwkv_g2"][:, hs]), "rw_sel": sel,
        }
        maps.append(m)
    return maps


STAGE = 99


def build_phase3(NT, nslots=128):
    nc = bass.Bass("TRN2", target_bir_lowering=False)
    x = _din(nc, "x", [NT * 128, D])
    mixT = _din(nc, "mixT", [NT, 128, 8, 128])
    wout = _din(nc, "w_out", [D, D])
    wq = _din(nc, "w_q", [D, D])
    skT_d = _din(nc, "skT", [64, 16 * 128])
    g1 = _din(nc, "gffn", [128, D])
    g2 = _din(nc, "gfin", [128, D])
    identd = _din(nc, "ident", [128, 128])
    iota_d = _din(nc, "iota", [128, 256])
    pu = _din(nc, "peer_u", [16384, D])
    pv = _din(nc, "peer_v", [16384, D])
    y = _dout(nc, "y", [NT * 128, D])
    puv16 = nc.dram_tensor("puv16", [16384, 2 * D], BF16).ap()
    P = Prog(nc)
    cst = [P.sb("cst%d" % i, [128, 4096], F32) for i in range(2)]
    cbf = [P.sb("cbf%d" % i, [128, 4096], BF16) for i in range(2)]
    ck = 0
    if nslots > 0:
        for (src, half) in ((pu, 0), (pv, 1)):
            sv_ = src.rearrange("(p r) d -> p (r d)", p=128)
            dv_ = puv16.rearrange("(p r) d -> p r d", p=128)
            for pc in range(32):
                i2 = ck % 2
                P.dma("sp", lambda e, i2=i2, sv_=sv_, pc=pc: e.dma_start(out=cst[i2][:, :], in_=sv_[:, pc * 4096:(pc + 1) * 4096]),
                      writes=["cst%d" % i2])
                eng = ("act", "pool", "dve")[ck % 3]
                if eng == "act":
                    P.op("act", lambda e, i2=i2: e.copy(out=cbf[i2][:, :], in_=cst[i2][:, :]), reads=["cst%d" % i2], writes=["cbf%d" % i2])
                else:
                    P.op(eng, lambda e, i2=i2: e.tensor_copy(out=cbf[i2][:, :], in_=cst[i2][:, :]), reads=["cst%d" % i2],
                         writes=["cbf%d" % i2])
                P.dma("sp", lambda e, i2=i2, dv_=dv_, pc=pc, half=half: e.dma_start(
                    out=dv_[:, pc * 4:(pc + 1) * 4, half * D:(half + 1) * D], in_=cbf[i2][:, :].rearrange("p (r d) -> p r d", d=D)),
                    reads=["cbf%d" % i2], writes=["puv16"])
                ck += 1
    wst = P.sb("wst", [128, D], F32)
    wo_bf = P.sb("wo_bf", [128, 8, D], BF16)
    wq_bf = P.sb("wq_bf", [128, 8, D], BF16)
    sk_bf = P.sb("sk_bf", [64, 16, 128], BF16)
    g1s = P.sb("g1s", [128, D], F32)
    g2s = P.sb("g2s", [128, D], F32)
    id_f = P.sb("id_f", [128, 128], F32)
    id_b = P.sb("id_b", [128, 128], BF16)
    P.dma("sp", lambda e: e.dma_start(out=g1s[:, :], in_=g1), writes=["g1"])
    P.dma("sp", lambda e: e.dma_start(out=g2s[:, :], in_=g2), writes=["g2"])
    P.dma("sp", lambda e: e.dma_start(out=id_f[:, :], in_=identd), writes=["idf"])
    P.op("dve", lambda e: e.tensor_copy(out=id_b[:, :], in_=id_f[:, :]), reads=["idf"], writes=["idb"])
    for (src, dst, nm) in ((wout, wo_bf, "wo"), (wq, wq_bf, "wq")):
        for dc in range(8):
            P.dma("sp", lambda e, src=src, dc=dc: e.dma_start(out=wst[:, :], in_=src[dc * 128:(dc + 1) * 128, :]), writes=["wst"])
            P.op("act", lambda e, dst=dst, dc=dc: e.copy(out=dst[:, dc, :], in_=wst[:, :]), reads=["wst"], writes=[nm])
    for hf in range(2):
        P.dma("sp", lambda e, hf=hf: e.dma_start(out=wst[0:64, :], in_=skT_d[:, hf * 1024:(hf + 1) * 1024]), writes=["wst"])
        P.op("act", lambda e, hf=hf: e.copy(out=sk_bf[:, hf * 8:(hf + 1) * 8, :],
                                            in_=wst[0:64, :].rearrange("p (h n) -> p h n", h=8)), reads=["wst"], writes=["sk"])

    xt = P.sb("xt", [128, D], F32)
    mt = P.sb("mt", [128, 8, 128], F32)
    mtb = P.sb("mtb", [128, 8, 128], BF16)
    x2 = P.sb("x2", [128, D], F32)
    xn = P.sb("xn", [128, D], F32)
    xnb = P.sb("xnb", [128, D], BF16)
    xnT = P.sb("xnT", [128, D], BF16)
    qT = P.sb("qT", [64, 16 * 128], BF16)
    junkb = P.sb("junkb", [128, D], BF16)
    junkD = P.sb("junkD", [128, D], F32)
    ss = P.sb("ss", [128, 1], F32)
    s_sb = P.sb("s_sb", [128, 16, 128], F32)
    s2 = P.sb("s2", [128, 128], F32)
    sv = P.sb("sv", [128, 16, 16], F32)
    si = P.sb("si", [128, 16, 16], U32)
    sif = P.sb("sif", [128, 16, 16], F32)
    cand = P.sb("cand", [128, 8, 256], F32)
    cidx = P.sb("cidx", [128, 8, 256], F32)
    c2 = P.sb("c2", [128, 256], F32)
    j256 = P.sb("j256", [128, 256], F32)
    tv = P.sb("tv", [128, 8, 16], F32)
    pi = P.sb("pi", [128, 8, 16], U32)
    pif = P.sb("pif", [128, 8, 16], F32)
    iota = P.sb("iota", [128, 256], F32)
    P.dma("sp", lambda e: e.dma_start(out=iota[:, :], in_=iota_d), writes=["iota"])
    negm = P.sb("negm", [128, 8], F32)
    gt = P.sb("gt", [128, 8, 16], F32)
    Z = P.sb("Z", [128, 8], F32)
    ef = P.sb("ef", [128, 128], F32)
    ei = P.sb("ei", [128, 128], I32)
    apre = P.sb("apre", [128, 128], F32)
    coef = P.sb("coef", [128, 128], F32)
    acc = P.sb("acc", [128, D], F32)
    yo = P.sb("yo", [128, D], F32)
    NG = 8
    gb = [P.sb("gbuf%d" % i, [128, 2 * D], BF16) for i in range(NG)]
    gl = P.sb("gl", [128, 128], F32)
    psA = P.ps("psA", [128, D], F32)
    psT = P.ps("psT", [128, D], BF16)
    psS = P.ps("psS", [128, 16, 128], F32)
    gk = 0

    def rms(src, srcn, dst, dstn, gs, gn):
        P.op("act", lambda e: e.activation(out=junkb[:, :], in_=src[:, :], func=AF.Square, accum_out=ss[:, 0:1]),
             reads=[srcn], writes=["junkb", "ss"])
        P.op("act", lambda e: e.activation(out=ss[:, :], in_=ss[:, :], func=AF.Sqrt, bias=1e-6, scale=1.0 / D),
             reads=["ss"], writes=["ss"])
        P.op("dve", lambda e: e.reciprocal(out=ss[:, :], in_=ss[:, :]), reads=["ss"], writes=["ss"])
        P.op("dve", lambda e: e.scalar_tensor_tensor(out=dst[:, :], in0=src[:, :], scalar=ss[:, 0:1], in1=gs[:, :],
                                                     op0=ALU.mult, op1=ALU.mult), reads=[srcn, "ss", gn], writes=[dstn])

    for i in range(NT):
        P.dma("sp", lambda e, i=i: e.dma_start(out=xt[:, :], in_=x[i * 128:(i + 1) * 128, :]), writes=["xt"])
        P.dma("sp", lambda e, i=i: e.dma_start(out=mt[:, :, :], in_=mixT[i]), writes=["mt"])
        P.op("act", lambda e: e.copy(out=mtb[:, :, :], in_=mt[:, :, :]), reads=["mt"], writes=["mtb"])
        for g in range(2):
            for kc in range(8):
                P.op("pe", lambda e, g=g, kc=kc: e.matmul(psA[:, g * 512:(g + 1) * 512], lhsT=mtb[:, kc, :],
                                                          rhs=wo_bf[:, kc, g * 512:(g + 1) * 512], start=(kc == 0), stop=(kc == 7)),
                     reads=["mtb", "wo"], writes=["psA%d" % g])
        for g in range(2):
            P.op("dve", lambda e, g=g: e.tensor_tensor(out=x2[:, g * 512:(g + 1) * 512], in0=psA[:, g * 512:(g + 1) * 512],
                                                       in1=xt[:, g * 512:(g + 1) * 512], op=ALU.add),
                 reads=["psA%d" % g, "xt"], writes=["x2"])
        rms(x2, "x2", xn, "xn", g1s, "g1")
        P.op("act", lambda e: e.copy(out=xnb[:, :], in_=xn[:, :]), reads=["xn"], writes=["xnb"])
        for dc in range(8 if STAGE >= 2 else 0):
            P.op("pe", lambda e, dc=dc: e.transpose(out=psT[:, dc * 128:(dc + 1) * 128], in_=xnb[:, dc * 128:(dc + 1) * 128],
                                                    identity=id_b[:, :]), reads=["xnb", "idb"], writes=["psT"])
        P.op("act", lambda e: e.copy(out=xnT[:, :], in_=psT[:, :]), reads=["psT"], writes=["xnT"])
        for hc in range(16 if STAGE >= 3 else 0):
            bank = hc % 2
            for dc in range(8):
                P.op("pe", lambda e, hc=hc, dc=dc, bank=bank: e.matmul(
                    psA[0:64, bank * 512:bank * 512 + 128], lhsT=wq_bf[:, dc, hc * 64:(hc + 1) * 64],
                    rhs=xnT[:, dc * 128:(dc + 1) * 128], start=(dc == 0), stop=(dc == 7)),
                    reads=["xnT", "wq"], writes=["psA%d" % bank])
            if hc % 2 == 0:
                P.op("act", lambda e, hc=hc, bank=bank: e.copy(out=qT[0:64, hc * 128:(hc + 1) * 128],
                                                               in_=psA[0:64, bank * 512:bank * 512 + 128]),
                     reads=["psA%d" % bank], writes=["qT"])
            else:
                P.op("dve", lambda e, hc=hc, bank=bank: e.tensor_copy(out=qT[0:64, hc * 128:(hc + 1) * 128],
                                                                      in_=psA[0:64, bank * 512:bank * 512 + 128]),
                     reads=["psA%d" % bank], writes=["qT"])
        for hc in range(16 if STAGE >= 4 else 0):
            P.op("pe", lambda e, hc=hc: e.matmul(psS[:, hc, :], lhsT=qT[0:64, hc * 128:(hc + 1) * 128],
                                                 rhs=sk_bf[0:64, hc, :], start=True, stop=True),
                 reads=["qT", "sk"], writes=["psS"])
        for bk in range(4):
            if bk % 2 == 0:
                P.op("act", lambda e, bk=bk: e.copy(out=s_sb[:, 4 * bk:4 * bk + 4, :], in_=psS[:, 4 * bk:4 * bk + 4, :]),
                     reads=["psS"], writes=["s_sb"])
            else:
                P.op("dve", lambda e, bk=bk: e.tensor_copy(out=s_sb[:, 4 * bk:4 * bk + 4, :], in_=psS[:, 4 * bk:4 * bk + 4, :]),
                     reads=["psS"], writes=["s_sb"])
        for hc in range(16 if STAGE >= 5 else 0):
            P.op("dve", lambda e, hc=hc: e.max(out=sv[:, hc, 0:8], in_=s_sb[:, hc, :]), reads=["s_sb"], writes=["sv"])
            P.op("dve", lambda e, hc=hc: e.max_index(out=si[:, hc, 0:8], in_max=sv[:, hc, 0:8], in_values=s_sb[:, hc, :]),
                 reads=["s_sb", "sv"], writes=["si"])
            P.op("dve", lambda e, hc=hc: e.match_replace(out=s2[:, :], in_to_replace=sv[:, hc, 0:8], in_values=s_sb[:, hc, :],
                                                         imm_value=-1e30), reads=["s_sb", "sv"], writes=["s2"])
            P.op("dve", lambda e, hc=hc: e.max(out=sv[:, hc, 8:16], in_=s2[:, :]), reads=["s2"], writes=["sv"])
            P.op("dve", lambda e, hc=hc: e.max_index(out=si[:, hc, 8:16], in_max=sv[:, hc, 8:16], in_values=s2[:, :]),
                 reads=["s2", "sv"], writes=["si"])
        P.op("dve", lambda e: e.tensor_copy(out=sif[:, :, :], in_=si[:, :, :]), reads=["si"], writes=["sif"])
        for h in range(8 if STAGE >= 6 else 0):
            P.op("dve", lambda e, h=h: e.tensor_single_scalar(out=sif[:, 2 * h, :], in_=sif[:, 2 * h, :], scalar=128.0, op=ALU.mult),
                 reads=["sif"], writes=["sif"])
            P.op("dve", lambda e, h=h: e.tensor_tensor(
                out=cand[:, h, :].rearrange("p (a b) -> p a b", a=16),
                in0=sv[:, 2 * h, :].unsqueeze(2).to_broadcast([128, 16, 16]),
                in1=sv[:, 2 * h + 1, :].unsqueeze(1).to_broadcast([128, 16, 16]), op=ALU.add), reads=["sv"], writes=["cand"])
            P.op("dve", lambda e, h=h: e.tensor_tensor(
                out=cidx[:, h, :].rearrange("p (a b) -> p a b", a=16),
                in0=sif[:, 2 * h, :].unsqueeze(2).to_broadcast([128, 16, 16]),
                in1=sif[:, 2 * h + 1, :].unsqueeze(1).to_broadcast([128, 16, 16]), op=ALU.add), reads=["sif"], writes=["cidx"])
            P.op("dve", lambda e, h=h: e.max(out=tv[:, h, 0:8], in_=cand[:, h, :]), reads=["cand"], writes=["tv"])
            P.op("dve", lambda e, h=h: e.max_index(out=pi[:, h, 0:8], in_max=tv[:, h, 0:8], in_values=cand[:, h, :]),
                 reads=["cand", "tv"], writes=["pi"])
            P.op("dve", lambda e, h=h: e.match_replace(out=c2[:, :], in_to_replace=tv[:, h, 0:8], in_values=cand[:, h, :],
                                                       imm_value=-1e30), reads=["cand", "tv"], writes=["c2"])
            P.op("dve", lambda e, h=h: e.max(out=tv[:, h, 8:16], in_=c2[:, :]), reads=["c2"], writes=["tv"])
            P.op("dve", lambda e, h=h: e.max_index(out=pi[:, h, 8:16], in_max=tv[:, h, 8:16], in_values=c2[:, :]),
                 reads=["c2", "tv"], writes=["pi"])
            P.op("dve", lambda e, h=h: e.tensor_copy(out=pif[:, h, :], in_=pi[:, h, :]), reads=["pi"], writes=["pif"])
            for k in range(16):
                P.op("dve", lambda e, h=h, k=k: e.scalar_tensor_tensor(
                    out=j256[:, :], in0=iota[:, :], scalar=pif[:, h, k:k + 1], in1=cidx[:, h, :], op0=ALU.is_equal,
                    op1=ALU.mult, accum_out=ef[:, h * 16 + k:h * 16 + k + 1]), reads=["iota", "cidx", "pif"], writes=["j256", "ef"])
        P.op("dve", lambda e: e.tensor_copy(out=ei[:, :], in_=ef[:, :]), reads=["ef"], writes=["ei"])
        P.op("dve", lambda e: e.tensor_single_scalar(out=negm[:, :], in_=tv[:, :, 0], scalar=-1.0, op=ALU.mult),
             reads=["tv"], writes=["negm"])
        for h in range(8 if STAGE >= 7 else 0):
            P.op("act", lambda e, h=h: e.activation(out=gt[:, h, :], in_=tv[:, h, :], func=AF.Exp, bias=negm[:, h:h + 1],
                                                    scale=1.0, accum_out=Z[:, h:h + 1]), reads=["tv", "negm"], writes=["gt", "Z"])
        P.fence("act", ["Z", "gt"])
        P.op("dve", lambda e: e.reciprocal(out=Z[:, :], in_=Z[:, :]), reads=["Z"], writes=["Z"])
        for h in range(8):
            P.op("dve", lambda e, h=h: e.tensor_scalar_mul(out=gt[:, h, :], in0=gt[:, h, :], scalar1=Z[:, h:h + 1]),
                 reads=["gt", "Z"], writes=["gt"])
        gtf = gt[:, :, :].rearrange("p h k -> p (h k)")
        slot_buf = {}

        def vacc(sl):
            r = slot_buf[sl]
            P.op("dve", lambda e, sl=sl: e.tensor_tensor(out=coef[:, sl:sl + 1], in0=gl[:, sl:sl + 1], in1=gtf[:, sl:sl + 1], op=ALU.mult),
                 reads=["gl%d" % sl, "gt"], writes=["coef%d" % sl])
            if sl == 0:
                P.op("dve", lambda e, r=r: e.tensor_scalar_mul(out=acc[:, :], in0=gb[r][:, D:2 * D], scalar1=coef[:, 0:1]),
                     reads=["gbuf%d" % r, "coef0"], writes=["acc"])
            else:
                P.op("dve", lambda e, r=r, sl=sl: e.scalar_tensor_tensor(
                    out=acc[:, :], in0=gb[r][:, D:2 * D], scalar=coef[:, sl:sl + 1], in1=acc[:, :], op0=ALU.mult, op1=ALU.add),
                    reads=["gbuf%d" % r, "coef%d" % sl, "acc"], writes=["acc"])

        LAG = 3
        for sl in range(nslots):
            r = gk % NG
            gk += 1
            slot_buf[sl] = r
            P.dma("pool", lambda e, r=r, sl=sl: e.indirect_dma_start(
                out=gb[r][:, :], out_offset=None, in_=puv16[:, :],
                in_offset=bass.IndirectOffsetOnAxis(ap=ei[:, sl:sl + 1], axis=0)), reads=["ei", "puv16"], writes=["gbuf%d" % r])
            P.op("dve", lambda e, r=r, sl=sl: e.scalar_tensor_tensor(
                out=junkD[:, :], in0=gb[r][:, 0:D], scalar=1.0, in1=xn[:, :], op0=ALU.mult, op1=ALU.mult,
                accum_out=apre[:, sl:sl + 1]), reads=["gbuf%d" % r, "xn"], writes=["junkD", "apre_raw%d" % sl])
            P.fence("dve", ["apre%d" % sl])
            P.op("act", lambda e, sl=sl: e.activation(out=gl[:, sl:sl + 1], in_=apre[:, sl:sl + 1], func=AF.Gelu),
                 reads=["apre%d" % sl], writes=["gl%d" % sl])
            if sl >= LAG:
                vacc(sl - LAG)
        for sl in range(max(0, nslots - LAG), nslots):
            vacc(sl)
        if nslots > 0:
            P.op("dve", lambda e: e.tensor_tensor(out=x2[:, :], in0=x2[:, :], in1=acc[:, :], op=ALU.add),
                 reads=["x2", "acc"], writes=["x2"])
        rms(x2, "x2", yo, "yo", g2s, "g2")
        if nslots == 0:
            P.op("dve", lambda e: e.tensor_copy(out=yo[:, 0:128], in_=ef[:, :]), reads=["ef", "yo"], writes=["yo"])
            P.op("dve", lambda e: e.tensor_copy(out=yo[:, 128:256], in_=gt[:, :, :].rearrange("p h k -> p (h k)")),
                 reads=["gt", "yo"], writes=["yo"])
        P.dma("sp", lambda e, i=i: e.dma_start(out=y[i * 128:(i + 1) * 128, :], in_=yo[:, :]), reads=["yo"], writes=["yout"])
    P.emit()
    return nc


def run_phase3(x_prompt, x_sample, mix_p, mix_s, w_out, norm_ffn_g, peer_w_q, peer_sub_keys, peer_u, peer_v, norm_final_g,
               NT=33, nslots=128):
    xp = np.ascontiguousarray(x_prompt).reshape(-1, D)
    xs = np.ascontiguousarray(x_sample).reshape(-1, D)
    nc = build_phase3(NT, nslots)
    g1 = np.ascontiguousarray(np.broadcast_to(norm_ffn_g.reshape(1, D), (128, D))).astype(np.float32)
    g2 = np.ascontiguousarray(np.broadcast_to(norm_final_g.reshape(1, D), (128, D))).astype(np.float32)
    skT = np.ascontiguousarray(np.transpose(peer_sub_keys, (3, 0, 1, 2)).reshape(64, 16 * 128))
    iota_h = np.ascontiguousarray(np.broadcast_to(np.arange(256, dtype=np.float32)[None, :], (128, 256)))
    in_maps = []
    for c in range(NCORE):
        xc = np.zeros((33 * 128, D), np.float32)
        mc = np.zeros((33 * 128, D), np.float32)
        xc[:4096] = xp[c * 4096:(c + 1) * 4096]
        xc[4096:4128] = xs[c * 32:(c + 1) * 32]
        mc[:4096] = mix_p[c * 4096:(c + 1) * 4096]
        mc[4096:4128] = mix_s[c * 32:(c + 1) * 32]
        mT = np.ascontiguousarray(np.transpose(mc.reshape(33, 128, 8, 128), (0, 3, 2, 1)))
        in_maps.append({"x": xc[:NT * 128], "mixT": mT[:NT], "w_out": np.ascontiguousarray(w_out), "w_q": np.ascontiguousarray(peer_w_q),
                        "skT": skT, "iota": iota_h, "gffn": g1, "gfin": g2, "ident": _ident(), "peer_u": np.ascontiguousarray(peer_u),
                        "peer_v": np.ascontiguousarray(peer_v)})
    res = run_bass_kernel_spmd(nc, in_maps, core_ids=list(range(NCORE)))
    yp = np.concatenate([res.results[c]["y"][:4096] for c in range(NCORE)], axis=0) if NT == 33 else None
    ysm = np.concatenate([res.results[c]["y"][4096:4128] for c in range(NCORE)], axis=0) if NT == 33 else None
    return yp, ysm, res


def kernel(x_prompt, x_sample, cache_fox_k, cache_fox_v, cache_fox_logf, state_rwkv, state_shift,
           norm_mix_g, w_in, fox_b_f, rwkv_mu, rwkv_w0, rwkv_w2, rwkv_a0, rwkv_a2, rwkv_g2,
           rwkv_k_k, rwkv_k_a, rwkv_r_k, rwkv_lnx_w, rwkv_lnx_b, w_out, norm_ffn_g,
           peer_w_q, peer_sub_keys, peer_u, peer_v, norm_final_g):
    f = lambda a: np.asarray(a, dtype=np.float32)
    x_prompt, x_sample = f(x_prompt), f(x_sample)
    pp, psm = run_phase1(x_prompt, x_sample, f(norm_mix_g)[0], f(w_in)[0], f(fox_b_f)[0])
    prm = {"rwkv_mu": f(rwkv_mu)[0], "rwkv_w0": f(rwkv_w0)[0], "rwkv_w2": f(rwkv_w2)[0], "rwkv_a0": f(rwkv_a0)[0],
           "rwkv_a2": f(rwkv_a2)[0], "rwkv_g2": f(rwkv_g2)[0], "rwkv_k_k": f(rwkv_k_k)[0], "rwkv_k_a": f(rwkv_k_a)[0],
           "rwkv_r_k": f(rwkv_r_k)[0], "rwkv_lnx_w": f(rwkv_lnx_w)[0], "rwkv_lnx_b": f(rwkv_lnx_b)[0]}
    maps = fox_inputs(pp, psm, f(cache_fox_k)[0], f(cache_fox_v)[0], f(cache_fox_logf)[0])
    rmaps = rw_inputs(pp, psm, f(state_rwkv)[0], f(state_shift)[0], prm)
    for m, r in zip(maps, rmaps):
        m.update(r)
    nc2 = build_phase2()
    res2 = run_bass_kernel_spmd(nc2, maps, core_ids=list(range(NCORE)))
    del maps, rmaps
    R2 = res2.results
    mix_p = np.empty((2, TP, 1024), np.float32)
    mix_s = np.empty((NSEQ_S, 16, 1024), np.float32)
    S_p = np.empty((1, 2, 8, 64, 64), np.float32)
    S_s = np.empty((1, NSEQ_S, 8, 64, 64), np.float32)
    for h in range(NCORE):
        mix_p[:, :, h * 64:(h + 1) * 64] = R2[h]["o_p"]
        mix_p[:, :, 512 + h * 64:512 + (h + 1) * 64] = R2[h]["rw_p"]
        mix_s[:, :, h * 64:(h + 1) * 64] = R2[h]["o_s"]
        mix_s[:, :, 512 + h * 64:512 + (h + 1) * 64] = R2[h]["rw_s"]
        S_p[0, :, h] = R2[h]["ST_p"].reshape(2, 64, 64)
        S_s[0, :, h] = R2[h]["ST_s"].reshape(NSEQ_S, 64, 64)
    yp, ysm, _ = run_phase3(x_prompt, x_sample, mix_p.reshape(-1, 1024), mix_s.reshape(-1, 1024), f(w_out)[0],
                            f(norm_ffn_g)[0], f(peer_w_q)[0], f(peer_sub_keys)[0], f(peer_u)[0], f(peer_v)[0],
                            f(norm_final_g))
    ppb = pp.reshape(2, TP, IN_COLS)
    pss = psm.reshape(NSEQ_S, 16, IN_COLS)
    c = np.ascontiguousarray
    return (
        c(yp.reshape(2, TP, 1024)), c(ysm.reshape(NSEQ_S, 16, 1024)),
        c(ppb[:, :, 512:1024].reshape(1, 2, TP, 8, 64)), c(ppb[:, :, 1024:1536].reshape(1, 2, TP, 8, 64)),
        c(ppb[:, :, 1536:1544].reshape(1, 2, TP, 8)), S_p, c(ppb[:, -1:, FOX_COLS:].reshape(1, 2, 1, RW_COLS)),
        c(pss[:, :, 512:1024].reshape(1, NSEQ_S, 16, 8, 64)), c(pss[:, :, 1024:1536].reshape(1, NSEQ_S, 16, 8, 64)),
        c(pss[:, :, 1536:1544].reshape(1, NSEQ_S, 16, 8)), S_s, c(pss[:, -1:, FOX_COLS:].reshape(1, NSEQ_S, 1, RW_COLS)),
    )
```

```python
from contextlib import ExitStack
import math
import numpy as np
import concourse.bass as bass
import concourse.mybir as mybir
from concourse.bass_utils import run_bass_kernel_spmd

F32 = mybir.dt.float32
BF16 = mybir.dt.bfloat16
I32 = mybir.dt.int32
U32 = mybir.dt.uint32
ALU = mybir.AluOpType
AF = mybir.ActivationFunctionType
AX = mybir.AxisListType

D = 1024
IN_COLS = 3240
FOX_COLS = 1544
RW_COLS = 1696
NCORE = 8


class Prog:
    ENGS = ("pe", "act", "dve", "pool", "sp")

    def __init__(self, nc):
        self.nc = nc
        self.st = ExitStack()
        self.ops = {e: [] for e in self.ENGS}
        self.cnt = {}
        self.waited = {e: {} for e in self.ENGS}
        self.lastw = {}
        self.readers = {}
        self.ndma = {e: 0 for e in self.ENGS}
        self.NS = 8
        self.nosame = set()
        self.defer = None
        self.fence_t = {}
        self.uid = 0

    def sb(self, name, shape, dt):
        return self.st.enter_context(self.nc.sbuf_tensor("sb_" + name, list(shape), dt))

    def ps(self, name, shape, dt):
        return self.st.enter_context(self.nc.psum_tensor("ps_" + name, list(shape), dt))

    def _deps(self, eng, reads, writes):
        deps = []
        for b in reads:
            if b in self.lastw:
                deps.extend(self.lastw[b].items())
        for b in writes:
            if b in self.lastw:
                deps.extend(self.lastw[b].items())
            deps.extend(self.readers.get(b, ()))
        best = {}
        for (k, v) in deps:
            if eng == "pe" and k == "pe":
                continue
            if k == eng and eng in self.nosame:
                continue
            if self.waited[eng].get(k, 0) >= v:
                continue
            best[k] = max(best.get(k, 0), v)
        for k, v in best.items():
            self.waited[eng][k] = v
        return list(best.items())

    def _record(self, tok, reads, writes):
        for b in reads:
            self.readers.setdefault(b, []).append(tok)
        for b in writes:
            self.lastw.setdefault(b, {})[tok[0]] = tok[1]
            self.readers[b] = []

    def drain(self, q, k):
        saved, self.nosame = self.nosame, set()
        d, self.defer = self.defer, None
        for _ in range(min(k, len(q))):
            kind, a = q.pop(0)
            getattr(self, kind)(*a)
        self.defer, self.nosame = d, saved

    def op(self, eng, fn, reads=(), writes=()):
        if self.defer is not None:
            self.defer.append(("op", (eng, fn, tuple(reads), tuple(writes))))
            return
        waits = self._deps(eng, reads, writes)
        self.cnt[eng] = self.cnt.get(eng, 0) + 1
        self.ops[eng].append((waits, fn, eng, 1))
        self._record((eng, self.cnt[eng]), reads, writes)

    def fence(self, eng, names):
        if eng not in self.fence_t:
            self.fence_t[eng] = self.sb("fence_" + eng, [128, 2], F32)
            t0_ = self.fence_t[eng]
            d_, self.defer = self.defer, None
            if eng == "act":
                self.op("act", lambda e: e.memzero(t0_[:, :]), writes=["fence_" + eng])
            else:
                self.op("dve", lambda e: e.memset(t0_[:, :], 0.0), writes=["fence_" + eng])
            self.defer = d_
        t = self.fence_t[eng]
        if self.defer is not None:
            self.defer.append(("fence", (eng, tuple(names))))
            return
        if eng == "act":
            self.op("act", lambda e: e.copy(out=t[:, 1:2], in_=t[:, 0:1]), reads=(), writes=list(names))
        else:
            self.op("dve", lambda e: e.tensor_copy(out=t[:, 1:2], in_=t[:, 0:1]), reads=(), writes=list(names))

    def dma(self, q, fn, reads=(), writes=()):
        if self.defer is not None:
            self.defer.append(("dma", (q, fn, tuple(reads), tuple(writes))))
            return
        waits = self._deps(q, reads, writes)
        k = "d_%s_%d" % (q, self.ndma[q] % (16 if q == "pool" else self.NS))
        self.ndma[q] += 1
        prev = self.cnt.get(k, 0)
        if prev and self.waited[q].get(k, 0) < prev:
            self.waited[q][k] = prev
            waits = [w for w in waits if w[0] != k] + [(k, prev)]
        self.cnt[k] = self.cnt.get(k, 0) + 16
        self.ops[q].append((waits, fn, k, 16))
        self._record((k, self.cnt[k]), reads, writes)

    def emit(self):
        nc = self.nc
        keys = sorted(self.cnt.keys())
        sems = {k: self.st.enter_context(nc.semaphore("s_" + k)) for k in keys}
        final = [(k, self.cnt[k]) for k in keys]
        ops = self.ops

        def run(name, e):
            for (waits, fn, k, inc) in ops[name]:
                for (wk, wv) in waits:
                    e.wait_ge(sems[wk], wv)
                fn(e).then_inc(sems[k], inc)
            if name == "sp":
                for (k, v) in final:
                    e.wait_ge(sems[k], v)

        with nc.Block() as block:
            @block.tensor
            def _(e):
                run("pe", e)

            @block.scalar
            def _(e):
                run("act", e)

            @block.vector
            def _(e):
                run("dve", e)

            @block.gpsimd
            def _(e):
                run("pool", e)

            @block.sync
            def _(e):
                run("sp", e)
        self.st.close()


def _din(nc, name, shape, dt=F32):
    return nc.dram_tensor(name, list(shape), dt, kind="ExternalInput").ap()


def _dout(nc, name, shape, dt=F32):
    return nc.dram_tensor(name, list(shape), dt, kind="ExternalOutput").ap()


def _load_cast(P, name, dram_ap, shape, stage, stage_name, q="sp", eng="act"):
    t = P.sb(name, shape, BF16)
    p, n = shape
    P.dma(q, lambda e: e.dma_start(out=stage[0:p, 0:n], in_=dram_ap), writes=[stage_name])
    if eng == "act":
        P.op("act", lambda e: e.copy(out=t[:, :], in_=stage[0:p, 0:n]), reads=[stage_name], writes=[name])
    else:
        P.op("dve", lambda e: e.tensor_copy(out=t[:, :], in_=stage[0:p, 0:n]), reads=[stage_name], writes=[name])
    return t


def build_phase1(NT):
    nc = bass.Bass("TRN2", target_bir_lowering=False)
    x = _din(nc, "x", [NT * 128, D])
    gbc = _din(nc, "gbc", [128, D])
    w = _din(nc, "w_in", [D, IN_COLS])
    bfb = _din(nc, "bfb", [128, 8])
    identd = _din(nc, "ident", [128, 128])
    proj = _dout(nc, "proj", [NT * 128, IN_COLS])
    P = Prog(nc)
    wst = P.sb("wst", [128, IN_COLS], F32)
    w_bf = P.sb("w_bf", [128, 8, IN_COLS], BF16)
    g_sb = P.sb("g_sb", [128, D], F32)
    bf_sb = P.sb("bf_sb", [128, 8], F32)
    id_f = P.sb("id_f", [128, 128], F32)
    id_b = P.sb("id_b", [128, 128], BF16)
    P.dma("sp", lambda e: e.dma_start(out=g_sb[:, :], in_=gbc), writes=["g"])
    P.dma("sp", lambda e: e.dma_start(out=bf_sb[:, :], in_=bfb), writes=["bf"])
    P.dma("sp", lambda e: e.dma_start(out=id_f[:, :], in_=identd), writes=["idf"])
    P.op("dve", lambda e: e.tensor_copy(out=id_b[:, :], in_=id_f[:, :]), reads=["idf"], writes=["idb"])
    for dc in range(8):
        P.dma("sp", lambda e, dc=dc: e.dma_start(out=wst[:, :], in_=w[dc * 128:(dc + 1) * 128, :]),
              writes=["wst"])
        P.op("act", lambda e, dc=dc: e.copy(out=w_bf[:, dc, :], in_=wst[:, :]), reads=["wst"], writes=["w%d" % dc])
    wnames = ["w%d" % dc for dc in range(8)]
    xt = [P.sb("xt%d" % i, [128, D], F32) for i in range(2)]
    junk = P.sb("junk", [128, D], BF16)
    ss = [P.sb("ss%d" % i, [128, 1], F32) for i in range(2)]
    rstd = [P.sb("rstd%d" % i, [128, 1], F32) for i in range(2)]
    h = [P.sb("h%d" % i, [128, D], BF16) for i in range(2)]
    hT = [P.sb("hT%d" % i, [128, D], BF16) for i in range(2)]
    pr = [P.sb("pr%d" % i, [128, IN_COLS], F32) for i in range(2)]
    lz = P.sb("lz", [128, 8], F32)
    psT = [P.ps("psT%d" % i, [128, D], BF16) for i in range(2)]
    psP = [P.ps("psP%d" % i, [128, 512], F32) for i in range(4)]
    groups = [(c0, min(c0 + 512, IN_COLS)) for c0 in range(0, IN_COLS, 512)]
    gi = 0
    for i in range(NT):
        b = i % 2
        X, H, HT, PR = xt[b], h[b], hT[b], pr[b]
        P.dma("sp", lambda e, X=X, i=i: e.dma_start(out=X[:, :], in_=x[i * 128:(i + 1) * 128, :]),
              writes=["xt%d" % b])
        P.op("act", lambda e, X=X, b=b: e.activation(out=junk[:, :], in_=X[:, :], func=AF.Square,
                                                      accum_out=ss[b][:, 0:1]),
             reads=["xt%d" % b], writes=["junk", "ss%d" % b])
        P.op("act", lambda e, b=b: e.activation(out=rstd[b][:, :], in_=ss[b][:, :], func=AF.Sqrt, bias=1e-6,
                                                scale=1.0 / D),
             reads=["ss%d" % b], writes=["rstd%d" % b])
        P.op("dve", lambda e, b=b: e.reciprocal(out=rstd[b][:, :], in_=rstd[b][:, :]),
             reads=["rstd%d" % b], writes=["rstd%d" % b])
        P.op("dve", lambda e, X=X, H=H, b=b: e.scalar_tensor_tensor(
            out=H[:, :], in0=X[:, :], scalar=rstd[b][:, 0:1], in1=g_sb[:, :], op0=ALU.mult, op1=ALU.mult),
            reads=["xt%d" % b, "rstd%d" % b, "g"], writes=["h%d" % b])
        for dc in range(8):
            P.op("pe", lambda e, H=H, b=b, dc=dc: e.transpose(
                out=psT[b][:, dc * 128:(dc + 1) * 128], in_=H[:, dc * 128:(dc + 1) * 128], identity=id_b[:, :]),
                reads=["h%d" % b, "idb"], writes=["psT%d" % b])
        P.op("act", lambda e, HT=HT, b=b: e.copy(out=HT[:, :], in_=psT[b][:, :]),
             reads=["psT%d" % b], writes=["hT%d" % b])
        for (c0, c1) in groups:
            pp = gi % 4
            gi += 1
            n = c1 - c0
            for dc in range(8):
                P.op("pe", lambda e, HT=HT, pp=pp, dc=dc, c0=c0, c1=c1, n=n: e.matmul(
                    psP[pp][:, 0:n], lhsT=HT[:, dc * 128:(dc + 1) * 128], rhs=w_bf[:, dc, c0:c1],
                    start=(dc == 0), stop=(dc == 7)),
                    reads=["hT%d" % b] + wnames, writes=["psP%d" % pp])
            if gi % 2 == 0:
                P.op("act", lambda e, PR=PR, pp=pp, c0=c0, c1=c1, n=n: e.copy(out=PR[:, c0:c1], in_=psP[pp][:, 0:n]),
                     reads=["psP%d" % pp], writes=["pr%d" % b])
            else:
                P.op("dve", lambda e, PR=PR, pp=pp, c0=c0, c1=c1, n=n: e.tensor_copy(out=PR[:, c0:c1],
                                                                                     in_=psP[pp][:, 0:n]),
                     reads=["psP%d" % pp], writes=["pr%d" % b])
        P.op("dve", lambda e, PR=PR: e.tensor_tensor(out=lz[:, :], in0=PR[:, 1536:1544], in1=bf_sb[:, :], op=ALU.add),
             reads=["pr%d" % b, "bf"], writes=["lz"])
        P.op("act", lambda e: e.activation(out=lz[:, :], in_=lz[:, :], func=AF.Exp, scale=-1.0),
             reads=["lz"], writes=["lz"])
        P.op("act", lambda e: e.activation(out=lz[:, :], in_=lz[:, :], func=AF.Ln, bias=1.0, scale=1.0),
             reads=["lz"], writes=["lz"])
        P.op("dve", lambda e, PR=PR: e.tensor_single_scalar(out=PR[:, 1536:1544], in_=lz[:, :], scalar=-1.0,
                                                            op=ALU.mult),
             reads=["lz"], writes=["pr%d" % b])
        P.dma("sp", lambda e, PR=PR, i=i: e.dma_start(out=proj[i * 128:(i + 1) * 128, :], in_=PR[:, :]),
              reads=["pr%d" % b], writes=["out%d" % i])
    P.emit()
    return nc


def _ident():
    return np.eye(128, dtype=np.float32)


def run_phase1(x_prompt, x_sample, norm_mix_g, w_in, fox_b_f):
    NT = 33
    xp = np.ascontiguousarray(x_prompt).reshape(-1, D)
    xs = np.ascontiguousarray(x_sample).reshape(-1, D)
    nc = build_phase1(NT)
    gbc = np.ascontiguousarray(np.broadcast_to(norm_mix_g.reshape(1, D), (128, D))).astype(np.float32)
    bfb = np.ascontiguousarray(np.broadcast_to(fox_b_f.reshape(1, 8), (128, 8))).astype(np.float32)
    w = np.ascontiguousarray(w_in.reshape(D, IN_COLS))
    in_maps = []
    for c in range(NCORE):
        xc = np.zeros((NT * 128, D), np.float32)
        xc[:4096] = xp[c * 4096:(c + 1) * 4096]
        xc[4096:4128] = xs[c * 32:(c + 1) * 32]
        in_maps.append({"x": xc, "gbc": gbc, "w_in": w, "bfb": bfb, "ident": _ident()})
    res = run_bass_kernel_spmd(nc, in_maps, core_ids=list(range(NCORE)))
    pp = np.concatenate([res.results[c]["proj"][:4096] for c in range(NCORE)], axis=0)
    psm = np.concatenate([res.results[c]["proj"][4096:4128] for c in range(NCORE)], axis=0)
    return pp, psm


TP = 16384
TS = 2176
NSEQ_S = 16


def _fox_consts(P, nc):
    c = {}
    tri_d = _din(nc, "tri", [128, 128])
    ones_d = _din(nc, "ones", [128, 128])
    id_d = _din(nc, "ident", [128, 128])
    mask_d = _din(nc, "mask", [128, 4 * 512])
    c["tri"] = P.sb("tri", [128, 128], F32)
    c["ones"] = P.sb("ones", [128, 128], F32)
    c["idf"] = P.sb("idf", [128, 128], F32)
    c["idb"] = P.sb("idb", [128, 128], BF16)
    c["maskf"] = P.sb("maskf", [128, 2048], F32)
    c["mask"] = P.sb("maskb", [128, 4, 512], BF16)
    P.dma("sp", lambda e: e.dma_start(out=c["tri"][:, :], in_=tri_d), writes=["tri"])
    P.dma("sp", lambda e: e.dma_start(out=c["ones"][:, :], in_=ones_d), writes=["ones"])
    P.dma("sp", lambda e: e.dma_start(out=c["idf"][:, :], in_=id_d), writes=["idf"])
    P.dma("sp", lambda e: e.dma_start(out=c["maskf"][:, :], in_=mask_d), writes=["maskf"])
    P.op("dve", lambda e: e.tensor_copy(out=c["idb"][:, :], in_=c["idf"][:, :]), reads=["idf"], writes=["idb"])
    P.op("dve", lambda e: e.tensor_copy(out=c["mask"][:, :, :], in_=c["maskf"][:, :].rearrange("p (a b) -> p a b", a=4)),
         reads=["maskf"], writes=["maskb"])
    return c


def _fox_seq(P, c, B, tag, NT, qT_src, nq_tot, kT_src, v_src, lf_src, groups, out_fn):
    T = NT * 128
    qT, kT, vv, stage = B["qT"], B["kT"], B["vv"], B["stage"]
    k = 0
    for (dst, src, n, nm) in ((qT, qT_src, nq_tot, "qT"), (kT, kT_src, T, "kT")):
        for c0 in range(0, n, 2048):
            w = min(2048, n - c0)
            s = k % 2
            k += 1
            P.dma("sp", lambda e, s=s, src=src, c0=c0, w=w: e.dma_start(out=stage[s][0:64, 0:w], in_=src[:, c0:c0 + w]),
                  writes=["stage%d" % s])
            eng = "act" if k % 2 else "dve"
            if eng == "act":
                P.op("act", lambda e, s=s, dst=dst, c0=c0, w=w: e.copy(out=dst[:, c0:c0 + w], in_=stage[s][0:64, 0:w]),
                     reads=["stage%d" % s], writes=[nm])
            else:
                P.op("dve", lambda e, s=s, dst=dst, c0=c0, w=w: e.tensor_copy(out=dst[:, c0:c0 + w], in_=stage[s][0:64, 0:w]),
                     reads=["stage%d" % s], writes=[nm])
    for j0 in range(0, NT, 32):
        nj = min(32, NT - j0)
        s = k % 2
        k += 1
        P.dma("sp", lambda e, s=s, j0=j0, nj=nj: e.dma_start(
            out=stage[s][:, 0:nj * 64].rearrange("p (j d) -> p j d", d=64), in_=v_src[:, j0:j0 + nj, :]),
            writes=["stage%d" % s])
        P.op("dve", lambda e, s=s, j0=j0, nj=nj: e.tensor_copy(
            out=vv[:, j0:j0 + nj, 0:64], in_=stage[s][:, 0:nj * 64].rearrange("p (j d) -> p j d", d=64)),
            reads=["stage%d" % s], writes=["vv"])
    L = B["L"]
    P.dma("sp", lambda e: e.dma_start(out=L[:, 0:NT], in_=lf_src), writes=["L"])
    cl_ps, tot_ps = B["cl_ps"], B["tot_ps"]
    P.op("pe", lambda e: e.matmul(cl_ps[:, 0:NT], lhsT=c["tri"][:, :], rhs=L[:, 0:NT], start=True, stop=True),
         reads=["L", "tri"], writes=["cl_ps"])
    P.op("pe", lambda e: e.matmul(tot_ps[:, 0:NT], lhsT=c["ones"][:, :], rhs=L[:, 0:NT], start=True, stop=True),
         reads=["L", "ones"], writes=["tot_ps"])
    sa, sbb = B["scanA"], B["scanB"]
    P.op("dve", lambda e: e.tensor_copy(out=sa[:, 0:NT], in_=tot_ps[:, 0:NT]), reads=["tot_ps"], writes=["scanA"])
    cur, nxt, cn, nn = sa, sbb, "scanA", "scanB"
    sh = 1
    while sh < NT:
        P.op("dve", lambda e, cur=cur, nxt=nxt, sh=sh: e.tensor_tensor(
            out=nxt[:, sh:NT], in0=cur[:, sh:NT], in1=cur[:, 0:NT - sh], op=ALU.add), reads=[cn], writes=[nn])
        P.op("dve", lambda e, cur=cur, nxt=nxt, sh=sh: e.tensor_copy(out=nxt[:, 0:sh], in_=cur[:, 0:sh]),
             reads=[cn], writes=[nn])
        cur, nxt, cn, nn = nxt, cur, nn, cn
        sh *= 2
    pex, negC = B["pex"], B["negC"]
    P.op("dve", lambda e, cur=cur: e.tensor_tensor(out=pex[:, 0:NT], in0=cur[:, 0:NT], in1=tot_ps[:, 0:NT],
                                                   op=ALU.subtract), reads=[cn, "tot_ps"], writes=["pex"])
    P.op("dve", lambda e: e.scalar_tensor_tensor(out=negC[:, 0:NT], in0=pex[:, 0:NT], scalar=-1.0, in1=cl_ps[:, 0:NT],
                                                 op0=ALU.mult, op1=ALU.subtract),
         reads=["pex", "cl_ps"], writes=["negC"])
    bias = B["bias"]
    for gi, (q0, nq, nk, d0, ct) in enumerate(groups):
        P.op("dve", lambda e, gi=gi, nk=nk, ct=ct: e.tensor_scalar(
            out=bias[:, gi, 0:nk], in0=negC[:, 0:nk], scalar1=pex[:, ct:ct + 1], scalar2=0.0,
            op0=ALU.add, op1=ALU.add), reads=["negC", "pex"], writes=["bias"])
    it = B["it"]
    for gi, (q0, nq, nk, d0, ct) in enumerate(groups):
        ob = B["gcount"] % 2
        B["gcount"] += 1
        OT = B["OT"][ob]
        def emit_score(j, it_):
            sb_ = it_ % 3
            sT = B["sT"][sb_]
            diag = j >= d0
            P.op("pe", lambda e, sT=sT, j=j, q0=q0, nq=nq, diag=diag: e.matmul(
                sT[:, 0:nq], lhsT=kT[:, j * 128:(j + 1) * 128], rhs=qT[:, q0:q0 + nq], start=True, stop=(not diag)),
                reads=["kT", "qT"], writes=["sT%d" % sb_])
            if diag:
                jl = j - d0
                P.op("pe", lambda e, sT=sT, jl=jl, nq=nq: e.matmul(
                    sT[:, 0:nq], lhsT=c["idb"][:, :], rhs=c["mask"][:, jl, 0:nq], start=False, stop=True),
                    reads=["idb", "maskb"], writes=["sT%d" % sb_])

        emit_score(0, it)
        if nk > 1:
            emit_score(1, it + 1)
        for j in range(nk):
            sb_ = it % 3
            pb = it % 3
            sT = B["sT"][sb_]
            pT = B["pT"][pb]
            if j + 2 < nk:
                emit_score(j + 2, it + 2)
            it += 1
            P.op("act", lambda e, sT=sT, pT=pT, gi=gi, j=j, nq=nq: e.activation(
                out=pT[:, 0:nq], in_=sT[:, 0:nq], func=AF.Exp, bias=bias[:, gi, j:j + 1], scale=0.125),
                reads=["sT%d" % sb_, "bias"], writes=["pT%d" % pb])
            P.op("pe", lambda e, OT=OT, pT=pT, j=j, nq=nq, nk=nk: e.matmul(
                OT[0:65, 0:nq], lhsT=vv[:, j, :], rhs=pT[:, 0:nq], start=(j == 0), stop=(j == nk - 1)),
                reads=["vv", "pT%d" % pb], writes=["OT%d" % ob])
        oT = B["oT"]
        P.op("act", lambda e, OT=OT, nq=nq: e.copy(out=oT[0:65, 0:nq], in_=OT[0:65, 0:nq]),
             reads=["OT%d" % ob], writes=["oT"])
        oq, rec, osb = B["oq"], B["rec"], B["osb"]
        nqi = (nq + 127) // 128
        for qi in range(nqi):
            w = min(128, nq - qi * 128)
            P.op("pe", lambda e, qi=qi, w=w: e.transpose(out=oq[0:w, qi, :], in_=oT[0:65, qi * 128:qi * 128 + w],
                                                         identity=c["idf"][0:65, 0:65]),
                 reads=["oT", "idf"], writes=["oq"])
        wmax = min(128, nq)
        for qi in range(nqi):
            P.op("dve", lambda e, qi=qi: e.reciprocal(out=rec[0:wmax, qi:qi + 1], in_=oq[0:wmax, qi, 64:65]),
                 reads=["oq"], writes=["rec"])
            P.op("dve", lambda e, qi=qi: e.tensor_scalar_mul(out=osb[0:wmax, qi, :], in0=oq[0:wmax, qi, 0:64],
                                                             scalar1=rec[0:wmax, qi:qi + 1]),
                 reads=["oq", "rec"], writes=["osb"])
        out_fn(P, osb, q0, nq, nqi, wmax)
    B["it"] = it


def _fox_bufs(P):
    B = {}
    B["qT"] = P.sb("qT", [64, TP], BF16)
    B["kT"] = P.sb("kT", [64, TP], BF16)
    B["vv"] = P.sb("vv", [128, 128, 65], BF16)
    B["stage"] = [P.sb("stage%d" % i, [128, 2048], F32) for i in range(2)]
    B["L"] = P.sb("L", [128, 128], F32)
    B["scanA"] = P.sb("scanA", [128, 128], F32)
    B["scanB"] = P.sb("scanB", [128, 128], F32)
    B["pex"] = P.sb("pex", [128, 128], F32)
    B["negC"] = P.sb("negC", [128, 128], F32)
    B["bias"] = P.sb("bias", [128, 32, 128], F32)
    B["pT"] = [P.sb("pT%d" % i, [128, 512], BF16) for i in range(3)]
    B["oT"] = P.sb("oT", [65, 512], F32)
    B["rec"] = P.sb("rec", [128, 4], F32)
    B["osb"] = P.sb("osb", [128, 4, 64], F32)
    B["cl_ps"] = P.ps("cl_ps", [128, 128], F32)
    B["tot_ps"] = P.ps("tot_ps", [128, 128], F32)
    B["sT"] = [P.ps("sT%d" % i, [128, 512], F32) for i in range(3)]
    B["OT"] = [P.ps("OT%d" % i, [128, 512], F32) for i in range(2)]
    B["oq"] = P.ps("oq", [128, 4, 65], F32)
    B["it"] = 0
    B["gcount"] = 0
    P.op("pool", lambda e: e.memset(B["vv"][:, :, 64:65], 1.0), writes=["vv"])
    return B


def build_phase2_fox(n_prompt=2, n_sample=NSEQ_S, ngroups=32):
    nc = bass.Bass("TRN2", target_bir_lowering=False)
    qTp = _din(nc, "qTp", [2, 64, TP])
    kTp = _din(nc, "kTp", [2, 64, TP])
    vp = _din(nc, "vp", [2, 128, 128, 64])
    lfp = _din(nc, "lfp", [2, 128, 128])
    qTs = _din(nc, "qTs", [NSEQ_S, 64, 16])
    kTs = _din(nc, "kTs", [NSEQ_S, 64, TS])
    vs = _din(nc, "vs", [NSEQ_S, 128, 17, 64])
    lfs = _din(nc, "lfs", [NSEQ_S, 128, 17])
    op_ = _dout(nc, "o_p", [2, TP, 64])
    os_ = _dout(nc, "o_s", [NSEQ_S, 16, 64])
    P = Prog(nc)
    c = _fox_consts(P, nc)
    B = _fox_bufs(P)
    for b in range(n_prompt):
        groups = [(512 * g, 512, 4 * g + 4, 4 * g, 4 * g + 2) for g in range(ngroups)]

        def out_fn(P, osb, q0, nq, nqi, wmax, b=b):
            P.dma("sp", lambda e: e.dma_start(
                out=op_[b, q0:q0 + nq, :].rearrange("(a p) d -> p a d", p=128), in_=osb[:, 0:nqi, :]),
                reads=["osb"], writes=["o_out"])
        _fox_seq(P, c, B, "p%d" % b, 128, qTp[b], TP, kTp[b], vp[b], lfp[b], groups, out_fn)
    for s in range(n_sample):
        groups = [(0, 16, 17, 16, 16)]

        def out_fn(P, osb, q0, nq, nqi, wmax, s=s):
            P.dma("sp", lambda e: e.dma_start(out=os_[s, :, :], in_=osb[0:16, 0, :]), reads=["osb"], writes=["o_out"])
        _fox_seq(P, c, B, "s%d" % s, 17, qTs[s], 16, kTs[s], vs[s], lfs[s], groups, out_fn)
    P.emit()
    return nc


def _fox_const_inputs():
    p = np.arange(128)
    tri = (p[:, None] <= p[None, :]).astype(np.float32)
    ones = np.ones((128, 128), np.float32)
    col = np.arange(512)
    mask = np.zeros((128, 4, 512), np.float32)
    for jl in range(4):
        mask[:, jl, :] = np.where(jl * 128 + p[:, None] > col[None, :], -30000.0, 0.0)
    return {"tri": tri, "ones": ones, "ident": _ident(), "mask": mask.reshape(128, 2048)}


def _tile_major(a, nt):
    return np.ascontiguousarray(np.swapaxes(a.reshape((nt, 128) + a.shape[1:]), 0, 1))


def fox_inputs(pp, psm, cache_k, cache_v, cache_lf):
    ppb = pp.reshape(2, TP, IN_COLS)
    pss = psm.reshape(NSEQ_S, 16, IN_COLS)
    maps = []
    for h in range(NCORE):
        m = dict(_fox_const_inputs())
        m["qTp"] = np.ascontiguousarray(np.swapaxes(ppb[:, :, h * 64:(h + 1) * 64], 1, 2))
        m["kTp"] = np.ascontiguousarray(np.swapaxes(ppb[:, :, 512 + h * 64:512 + (h + 1) * 64], 1, 2))
        m["vp"] = np.stack([_tile_major(ppb[b, :, 1024 + h * 64:1024 + (h + 1) * 64], 128) for b in range(2)])
        m["lfp"] = np.stack([_tile_major(ppb[b, :, 1536 + h], 128) for b in range(2)])
        kfull = np.zeros((NSEQ_S, TS, 64), np.float32)
        vfull = np.zeros((NSEQ_S, TS, 64), np.float32)
        lfull = np.zeros((NSEQ_S, TS), np.float32)
        kfull[:, :2048] = cache_k[:, :, h, :]
        vfull[:, :2048] = cache_v[:, :, h, :]
        lfull[:, :2048] = cache_lf[:, :, h]
        kfull[:, 2048:2064] = pss[:, :, 512 + h * 64:512 + (h + 1) * 64]
        vfull[:, 2048:2064] = pss[:, :, 1024 + h * 64:1024 + (h + 1) * 64]
        lfull[:, 2048:2064] = pss[:, :, 1536 + h]
        m["qTs"] = np.ascontiguousarray(np.swapaxes(pss[:, :, h * 64:(h + 1) * 64], 1, 2))
        m["kTs"] = np.ascontiguousarray(np.swapaxes(kfull, 1, 2))
        m["vs"] = np.stack([_tile_major(vfull[s], 17) for s in range(NSEQ_S)])
        m["lfs"] = np.stack([_tile_major(lfull[s], 17) for s in range(NSEQ_S)])
        maps.append(m)
    return maps


NPAR = 352 + 7 * 64
EXPM05 = math.exp(-0.5)


def _rw_setup(P, nc, B):
    R = {}
    par_d = _din(nc, "rw_par", [128, NPAR])
    w2_d = _din(nc, "rw_w2", [32, 64])
    a2_d = _din(nc, "rw_a2", [32, 64])
    g2_d = _din(nc, "rw_g2", [96, 64])
    sel_d = _din(nc, "rw_sel", [6, 128])
    R["par"] = P.sb("rw_par", [128, NPAR], F32)
    R["w2"] = P.sb("rw_w2", [32, 64], F32)
    R["a2"] = P.sb("rw_a2", [32, 64], F32)
    R["g2"] = P.sb("rw_g2", [96, 64], F32)
    R["sel"] = P.sb("rw_sel", [6, 128], F32)
    R["selb"] = P.sb("rw_selb", [6, 128], BF16)
    R["omk"] = P.sb("rw_omk", [128, 64], F32)
    for nm, d_ in (("par", par_d), ("w2", w2_d), ("a2", a2_d), ("g2", g2_d), ("sel", sel_d)):
        P.dma("sp", lambda e, nm=nm, d_=d_: e.dma_start(out=R[nm][:, :], in_=d_), writes=["rwc_" + nm])
    P.op("dve", lambda e: e.tensor_copy(out=R["selb"][:, :], in_=R["sel"][:, :]), reads=["rwc_sel"], writes=["rwc_selb"])
    R["R3"] = [[P.sb("rw_R3_%d_%d" % (b, t), [128, 3, 320], BF16) for t in range(2)] for b in range(2)]
    R["r1"] = P.sb("rw_r1", [128, 320], F32)
    R["r2"] = P.sb("rw_r2", [128, 320], F32)
    o = 352
    R["mu"] = R["par"][:, 0:352]
    names = ["w0", "a0", "kk", "ka", "rk", "lnw", "lnb"]
    for i, nm in enumerate(names):
        R[nm] = R["par"][:, o + i * 64:o + (i + 1) * 64]
    P.op("dve", lambda e: e.tensor_scalar(out=R["omk"][:, :], in0=R["ka"], scalar1=-1.0, scalar2=1.0,
                                          op0=ALU.mult, op1=ALU.add), reads=["rwc_par"], writes=["rwc_omk"])
    R["cur"] = [P.sb("rw_cur%d" % b, [128, 352], F32) for b in range(2)]
    R["prv"] = [P.sb("rw_prv%d" % b, [128, 352], F32) for b in range(2)]
    R["R"] = [[P.sb("rw_R%d_%d" % (b, t), [128, 320], F32) for t in range(2)] for b in range(2)]
    R["GB"] = [[P.sb("rw_GB%d_%d" % (b, t), [128, 128], F32) for t in range(2)] for b in range(2)]
    R["VV"] = [P.sb("rw_VV%d" % t, [128, 128], F32) for t in range(2)]
    R["vT"] = [P.sb("rw_vT%d" % t, [128, 128], F32) for t in range(2)]
    R["yT"] = [P.sb("rw_yT%d" % t, [128, 128], F32) for t in range(2)]
    R["twl"] = P.sb("rw_twl", [32, 128], F32)
    R["alT"] = P.sb("rw_alT", [32, 128], F32)
    R["sgl"] = P.sb("rw_sgl", [96, 128], F32)
    for nm in ("zt", "at", "kkt", "tmp", "t1", "junk", "cen", "ob"):
        R[nm] = P.sb("rw_" + nm, [128, 64], F32)
    for nm in ("ssq", "rks", "mean", "var", "sk"):
        R[nm] = P.sb("rw_" + nm, [128, 1], F32)
    R["ysb"] = P.sb("rw_ysb", [128, 128], F32)
    R["S"] = P.sb("rw_S", [128, 64], F32)
    R["stmp"] = P.sb("rw_stmp", [128, 64], F32)
    R["rowbuf"] = [P.sb("rw_rowbuf%d" % i, [6, 16 * 320], BF16) for i in range(2)]
    R["rowp"] = [B["sT"][0], B["sT"][1]]
    R["trp"] = B["OT"][0]
    R["lop"] = B["OT"][1]
    R["vtp"] = B["cl_ps"]
    R["ytp"] = B["tot_ps"]
    R["k"] = 0
    R["step"] = 0
    return R


def _rw_prep(P, c, R, n, ntok, cur_src, prev_src, rows_scr):
    tp = n % 2
    idf = c["idf"]
    t0 = n * ntok
    for b in range(2):
        cur, prv = R["cur"][b], R["prv"][b]
        cn, pn = "rw_cur%d" % b, "rw_prv%d" % b
        Rt, GB = R["R"][b][tp], R["GB"][b][tp]
        rn, gn = "rw_R%d_%d" % (b, tp), "rw_GB%d_%d" % (b, tp)
        P.dma("sp", lambda e, cur=cur, b=b: e.dma_start(out=cur[0:ntok, :], in_=cur_src(b, n)), writes=[cn])
        P.dma("sp", lambda e, prv=prv, b=b: e.dma_start(out=prv[0:ntok, :], in_=prev_src(b, n)), writes=[pn])
        P.op("pool", lambda e, cur=cur, prv=prv: e.tensor_tensor(out=prv[0:ntok, :], in0=prv[0:ntok, :], in1=cur[0:ntok, :],
                                                                op=ALU.subtract), reads=[cn, pn], writes=[pn])
        P.op("pool", lambda e, prv=prv: e.tensor_tensor(out=prv[0:ntok, :], in0=prv[0:ntok, :], in1=R["mu"][0:ntok, :],
                                                       op=ALU.mult), reads=[pn, "rwc_par"], writes=[pn])
        P.op("pool", lambda e, cur=cur, prv=prv: e.tensor_tensor(out=cur[0:ntok, :], in0=cur[0:ntok, :], in1=prv[0:ntok, :],
                                                                op=ALU.add), reads=[cn, pn], writes=[cn])
        trp, lop = R["trp"], R["lop"]
        for (o0, c0, c1, m) in ((0, 192, 224, 32), (128, 224, 256, 32), (256, 256, 352, 96)):
            P.op("pe", lambda e, cur=cur, o0=o0, c0=c0, c1=c1, m=m: e.transpose(
                out=trp[0:m, o0:o0 + ntok], in_=cur[0:ntok, c0:c1], identity=idf[0:ntok, 0:ntok]),
                reads=[cn, "idf"], writes=["OT0"])
        P.op("act", lambda e: e.activation(out=R["twl"][0:32, 0:ntok], in_=trp[0:32, 0:ntok], func=AF.Tanh),
             reads=["OT0"], writes=["rw_twl"])
        P.op("act", lambda e: e.copy(out=R["alT"][0:32, 0:ntok], in_=trp[0:32, 128:128 + ntok]),
             reads=["OT0"], writes=["rw_alT"])
        P.op("act", lambda e: e.activation(out=R["sgl"][0:96, 0:ntok], in_=trp[0:96, 256:256 + ntok], func=AF.Sigmoid),
             reads=["OT0"], writes=["rw_sgl"])
        P.op("pe", lambda e: e.matmul(lop[0:ntok, 0:64], lhsT=R["twl"][0:32, 0:ntok], rhs=R["w2"][:, :], start=True, stop=True),
             reads=["rw_twl", "rwc_w2"], writes=["OT1"])
        P.op("pe", lambda e: e.matmul(lop[0:ntok, 64:128], lhsT=R["alT"][0:32, 0:ntok], rhs=R["a2"][:, :], start=True, stop=True),
             reads=["rw_alT", "rwc_a2"], writes=["OT1"])
        P.op("pe", lambda e: e.matmul(lop[0:ntok, 128:192], lhsT=R["sgl"][0:96, 0:ntok], rhs=R["g2"][:, :], start=True, stop=True),
             reads=["rw_sgl", "rwc_g2"], writes=["OT1"])
        zt, at, kkt, tmp, t1, junk = R["zt"], R["at"], R["kkt"], R["tmp"], R["t1"], R["junk"]
        ssq, rks = R["ssq"], R["rks"]
        P.op("dve", lambda e: e.tensor_tensor(out=zt[0:ntok, :], in0=lop[0:ntok, 0:64], in1=R["w0"][0:ntok, :], op=ALU.add),
             reads=["OT1", "rwc_par"], writes=["rw_zt"])
        P.op("act", lambda e: e.activation(out=zt[0:ntok, :], in_=zt[0:ntok, :], func=AF.Sigmoid),
             reads=["rw_zt"], writes=["rw_zt"])
        P.op("act", lambda e, Rt=Rt: e.activation(out=Rt[0:ntok, 0:64], in_=zt[0:ntok, :], func=AF.Exp, scale=-EXPM05),
             reads=["rw_zt"], writes=[rn])
        P.op("dve", lambda e: e.tensor_tensor(out=at[0:ntok, :], in0=lop[0:ntok, 64:128], in1=R["a0"][0:ntok, :], op=ALU.add),
             reads=["OT1", "rwc_par"], writes=["rw_at"])
        P.op("act", lambda e: e.activation(out=at[0:ntok, :], in_=at[0:ntok, :], func=AF.Sigmoid),
             reads=["rw_at"], writes=["rw_at"])
        P.op("act", lambda e, GB=GB: e.copy(out=GB[0:ntok, 0:64], in_=lop[0:ntok, 128:192]), reads=["OT1"], writes=[gn])
        P.op("pool", lambda e, cur=cur: e.tensor_tensor(out=kkt[0:ntok, :], in0=cur[0:ntok, 64:128], in1=R["kk"][0:ntok, :],
                                                       op=ALU.mult), reads=[cn, "rwc_par"], writes=["rw_kkt"])
        P.op("dve", lambda e: e.scalar_tensor_tensor(out=junk[0:ntok, :], in0=kkt[0:ntok, :], scalar=1.0, in1=kkt[0:ntok, :],
                                                     op0=ALU.mult, op1=ALU.mult, accum_out=ssq[0:ntok, 0:1]),
             reads=["rw_kkt"], writes=["rw_junk", "rw_ssq"])
        P.fence("dve", ["rw_ssq"])
        P.op("act", lambda e: e.activation(out=ssq[0:ntok, :], in_=ssq[0:ntok, :], func=AF.Sqrt), reads=["rw_ssq"], writes=["rw_ssq"])
        P.op("dve", lambda e: e.tensor_scalar_max(out=ssq[0:ntok, :], in0=ssq[0:ntok, :], scalar1=1e-12),
             reads=["rw_ssq"], writes=["rw_ssq"])
        P.op("dve", lambda e: e.reciprocal(out=ssq[0:ntok, :], in_=ssq[0:ntok, :]), reads=["rw_ssq"], writes=["rw_ssq"])
        P.op("pool", lambda e: e.tensor_scalar_mul(out=kkt[0:ntok, :], in0=kkt[0:ntok, :], scalar1=ssq[0:ntok, 0:1]),
             reads=["rw_kkt", "rw_ssq"], writes=["rw_kkt"])
        P.op("pool", lambda e, Rt=Rt: e.tensor_single_scalar(out=Rt[0:ntok, 64:128], in_=kkt[0:ntok, :], scalar=-1.0, op=ALU.mult),
             reads=["rw_kkt"], writes=[rn])
        P.op("pool", lambda e, Rt=Rt: e.tensor_tensor(out=Rt[0:ntok, 128:192], in0=kkt[0:ntok, :], in1=at[0:ntok, :], op=ALU.mult),
             reads=["rw_kkt", "rw_at"], writes=[rn])
        P.op("pool", lambda e: e.tensor_tensor(out=tmp[0:ntok, :], in0=at[0:ntok, :], in1=R["ka"][0:ntok, :], op=ALU.mult),
             reads=["rw_at", "rwc_par"], writes=["rw_tmp"])
        P.op("pool", lambda e: e.tensor_tensor(out=tmp[0:ntok, :], in0=tmp[0:ntok, :], in1=R["omk"][0:ntok, :], op=ALU.add),
             reads=["rw_tmp", "rwc_omk"], writes=["rw_tmp"])
        P.op("pool", lambda e, cur=cur, Rt=Rt: e.tensor_tensor(out=Rt[0:ntok, 192:256], in0=cur[0:ntok, 64:128], in1=tmp[0:ntok, :],
                                                              op=ALU.mult), reads=[cn, "rw_tmp"], writes=[rn])
        P.op("act", lambda e, cur=cur, Rt=Rt: e.copy(out=Rt[0:ntok, 256:320], in_=cur[0:ntok, 0:64]), reads=[cn], writes=[rn])
        P.op("pool", lambda e, cur=cur, Rt=Rt: e.tensor_tensor(out=t1[0:ntok, :], in0=cur[0:ntok, 0:64], in1=Rt[0:ntok, 192:256],
                                                              op=ALU.mult), reads=[cn, rn], writes=["rw_t1"])
        P.op("dve", lambda e: e.scalar_tensor_tensor(out=junk[0:ntok, :], in0=t1[0:ntok, :], scalar=1.0, in1=R["rk"][0:ntok, :],
                                                     op0=ALU.mult, op1=ALU.mult, accum_out=rks[0:ntok, 0:1]),
             reads=["rw_t1", "rwc_par"], writes=["rw_junk", "rw_rks"])
        P.op("pool", lambda e, cur=cur, GB=GB: e.tensor_scalar_mul(out=GB[0:ntok, 64:128], in0=cur[0:ntok, 128:192],
                                                                  scalar1=rks[0:ntok, 0:1]),
             reads=[cn, "rw_rks"], writes=[gn])
        P.op("act", lambda e, cur=cur, b=b: e.copy(out=R["VV"][tp][0:ntok, b * 64:(b + 1) * 64], in_=cur[0:ntok, 128:192]),
             reads=[cn], writes=["rw_VV%d" % tp])
        R3 = R["R3"][b][tp]
        r3n = "rw_R3_%d_%d" % (b, tp)
        r1, r2 = R["r1"], R["r2"]
        P.op("act", lambda e, Rt=Rt, R3=R3: e.copy(out=R3[0:ntok, 0, :], in_=Rt[0:ntok, :]), reads=[rn], writes=[r3n])
        P.op("pool", lambda e, Rt=Rt, R3=R3: e.tensor_tensor(out=r1[0:ntok, :], in0=Rt[0:ntok, :], in1=R3[0:ntok, 0, :],
                                                            op=ALU.subtract), reads=[rn, r3n], writes=["rw_r1"])
        P.op("act", lambda e, R3=R3: e.copy(out=R3[0:ntok, 1, :], in_=r1[0:ntok, :]), reads=["rw_r1"], writes=[r3n])
        P.op("pool", lambda e, R3=R3: e.tensor_tensor(out=r2[0:ntok, :], in0=r1[0:ntok, :], in1=R3[0:ntok, 1, :],
                                                     op=ALU.subtract), reads=["rw_r1", r3n], writes=["rw_r2"])
        P.op("act", lambda e, R3=R3: e.copy(out=R3[0:ntok, 2, :], in_=r2[0:ntok, :]), reads=["rw_r2"], writes=[r3n])
        P.dma("sp", lambda e, R3=R3, b=b: e.dma_start(out=rows_scr[b, :, t0:t0 + ntok, :].rearrange("p t c -> t p c"),
                                                      in_=R3[0:ntok, :, :]),
              reads=[r3n], writes=["rows%d_%d" % (b, tp)])
    P.op("pe", lambda e: e.transpose(out=R["vtp"][:, 0:ntok], in_=R["VV"][tp][0:ntok, :], identity=idf[0:ntok, 0:ntok]),
         reads=["rw_VV%d" % tp, "idf"], writes=["cl_ps"])
    P.op("act", lambda e: e.copy(out=R["vT"][tp][:, 0:ntok], in_=R["vtp"][:, 0:ntok]), reads=["cl_ps"], writes=["rw_vT%d" % tp])


def _rw_scan(P, c, R, n, ntok, rows_scr, q=None):
    P.nosame = {"dve"}
    R["q"] = q if q is not None else []
    R["per"] = -(-len(R["q"]) // max(1, ntok - 8))
    _rw_scan_body(P, c, R, n, ntok, rows_scr)
    P.nosame = set()
    P.drain(R["q"], len(R["q"]))


def _rw_scan_body(P, c, R, n, ntok, rows_scr):
    tp = n % 2
    t0 = n * ntok
    S, stmp, sk = R["S"], R["stmp"], R["sk"]
    vT, yT = R["vT"][tp], R["yT"][tp]
    vn, yn = "rw_vT%d" % tp, "rw_yT%d" % tp
    for blk in range(0, ntok, 16):
        nb = min(16, ntok - blk)
        rb = R["k"] % 2
        R["k"] += 1
        rowbuf = R["rowbuf"][rb]
        P.dma("sp", lambda e, rowbuf=rowbuf, blk=blk, nb=nb: e.dma_start(
            out=rowbuf[0:6, 0:nb * 320].rearrange("q (s c) -> q s c", c=320),
            in_=rows_scr[:, :, t0 + blk:t0 + blk + nb, :].rearrange("b p s c -> (b p) s c")),
            reads=["rows0_%d" % tp, "rows1_%d" % tp], writes=["rw_rowbuf%d" % rb])
        for s in range(nb):
            pb = R["step"] % 2
            R["step"] += 1
            rowp = R["rowp"][pb]
            pn = "sT%d" % pb
            t = blk + s
            P.op("pe", lambda e, rowp=rowp, rowbuf=rowbuf, s=s: e.matmul(
                rowp[:, 0:320], lhsT=R["selb"][0:6, :], rhs=rowbuf[0:6, s * 320:(s + 1) * 320], start=True, stop=True),
                reads=["rw_rowbuf%d" % rb, "rwc_selb"], writes=[pn])
            P.op("dve", lambda e, rowp=rowp: e.scalar_tensor_tensor(
                out=stmp[:, :], in0=S[:, :], scalar=1.0, in1=rowp[:, 64:128], op0=ALU.mult, op1=ALU.mult,
                accum_out=sk[:, 0:1]), reads=["rw_S", pn], writes=["rw_stmp", "rw_sk"])
            P.op("dve", lambda e, rowp=rowp: e.tensor_tensor(out=S[:, :], in0=S[:, :], in1=rowp[:, 0:64], op=ALU.mult),
                 reads=["rw_S", pn], writes=["rw_S"])
            P.op("dve", lambda e, rowp=rowp: e.scalar_tensor_tensor(
                out=S[:, :], in0=rowp[:, 128:192], scalar=sk[:, 0:1], in1=S[:, :], op0=ALU.mult, op1=ALU.add),
                reads=["rw_S", "rw_sk", pn], writes=["rw_S"])
            P.op("dve", lambda e, rowp=rowp, t=t: e.scalar_tensor_tensor(
                out=S[:, :], in0=rowp[:, 192:256], scalar=vT[:, t:t + 1], in1=S[:, :], op0=ALU.mult, op1=ALU.add),
                reads=["rw_S", vn, pn], writes=["rw_S"])
            P.op("dve", lambda e, rowp=rowp, t=t: e.scalar_tensor_tensor(
                out=stmp[:, :], in0=S[:, :], scalar=1.0, in1=rowp[:, 256:320], op0=ALU.mult, op1=ALU.mult,
                accum_out=yT[:, t:t + 1]), reads=["rw_S", pn], writes=["rw_stmp", yn])
            if R["q"]:
                P.drain(R["q"], R["per"])


def _rw_post(P, c, R, n, ntok, out_dst):
    tp = n % 2
    idf = c["idf"]
    ytp = R["ytp"]
    cen, ob, junk, mean, var = R["cen"], R["ob"], R["junk"], R["mean"], R["var"]
    ysb = R["ysb"]
    P.op("pe", lambda e: e.transpose(out=ytp[0:ntok, 0:128], in_=R["yT"][tp][:, 0:ntok], identity=idf[:, :]),
         reads=["rw_yT%d" % tp, "idf"], writes=["tot_ps"])
    P.op("act", lambda e: e.copy(out=ysb[0:ntok, :], in_=ytp[0:ntok, 0:128]), reads=["tot_ps"], writes=["rw_ysb"])
    for b in range(2):
        GB = R["GB"][b][tp]
        gn = "rw_GB%d_%d" % (b, tp)
        ysl = ysb[0:ntok, b * 64:(b + 1) * 64]
        P.op("dve", lambda e, ysl=ysl: e.tensor_reduce(out=mean[0:ntok, :], in_=ysl, axis=AX.X, op=ALU.add),
             reads=["rw_ysb"], writes=["rw_mean"])
        P.op("pool", lambda e: e.tensor_single_scalar(out=mean[0:ntok, :], in_=mean[0:ntok, :], scalar=1.0 / 64, op=ALU.mult),
             reads=["rw_mean"], writes=["rw_mean"])
        P.op("pool", lambda e, ysl=ysl: e.tensor_scalar(out=cen[0:ntok, :], in0=ysl, scalar1=mean[0:ntok, 0:1], scalar2=0.0,
                                                        op0=ALU.subtract, op1=ALU.add),
             reads=["rw_ysb", "rw_mean"], writes=["rw_cen"])
        P.op("dve", lambda e: e.scalar_tensor_tensor(out=junk[0:ntok, :], in0=cen[0:ntok, :], scalar=1.0, in1=cen[0:ntok, :],
                                                     op0=ALU.mult, op1=ALU.mult, accum_out=var[0:ntok, 0:1]),
             reads=["rw_cen"], writes=["rw_junk", "rw_var"])
        P.fence("dve", ["rw_var"])
        P.op("act", lambda e: e.activation(out=var[0:ntok, :], in_=var[0:ntok, :], func=AF.Sqrt, bias=64e-5, scale=1.0 / 64),
             reads=["rw_var"], writes=["rw_var"])
        P.op("dve", lambda e: e.reciprocal(out=var[0:ntok, :], in_=var[0:ntok, :]), reads=["rw_var"], writes=["rw_var"])
        P.op("pool", lambda e: e.tensor_scalar_mul(out=cen[0:ntok, :], in0=cen[0:ntok, :], scalar1=var[0:ntok, 0:1]),
             reads=["rw_cen", "rw_var"], writes=["rw_cen"])
        P.op("pool", lambda e: e.tensor_tensor(out=cen[0:ntok, :], in0=cen[0:ntok, :], in1=R["lnw"][0:ntok, :], op=ALU.mult),
             reads=["rw_cen", "rwc_par"], writes=["rw_cen"])
        P.op("pool", lambda e: e.tensor_tensor(out=cen[0:ntok, :], in0=cen[0:ntok, :], in1=R["lnb"][0:ntok, :], op=ALU.add),
             reads=["rw_cen", "rwc_par"], writes=["rw_cen"])
        P.op("pool", lambda e, GB=GB: e.tensor_tensor(out=cen[0:ntok, :], in0=cen[0:ntok, :], in1=GB[0:ntok, 64:128], op=ALU.add),
             reads=["rw_cen", gn], writes=["rw_cen"])
        P.op("pool", lambda e, GB=GB: e.tensor_tensor(out=ob[0:ntok, :], in0=cen[0:ntok, :], in1=GB[0:ntok, 0:64], op=ALU.mult),
             reads=["rw_cen", gn], writes=["rw_ob"])
        P.dma("sp", lambda e, b=b: e.dma_start(out=out_dst(b, n), in_=ob[0:ntok, :]), reads=["rw_ob"], writes=["rw_out"])


def _rw_pair(P, c, R, ntiles, ntok, cur_src, prev_src, rows_scr, S0_src, out_dst, ST_dst):
    S = R["S"]
    if S0_src is None:
        P.op("dve", lambda e: e.memset(S[:, :], 0.0), writes=["rw_S"])
    else:
        P.dma("sp", lambda e: e.dma_start(out=S[:, :], in_=S0_src), writes=["rw_S"])
    _rw_prep(P, c, R, 0, ntok, cur_src, prev_src, rows_scr)
    pend = []
    for n in range(ntiles):
        P.defer = pend
        if n + 1 < ntiles:
            _rw_prep(P, c, R, n + 1, ntok, cur_src, prev_src, rows_scr)
        P.defer = None
        _rw_scan(P, c, R, n, ntok, rows_scr, pend)
        pend = []
        P.defer = pend
        _rw_post(P, c, R, n, ntok, out_dst)
        P.defer = None
    P.drain(pend, len(pend))
    P.dma("sp", lambda e: e.dma_start(out=ST_dst, in_=S[:, :]), reads=["rw_S"], writes=["rw_STout"])


def build_phase2(n_prompt=2, n_sample=NSEQ_S, ngroups=32, rw_tiles=128, rw_pairs=8, do_fox=True):
    nc = bass.Bass("TRN2", target_bir_lowering=False)
    qTp = _din(nc, "qTp", [2, 64, TP])
    kTp = _din(nc, "kTp", [2, 64, TP])
    vp = _din(nc, "vp", [2, 128, 128, 64])
    lfp = _din(nc, "lfp", [2, 128, 128])
    qTs = _din(nc, "qTs", [NSEQ_S, 64, 16])
    kTs = _din(nc, "kTs", [NSEQ_S, 64, TS])
    vs = _din(nc, "vs", [NSEQ_S, 128, 17, 64])
    lfs = _din(nc, "lfs", [NSEQ_S, 128, 17])
    op_ = _dout(nc, "o_p", [2, TP, 64])
    os_ = _dout(nc, "o_s", [NSEQ_S, 16, 64])
    curp = _din(nc, "rw_curp", [2, TP, 352])
    prvp = _din(nc, "rw_prvp", [2, TP, 352])
    curs = _din(nc, "rw_curs", [NSEQ_S, 16, 352])
    prvs = _din(nc, "rw_prvs", [NSEQ_S, 16, 352])
    S0s = _din(nc, "rw_S0s", [8, 128, 64])
    rwp = _dout(nc, "rw_p", [2, TP, 64])
    rws = _dout(nc, "rw_s", [NSEQ_S, 16, 64])
    STp = _dout(nc, "ST_p", [128, 64])
    STs = _dout(nc, "ST_s", [8, 128, 64])
    rows_p = nc.dram_tensor("rows_p", [2, 3, TP, 320], BF16).ap()
    rows_s = nc.dram_tensor("rows_s", [8, 2, 3, 16, 320], BF16).ap()
    P = Prog(nc)
    c = _fox_consts(P, nc)
    B = _fox_bufs(P)
    if do_fox:
        for b in range(n_prompt):
            groups = [(512 * g, 512, 4 * g + 4, 4 * g, 4 * g + 2) for g in range(ngroups)]

            def out_fn(P, osb, q0, nq, nqi, wmax, b=b):
                P.dma("sp", lambda e: e.dma_start(
                    out=op_[b, q0:q0 + nq, :].rearrange("(a p) d -> p a d", p=128), in_=osb[:, 0:nqi, :]),
                    reads=["osb"], writes=["o_out"])
            _fox_seq(P, c, B, "p%d" % b, 128, qTp[b], TP, kTp[b], vp[b], lfp[b], groups, out_fn)
        for s in range(n_sample):
            groups = [(0, 16, 17, 16, 16)]

            def out_fn(P, osb, q0, nq, nqi, wmax, s=s):
                P.dma("sp", lambda e: e.dma_start(out=os_[s, :, :], in_=osb[0:16, 0, :]), reads=["osb"], writes=["o_out"])
            _fox_seq(P, c, B, "s%d" % s, 17, qTs[s], 16, kTs[s], vs[s], lfs[s], groups, out_fn)
    R = _rw_setup(P, nc, B)
    if rw_tiles > 0:
        _rw_pair(P, c, R, rw_tiles, 128,
                 lambda b, n: curp[b, n * 128:(n + 1) * 128, :], lambda b, n: prvp[b, n * 128:(n + 1) * 128, :],
                 rows_p, None, lambda b, n: rwp[b, n * 128:(n + 1) * 128, :], STp)
    for pr in range(rw_pairs):
        _rw_pair(P, c, R, 1, 16,
                 lambda b, n, pr=pr: curs[2 * pr + b, :, :], lambda b, n, pr=pr: prvs[2 * pr + b, :, :],
                 rows_s[pr], S0s[pr], lambda b, n, pr=pr: rws[2 * pr + b, :, :], STs[pr])
    P.emit()
    return nc


def rw_inputs(pp, psm, state_rwkv, state_shift, prm):
    ppb = pp.reshape(2, TP, IN_COLS)[:, :, FOX_COLS:]
    pss = psm.reshape(NSEQ_S, 16, IN_COLS)[:, :, FOX_COLS:]
    prev_p = np.zeros_like(ppb)
    prev_p[:, 1:] = ppb[:, :-1]
    prev_s = np.empty_like(pss)
    prev_s[:, 1:] = pss[:, :-1]
    prev_s[:, 0] = state_shift[:, 0, :]
    maps = []
    sel = np.zeros((6, 128), np.float32)
    sel[0:3, :64] = 1.0
    sel[3:6, 64:] = 1.0
    for h in range(NCORE):
        cols = np.concatenate([np.arange(h * 64, (h + 1) * 64), 512 + np.arange(h * 64, (h + 1) * 64),
                               1024 + np.arange(h * 64, (h + 1) * 64), np.arange(1536, 1696)])
        hs = slice(h * 64, (h + 1) * 64)
        par = np.concatenate([prm["rwkv_mu"][cols], prm["rwkv_w0"][hs], prm["rwkv_a0"][hs], prm["rwkv_k_k"][hs],
                              prm["rwkv_k_a"][hs], prm["rwkv_r_k"][h], prm["rwkv_lnx_w"][hs], prm["rwkv_lnx_b"][hs]])
        m = {
            "rw_curp": np.ascontiguousarray(ppb[:, :, cols]), "rw_prvp": np.ascontiguousarray(prev_p[:, :, cols]),
            "rw_curs": np.ascontiguousarray(pss[:, :, cols]), "rw_prvs": np.ascontiguousarray(prev_s[:, :, cols]),
            "rw_S0s": np.ascontiguousarray(state_rwkv[:, h].reshape(8, 128, 64)),
            "rw_par": np.ascontiguousarray(np.broadcast_to(par[None, :], (128, NPAR))).astype(np.float32),
            "rw_w2": np.ascontiguousarray(prm["rwkv_w2"][:, hs]), "rw_a2": np.ascontiguousarray(prm["rwkv_a2"][:, hs]),
            "rw_g2": np.ascontiguousarray(prm["rwkv_g2"][:, hs]), "rw_sel": sel,
        }
        maps.append(m)
    return maps


STAGE = 99


def build_phase3(NT, nslots=128):
    nc = bass.Bass("TRN2", target_bir_lowering=False)
    x = _din(nc, "x", [NT * 128, D])
    mixT = _din(nc, "mixT", [NT, 128, 8, 128])
    wout = _din(nc, "w_out", [D, D])
    wq = _din(nc, "w_q", [D, D])
    skT_d = _din(nc, "skT", [64, 16 * 128])
    g1 = _din(nc, "gffn", [128, D])
    g2 = _din(nc, "gfin", [128, D])
    identd = _din(nc, "ident", [128, 128])
    iota_d = _din(nc, "iota", [128, 256])
    pu = _din(nc, "peer_u", [16384, D])
    pv = _din(nc, "peer_v", [16384, D])
    y = _dout(nc, "y", [NT * 128, D])
    puv16 = nc.dram_tensor("puv16", [16384, 2 * D], BF16).ap()
    P = Prog(nc)
    cst = [P.sb("cst%d" % i, [128, 2048], F32) for i in range(2)]
    cbf = [P.sb("cbf%d" % i, [128, 2048], BF16) for i in range(2)]
    ck = 0
    if nslots > 0:
        for (src, half) in ((pu, 0), (pv, 1)):
            sv_ = src.rearrange("(p r) d -> p (r d)", p=128)
            dv_ = puv16.rearrange("(p r) d -> p r d", p=128)
            for pc in range(64):
                i2 = ck % 2
                P.dma("sp", lambda e, i2=i2, sv_=sv_, pc=pc: e.dma_start(out=cst[i2][:, :], in_=sv_[:, pc * 2048:(pc + 1) * 2048]),
                      writes=["cst%d" % i2])
                eng = ("act", "pool", "dve")[ck % 3]
                if eng == "act":
                    P.op("act", lambda e, i2=i2: e.copy(out=cbf[i2][:, :], in_=cst[i2][:, :]), reads=["cst%d" % i2], writes=["cbf%d" % i2])
                else:
                    P.op(eng, lambda e, i2=i2: e.tensor_copy(out=cbf[i2][:, :], in_=cst[i2][:, :]), reads=["cst%d" % i2],
                         writes=["cbf%d" % i2])
                P.dma("sp", lambda e, i2=i2, dv_=dv_, pc=pc, half=half: e.dma_start(
                    out=dv_[:, pc * 2:(pc + 1) * 2, half * D:(half + 1) * D], in_=cbf[i2][:, :].rearrange("p (r d) -> p r d", d=D)),
                    reads=["cbf%d" % i2], writes=["puv16"])
                ck += 1
    wst = P.sb("wst", [128, D], F32)
    wo_bf = P.sb("wo_bf", [128, 8, D], BF16)
    wq_bf = P.sb("wq_bf", [128, 8, D], BF16)
    sk_bf = P.sb("sk_bf", [64, 16, 128], BF16)
    g1s = P.sb("g1s", [128, D], F32)
    g2s = P.sb("g2s", [128, D], F32)
    id_f = P.sb("id_f", [128, 128], F32)
    id_b = P.sb("id_b", [128, 128], BF16)
    P.dma("sp", lambda e: e.dma_start(out=g1s[:, :], in_=g1), writes=["g1"])
    P.dma("sp", lambda e: e.dma_start(out=g2s[:, :], in_=g2), writes=["g2"])
    P.dma("sp", lambda e: e.dma_start(out=id_f[:, :], in_=identd), writes=["idf"])
    P.op("dve", lambda e: e.tensor_copy(out=id_b[:, :], in_=id_f[:, :]), reads=["idf"], writes=["idb"])
    for (src, dst, nm) in ((wout, wo_bf, "wo"), (wq, wq_bf, "wq")):
        for dc in range(8):
            P.dma("sp", lambda e, src=src, dc=dc: e.dma_start(out=wst[:, :], in_=src[dc * 128:(dc + 1) * 128, :]), writes=["wst"])
            P.op("act", lambda e, dst=dst, dc=dc: e.copy(out=dst[:, dc, :], in_=wst[:, :]), reads=["wst"], writes=[nm])
    for hf in range(2):
        P.dma("sp", lambda e, hf=hf: e.dma_start(out=wst[0:64, :], in_=skT_d[:, hf * 1024:(hf + 1) * 1024]), writes=["wst"])
        P.op("act", lambda e, hf=hf: e.copy(out=sk_bf[:, hf * 8:(hf + 1) * 8, :],
                                            in_=wst[0:64, :].rearrange("p (h n) -> p h n", h=8)), reads=["wst"], writes=["sk"])

    xt = P.sb("xt", [128, D], F32)
    mt = P.sb("mt", [128, 8, 128], F32)
    mtb = P.sb("mtb", [128, 8, 128], BF16)
    x2_ = [P.sb("x2_%d" % t, [128, D], F32) for t in range(2)]
    xn_ = [P.sb("xn_%d" % t, [128, D], F32) for t in range(2)]
    xnb = P.sb("xnb", [128, D], BF16)
    xnT = P.sb("xnT", [128, D], BF16)
    qT = P.sb("qT", [64, 16 * 128], BF16)
    junkb = P.sb("junkb", [128, D], BF16)
    junkD = P.sb("junkD", [128, D], F32)
    ss = P.sb("ss", [128, 1], F32)
    s_sb = P.sb("s_sb", [128, 16, 128], F32)
    s2 = P.sb("s2", [128, 128], F32)
    sv = P.sb("sv", [128, 16, 16], F32)
    si = P.sb("si", [128, 16, 16], U32)
    sif = P.sb("sif", [128, 16, 16], F32)
    cand = P.sb("cand", [128, 8, 256], F32)
    cidx = P.sb("cidx", [128, 8, 256], F32)
    c2 = P.sb("c2", [128, 256], F32)
    j256 = P.sb("j256", [128, 256], F32)
    tv = P.sb("tv", [128, 8, 16], F32)
    pi = P.sb("pi", [128, 8, 16], U32)
    pif = P.sb("pif", [128, 8, 16], F32)
    iota = P.sb("iota", [128, 256], F32)
    P.dma("sp", lambda e: e.dma_start(out=iota[:, :], in_=iota_d), writes=["iota"])
    negm = P.sb("negm", [128, 8], F32)
    gt_ = [P.sb("gt_%d" % t, [128, 8, 16], F32) for t in range(2)]
    Z = P.sb("Z", [128, 8], F32)
    ef = P.sb("ef", [128, 128], F32)
    ei_ = [P.sb("ei_%d" % t, [128, 128], I32) for t in range(2)]
    apre = P.sb("apre", [128, 128], F32)
    coef = P.sb("coef", [128, 128], F32)
    acc = P.sb("acc", [128, D], F32)
    yo = P.sb("yo", [128, D], F32)
    NG = 7
    gb = [P.sb("gbuf%d" % i, [128, 2 * D], BF16) for i in range(NG)]
    gl = P.sb("gl", [128, 128], F32)
    psA = P.ps("psA", [128, D], F32)
    psT = P.ps("psT", [128, D], BF16)
    psS = P.ps("psS", [128, 16, 128], F32)
    gk = 0

    def rms(src, srcn, dst, dstn, gs, gn):
        P.op("act", lambda e: e.activation(out=junkb[:, :], in_=src[:, :], func=AF.Square, accum_out=ss[:, 0:1]),
             reads=[srcn], writes=["junkb", "ss"])
        P.op("act", lambda e: e.activation(out=ss[:, :], in_=ss[:, :], func=AF.Sqrt, bias=1e-6, scale=1.0 / D),
             reads=["ss"], writes=["ss"])
        P.op("dve", lambda e: e.reciprocal(out=ss[:, :], in_=ss[:, :]), reads=["ss"], writes=["ss"])
        P.op("dve", lambda e: e.scalar_tensor_tensor(out=dst[:, :], in0=src[:, :], scalar=ss[:, 0:1], in1=gs[:, :],
                                                     op0=ALU.mult, op1=ALU.mult), reads=[srcn, "ss", gn], writes=[dstn])

    def head_(i):
        P.dma("sp", lambda e, i=i: e.dma_start(out=xt[:, :], in_=x[i * 128:(i + 1) * 128, :]), writes=["xt"])
        P.dma("sp", lambda e, i=i: e.dma_start(out=mt[:, :, :], in_=mixT[i]), writes=["mt"])
        P.op("act", lambda e, i=i: e.copy(out=mtb[:, :, :], in_=mt[:, :, :]), reads=["mt"], writes=["mtb"])
        for g in range(2):
            for kc in range(8):
                P.op("pe", lambda e, g=g, kc=kc, i=i: e.matmul(psA[:, g * 512:(g + 1) * 512], lhsT=mtb[:, kc, :],
                                                          rhs=wo_bf[:, kc, g * 512:(g + 1) * 512], start=(kc == 0), stop=(kc == 7)),
                     reads=["mtb", "wo"], writes=["psA%d" % g])
        for g in range(2):
            P.op("dve", lambda e, g=g, i=i: e.tensor_tensor(out=x2_[i % 2][:, g * 512:(g + 1) * 512], in0=psA[:, g * 512:(g + 1) * 512],
                                                       in1=xt[:, g * 512:(g + 1) * 512], op=ALU.add),
                 reads=["psA%d" % g, "xt"], writes=["x2%d" % (i % 2)])
        rms(x2_[i % 2], "x2%d" % (i % 2), xn_[i % 2], "xn%d" % (i % 2), g1s, "g1")
        P.op("act", lambda e, i=i: e.copy(out=xnb[:, :], in_=xn_[i % 2][:, :]), reads=["xn%d" % (i % 2)], writes=["xnb"])
        for dc in range(8 if STAGE >= 2 else 0):
            P.op("pe", lambda e, dc=dc, i=i: e.transpose(out=psT[:, dc * 128:(dc + 1) * 128], in_=xnb[:, dc * 128:(dc + 1) * 128],
                                                    identity=id_b[:, :]), reads=["xnb", "idb"], writes=["psT"])
        P.op("act", lambda e, i=i: e.copy(out=xnT[:, :], in_=psT[:, :]), reads=["psT"], writes=["xnT"])
        for hc in range(16 if STAGE >= 3 else 0):
            bank = hc % 2
            for dc in range(8):
                P.op("pe", lambda e, hc=hc, dc=dc, bank=bank, i=i: e.matmul(
                    psA[0:64, bank * 512:bank * 512 + 128], lhsT=wq_bf[:, dc, hc * 64:(hc + 1) * 64],
                    rhs=xnT[:, dc * 128:(dc + 1) * 128], start=(dc == 0), stop=(dc == 7)),
                    reads=["xnT", "wq"], writes=["psA%d" % bank])
            if hc % 2 == 0:
                P.op("act", lambda e, hc=hc, bank=bank, i=i: e.copy(out=qT[0:64, hc * 128:(hc + 1) * 128],
                                                               in_=psA[0:64, bank * 512:bank * 512 + 128]),
                     reads=["psA%d" % bank], writes=["qT"])
            else:
                P.op("dve", lambda e, hc=hc, bank=bank, i=i: e.tensor_copy(out=qT[0:64, hc * 128:(hc + 1) * 128],
                                                                      in_=psA[0:64, bank * 512:bank * 512 + 128]),
                     reads=["psA%d" % bank], writes=["qT"])
        for hc in range(16 if STAGE >= 4 else 0):
            P.op("pe", lambda e, hc=hc, i=i: e.matmul(psS[:, hc, :], lhsT=qT[0:64, hc * 128:(hc + 1) * 128],
                                                 rhs=sk_bf[0:64, hc, :], start=True, stop=True),
                 reads=["qT", "sk"], writes=["psS"])
        for bk in range(4):
            if bk % 2 == 0:
                P.op("act", lambda e, bk=bk, i=i: e.copy(out=s_sb[:, 4 * bk:4 * bk + 4, :], in_=psS[:, 4 * bk:4 * bk + 4, :]),
                     reads=["psS"], writes=["s_sb"])
            else:
                P.op("dve", lambda e, bk=bk, i=i: e.tensor_copy(out=s_sb[:, 4 * bk:4 * bk + 4, :], in_=psS[:, 4 * bk:4 * bk + 4, :]),
                     reads=["psS"], writes=["s_sb"])
        for hc in range(16 if STAGE >= 5 else 0):
            P.op("dve", lambda e, hc=hc, i=i: e.max(out=sv[:, hc, 0:8], in_=s_sb[:, hc, :]), reads=["s_sb"], writes=["sv"])
            P.op("dve", lambda e, hc=hc, i=i: e.max_index(out=si[:, hc, 0:8], in_max=sv[:, hc, 0:8], in_values=s_sb[:, hc, :]),
                 reads=["s_sb", "sv"], writes=["si"])
            P.op("dve", lambda e, hc=hc, i=i: e.match_replace(out=s2[:, :], in_to_replace=sv[:, hc, 0:8], in_values=s_sb[:, hc, :],
                                                         imm_value=-1e30), reads=["s_sb", "sv"], writes=["s2"])
            P.op("dve", lambda e, hc=hc, i=i: e.max(out=sv[:, hc, 8:16], in_=s2[:, :]), reads=["s2"], writes=["sv"])
            P.op("dve", lambda e, hc=hc, i=i: e.max_index(out=si[:, hc, 8:16], in_max=sv[:, hc, 8:16], in_values=s2[:, :]),
                 reads=["s2", "sv"], writes=["si"])
        P.op("dve", lambda e, i=i: e.tensor_copy(out=sif[:, :, :], in_=si[:, :, :]), reads=["si"], writes=["sif"])
        for h in range(8 if STAGE >= 6 else 0):
            P.op("dve", lambda e, h=h, i=i: e.tensor_single_scalar(out=sif[:, 2 * h, :], in_=sif[:, 2 * h, :], scalar=128.0, op=ALU.mult),
                 reads=["sif"], writes=["sif"])
            P.op("dve", lambda e, h=h, i=i: e.tensor_tensor(
                out=cand[:, h, :].rearrange("p (a b) -> p a b", a=16),
                in0=sv[:, 2 * h, :].unsqueeze(2).to_broadcast([128, 16, 16]),
                in1=sv[:, 2 * h + 1, :].unsqueeze(1).to_broadcast([128, 16, 16]), op=ALU.add), reads=["sv"], writes=["cand"])
            P.op("dve", lambda e, h=h, i=i: e.tensor_tensor(
                out=cidx[:, h, :].rearrange("p (a b) -> p a b", a=16),
                in0=sif[:, 2 * h, :].unsqueeze(2).to_broadcast([128, 16, 16]),
                in1=sif[:, 2 * h + 1, :].unsqueeze(1).to_broadcast([128, 16, 16]), op=ALU.add), reads=["sif"], writes=["cidx"])
            P.op("dve", lambda e, h=h, i=i: e.max(out=tv[:, h, 0:8], in_=cand[:, h, :]), reads=["cand"], writes=["tv"])
            P.op("dve", lambda e, h=h, i=i: e.max_index(out=pi[:, h, 0:8], in_max=tv[:, h, 0:8], in_values=cand[:, h, :]),
                 reads=["cand", "tv"], writes=["pi"])
            P.op("dve", lambda e, h=h, i=i: e.match_replace(out=c2[:, :], in_to_replace=tv[:, h, 0:8], in_values=cand[:, h, :],
                                                       imm_value=-1e30), reads=["cand", "tv"], writes=["c2"])
            P.op("dve", lambda e, h=h, i=i: e.max(out=tv[:, h, 8:16], in_=c2[:, :]), reads=["c2"], writes=["tv"])
            P.op("dve", lambda e, h=h, i=i: e.max_index(out=pi[:, h, 8:16], in_max=tv[:, h, 8:16], in_values=c2[:, :]),
                 reads=["c2", "tv"], writes=["pi"])
            P.op("dve", lambda e, h=h, i=i: e.tensor_copy(out=pif[:, h, :], in_=pi[:, h, :]), reads=["pi"], writes=["pif"])
            for k in range(16):
                P.op("dve", lambda e, h=h, k=k, i=i: e.scalar_tensor_tensor(
                    out=j256[:, :], in0=iota[:, :], scalar=pif[:, h, k:k + 1], in1=cidx[:, h, :], op0=ALU.is_equal,
                    op1=ALU.mult, accum_out=ef[:, h * 16 + k:h * 16 + k + 1]), reads=["iota", "cidx", "pif"], writes=["j256", "ef"])
        P.op("dve", lambda e, i=i: e.tensor_copy(out=ei_[i % 2][:, :], in_=ef[:, :]), reads=["ef"], writes=["ei%d" % (i % 2)])
        P.op("dve", lambda e, i=i: e.tensor_single_scalar(out=negm[:, :], in_=tv[:, :, 0], scalar=-1.0, op=ALU.mult),
             reads=["tv"], writes=["negm"])
        for h in range(8 if STAGE >= 7 else 0):
            P.op("act", lambda e, h=h, i=i: e.activation(out=gt_[i % 2][:, h, :], in_=tv[:, h, :], func=AF.Exp, bias=negm[:, h:h + 1],
                                                    scale=1.0, accum_out=Z[:, h:h + 1]), reads=["tv", "negm"], writes=["gt%d" % (i % 2), "Z"])
        P.fence("act", ["Z", "gt%d" % (i % 2)])
        P.op("dve", lambda e, i=i: e.reciprocal(out=Z[:, :], in_=Z[:, :]), reads=["Z"], writes=["Z"])
        for h in range(8):
            P.op("dve", lambda e, h=h, i=i: e.tensor_scalar_mul(out=gt_[i % 2][:, h, :], in0=gt_[i % 2][:, h, :], scalar1=Z[:, h:h + 1]),
                 reads=["gt%d" % (i % 2), "Z"], writes=["gt%d" % (i % 2)])

    def loop_(i, q):
        nonlocal gk
        per = -(-len(q) // max(1, nslots - 16)) if nslots > 0 else 0
        gtf = gt_[i % 2][:, :, :].rearrange("p h k -> p (h k)")
        slot_buf = {}

        def vacc(sl):
            r = slot_buf[sl]
            P.op("dve", lambda e, sl=sl, i=i: e.tensor_tensor(out=coef[:, sl:sl + 1], in0=gl[:, sl:sl + 1], in1=gtf[:, sl:sl + 1], op=ALU.mult),
                 reads=["gl%d" % sl, "gt%d" % (i % 2)], writes=["coef%d" % sl])
            if sl == 0:
                P.op("dve", lambda e, r=r, i=i: e.tensor_scalar_mul(out=acc[:, :], in0=gb[r][:, D:2 * D], scalar1=coef[:, 0:1]),
                     reads=["gbuf%d" % r, "coef0"], writes=["acc"])
            else:
                P.op("dve", lambda e, r=r, sl=sl, i=i: e.scalar_tensor_tensor(
                    out=acc[:, :], in0=gb[r][:, D:2 * D], scalar=coef[:, sl:sl + 1], in1=acc[:, :], op0=ALU.mult, op1=ALU.add),
                    reads=["gbuf%d" % r, "coef%d" % sl, "acc"], writes=["acc"])

        LAG = 3
        for sl in range(nslots):
            r = gk % NG
            gk += 1
            slot_buf[sl] = r
            P.dma("pool", lambda e, r=r, sl=sl, i=i: e.indirect_dma_start(
                out=gb[r][:, :], out_offset=None, in_=puv16[:, :],
                in_offset=bass.IndirectOffsetOnAxis(ap=ei_[i % 2][:, sl:sl + 1], axis=0)), reads=["ei%d" % (i % 2), "puv16"], writes=["gbuf%d" % r])
            P.op("dve", lambda e, r=r, sl=sl, i=i: e.scalar_tensor_tensor(
                out=junkD[:, :], in0=gb[r][:, 0:D], scalar=1.0, in1=xn_[i % 2][:, :], op0=ALU.mult, op1=ALU.mult,
                accum_out=apre[:, sl:sl + 1]), reads=["gbuf%d" % r, "xn%d" % (i % 2)], writes=["junkD", "apre_raw%d" % sl])
            P.fence("dve", ["apre%d" % sl])
            P.op("act", lambda e, sl=sl, i=i: e.activation(out=gl[:, sl:sl + 1], in_=apre[:, sl:sl + 1], func=AF.Gelu),
                 reads=["apre%d" % sl], writes=["gl%d" % sl])
            if sl >= LAG:
                vacc(sl - LAG)
            if q:
                P.drain(q, per)
        for sl in range(max(0, nslots - LAG), nslots):
            vacc(sl)
        P.drain(q, len(q))

    def tail_(i):
        if nslots > 0:
            P.op("dve", lambda e, i=i: e.tensor_tensor(out=x2_[i % 2][:, :], in0=x2_[i % 2][:, :], in1=acc[:, :], op=ALU.add),
                 reads=["x2%d" % (i % 2), "acc"], writes=["x2%d" % (i % 2)])
        rms(x2_[i % 2], "x2%d" % (i % 2), yo, "yo", g2s, "g2")
        if nslots == 0:
            P.op("dve", lambda e, i=i: e.tensor_copy(out=yo[:, 0:128], in_=ef[:, :]), reads=["ef", "yo"], writes=["yo"])
            P.op("dve", lambda e, i=i: e.tensor_copy(out=yo[:, 128:256], in_=gt_[i % 2][:, :, :].rearrange("p h k -> p (h k)")),
                 reads=["gt%d" % (i % 2), "yo"], writes=["yo"])
        P.dma("sp", lambda e, i=i: e.dma_start(out=y[i * 128:(i + 1) * 128, :], in_=yo[:, :]), reads=["yo"], writes=["yout"])

    head_(0)
    for i in range(NT):
        q = []
        if i + 1 < NT:
            P.defer = q
            head_(i + 1)
            P.defer = None
        loop_(i, q)
        tail_(i)
    P.emit()
    return nc


def run_phase3(x_prompt, x_sample, mix_p, mix_s, w_out, norm_ffn_g, peer_w_q, peer_sub_keys, peer_u, peer_v, norm_final_g,
               NT=33, nslots=128):
    xp = np.ascontiguousarray(x_prompt).reshape(-1, D)
    xs = np.ascontiguousarray(x_sample).reshape(-1, D)
    nc = build_phase3(NT, nslots)
    g1 = np.ascontiguousarray(np.broadcast_to(norm_ffn_g.reshape(1, D), (128, D))).astype(np.float32)
    g2 = np.ascontiguousarray(np.broadcast_to(norm_final_g.reshape(1, D), (128, D))).astype(np.float32)
    skT = np.ascontiguousarray(np.transpose(peer_sub_keys, (3, 0, 1, 2)).reshape(64, 16 * 128))
    iota_h = np.ascontiguousarray(np.broadcast_to(np.arange(256, dtype=np.float32)[None, :], (128, 256)))
    in_maps = []
    for c in range(NCORE):
        xc = np.zeros((33 * 128, D), np.float32)
        mc = np.zeros((33 * 128, D), np.float32)
        xc[:4096] = xp[c * 4096:(c + 1) * 4096]
        xc[4096:4128] = xs[c * 32:(c + 1) * 32]
        mc[:4096] = mix_p[c * 4096:(c + 1) * 4096]
        mc[4096:4128] = mix_s[c * 32:(c + 1) * 32]
        mT = np.ascontiguousarray(np.transpose(mc.reshape(33, 128, 8, 128), (0, 3, 2, 1)))
        in_maps.append({"x": xc[:NT * 128], "mixT": mT[:NT], "w_out": np.ascontiguousarray(w_out), "w_q": np.ascontiguousarray(peer_w_q),
                        "skT": skT, "iota": iota_h, "gffn": g1, "gfin": g2, "ident": _ident(), "peer_u": np.ascontiguousarray(peer_u),
                        "peer_v": np.ascontiguousarray(peer_v)})
    res = run_bass_kernel_spmd(nc, in_maps, core_ids=list(range(NCORE)))
    yp = np.concatenate([res.results[c]["y"][:4096] for c in range(NCORE)], axis=0) if NT == 33 else None
    ysm = np.concatenate([res.results[c]["y"][4096:4128] for c in range(NCORE)], axis=0) if NT == 33 else None
    return yp, ysm, res


def kernel(x_prompt, x_sample, cache_fox_k, cache_fox_v, cache_fox_logf, state_rwkv, state_shift,
           norm_mix_g, w_in, fox_b_f, rwkv_mu, rwkv_w0, rwkv_w2, rwkv_a0, rwkv_a2, rwkv_g2,
           rwkv_k_k, rwkv_k_a, rwkv_r_k, rwkv_lnx_w, rwkv_lnx_b, w_out, norm_ffn_g,
           peer_w_q, peer_sub_keys, peer_u, peer_v, norm_final_g):
    f = lambda a: np.asarray(a, dtype=np.float32)
    x_prompt, x_sample = f(x_prompt), f(x_sample)
    pp, psm = run_phase1(x_prompt, x_sample, f(norm_mix_g)[0], f(w_in)[0], f(fox_b_f)[0])
    prm = {"rwkv_mu": f(rwkv_mu)[0], "rwkv_w0": f(rwkv_w0)[0], "rwkv_w2": f(rwkv_w2)[0], "rwkv_a0": f(rwkv_a0)[0],
           "rwkv_a2": f(rwkv_a2)[0], "rwkv_g2": f(rwkv_g2)[0], "rwkv_k_k": f(rwkv_k_k)[0], "rwkv_k_a": f(rwkv_k_a)[0],
           "rwkv_r_k": f(rwkv_r_k)[0], "rwkv_lnx_w": f(rwkv_lnx_w)[0], "rwkv_lnx_b": f(rwkv_lnx_b)[0]}
    maps = fox_inputs(pp, psm, f(cache_fox_k)[0], f(cache_fox_v)[0], f(cache_fox_logf)[0])
    rmaps = rw_inputs(pp, psm, f(state_rwkv)[0], f(state_shift)[0], prm)
    for m, r in zip(maps, rmaps):
        m.update(r)
    nc2 = build_phase2()
    res2 = run_bass_kernel_spmd(nc2, maps, core_ids=list(range(NCORE)))
    del maps, rmaps
    R2 = res2.results
    mix_p = np.empty((2, TP, 1024), np.float32)
    mix_s = np.empty((NSEQ_S, 16, 1024), np.float32)
    S_p = np.empty((1, 2, 8, 64, 64), np.float32)
    S_s = np.empty((1, NSEQ_S, 8, 64, 64), np.float32)
    for h in range(NCORE):
        mix_p[:, :, h * 64:(h + 1) * 64] = R2[h]["o_p"]
        mix_p[:, :, 512 + h * 64:512 + (h + 1) * 64] = R2[h]["rw_p"]
        mix_s[:, :, h * 64:(h + 1) * 64] = R2[h]["o_s"]
        mix_s[:, :, 512 + h * 64:512 + (h + 1) * 64] = R2[h]["rw_s"]
        S_p[0, :, h] = R2[h]["ST_p"].reshape(2, 64, 64)
        S_s[0, :, h] = R2[h]["ST_s"].reshape(NSEQ_S, 64, 64)
    yp, ysm, _ = run_phase3(x_prompt, x_sample, mix_p.reshape(-1, 1024), mix_s.reshape(-1, 1024), f(w_out)[0],
                            f(norm_ffn_g)[0], f(peer_w_q)[0], f(peer_sub_keys)[0], f(peer_u)[0], f(peer_v)[0],
                            f(norm_final_g))
    ppb = pp.reshape(2, TP, IN_COLS)
    pss = psm.reshape(NSEQ_S, 16, IN_COLS)
    c = np.ascontiguousarray
    return (
        c(yp.reshape(2, TP, 1024)), c(ysm.reshape(NSEQ_S, 16, 1024)),
        c(ppb[:, :, 512:1024].reshape(1, 2, TP, 8, 64)), c(ppb[:, :, 1024:1536].reshape(1, 2, TP, 8, 64)),
        c(ppb[:, :, 1536:1544].reshape(1, 2, TP, 8)), S_p, c(ppb[:, -1:, FOX_COLS:].reshape(1, 2, 1, RW_COLS)),
        c(pss[:, :, 512:1024].reshape(1, NSEQ_S, 16, 8, 64)), c(pss[:, :, 1024:1536].reshape(1, NSEQ_S, 16, 8, 64)),
        c(pss[:, :, 1536:1544].reshape(1, NSEQ_S, 16, 8)), S_s, c(pss[:, -1:, FOX_COLS:].reshape(1, NSEQ_S, 1, RW_COLS)),
    )
```
